# Optimizing a Trainium2 kernel written in Bass

```python
import math
import jax, jax.numpy as jnp
from jax import lax
import numpy as np

D_MODEL = 1024
BATCH = 16
SEQ = 2048
DEPTH = 2
DEC_BATCH = 128
DEC_SEQ = 4
PAST_LEN = 16384
PAGE_SIZE = 128

HEAD_DIM = 64
N_MIXERS = 4
GROUP_WIDTH = D_MODEL // N_MIXERS
MIX_WIDTH = N_MIXERS * GROUP_WIDTH
NORM_EPS = 1e-6
RW_WIDTH = GROUP_WIDTH
RW_HEADS = RW_WIDTH // HEAD_DIM
RW_DECAY_RANK = 64
RW_ICLR_RANK = 64
RW_GATE_RANK = 128
RW_LN_EPS = 64e-5
RW_SPLITS = (RW_WIDTH, RW_WIDTH, RW_WIDTH, RW_DECAY_RANK, RW_ICLR_RANK, RW_GATE_RANK)
RW_PROJ = sum(RW_SPLITS)
S5_WIDTH = GROUP_WIDTH
S5_CH = 16
S5_GROUPS = S5_WIDTH // S5_CH
S5_STATE = 64
GDN_WIDTH = GROUP_WIDTH
GDN_HEADS = GDN_WIDTH // HEAD_DIM
GDN_CONV = 4
GDN_CHUNK = 64
GDN_CONV_CH = 3 * GDN_WIDTH
GDN_SPLITS = (GDN_CONV_CH, GDN_HEADS, GDN_HEADS, GDN_WIDTH)
GDN_PROJ = sum(GDN_SPLITS)
SWA_WIDTH = GROUP_WIDTH
SWA_HEADS = SWA_WIDTH // HEAD_DIM
SWA_KV_HEADS = max(1, SWA_HEADS // 2)
SWA_GROUP = SWA_HEADS // SWA_KV_HEADS
WINDOW = 128
SWA_SPLITS = (SWA_WIDTH, SWA_KV_HEADS * HEAD_DIM, SWA_KV_HEADS * HEAD_DIM)
SWA_PROJ = sum(SWA_SPLITS)
ROPE_DIM = HEAD_DIM // 4
ROPE_THETA = 500000.0
PROJ_SPLITS = (RW_PROJ, S5_WIDTH, GDN_PROJ, SWA_PROJ)
PROJ_WIDTH = sum(PROJ_SPLITS)
D_FF = 4 * D_MODEL
N_STATE_KINDS = 8

kernel_name = 'hybrid_rwkv7_s5_gdn_swa_decode_step'


def split_cols(x, sizes):
    idx = np.cumsum(sizes)[:-1].tolist()
    return jnp.split(x, idx, axis=-1)


def rms_norm(x, g):
    x32 = x.astype(jnp.float32)
    return x32 * lax.rsqrt(jnp.mean(x32 * x32, axis=-1, keepdims=True) + NORM_EPS) * g


def l2_normalize(x):
    return x * lax.rsqrt(jnp.maximum(jnp.sum(x * x, axis=-1, keepdims=True), 1e-12))


def causal_dwconv(x, buf, w):
    width = w.shape[0]
    L = x.shape[1]
    xp = jnp.concatenate([buf, x], axis=1)
    y = sum(xp[:, i:i + L] * w[i] for i in range(width))
    return y, xp[:, L:]


def partial_rope(x, pos):
    half = ROPE_DIM // 2
    inv = ROPE_THETA ** (-jnp.arange(0, ROPE_DIM, 2, dtype=jnp.float32) / ROPE_DIM)
    ang = pos.astype(jnp.float32)[:, None] * inv[None, :]
    cos = jnp.cos(ang)[None, :, None, :]
    sin = jnp.sin(ang)[None, :, None, :]
    x1 = x[..., :half]
    x2 = x[..., half:ROPE_DIM]
    return jnp.concatenate([x1 * cos - x2 * sin, x2 * cos + x1 * sin, x[..., ROPE_DIM:]], axis=-1)


def rwkv7_mixer(p, shift_buf, S0, prm):
    B, L, _ = p.shape
    prev = jnp.concatenate([shift_buf, p[:, :-1]], axis=1)
    xs = p + (prev - p) * prm['rw_mu']
    r, k, v, wd, ad, gd = split_cols(xs, RW_SPLITS)
    z_w = prm['rw_w0'] + jnp.tanh(wd) @ prm['rw_w2']
    w = jnp.exp(-jnp.exp(-jax.nn.softplus(-z_w) - 0.5))
    a = jax.nn.sigmoid(prm['rw_a0'] + ad @ prm['rw_a2'])
    g = jax.nn.sigmoid(gd) @ prm['rw_g2']
    heads = lambda t: t.reshape(B, L, RW_HEADS, HEAD_DIM)
    kk = l2_normalize(heads(k * prm['rw_kk']))
    k = k * (1.0 + (a - 1.0) * prm['rw_ka'])
    r, w, k, v, a = map(heads, (r, w, k, v, a))

    def step(S, inp):
        r_t, w_t, k_t, v_t, kk_t, a_t = inp
        sa = jnp.einsum('bhvk,bhk->bhv', S, -kk_t)
        S = (S * w_t[:, :, None, :] + sa[..., None] * (kk_t * a_t)[:, :, None, :]
             + v_t[..., None] * k_t[:, :, None, :])
        return S, jnp.einsum('bhvk,bhk->bhv', S, r_t)

    seq = tuple(jnp.moveaxis(t, 1, 0) for t in (r, w, k, v, kk, a))
    S, y = lax.scan(step, S0, seq)
    y = jnp.moveaxis(y, 0, 1)
    mu = jnp.mean(y, axis=-1, keepdims=True)
    var = jnp.mean(jnp.square(y - mu), axis=-1, keepdims=True)
    yn = ((y - mu) * lax.rsqrt(var + RW_LN_EPS)).reshape(B, L, RW_WIDTH) * prm['rw_ln_w'] + prm['rw_ln_b']
    bonus = (jnp.sum(r * k * prm['rw_rk'], axis=-1, keepdims=True) * v).reshape(B, L, RW_WIDTH)
    return (yn + bonus) * g, p[:, -1:], S


def _complex_affine_combine(e1, e2):
    a1r, a1i, b1r, b1i = e1
    a2r, a2i, b2r, b2i = e2
    return (a2r * a1r - a2i * a1i, a2r * a1i + a2i * a1r,
            a2r * b1r - a2i * b1i + b2r, a2r * b1i + a2i * b1r + b2i)


def s5_mixer(u, x0_re, x0_im, prm):
    B, L, _ = u.shape
    ug = u.reshape(B, L, S5_GROUPS, S5_CH)
    a_re, a_im = prm['s5_a_re'], prm['s5_a_im']
    dt = jnp.exp(prm['s5_log_dt'])[:, None]
    mag = jnp.exp(dt * a_re)
    ab_re = mag * jnp.cos(dt * a_im)
    ab_im = mag * jnp.sin(dt * a_im)
    den = a_re * a_re + a_im * a_im
    nr = ab_re - 1.0
    cf_re = (nr * a_re + ab_im * a_im) / den
    cf_im = (ab_im * a_re - nr * a_im) / den
    bb_re = cf_re[..., None] * prm['s5_b_re'] - cf_im[..., None] * prm['s5_b_im']
    bb_im = cf_re[..., None] * prm['s5_b_im'] + cf_im[..., None] * prm['s5_b_re']
    bu_re = jnp.einsum('blgc,gnc->blgn', ug, bb_re)
    bu_im = jnp.einsum('blgc,gnc->blgn', ug, bb_im)
    elems = (jnp.broadcast_to(ab_re, bu_re.shape), jnp.broadcast_to(ab_im, bu_im.shape), bu_re, bu_im)
    p_re, p_im, h_re, h_im = lax.associative_scan(_complex_affine_combine, elems, axis=1)
    h_re = h_re + p_re * x0_re[:, None] - p_im * x0_im[:, None]
    h_im = h_im + p_re * x0_im[:, None] + p_im * x0_re[:, None]
    y = (jnp.einsum('blgn,gcn->blgc', h_re, prm['s5_c_re'])
         - jnp.einsum('blgn,gcn->blgc', h_im, prm['s5_c_im']))
    y = y.reshape(B, L, S5_WIDTH) + prm['s5_d'] * u
    z = jax.nn.gelu(y)
    out = z * jax.nn.sigmoid(z @ prm['s5_w_glu'] + prm['s5_b_glu'])
    return out, h_re[:, -1], h_im[:, -1]


def gated_delta_chunked(q, k, v, beta, g, S0):
    B, L, H, D = q.shape
    C = GDN_CHUNK if L % GDN_CHUNK == 0 else L
    n = L // C

    def chunks(t):
        t = t.reshape((B, n, C) + t.shape[2:])
        return jnp.moveaxis(jnp.moveaxis(t, 1, 0), 3, 2)

    incl = jnp.tril(jnp.ones((C, C), dtype=bool))
    strict = jnp.tril(jnp.ones((C, C), dtype=bool), -1)
    eye = jnp.eye(C, dtype=jnp.float32)

    def step(S, inp):
        qc, kc, vc, bc, gc = inp
        G = jnp.cumsum(gc, axis=-1)
        decay = jnp.exp(jnp.where(incl, G[..., :, None] - G[..., None, :], -jnp.inf))
        A = jnp.where(strict, bc[..., :, None] * decay * jnp.einsum('bhtd,bhjd->bhtj', kc, kc), 0.0)
        gam = jnp.exp(G)[..., None]
        rhs = bc[..., None] * (vc - gam * jnp.einsum('bhvk,bhtk->bhtv', S, kc))
        U = lax.linalg.triangular_solve(eye + A, rhs, left_side=True, lower=True, unit_diagonal=True)
        o = (gam * jnp.einsum('bhvk,bhtk->bhtv', S, qc)
             + jnp.einsum('bhtj,bhjv->bhtv', jnp.einsum('bhtd,bhjd->bhtj', qc, kc) * decay, U))
        G_end = G[..., -1:]
        S_new = (jnp.exp(G_end)[..., None] * S
                 + jnp.einsum('bhjv,bhjk->bhvk', U * jnp.exp(G_end - G)[..., None], kc))
        return S_new, o

    S, o = lax.scan(step, S0, tuple(map(chunks, (q, k, v, beta, g))))
    o = jnp.swapaxes(jnp.moveaxis(o, 0, 1), 2, 3).reshape(B, L, H, D)
    return o, S


def gdn_mixer(p, conv_buf, S0, prm):
    B, L, _ = p.shape
    qkv, b_in, a_in, z = split_cols(p, GDN_SPLITS)
    qkv, new_buf = causal_dwconv(qkv, conv_buf, prm['gdn_conv_w'])
    q, k, v = split_cols(jax.nn.silu(qkv), (GDN_WIDTH, GDN_WIDTH, GDN_WIDTH))
    heads = lambda t: t.reshape(B, L, GDN_HEADS, HEAD_DIM)
    q = l2_normalize(heads(q)) * HEAD_DIM ** -0.5
    k = l2_normalize(heads(k))
    beta = jax.nn.sigmoid(b_in)
    g = -jnp.exp(prm['gdn_a_log']) * jax.nn.softplus(a_in + prm['gdn_dt_bias'])
    o, S = gated_delta_chunked(q, k, heads(v), beta, g, S0)
    o = rms_norm(o, prm['gdn_norm_w']) * jax.nn.silu(heads(z))
    return o.reshape(B, L, GDN_WIDTH), new_buf, S


def swa_mixer(p, k_buf, v_buf, start, sinks):
    B, L, _ = p.shape
    q, k, v = split_cols(p, SWA_SPLITS)
    pos = start + jnp.arange(L)
    q = partial_rope(q.reshape(B, L, SWA_HEADS, HEAD_DIM), pos)
    k = partial_rope(k.reshape(B, L, SWA_KV_HEADS, HEAD_DIM), pos)
    v = v.reshape(B, L, SWA_KV_HEADS, HEAD_DIM)
    kc = jnp.concatenate([k_buf, k], axis=1)
    vc = jnp.concatenate([v_buf, v], axis=1)
    Wb = k_buf.shape[1]
    Qb = WINDOW if L % WINDOW == 0 else L
    nb = L // Qb
    idx = (jnp.arange(nb) * Qb)[:, None] + jnp.arange(Wb + Qb)[None, :]
    kb = kc[:, idx]
    vb = vc[:, idx]
    qb = q.reshape(B, nb, Qb, SWA_KV_HEADS, SWA_GROUP, HEAD_DIM)
    s = jnp.einsum('bnqhgd,bnkhd->bnhgqk', qb, kb) * HEAD_DIM ** -0.5
    qpos = start + jnp.arange(L).reshape(nb, Qb)
    kpos = start - Wb + idx
    rel = qpos[:, :, None] - kpos[:, None, :]
    mask = (rel >= 0) & (rel < WINDOW) & (kpos[:, None, :] >= 0)
    s = jnp.where(mask[None, :, None, None], s, -jnp.inf)
    sink = jnp.broadcast_to(sinks.reshape(1, 1, SWA_KV_HEADS, SWA_GROUP, 1, 1), s.shape[:-1] + (1,))
    pr = jax.nn.softmax(jnp.concatenate([s, sink], axis=-1), axis=-1)[..., :-1]
    o = jnp.einsum('bnhgqk,bnkhd->bnqhgd', pr, vb).reshape(B, L, SWA_WIDTH)
    return o, kc[:, -Wb:], vc[:, -Wb:]


def decoder_layer(x, start, st, prm):
    rw_S, rw_shift, s5_re, s5_im, gdn_S, gdn_conv, swa_k, swa_v = st
    h = rms_norm(x, prm['g_mix_pre'])
    p_rw, p_s5, p_gdn, p_swa = split_cols(h @ prm['w_in'], PROJ_SPLITS)
    o_rw, rw_shift, rw_S = rwkv7_mixer(p_rw, rw_shift, rw_S, prm)
    o_s5, s5_re, s5_im = s5_mixer(p_s5, s5_re, s5_im, prm)
    o_gdn, gdn_conv, gdn_S = gdn_mixer(p_gdn, gdn_conv, gdn_S, prm)
    o_swa, swa_k, swa_v = swa_mixer(p_swa, swa_k, swa_v, start, prm['swa_sinks'])
    mix = jnp.concatenate([o_rw, o_s5, o_gdn, o_swa], axis=-1) @ prm['w_out']
    x = x + rms_norm(mix, prm['g_mix_post'])
    h = rms_norm(x, prm['g_mlp_pre'])
    f = jnp.square(jax.nn.relu(h @ prm['w_up'])) @ prm['w_down']
    x = x + rms_norm(f, prm['g_mlp_post'])
    return x, (rw_S, rw_shift, s5_re, s5_im, gdn_S, gdn_conv, swa_k, swa_v)


def stack_layers(states, i):
    return jnp.stack([st[i] for st in states], axis=0)


def setup_inputs(seed: int = 0) -> dict:
    key = jax.random.key(seed)
    ks = iter(jax.random.split(key, 64))
    nrm = lambda shape, scale: scale * jax.random.normal(next(ks), shape, jnp.float32)
    uni = lambda shape, lo, hi: jax.random.uniform(next(ks), shape, jnp.float32, lo, hi)
    L = DEPTH
    win_buf = min(WINDOW, PAST_LEN)
    gdn_dt = jnp.exp(uni((L, GDN_HEADS), math.log(1e-3), math.log(1e-1)))
    return {
        'x_prompt': nrm((BATCH, SEQ, D_MODEL), 1.0),
        'x_sample': nrm((DEC_BATCH, DEC_SEQ, D_MODEL), 1.0),
        'state_rwkv': nrm((L, DEC_BATCH, RW_HEADS, HEAD_DIM, HEAD_DIM), 0.3),
        'state_rwkv_shift': nrm((L, DEC_BATCH, 1, RW_PROJ), 1.0),
        'state_s5_re': nrm((L, DEC_BATCH, S5_GROUPS, S5_STATE), 0.5),
        'state_s5_im': nrm((L, DEC_BATCH, S5_GROUPS, S5_STATE), 0.5),
        'state_gdn': nrm((L, DEC_BATCH, GDN_HEADS, HEAD_DIM, HEAD_DIM), 0.1),
        'state_gdn_conv': nrm((L, DEC_BATCH, GDN_CONV - 1, GDN_CONV_CH), 1.0),
        'cache_swa_k': nrm((L, DEC_BATCH, win_buf, SWA_KV_HEADS, HEAD_DIM), 1.0),
        'cache_swa_v': nrm((L, DEC_BATCH, win_buf, SWA_KV_HEADS, HEAD_DIM), 1.0),
        'g_mix_pre': 1.0 + nrm((L, D_MODEL), 0.02),
        'g_mix_post': 1.0 + nrm((L, D_MODEL), 0.02),
        'g_mlp_pre': 1.0 + nrm((L, D_MODEL), 0.02),
        'g_mlp_post': 1.0 + nrm((L, D_MODEL), 0.02),
        'w_in': nrm((L, D_MODEL, PROJ_WIDTH), D_MODEL ** -0.5),
        'w_out': nrm((L, MIX_WIDTH, D_MODEL), MIX_WIDTH ** -0.5),
        'rw_mu': uni((L, RW_PROJ), 0.2, 0.8),
        'rw_w0': uni((L, RW_WIDTH), -4.0, 1.0),
        'rw_w2': nrm((L, RW_DECAY_RANK, RW_WIDTH), 0.1),
        'rw_a0': nrm((L, RW_WIDTH), 0.1),
        'rw_a2': nrm((L, RW_ICLR_RANK, RW_WIDTH), 0.5 * RW_ICLR_RANK ** -0.5),
        'rw_g2': nrm((L, RW_GATE_RANK, RW_WIDTH), RW_GATE_RANK ** -0.5),
        'rw_kk': 0.85 + nrm((L, RW_WIDTH), 0.02),
        'rw_ka': 1.0 + nrm((L, RW_WIDTH), 0.02),
        'rw_rk': nrm((L, RW_HEADS, HEAD_DIM), 0.1),
        'rw_ln_w': 1.0 + nrm((L, RW_WIDTH), 0.02),
        'rw_ln_b': nrm((L, RW_WIDTH), 0.02),
        's5_a_re': -0.5 + nrm((L, S5_GROUPS, S5_STATE), 0.01),
        's5_a_im': jnp.broadcast_to(jnp.pi * jnp.arange(S5_STATE, dtype=jnp.float32), (L, S5_GROUPS, S5_STATE))
                   + nrm((L, S5_GROUPS, S5_STATE), 0.01),
        's5_log_dt': uni((L, S5_GROUPS), math.log(1e-3), math.log(1e-1)),
        's5_b_re': nrm((L, S5_GROUPS, S5_STATE, S5_CH), (2 * S5_CH) ** -0.5),
        's5_b_im': nrm((L, S5_GROUPS, S5_STATE, S5_CH), (2 * S5_CH) ** -0.5),
        's5_c_re': nrm((L, S5_GROUPS, S5_CH, S5_STATE), (2 * S5_STATE) ** -0.5),
        's5_c_im': nrm((L, S5_GROUPS, S5_CH, S5_STATE), (2 * S5_STATE) ** -0.5),
        's5_d': nrm((L, S5_WIDTH), 1.0),
        's5_w_glu': nrm((L, S5_WIDTH, S5_WIDTH), S5_WIDTH ** -0.5),
        's5_b_glu': nrm((L, S5_WIDTH), 0.02),
        'gdn_conv_w': nrm((L, GDN_CONV, GDN_CONV_CH), 0.5),
        'gdn_a_log': jnp.log(uni((L, GDN_HEADS), 1.0, 16.0)),
        'gdn_dt_bias': gdn_dt + jnp.log(-jnp.expm1(-gdn_dt)),
        'gdn_norm_w': 1.0 + nrm((L, HEAD_DIM), 0.02),
        'swa_sinks': nrm((L, SWA_HEADS), 1.0),
        'w_up': nrm((L, D_MODEL, D_FF), D_MODEL ** -0.5),
        'w_down': nrm((L, D_FF, D_MODEL), D_FF ** -0.5),
    }


def reference(x_prompt, x_sample, state_rwkv, state_rwkv_shift, state_s5_re, state_s5_im, state_gdn,
              state_gdn_conv, cache_swa_k, cache_swa_v, g_mix_pre, g_mix_post, g_mlp_pre, g_mlp_post,
              w_in, w_out, rw_mu, rw_w0, rw_w2, rw_a0, rw_a2, rw_g2, rw_kk, rw_ka, rw_rk, rw_ln_w, rw_ln_b,
              s5_a_re, s5_a_im, s5_log_dt, s5_b_re, s5_b_im, s5_c_re, s5_c_im, s5_d, s5_w_glu, s5_b_glu,
              gdn_conv_w, gdn_a_log, gdn_dt_bias, gdn_norm_w, swa_sinks, w_up, w_down):
    f32 = jnp.float32
    xp = x_prompt.astype(f32)
    xs = x_sample.astype(f32)
    zp = lambda *shape: jnp.zeros((BATCH,) + shape, f32)
    prompt_init = (zp(RW_HEADS, HEAD_DIM, HEAD_DIM), zp(1, RW_PROJ), zp(S5_GROUPS, S5_STATE),
                   zp(S5_GROUPS, S5_STATE), zp(GDN_HEADS, HEAD_DIM, HEAD_DIM), zp(GDN_CONV - 1, GDN_CONV_CH),
                   zp(WINDOW, SWA_KV_HEADS, HEAD_DIM), zp(WINDOW, SWA_KV_HEADS, HEAD_DIM))
    new_p, new_s = [], []
    for l in range(DEPTH):
        prm = {n: a[l].astype(f32) for n, a in (
            ('g_mix_pre', g_mix_pre), ('g_mix_post', g_mix_post), ('g_mlp_pre', g_mlp_pre),
            ('g_mlp_post', g_mlp_post), ('w_in', w_in), ('w_out', w_out), ('rw_mu', rw_mu), ('rw_w0', rw_w0),
            ('rw_w2', rw_w2), ('rw_a0', rw_a0), ('rw_a2', rw_a2), ('rw_g2', rw_g2), ('rw_kk', rw_kk),
            ('rw_ka', rw_ka), ('rw_rk', rw_rk), ('rw_ln_w', rw_ln_w), ('rw_ln_b', rw_ln_b),
            ('s5_a_re', s5_a_re), ('s5_a_im', s5_a_im), ('s5_log_dt', s5_log_dt), ('s5_b_re', s5_b_re),
            ('s5_b_im', s5_b_im), ('s5_c_re', s5_c_re), ('s5_c_im', s5_c_im), ('s5_d', s5_d),
            ('s5_w_glu', s5_w_glu), ('s5_b_glu', s5_b_glu), ('gdn_conv_w', gdn_conv_w),
            ('gdn_a_log', gdn_a_log), ('gdn_dt_bias', gdn_dt_bias), ('gdn_norm_w', gdn_norm_w),
            ('swa_sinks', swa_sinks), ('w_up', w_up), ('w_down', w_down))}
        xp, st_p = decoder_layer(xp, 0, prompt_init, prm)
        st_in = tuple(a[l].astype(f32) for a in (state_rwkv, state_rwkv_shift, state_s5_re, state_s5_im,
                                                 state_gdn, state_gdn_conv, cache_swa_k, cache_swa_v))
        xs, st_s = decoder_layer(xs, PAST_LEN, st_in, prm)
        new_p.append(st_p)
        new_s.append(st_s)
    y_prompt = xp.astype(x_prompt.dtype)
    y_sample = xs.astype(x_sample.dtype)
    return (y_prompt, y_sample,
            stack_layers(new_p, 0), stack_layers(new_s, 0),
            stack_layers(new_p, 1), stack_layers(new_s, 1),
            stack_layers(new_p, 2), stack_layers(new_s, 2),
            stack_layers(new_p, 3), stack_layers(new_s, 3),
            stack_layers(new_p, 4), stack_layers(new_s, 4),
            stack_layers(new_p, 5), stack_layers(new_s, 5),
            stack_layers(new_p, 6), stack_layers(new_s, 6),
            stack_layers(new_p, 7), stack_layers(new_s, 7))
```

```python
import math
from contextlib import ExitStack
import numpy as np
import ml_dtypes
import concourse.bass as bass
import concourse.mybir as mybir
from concourse.bass_utils import run_bass_kernel_spmd

F32 = mybir.dt.float32
BF16 = mybir.dt.bfloat16
F32R = mybir.dt.float32r


def R_(ap):
    return ap.bitcast(F32R)
AF = mybir.ActivationFunctionType
ALU = mybir.AluOpType
AX = mybir.AxisListType

D = 1024
DEPTH = 2
HD = 64
PROJ = 2824
DFF = 4096
EPS = 1e-6
PAST = 16384
TSTEP = 256


class Tl:
    def __init__(self, h):
        self.h = h
        self.w = None
        self.r = {}

    def __getitem__(self, k):
        return self.h[k]


class _Rec:
    def __init__(self):
        self.call = None

    def __getattr__(self, name):
        def f(*a, **k):
            self.call = (name, a, k)
            return None
        return f


class Sched:
    ENGS = ("pe", "dve", "act", "pool", "sp")

    def __init__(self, nc, es):
        self.nc = nc
        self.es = es
        self.eng = {"pe": nc.tensor, "dve": nc.vector, "act": nc.scalar, "pool": nc.gpsimd, "sp": nc.sync}
        self.prog = []
        self.ninst = 0

    def op(self, e, fn, rd=(), wr=()):
        r = _Rec()
        fn(r)
        assert r.call is not None
        self.prog.append(("op", e, r.call, list(rd), list(wr)))

    def dma(self, q, out, in_, rd=(), wr=()):
        self.prog.append(("dma", q, (out, in_), list(rd), list(wr)))

    def finish(self, q="sp"):
        import os
        SERIAL = int(os.environ.get("KSERIAL", "0"))
        last_ps = None
        nc, es = self.nc, self.es
        prog = self.prog
        n = len(prog)
        deps = [None] * n
        needs = [False] * n
        local = [0] * n
        lcnt = {e: 0 for e in self.ENGS}
        def rowrng(call):
            name, a, k = call
            ap = k.get("lhsT") if name == "matmul" else (a[1] if len(a) > 1 else k.get("in_"))
            b = ap.base_partition()
            kk = ap.shape[0]
            sz = 32 if kk <= 32 else (64 if kk <= 64 else 128)
            b = (b // sz) * sz
            return (b, b + sz)

        def rows_disjoint(r1, r2):
            return r1[1] <= r2[0] or r2[1] <= r1[0]

        for i, (kind, e, call, rd, wr) in enumerate(prog):
            lcnt[e] += 1
            local[i] = lcnt[e]
            d = set()
            for t in rd:
                if t.w is not None:
                    d.add(t.w)
                if getattr(t, "psum", False):
                    for k_, j in t.r.items():
                        if k_ != e:
                            d.add(j)
            for t in wr:
                if t.w is not None:
                    d.add(t.w)
                for j in t.r.values():
                    d.add(j)
            keep = set()
            for j in d:
                kj, ej = prog[j][0], prog[j][1]
                if kind == "op" and kj == "op" and ej == e:
                    if e == "pe":
                        if rows_disjoint(rowrng(call), rowrng(prog[j][2])):
                            keep.add(j)
                        continue
                keep.add(j)
            if SERIAL == 1 and i > 0:
                keep.add(i - 1)
            if SERIAL == 2 and any(getattr(t, "psum", False) for t in list(rd) + list(wr)):
                if last_ps is not None and not (e == "pe" and prog[last_ps][1] == "pe"):
                    keep.add(last_ps)
                last_ps = i
            if SERIAL == 3 and e == "pool" and i > 0:
                keep.add(i - 1)
            if SERIAL == 3 and i > 0 and prog[i - 1][1] == "pool":
                keep.add(i - 1)
            deps[i] = keep
            for j in keep:
                needs[j] = True
            rk = e if kind == "op" else ("dma", i)
            for t in rd:
                t.r[rk] = i
            for t in wr:
                t.w = i
                t.r = {}
        semh, cnt, epoch = {}, {}, {}
        for e in ("pe", "dve", "act", "pool"):
            epoch[e], cnt[e] = 0, 0
            semh[(e, 0)] = es.enter_context(nc.semaphore("s_%s_0" % e))
        ndma = 24
        dsem = [es.enter_context(nc.semaphore("s_dma_%d" % i)) for i in range(ndma)]
        for i in range(ndma):
            semh[("dma", i)] = dsem[i]
        dcnt = [0] * ndma
        dnext = 0
        waited = {e: {} for e in self.ENGS}
        tokn = [None] * n

        def wait(e, need):
            for k, v in need.items():
                if waited[e].get(k, 0) < v:
                    self.eng[e].wait_ge(semh[k], v)
                    waited[e][k] = v
                    self.ninst += 1

        for i, (kind, e, call, rd, wr) in enumerate(prog):
            need = {}
            for j in deps[i]:
                k, v = tokn[j]
                if need.get(k, 0) < v:
                    need[k] = v
            if kind == "op":
                wait(e, need)
                ins = getattr(self.eng[e], call[0])(*call[1], **call[2])
                if needs[i]:
                    if cnt[e] >= 30000:
                        epoch[e] += 1
                        cnt[e] = 0
                        semh[(e, epoch[e])] = es.enter_context(nc.semaphore("s_%s_%d" % (e, epoch[e])))
                    cnt[e] += 1
                    key = (e, epoch[e])
                    ins.then_inc(semh[key], 1)
                    tokn[i] = (key, cnt[e])
            else:
                j = dnext
                dnext = (dnext + 1) % ndma
                if dcnt[j] > 0:
                    need[("dma", j)] = max(need.get(("dma", j), 0), dcnt[j])
                wait(e, need)
                ins = self.eng[e].dma_start(out=call[0], in_=call[1])
                dcnt[j] += 16
                ins.then_inc(dsem[j], 16)
                tokn[i] = (("dma", j), dcnt[j])
            self.ninst += 1
        need = {("dma", j): dcnt[j] for j in range(ndma) if dcnt[j] > 0}
        wait(q, need)
        self.ninc = sum(needs)


RW0, S50, GD0, SW0 = 0, 1024, 1280, 2312
WIN_TILES = [
    (0, 512, [(0, 128, 0), (128, 256, 1), (256, 384, 2), (384, 512, 3)]),
    (512, 1024, [(512, 640, 4), (640, 768, 5), (768, 896, 6), (896, 1024, 7)]),
    (1024, 1536, [(1024, 1152, 8), (1152, 1280, 9), (1280, 1408, 10), (1408, 1536, 11)]),
    (1536, 2048, [(1536, 1664, 12), (1664, 1792, 13), (1792, 1920, 14), (1920, 2048, 15)]),
    (2048, 2312, [(2048, 2056, 16), (2056, 2184, 17), (2184, 2312, 18)]),
    (2312, 2824, [(2312, 2440, 19), (2440, 2568, 20), (2568, 2696, 21), (2696, 2824, 22)]),
]
NPCH = 23


class Kern:
    def __init__(self, n_seq_p, seq_len, n_seq_s, mixers=("rw", "s5", "gdn", "swa"), dbg=False):
        self.NSP, self.SEQ, self.NSS = n_seq_p, seq_len, n_seq_s
        self.mixers = mixers
        self.nsteps = seq_len // TSTEP
        self.nc = bass.Bass("TRN2", target_bir_lowering=False)
        self.dbg = dbg

    def din(self, name, shape, dt=F32):
        return self.nc.dram_tensor(name, list(shape), dt, kind="ExternalInput").ap()

    def dout(self, name, shape, dt=F32):
        return self.nc.dram_tensor(name, list(shape), dt, kind="ExternalOutput").ap()

    def sb(self, name, shape, dt=F32):
        return Tl(self.es.enter_context(self.nc.sbuf_tensor(name, list(shape), dt)))

    def ps(self, name, shape, dt=F32):
        t = Tl(self.es.enter_context(self.nc.psum_tensor(name, list(shape), dt)))
        t.psum = True
        return t

    def build(self):
        nc = self.nc
        NSP, SEQ, NSS = self.NSP, self.SEQ, self.NSS
        NTP = NSP * SEQ
        NTS = NSS * 4
        self.xT_p = self.din("xT_p", [D, NTP])
        self.xT_s = self.din("xT_s", [D, NTS])
        self.w_in = self.din("w_in", [DEPTH, D, PROJ])
        self.w_out = self.din("w_out", [DEPTH, D, D])
        self.w_up = self.din("w_up", [DEPTH, D, DFF])
        self.w_down = self.din("w_down", [DEPTH, DFF, D])
        self.pp_d = self.din("pp", [DEPTH, 128, NPP])
        self.yT_p = self.dout("yT_p", [D, NTP])
        self.yT_s = self.dout("yT_s", [D, NTS])
        self.o_shift_p = self.dout("o_shift_p", [DEPTH, 1024, NSP])
        self.o_shift_s = self.dout("o_shift_s", [DEPTH, 1024, NSS])
        self.cst_d = self.din("cst", [128, NCST])
        self.rw_s = self.din("rw_s", [DEPTH, 128, 2, NSS, 64])
        self.rwsh_s = self.din("rwsh_s", [DEPTH, 128, 8, NSS])
        self.rw_w2 = self.din("rw_w2", [DEPTH, 64, 256])
        self.rw_a2 = self.din("rw_a2", [DEPTH, 64, 256])
        self.rw_g2 = self.din("rw_g2", [DEPTH, 128, 256])
        self.o_rw_p = self.dout("o_rw_p", [DEPTH, 128, 2, NSP, 64])
        self.o_rw_s = self.dout("o_rw_s", [DEPTH, 128, 2, NSS, 64])
        self.gdn_s = self.din("gdn_s", [DEPTH, 128, 2, NSS, 64])
        self.conv_s = self.din("conv_s", [DEPTH, 128, 6, NSS, 3])
        self.o_gdn_p = self.dout("o_gdn_p", [DEPTH, 128, 2, NSP, 64])
        self.o_gdn_s = self.dout("o_gdn_s", [DEPTH, 128, 2, NSS, 64])
        self.o_conv_p = self.dout("o_conv_p", [DEPTH, NSP, 3, 768])
        self.o_conv_s = self.dout("o_conv_s", [DEPTH, NSS, 3, 768])
        self.s5B = self.din("s5B", [DEPTH, 128, 2 * 8 * 128])
        self.s5C = self.din("s5C", [DEPTH, 128, 2 * 8 * 128])
        self.s5glu = self.din("s5glu", [DEPTH, 256, 256])
        self.s5s_re = self.din("s5s_re", [DEPTH, 128, 8, NSS])
        self.s5s_im = self.din("s5s_im", [DEPTH, 128, 8, NSS])
        self.o_s5re_p = self.dout("o_s5re_p", [DEPTH, 128, 8, NSP])
        self.o_s5im_p = self.dout("o_s5im_p", [DEPTH, 128, 8, NSP])
        self.o_s5re_s = self.dout("o_s5re_s", [DEPTH, 128, 8, NSS])
        self.o_s5im_s = self.dout("o_s5im_s", [DEPTH, 128, 8, NSS])
        self.ropeP = self.din("ropeP", [64, 2, SEQ])
        self.ropeS = self.din("ropeS", [64, 2, 4])
        self.kc_s = self.din("kc_s", [DEPTH, NSS, 2, 64, 128])
        self.vc_s = self.din("vc_s", [DEPTH, NSS, 128, 128])
        self.o_swak_p = self.dout("o_swak_p", [DEPTH, NSP, 2, 64, 128])
        self.o_swak_s = self.dout("o_swak_s", [DEPTH, NSS, 2, 64, 128])
        self.o_swav_p = self.dout("o_swav_p", [DEPTH, NSP, 128, 128])
        self.o_swav_s = self.dout("o_swav_s", [DEPTH, NSS, 128, 128])

        import os
        self.wcache = {}
        self.wring = 0
        self.wscr_off = 0
        self.wscr = None
        if os.environ.get("KNOWSCR", "0") != "1":
            nel = DEPTH * (D * PROJ + D * D + D * DFF + DFF * D) // 128
            self.wscr = self.nc.dram_tensor("wscr", [128, nel], BF16, kind="Internal").ap()
        with ExitStack() as es:
            self.es = es
            es.enter_context(nc.allow_non_contiguous_dma(reason="small strided state/param io"))
            self.S = Sched(nc, es)
            self.alloc()
            self.setup_consts()
            import os
            grp = os.environ.get("KGROUPS", "sp")
            if "s" in grp:
                self.run_group("s", 0)
            if "p" in grp:
                for st in range(self.nsteps):
                    self.run_group("p", st)
            self.S.finish("sp")
        return nc

    def alloc(self):
        NT = 512
        self.x = [self.sb("x%d" % c, [128, NT]) for c in range(8)]
        self.xn = [self.sb("xn%d" % c, [128, NT], BF16) for c in range(8)]
        self.p = [self.sb("p%d" % c, [128, NT]) for c in range(NPCH)]
        self.mix = self.xn
        self.tmp = [self.sb("tmp%d" % c, [128, NT]) for c in range(8)]
        self.stage = [self.sb("stage%d" % i, [128, 2048]) for i in range(2)]
        self.wbf = [self.sb("wbf%d" % i, [128, 2048], BF16) for i in range(2)]
        self.wi = 0
        self.sq = [self.sb("sq%d" % i, [128, NT], BF16) for i in range(2)]
        self.rstd = self.sb("rstd", [128, NT])
        self.pp = [self.sb("ppsb%d" % l, [128, NPP]) for l in range(DEPTH)]
        self.ones_bf = self.sb("ones_bf", [128, 128], BF16)
        self.ident = self.sb("ident", [128, 128])
        self.ps_lin = [self.ps("ps_lin%d" % i, [128, 512]) for i in range(2)]
        self.pli = 0
        self.ps_ss = self.ps("ps_ss", [128, 512])
        self.mp = [self.ps("mp%d" % i, [128, 512]) for i in range(5)]
        self.cst = self.sb("cst_sb", [128, NCST])
        self.NSC = 26
        self.sc = [self.sb("sc%d" % i, [128, 512]) for i in range(self.NSC)]
        self.esink = [self.sb("esink%d" % l, [128, 2]) for l in range(DEPTH)]
        self.nn_t = [self.sb("nn0", [64, 512])]
        self.nx_t = [self.sb("nx0", [64, 256])]
        self.nnp = [[self.sb("nnp%d_%d" % (pr, i), [64, 256]) for i in range(2)] for pr in range(2)]
        self.nxp = [[self.sb("nxp%d_%d" % (pr, i), [64, 128]) for i in range(2)] for pr in range(2)]
        self.s5k = [self.sb("s5k%d" % l, [128, 16 * 8]) for l in range(DEPTH)]
        self.s5h = [[self.sb("s5h%d_%d" % (l, i), [128, 8 * self.NSP]) for i in range(2)] for l in range(DEPTH)]
        self.s5w = self.sb("s5w", [128, 512])
        self.gH = [self.sb("gH%d" % l, [128, 2 * self.NSP * 64]) for l in range(DEPTH)]
        self.rH = [self.sb("rH%d" % l, [128, 2 * self.NSP * 64]) for l in range(DEPTH)]
        self.rsh = [self.sb("rsh%d" % l, [128, 8 * self.NSP]) for l in range(DEPTH)]
        self.gcb = [self.sb("gcb%d" % l, [128, 6 * self.NSP * 3]) for l in range(DEPTH)]
        self.gnea = [self.sb("gnea%d" % l, [128, 1]) for l in range(DEPTH)]
        self.s5i = self.sb("s5i", [128, 256], mybir.dt.int32)
        self.khist = [[self.sb("khist%d_%d" % (l, k), [64, self.NSP * 128]) for k in range(2)] for l in range(DEPTH)]
        self.vhist = [self.sb("vhist%d" % l, [128, self.NSP * 128]) for l in range(DEPTH)]

    def setup_consts(self):
        S = self.S
        S.op("pool", lambda e: e.memset(self.ones_bf[:], 1.0), wr=[self.ones_bf])
        S.op("pool", lambda e: e.memset(self.ident[:], 0.0), wr=[self.ident])
        S.op("pool", lambda e: e.affine_select(out=self.ident[:], in_=self.ident[:], pattern=[[-1, 128]], base=0,
                                               channel_multiplier=1, compare_op=ALU.not_equal, fill=1.0),
             rd=[self.ident], wr=[self.ident])
        S.dma("sp", self.cst[:], self.cst_d, wr=[self.cst])
        for l in range(DEPTH):
            S.dma("sp", self.pp[l][:], self.pp_d[l], wr=[self.pp[l]])
        for l in range(DEPTH):
            for i in range(2):
                S.op("pool", lambda e: e.memset(self.s5h[l][i][:], 0.0), wr=[self.s5h[l][i]])
            self.s5_setup(l)
            S.op("pool", lambda e: e.memset(self.gH[l][:], 0.0), wr=[self.gH[l]])
            S.op("pool", lambda e: e.memset(self.rH[l][:], 0.0), wr=[self.rH[l]])
            S.op("pool", lambda e: e.memset(self.rsh[l][:], 0.0), wr=[self.rsh[l]])
            S.op("pool", lambda e: e.memset(self.gcb[l][:], 0.0), wr=[self.gcb[l]])
            o_ = PPO["gdn_alog"]
            S.op("act", lambda e: e.activation(out=self.gnea[l][:], in_=self.pp[l][:, o_:o_ + 1], func=AF.Exp),
                 rd=[self.pp[l]], wr=[self.gnea[l]])
            S.op("dve", lambda e: e.tensor_scalar(self.gnea[l][:], self.gnea[l][:], -1.0, None, ALU.mult),
                 rd=[self.gnea[l]], wr=[self.gnea[l]])
        for l in range(DEPTH):
            o = PPO["sink"]
            S.op("act", lambda e: e.activation(out=self.esink[l][:], in_=self.pp[l][:, o:o + 2], func=AF.Exp),
                 rd=[self.pp[l]], wr=[self.esink[l]])

    def cs(self, name, p0=0, p1=128):
        o, n = CSO[name]
        return self.cst[p0:p1, o:o + n]

    def ppc(self, l, name, c=0):
        o = PPO[name] + c
        return self.pp[l][:, o:o + 1]

    def linear(self, w2d, kc, tiles, rhs, NT, consume, wkey=None):
        S = self.S
        groups = [g_ for (_, _, gs) in tiles for g_ in gs]

        def load(k0, k1, c0, c1):
            b = self.wi
            self.wi ^= 1
            nk, ncol = k1 - k0, c1 - c0
            st, wb = self.stage[b], self.wbf[b]
            ck = (wkey, k0, k1, c0, c1)
            if wkey is not None and ck in self.wcache:
                self.wi ^= 1
                r_ = self.wring
                self.wring = (self.wring + 1) % 4
                if r_ < 2:
                    wb = self.wbf[r_]
                    wap = wb[:, 0:nk * ncol]
                else:
                    wb = self.stage[r_ - 2]
                    wap = wb[:, :].bitcast(BF16)[:, 0:nk * ncol]
                off, dep = self.wcache[ck]
                S.dma("sp", wap, self.wscr[:, off:off + nk * ncol], rd=[dep], wr=[wb])
                return wb, wap.rearrange("p (k n) -> p k n", k=nk)
            S.dma("sp", st[:, 0:nk * ncol].rearrange("p (k n) -> p k n", k=nk),
                  w2d[k0 * 128:k1 * 128, c0:c1].rearrange("(k p) n -> p k n", p=128), wr=[st])
            S.op("act", lambda e: e.activation(out=wb[:, 0:nk * ncol], in_=st[:, 0:nk * ncol], func=AF.Copy), rd=[st], wr=[wb])
            if wkey is not None and self.wscr is not None:
                off = self.wscr_off
                self.wscr_off += nk * ncol
                dep = Tl(None)
                S.dma("act", self.wscr[:, off:off + nk * ncol], wb[:, 0:nk * ncol], rd=[wb], wr=[dep])
                self.wcache[ck] = (off, dep)
            return wb, wb[:, 0:nk * ncol].rearrange("p (k n) -> p k n", k=nk)

        loads = []
        if kc * 256 <= 2048:
            i = 0
            while i < len(groups):
                batch = [groups[i]]
                if i + 1 < len(groups) and groups[i + 1][0] == groups[i][1] and groups[i + 1][1] - groups[i][0] <= 2048 // kc:
                    batch.append(groups[i + 1])
                i += len(batch)
                loads.append((0, kc, batch[0][0], batch[-1][1], [(m0, m1, tag, True, True) for (m0, m1, tag) in batch]))
        else:
            kseg = 2048 // 128
            for (m0, m1, tag) in groups:
                for k0 in range(0, kc, kseg):
                    loads.append((k0, k0 + kseg, m0, m1, [(m0, m1, tag, k0 == 0, k0 + kseg == kc)]))
        cached = wkey is not None and all((wkey, l_[0], l_[1], l_[2], l_[3]) in self.wcache for l_ in loads)
        depth = 3 if cached else 1
        q = [load(*loads[j][0:4]) for j in range(min(depth, len(loads)))]
        pst = None
        for li, (k0, k1, c0, c1, grp) in enumerate(loads):
            cur = q.pop(0)
            if li + depth < len(loads):
                q.append(load(*loads[li + depth][0:4]))
            wb, wv = cur
            for (m0, m1, tag, first, lastk) in grp:
                if first:
                    pst = self.ps_lin[self.pli]
                    self.pli ^= 1
                m = m1 - m0
                for k in range(k0, k1):
                    r_ap, r_t = rhs(k)
                    S.op("pe", lambda e: e.matmul(pst[0:m, 0:NT], lhsT=wv[:, k - k0, m0 - c0:m1 - c0], rhs=r_ap,
                                                  start=(k == 0), stop=(k == kc - 1)), rd=[wb] + r_t, wr=[pst])
                if lastk:
                    consume(tag, pst, m)

    def norm_stats(self, src, nch, NT):
        S = self.S
        for c in range(nch):
            a, t = src(c)
            sq = self.sq[c % 2]
            if c % 2 == 0:
                S.op("act", lambda e: e.activation(out=sq[:, 0:NT], in_=a, func=AF.Square), rd=t, wr=[sq])
            else:
                S.op("dve", lambda e: e.tensor_tensor(out=sq[:, 0:NT], in0=a, in1=a, op=ALU.mult), rd=t, wr=[sq])
            S.op("pe", lambda e: e.matmul(self.ps_ss[:, 0:NT], lhsT=self.ones_bf[:], rhs=sq[:, 0:NT],
                                          start=(c == 0), stop=(c == nch - 1)), rd=[sq, self.ones_bf], wr=[self.ps_ss])
        S.op("act", lambda e: e.activation(out=self.rstd[:, 0:NT], in_=self.ps_ss[:, 0:NT], func=AF.Sqrt,
                                           bias=EPS, scale=1.0 / (nch * 128)), rd=[self.ps_ss], wr=[self.rstd])
        S.op("dve", lambda e: e.reciprocal(self.rstd[:, 0:NT], self.rstd[:, 0:NT]), rd=[self.rstd], wr=[self.rstd])

    def prenorm(self, l, gname, NT):
        S = self.S
        self.norm_stats(lambda c: (self.x[c][:, 0:NT], [self.x[c]]), 8, NT)
        for c in range(8):
            S.op("dve", lambda e: e.scalar_tensor_tensor(out=self.xn[c][:, 0:NT], in0=self.x[c][:, 0:NT],
                                                         scalar=self.ppc(l, gname, c), in1=self.rstd[:, 0:NT],
                                                         op0=ALU.mult, op1=ALU.mult),
                 rd=[self.x[c], self.rstd, self.pp[l]], wr=[self.xn[c]])

    def postnorm_residual(self, l, gname, NT):
        S = self.S
        self.norm_stats(lambda c: (self.tmp[c][:, 0:NT], [self.tmp[c]]), 8, NT)
        for c in range(8):
            S.op("dve", lambda e: e.scalar_tensor_tensor(out=self.tmp[c][:, 0:NT], in0=self.tmp[c][:, 0:NT],
                                                         scalar=self.ppc(l, gname, c), in1=self.rstd[:, 0:NT],
                                                         op0=ALU.mult, op1=ALU.mult),
                 rd=[self.tmp[c], self.rstd, self.pp[l]], wr=[self.tmp[c]])
            S.op("dve", lambda e: e.tensor_tensor(out=self.x[c][:, 0:NT], in0=self.x[c][:, 0:NT],
                                                   in1=self.tmp[c][:, 0:NT], op=ALU.add),
                 rd=[self.x[c], self.tmp[c]], wr=[self.x[c]])


    def head_rms(self, src, NT, scale, bias_eps, rs_t):
        S = self.S
        sq = self.sc[19]
        S.op("act", lambda e: e.activation(out=sq[:, 0:NT], in_=src[:, 0:NT], func=AF.Square), rd=[src], wr=[sq])
        ps = self.mp[4]
        S.op("pe", lambda e: e.matmul(ps[:, 0:NT], lhsT=self.cs("blk64"), rhs=sq[:, 0:NT], start=True, stop=True),
             rd=[self.cst, sq], wr=[ps])
        if bias_eps is None:
            S.op("dve", lambda e: e.tensor_scalar(rs_t[:, 0:NT], ps[:, 0:NT], 1e-12, None, ALU.max), rd=[ps], wr=[rs_t])
            S.op("act", lambda e: e.activation(out=rs_t[:, 0:NT], in_=rs_t[:, 0:NT], func=AF.Sqrt), rd=[rs_t], wr=[rs_t])
        else:
            S.op("act", lambda e: e.activation(out=rs_t[:, 0:NT], in_=ps[:, 0:NT], func=AF.Sqrt, bias=bias_eps, scale=scale),
                 rd=[ps], wr=[rs_t])
        S.op("dve", lambda e: e.reciprocal(rs_t[:, 0:NT], rs_t[:, 0:NT]), rd=[rs_t], wr=[rs_t])

    def neumann(self, NN0, X0, levels):
        S = self.S
        psq = [self.mp[2], self.ps_lin[0]]
        psx = [self.mp[3], self.ps_lin[1]]
        for k in range(levels):
            for pr in range(2):
                if k == 0:
                    nt, xt = NN0, X0
                    nv = R_(NN0[0:64, 0:512]).rearrange("p (h n) -> p h n", h=4)[:, pr * 2:pr * 2 + 2, :]
                    xin = X0[0:64, pr * 128:(pr + 1) * 128]
                else:
                    nt, xt = self.nnp[pr][(k - 1) % 2], self.nxp[pr][(k - 1) % 2]
                    nv = R_(nt[0:64, 0:256]).rearrange("p (h n) -> p h n", h=2)
                    xin = xt[0:64, 0:128]
                xo = self.nxp[pr][k % 2]
                px = psx[pr]
                for hh in range(2):
                    S.op("pe", lambda e: e.matmul(px[0:64, hh * 64:(hh + 1) * 64], lhsT=nv[:, hh, 0:64], rhs=R_(xin[:, hh * 64:(hh + 1) * 64]),
                                                  start=True, stop=True), rd=[nt, xt], wr=[px])
                S.op("dve", lambda e: e.tensor_tensor(out=R_(xo[0:64, 0:128]), in0=px[0:64, 0:128], in1=xin, op=ALU.add),
                     rd=[px, xt], wr=[xo])
                if k < levels - 1:
                    no = self.nnp[pr][k % 2]
                    pq = psq[pr]
                    for hh in range(2):
                        S.op("pe", lambda e: e.matmul(pq[0:64, hh * 128:hh * 128 + 64], lhsT=nv[:, hh, 64:128], rhs=nv[:, hh, 0:64],
                                                      start=True, stop=True), rd=[nt], wr=[pq])
                        S.op("pe", lambda e: e.matmul(pq[0:64, hh * 128 + 64:hh * 128 + 128], lhsT=nv[:, hh, 0:64], rhs=nv[:, hh, 64:128],
                                                      start=True, stop=True), rd=[nt], wr=[pq])
                    S.op("act", lambda e: e.activation(out=R_(no[0:64, 0:256]), in_=pq[0:64, 0:256], func=AF.Copy), rd=[pq], wr=[no])
        fin = (levels - 1) % 2

        def uh(h):
            t = self.nxp[h // 2][fin]
            return t[0:64, (h % 2) * 64:(h % 2) * 64 + 64], t
        return uh

    def colop(self, out_t, out3, in3, col4, op, rd):
        for h in range(4):
            self.S.op("dve", lambda e: e.tensor_scalar(out3[:, h, :], in3[:, h, :], col4[:, h:h + 1], None, op), rd=rd, wr=[out_t])

    def instances(self, g, nseq, T):
        if g == "s":
            return [(0, list(range(nseq)), 4, "s", 2)]
        return [(s_ * T + ch * 64, [s_], 64, "p", 6) for ch in range(T // 64) for s_ in range(nseq)]

    def pad_seq(self, dst, src_ap, rd):
        S = self.S
        eye = self.cs("eye16").rearrange("p (a b) -> p a b", a=16).unsqueeze(3).to_broadcast([128, 16, 16, 4])
        in0 = src_ap.rearrange("p (b i) -> p b i", i=4).unsqueeze(1).to_broadcast([128, 16, 16, 4])
        S.op("pool", lambda e: e.tensor_tensor(out=dst.rearrange("p (a b i) -> p a b i", a=16, b=16), in0=in0, in1=eye, op=ALU.mult),
             rd=rd + [self.cst], wr=[])


    def rwkv(self, g, st, l, NT, nseq, T, last):
        S = self.S
        samp = (g == "s")
        mp, sc, cst, tmp, pp, p = self.mp, self.sc, self.cst, self.tmp, self.pp[l], self.p
        sfx = "s" if samp else "p"
        v3 = lambda ap: ap.rearrange("p (s t) -> p s t", t=T)
        ppc = lambda n, c: pp[:, PPO[n] + c:PPO[n] + c + 1]
        if samp:
            sh = sc[24]
            S.dma("sp", sh[:, 0:8 * nseq].rearrange("p (c s) -> p c s", c=8), self.rwsh_s[l], wr=[sh])
            stH = self.stage[self.wi]
            self.wi ^= 1
            S.dma("sp", stH[:, 0:2 * nseq * 64].rearrange("p (a s v) -> p a s v", a=2, s=nseq), self.rw_s[l], wr=[stH])
            Ht = stH
        else:
            sh = self.rsh[l]
            Ht = self.rH[l]
        shv = sh[:, 0:8 * nseq].rearrange("p (c s) -> p c s", c=8)
        Hv = Ht[:, 0:2 * nseq * 64].rearrange("p (a s v) -> p a s v", a=2, s=nseq)
        wm = sc[25]
        for c in range(8):
            pv, tv = v3(p[c][:, 0:NT]), v3(tmp[c][:, 0:NT])
            if T > 1:
                S.op("dve", lambda e: e.tensor_tensor(out=tv[:, :, 1:T], in0=pv[:, :, 0:T - 1], in1=pv[:, :, 1:T], op=ALU.subtract),
                     rd=[p[c]], wr=[tmp[c]])
            S.op("pool", lambda e: e.tensor_tensor(out=tv[:, :, 0], in0=shv[:, c, :], in1=pv[:, :, 0], op=ALU.subtract),
                 rd=[p[c], sh], wr=[tmp[c]])
            S.op("dve", lambda e: e.scalar_tensor_tensor(out=tmp[c][:, 0:NT], in0=tmp[c][:, 0:NT], scalar=ppc("rw_mu", c), in1=p[c][:, 0:NT],
                                                         op0=ALU.mult, op1=ALU.add), rd=[tmp[c], pp, p[c]], wr=[tmp[c]])
            if not samp:
                S.op("pool", lambda e: e.tensor_copy(shv[:, c, :], pv[:, :, T - 1]), rd=[p[c]], wr=[sh])
        rT, kT, vT = tmp[0:2], tmp[2:4], tmp[4:6]
        S.op("act", lambda e: e.activation(out=tmp[6][0:64, 0:NT], in_=tmp[6][0:64, 0:NT], func=AF.Tanh), rd=[tmp[6]], wr=[tmp[6]])
        S.op("act", lambda e: e.activation(out=tmp[7][:, 0:NT], in_=tmp[7][:, 0:NT], func=AF.Sigmoid), rd=[tmp[7]], wr=[tmp[7]])
        lw, aa, KK, gate, Wi = sc[0:2], sc[2:4], sc[4:6], sc[6:8], sc[8:10]
        At, Rt, Bt, Kt = sc[10:12], sc[12:14], sc[14:16], sc[16:18]
        bonus, yall = sc[20:22], sc[22:24]
        rs = sc[25]
        wmv = wm[:, 0:384].rearrange("p (a n) -> p a n", a=3)
        for pr in range(2):
            cs_ = slice(pr * 128, (pr + 1) * 128)
            S.dma("sp", wmv[0:64, 0, :], self.rw_w2[l][:, cs_], wr=[wm])
            S.dma("sp", wmv[64:128, 1, :], self.rw_a2[l][:, cs_], wr=[wm])
            S.dma("sp", wmv[:, 2, :], self.rw_g2[l][:, cs_], wr=[wm])
            ps = mp[0]
            S.op("pe", lambda e: e.matmul(ps[:, 0:NT], lhsT=wmv[0:64, 0, :], rhs=tmp[6][0:64, 0:NT], start=True, stop=True),
                 rd=[wm, tmp[6]], wr=[ps])
            S.op("act", lambda e: e.activation(out=lw[pr][:, 0:NT], in_=ps[:, 0:NT], func=AF.Sigmoid, bias=ppc("rw_w0", pr)),
                 rd=[ps, pp], wr=[lw[pr]])
            S.op("dve", lambda e: e.tensor_scalar(lw[pr][:, 0:NT], lw[pr][:, 0:NT], -math.exp(-0.5), None, ALU.mult), rd=[lw[pr]], wr=[lw[pr]])
            ps = mp[1]
            S.op("pe", lambda e: e.matmul(ps[:, 0:NT], lhsT=wmv[64:128, 1, :], rhs=tmp[6][64:128, 0:NT], start=True, stop=True),
                 rd=[wm, tmp[6]], wr=[ps])
            S.op("act", lambda e: e.activation(out=aa[pr][:, 0:NT], in_=ps[:, 0:NT], func=AF.Sigmoid, bias=ppc("rw_a0", pr)),
                 rd=[ps, pp], wr=[aa[pr]])
            ps = mp[2]
            S.op("pe", lambda e: e.matmul(ps[:, 0:NT], lhsT=wmv[:, 2, :], rhs=tmp[7][:, 0:NT], start=True, stop=True),
                 rd=[wm, tmp[7]], wr=[ps])
            S.op("act", lambda e: e.activation(out=gate[pr][:, 0:NT], in_=ps[:, 0:NT], func=AF.Copy), rd=[ps], wr=[gate[pr]])
            S.op("dve", lambda e: e.tensor_scalar(KK[pr][:, 0:NT], kT[pr][:, 0:NT], ppc("rw_kk", pr), None, ALU.mult), rd=[kT[pr], pp], wr=[KK[pr]])
            self.head_rms(KK[pr], NT, None, None, rs)
            S.op("dve", lambda e: e.tensor_tensor(out=KK[pr][:, 0:NT], in0=KK[pr][:, 0:NT], in1=rs[:, 0:NT], op=ALU.mult), rd=[KK[pr], rs], wr=[KK[pr]])
            S.op("dve", lambda e: e.tensor_scalar(rs[:, 0:NT], aa[pr][:, 0:NT], -1.0, None, ALU.add), rd=[aa[pr]], wr=[rs])
            S.op("dve", lambda e: e.tensor_scalar(rs[:, 0:NT], rs[:, 0:NT], ppc("rw_ka", pr), 1.0, ALU.mult, ALU.add), rd=[rs, pp], wr=[rs])
            S.op("dve", lambda e: e.tensor_tensor(out=kT[pr][:, 0:NT], in0=kT[pr][:, 0:NT], in1=rs[:, 0:NT], op=ALU.mult), rd=[kT[pr], rs], wr=[kT[pr]])
            cum = rs
            S.op("dve", lambda e: e.tensor_tensor_scan(cum[:, 0:NT], self.cs("rmask_" + sfx)[:, 0:NT], lw[pr][:, 0:NT], 0.0, ALU.mult, ALU.add),
                 rd=[lw[pr], cst], wr=[cum])
            S.op("act", lambda e: e.activation(out=Wi[pr][:, 0:NT], in_=cum[:, 0:NT], func=AF.Exp), rd=[cum], wr=[Wi[pr]])
            S.op("dve", lambda e: e.tensor_tensor(out=lw[pr][:, 0:NT], in0=cum[:, 0:NT], in1=lw[pr][:, 0:NT], op=ALU.subtract), rd=[cum, lw[pr]], wr=[lw[pr]])
            S.op("act", lambda e: e.activation(out=lw[pr][:, 0:NT], in_=lw[pr][:, 0:NT], func=AF.Exp), rd=[lw[pr]], wr=[lw[pr]])
            S.op("act", lambda e: e.activation(out=cum[:, 0:NT], in_=cum[:, 0:NT], func=AF.Exp, scale=-1.0), rd=[cum], wr=[cum])
            S.op("dve", lambda e: e.scalar_tensor_tensor(out=At[pr][:, 0:NT], in0=KK[pr][:, 0:NT], scalar=-1.0, in1=lw[pr][:, 0:NT],
                                                         op0=ALU.mult, op1=ALU.mult), rd=[KK[pr], lw[pr]], wr=[At[pr]])
            S.op("dve", lambda e: e.tensor_tensor(out=Rt[pr][:, 0:NT], in0=rT[pr][:, 0:NT], in1=Wi[pr][:, 0:NT], op=ALU.mult), rd=[rT[pr], Wi[pr]], wr=[Rt[pr]])
            S.op("dve", lambda e: e.tensor_tensor(out=Bt[pr][:, 0:NT], in0=KK[pr][:, 0:NT], in1=aa[pr][:, 0:NT], op=ALU.mult), rd=[KK[pr], aa[pr]], wr=[Bt[pr]])
            S.op("dve", lambda e: e.tensor_tensor(out=Bt[pr][:, 0:NT], in0=Bt[pr][:, 0:NT], in1=cum[:, 0:NT], op=ALU.mult), rd=[Bt[pr], cum], wr=[Bt[pr]])
            S.op("dve", lambda e: e.tensor_tensor(out=Kt[pr][:, 0:NT], in0=kT[pr][:, 0:NT], in1=cum[:, 0:NT], op=ALU.mult), rd=[kT[pr], cum], wr=[Kt[pr]])
            S.op("dve", lambda e: e.scalar_tensor_tensor(out=bonus[pr][:, 0:NT], in0=rT[pr][:, 0:NT], scalar=ppc("rw_rk", pr), in1=kT[pr][:, 0:NT],
                                                         op0=ALU.mult, op1=ALU.mult), rd=[rT[pr], pp, kT[pr]], wr=[bonus[pr]])
            ps = mp[3]
            S.op("pe", lambda e: e.matmul(ps[:, 0:NT], lhsT=self.cs("blk64"), rhs=bonus[pr][:, 0:NT], start=True, stop=True),
                 rd=[cst, bonus[pr]], wr=[ps])
            S.op("dve", lambda e: e.tensor_tensor(out=bonus[pr][:, 0:NT], in0=ps[:, 0:NT], in1=vT[pr][:, 0:NT], op=ALU.mult), rd=[ps, vT[pr]], wr=[bonus[pr]])
        f3 = lambda t_: t_[0:64, 0:256].rearrange("p (h i) -> p h i", h=4)
        for (c0, seqs, C, ms, levels) in self.instances(g, nseq, T):
            cols = slice(c0, c0 + 64)
            ns = len(seqs)
            bm = lambda name: self.cs(name + "_" + ms, 0, 64).unsqueeze(1).to_broadcast([64, 4, 64])
            pA, pB, pC = mp[2], mp[3], mp[4]
            for h in range(4):
                pr, r0 = h // 2, (h % 2) * 64
                a_, r_, b_, k_ = (t_[pr][r0:r0 + 64, cols] for t_ in (At, Rt, Bt, Kt))
                rd_ = [At[pr], Rt[pr], Bt[pr], Kt[pr]]
                for (dst, o_, l_, rr_) in ((pA, h * 128, b_, a_), (pA, h * 128 + 64, a_, b_), (pB, h * 128, b_, r_),
                                           (pB, h * 128 + 64, k_, a_), (pC, h * 64, k_, r_)):
                    S.op("pe", lambda e: e.matmul(dst[0:64, o_:o_ + 64], lhsT=l_, rhs=rr_, start=True, stop=True), rd=rd_, wr=[dst])
            NN, AT2, AT3 = self.nn_t[0], sc[25], sc[0]
            v4 = lambda t_: t_[0:64, 0:512].rearrange("p (h n) -> p h n", h=4)
            S.op("dve", lambda e: e.tensor_tensor(out=R_(v4(NN)[:, :, 0:64]), in0=v4(pA)[:, :, 0:64], in1=bm("triU"), op=ALU.mult), rd=[pA, cst], wr=[NN])
            S.op("dve", lambda e: e.tensor_tensor(out=R_(v4(NN)[:, :, 64:128]), in0=v4(pA)[:, :, 64:128], in1=bm("triL"), op=ALU.mult), rd=[pA, cst], wr=[NN])
            S.op("dve", lambda e: e.tensor_tensor(out=v4(AT2)[:, :, 0:64], in0=v4(pB)[:, :, 0:64], in1=bm("incU"), op=ALU.mult), rd=[pB, cst], wr=[AT2])
            S.op("dve", lambda e: e.tensor_tensor(out=v4(AT2)[:, :, 64:128], in0=v4(pB)[:, :, 64:128], in1=bm("triU"), op=ALU.mult), rd=[pB, cst], wr=[AT2])
            S.op("dve", lambda e: e.tensor_tensor(out=f3(AT3), in0=f3(pC), in1=bm("incU"), op=ALU.mult), rd=[pC, cst], wr=[AT3])
            ATrb = lambda h: v4(AT2)[:, h, 0:64]
            ATak = lambda h: v4(AT2)[:, h, 64:128]
            ATrk = lambda h: f3(AT3)[:, h, :]
            pT0, pT1 = mp[0], mp[1]
            for pr in range(2):
                S.op("pe", lambda e: e.transpose(pT0[0:64, pr * 128:(pr + 1) * 128], vT[pr][:, cols], self.ident[:]), rd=[vT[pr], self.ident], wr=[pT0])
                S.op("pe", lambda e: e.transpose(pT0[0:64, 256 + pr * 128:256 + (pr + 1) * 128], Bt[pr][:, cols], self.ident[:]), rd=[Bt[pr], self.ident], wr=[pT0])
                S.op("pe", lambda e: e.transpose(pT1[0:64, pr * 128:(pr + 1) * 128], Kt[pr][:, cols], self.ident[:]), rd=[Kt[pr], self.ident], wr=[pT1])
            Vtok, Btok, Ktok = sc[1], sc[2], sc[3]
            S.op("act", lambda e: e.activation(out=Vtok[0:64, 0:256], in_=pT0[0:64, 0:256], func=AF.Copy), rd=[pT0], wr=[Vtok])
            S.op("act", lambda e: e.activation(out=Btok[0:64, 0:256], in_=pT0[0:64, 256:512], func=AF.Copy), rd=[pT0], wr=[Btok])
            S.op("act", lambda e: e.activation(out=Ktok[0:64, 0:256], in_=pT1[0:64, 0:256], func=AF.Copy), rd=[pT1], wr=[Ktok])
            if ns > 1:
                apad = [(p[0], p[1]), (p[2], p[3])]
                rpad = [(p[4], p[5]), (p[6], p[7])]
                padv = {}
                for pr in range(2):
                    for nm, src_t, tl in (("a", At[pr], apad[pr]), ("r", Rt[pr], rpad[pr])):
                        for half in range(2):
                            eye = self.cs("eye16").rearrange("p (a b) -> p a b", a=16)[:, half * 8:(half + 1) * 8, :].unsqueeze(3).to_broadcast([128, 8, 16, 4])
                            in0 = src_t[:, cols].rearrange("p (b i) -> p b i", i=4).unsqueeze(1).to_broadcast([128, 8, 16, 4])
                            S.op("pool", lambda e: e.tensor_tensor(out=tl[half][:, 0:512].rearrange("p (a b i) -> p a b i", a=8, b=16),
                                                                   in0=in0, in1=eye, op=ALU.mult), rd=[src_t, cst], wr=[tl[half]])
                        padv[(nm, pr)] = tl
                lk = lambda nm, pr, r0, si: (padv[(nm, pr)][si // 8][r0:r0 + 64, (si % 8) * 64:(si % 8) * 64 + 64], [padv[(nm, pr)][si // 8]])
            else:
                lk = lambda nm, pr, r0, si: ((At if nm == "a" else Rt)[pr][r0:r0 + 64, cols], [(At if nm == "a" else Rt)[pr]])
            pK = mp[1]
            for h in range(4):
                pr, r0 = h // 2, (h % 2) * 64
                for si, s_ in enumerate(seqs):
                    a_, t_ = lk("a", pr, r0, si)
                    S.op("pe", lambda e: e.matmul(pK[0:64, 256 + h * 64:256 + (h + 1) * 64], lhsT=a_, rhs=Hv[r0:r0 + 64, pr, s_, :],
                                                  start=(si == 0), stop=False), rd=t_ + [Ht], wr=[pK])
                S.op("pe", lambda e: e.matmul(pK[0:64, 256 + h * 64:256 + (h + 1) * 64], lhsT=ATak(h), rhs=Vtok[0:64, h * 64:(h + 1) * 64],
                                              start=False, stop=True), rd=[AT2, Vtok], wr=[pK])
            X = self.nx_t[0]
            S.op("act", lambda e: e.activation(out=R_(X[0:64, 0:256]), in_=pK[0:64, 256:512], func=AF.Copy), rd=[pK], wr=[X])
            uh = self.neumann(NN, X, levels)
            pY = mp[0]
            for h in range(4):
                pr, r0 = h // 2, (h % 2) * 64
                o_ = pY[r0:r0 + 64, pr * 64:(pr + 1) * 64]
                for si, s_ in enumerate(seqs):
                    a_, t_ = lk("r", pr, r0, si)
                    S.op("pe", lambda e: e.matmul(o_, lhsT=Hv[r0:r0 + 64, pr, s_, :], rhs=a_, start=(si == 0), stop=False), rd=t_ + [Ht], wr=[pY])
                S.op("pe", lambda e: e.matmul(o_, lhsT=uh(h)[0], rhs=ATrb(h), start=False, stop=False), rd=[uh(h)[1], AT2], wr=[pY])
                S.op("pe", lambda e: e.matmul(o_, lhsT=Vtok[0:64, h * 64:(h + 1) * 64], rhs=ATrk(h), start=False, stop=True), rd=[Vtok, AT3], wr=[pY])
            for pr in range(2):
                S.op("act", lambda e: e.activation(out=yall[pr][:, cols], in_=pY[:, pr * 64:(pr + 1) * 64], func=AF.Copy), rd=[pY], wr=[yall[pr]])
            for pr in range(2):
                pH = [mp[2], mp[3]]
                for hh in range(2):
                    h = pr * 2 + hh
                    r0 = hh * 64
                    if ns > 1:
                        Up = (tmp[0], tmp[1])
                        Vp = (tmp[2], tmp[3])
                        for half in range(2):
                            bs_ = self.cs("bsel", 0, 64)[:, half * 8:(half + 1) * 8].unsqueeze(2).to_broadcast([64, 8, 64])
                            for (dst_, src_ap, src_t) in ((Up, uh(h)[0], uh(h)[1]), (Vp, Vtok[0:64, h * 64:(h + 1) * 64], Vtok)):
                                S.op("pool", lambda e: e.tensor_tensor(
                                    out=dst_[half][0:64, 0:512].rearrange("p (s v) -> p s v", s=8),
                                    in0=src_ap.unsqueeze(1).to_broadcast([64, 8, 64]), in1=bs_, op=ALU.mult),
                                    rd=[src_t, cst], wr=[dst_[half]])
                            S.op("pe", lambda e: e.matmul(pH[half][r0:r0 + 64, 0:512], lhsT=Btok[0:64, h * 64:(h + 1) * 64], rhs=Up[half][0:64, 0:512],
                                                          start=True, stop=False), rd=[Btok, Up[half]], wr=[pH[half]])
                            S.op("pe", lambda e: e.matmul(pH[half][r0:r0 + 64, 0:512], lhsT=Ktok[0:64, h * 64:(h + 1) * 64], rhs=Vp[half][0:64, 0:512],
                                                          start=False, stop=True), rd=[Ktok, Vp[half]], wr=[pH[half]])
                    else:
                        S.op("pe", lambda e: e.matmul(pH[0][r0:r0 + 64, 0:64], lhsT=Btok[0:64, h * 64:(h + 1) * 64], rhs=uh(h)[0],
                                                      start=True, stop=False), rd=[Btok, uh(h)[1]], wr=[pH[0]])
                        S.op("pe", lambda e: e.matmul(pH[0][r0:r0 + 64, 0:64], lhsT=Ktok[0:64, h * 64:(h + 1) * 64], rhs=Vtok[0:64, h * 64:(h + 1) * 64],
                                                      start=False, stop=True), rd=[Ktok, Vtok], wr=[pH[0]])
                    wC = Wi[pr][r0:r0 + 64, cols].rearrange("p (s i) -> p s i", i=C)[:, :, C - 1]
                    if ns > 1:
                        for half in range(2):
                            hs2 = Hv[r0:r0 + 64, pr, half * 8:(half + 1) * 8, :]
                            S.op("dve", lambda e: e.tensor_tensor(out=hs2, in0=hs2, in1=pH[half][r0:r0 + 64, 0:512].rearrange("p (s v) -> p s v", s=8),
                                                                  op=ALU.add), rd=[Ht, pH[half]], wr=[Ht])
                        for si in range(ns):
                            hs1 = Hv[r0:r0 + 64, pr, si, :]
                            S.op("dve", lambda e: e.tensor_scalar(hs1, hs1, wC[:, si:si + 1], None, ALU.mult), rd=[Ht, Wi[pr]], wr=[Ht])
                    else:
                        hsl = Hv[r0:r0 + 64, pr, seqs[0], :]
                        S.op("dve", lambda e: e.tensor_tensor(out=hsl, in0=hsl, in1=pH[0][r0:r0 + 64, 0:64], op=ALU.add), rd=[Ht, pH[0]], wr=[Ht])
                        S.op("dve", lambda e: e.tensor_scalar(hsl, hsl, Wi[pr][r0:r0 + 64, c0 + 63:c0 + 64], None, ALU.mult), rd=[Ht, Wi[pr]], wr=[Ht])
        if last:
            og = self.o_rw_s if samp else self.o_rw_p
            S.dma("act", og[l], Hv, rd=[Ht])
        if "rw" not in self.mixers:
            return
        for pr in range(2):
            y = yall[pr]
            psM, psV = mp[0], mp[1]
            sq = sc[19]
            S.op("pe", lambda e: e.matmul(psM[:, 0:NT], lhsT=self.cs("blk64"), rhs=y[:, 0:NT], start=True, stop=True), rd=[cst, y], wr=[psM])
            S.op("act", lambda e: e.activation(out=sq[:, 0:NT], in_=y[:, 0:NT], func=AF.Square), rd=[y], wr=[sq])
            S.op("pe", lambda e: e.matmul(psV[:, 0:NT], lhsT=self.cs("blk64"), rhs=sq[:, 0:NT], start=True, stop=True), rd=[cst, sq], wr=[psV])
            mean, var = sc[0], sc[1]
            S.op("act", lambda e: e.activation(out=mean[:, 0:NT], in_=psM[:, 0:NT], func=AF.Copy, scale=1.0 / 64), rd=[psM], wr=[mean])
            S.op("dve", lambda e: e.tensor_tensor(out=var[:, 0:NT], in0=mean[:, 0:NT], in1=mean[:, 0:NT], op=ALU.mult), rd=[mean], wr=[var])
            S.op("dve", lambda e: e.scalar_tensor_tensor(out=var[:, 0:NT], in0=psV[:, 0:NT], scalar=1.0 / 64, in1=var[:, 0:NT],
                                                         op0=ALU.mult, op1=ALU.subtract), rd=[psV, var], wr=[var])
            S.op("act", lambda e: e.activation(out=var[:, 0:NT], in_=var[:, 0:NT], func=AF.Sqrt, bias=64e-5), rd=[var], wr=[var])
            S.op("dve", lambda e: e.reciprocal(var[:, 0:NT], var[:, 0:NT]), rd=[var], wr=[var])
            S.op("dve", lambda e: e.tensor_tensor(out=y[:, 0:NT], in0=y[:, 0:NT], in1=mean[:, 0:NT], op=ALU.subtract), rd=[y, mean], wr=[y])
            S.op("dve", lambda e: e.tensor_tensor(out=y[:, 0:NT], in0=y[:, 0:NT], in1=var[:, 0:NT], op=ALU.mult), rd=[y, var], wr=[y])
            S.op("dve", lambda e: e.tensor_scalar(y[:, 0:NT], y[:, 0:NT], ppc("rw_ln_w", pr), ppc("rw_ln_b", pr), ALU.mult, ALU.add), rd=[y, pp], wr=[y])
            S.op("dve", lambda e: e.tensor_tensor(out=y[:, 0:NT], in0=y[:, 0:NT], in1=bonus[pr][:, 0:NT], op=ALU.add), rd=[y, bonus[pr]], wr=[y])
            S.op("dve", lambda e: e.tensor_tensor(out=self.mix[pr][:, 0:NT], in0=y[:, 0:NT], in1=gate[pr][:, 0:NT], op=ALU.mult),
                 rd=[y, gate[pr]], wr=[self.mix[pr]])

    def gdn(self, g, st, l, NT, nseq, T, last):
        S = self.S
        samp = (g == "s")
        mp, sc, cst, tmp, pp = self.mp, self.sc, self.cst, self.tmp, self.pp[l]
        sfx = "s" if samp else "p"
        import os
        STOP = float(os.environ.get("KGDN_STOP", "99"))
        v3 = lambda ap: ap.rearrange("p (s t) -> p s t", t=T)
        if samp:
            cb = sc[17]
            S.dma("sp", cb[:, 0:6 * nseq * 3].rearrange("p (c s t) -> p c s t", c=6, s=nseq), self.conv_s[l], wr=[cb])
            stH = self.stage[self.wi]
            self.wi ^= 1
            S.dma("sp", stH[:, 0:2 * nseq * 64].rearrange("p (a s v) -> p a s v", a=2, s=nseq), self.gdn_s[l], wr=[stH])
            Ht = stH
        else:
            cb = self.gcb[l]
            Ht = self.gH[l]
        cbv = cb[:, 0:6 * nseq * 3].rearrange("p (c s t) -> p c s t", c=6, s=nseq)
        Hv = Ht[:, 0:2 * nseq * 64].rearrange("p (a s v) -> p a s v", a=2, s=nseq)
        for c in range(6):
            x = self.p[10 + c]
            xv = v3(x[:, 0:NT])
            acc = tmp[c]
            av = v3(acc[:, 0:NT])
            w = lambda i: pp[:, PPO["gdn_conv_w"] + i * 6 + c:PPO["gdn_conv_w"] + i * 6 + c + 1]
            S.op("dve", lambda e: e.tensor_scalar(acc[:, 0:NT], x[:, 0:NT], w(3), None, ALU.mult), rd=[x, pp], wr=[acc])
            for i in (1, 2, 3):
                S.op("dve", lambda e: e.scalar_tensor_tensor(out=av[:, :, i:T], in0=xv[:, :, 0:T - i], scalar=w(3 - i), in1=av[:, :, i:T],
                                                             op0=ALU.mult, op1=ALU.add), rd=[x, pp, acc], wr=[acc])
                S.op("dve", lambda e: e.scalar_tensor_tensor(out=av[:, :, 0:i], in0=cbv[:, c, :, 3 - i:3], scalar=w(3 - i), in1=av[:, :, 0:i],
                                                             op0=ALU.mult, op1=ALU.add), rd=[cb, pp, acc], wr=[acc])
            if last:
                oc_ = self.o_conv_s if samp else self.o_conv_p
                for t_ in range(3):
                    S.dma("act", oc_[l][:, t_, c * 128:(c + 1) * 128].rearrange("s p -> p s"), xv[:, :, T - 3 + t_], rd=[x])
            if not samp:
                S.op("pool", lambda e: e.tensor_copy(cbv[:, c, :, :], xv[:, :, T - 3:T]), rd=[x], wr=[cb])
            S.op("act", lambda e: e.activation(out=acc[:, 0:NT], in_=acc[:, 0:NT], func=AF.Silu), rd=[acc], wr=[acc])
        if STOP <= 1:
            return
        rs = sc[16]
        for c in range(4):
            self.head_rms(tmp[c], NT, None, None, rs)
            if c < 2:
                S.op("dve", lambda e: e.scalar_tensor_tensor(out=tmp[c][:, 0:NT], in0=tmp[c][:, 0:NT], scalar=0.125, in1=rs[:, 0:NT],
                                                             op0=ALU.mult, op1=ALU.mult), rd=[tmp[c], rs], wr=[tmp[c]])
            else:
                S.op("dve", lambda e: e.tensor_tensor(out=tmp[c][:, 0:NT], in0=tmp[c][:, 0:NT], in1=rs[:, 0:NT], op=ALU.mult),
                     rd=[tmp[c], rs], wr=[tmp[c]])
        qT, kT, vT = tmp[0:2], tmp[2:4], tmp[4:6]
        if STOP <= 2:
            return
        bg = sc[15]
        bg2 = sc[14]
        S.op("act", lambda e: e.activation(out=bg[0:8, 0:NT], in_=self.p[16][0:8, 0:NT], func=AF.Sigmoid), rd=[self.p[16]], wr=[bg])
        S.op("act", lambda e: e.activation(out=bg2[0:8, 0:NT], in_=self.p[16][0:8, 0:NT], func=AF.Exp,
                                           bias=pp[0:8, PPO["gdn_dtb"]:PPO["gdn_dtb"] + 1]), rd=[self.p[16], pp], wr=[bg2])
        S.op("act", lambda e: e.activation(out=bg2[0:8, 0:NT], in_=bg2[0:8, 0:NT], func=AF.Ln, bias=1.0), rd=[bg2], wr=[bg2])
        S.op("dve", lambda e: e.tensor_scalar(bg2[0:8, 0:NT], bg2[0:8, 0:NT], self.gnea[l][0:8, 0:1], None, ALU.mult),
             rd=[bg2, self.gnea[l]], wr=[bg2])
        S.op("dve", lambda e: e.tensor_tensor_scan(bg2[0:8, 0:NT], self.cs("rmask_" + sfx)[0:8, 0:NT], bg2[0:8, 0:NT], 0.0, ALU.mult, ALU.add),
             rd=[bg2, cst], wr=[bg2])
        oall = [sc[12], sc[13]]
        if STOP <= 3:
            return
        selrow = self.cs("selrow").rearrange("p (h n) -> p h n", h=4)
        selpair = self.cs("selpair").rearrange("p (a n) -> p a n", a=2)
        for (c0, seqs, C, ms, levels) in self.instances(g, nseq, T):
            cols = slice(c0, c0 + 64)
            ns = len(seqs)
            pt = mp[0]
            S.op("pe", lambda e: e.transpose(pt[0:64, 0:8], bg[0:8, cols], self.ident[0:8, 0:8]), rd=[bg, self.ident], wr=[pt])
            S.op("pe", lambda e: e.transpose(pt[0:64, 8:16], bg2[0:8, cols], self.ident[0:8, 0:8]), rd=[bg2, self.ident], wr=[pt])
            cT = sc[0]
            S.op("act", lambda e: e.activation(out=cT[0:64, 0:16], in_=pt[0:64, 0:16], func=AF.Copy), rd=[pt], wr=[cT])
            beta_c, gc_c = cT[0:64, 0:4], cT[0:64, 12:16]
            if STOP <= 4:
                continue
            pG = mp[1]
            for h in range(4):
                S.op("pe", lambda e: e.matmul(pG[:, h * 64:(h + 1) * 64], lhsT=selrow[0:8, h, :], rhs=bg2[0:8, cols], start=True, stop=True),
                     rd=[cst, bg2], wr=[pG])
            pGv = pG[0:64, 0:256].rearrange("p (h i) -> p h i", h=4)
            exG = sc[1]
            S.op("act", lambda e: e.activation(out=exG[:, 0:256], in_=pG[:, 0:256], func=AF.Exp), rd=[pG], wr=[exG])
            exGv = exG[:, 0:256].rearrange("p (h i) -> p h i", h=4)
            if STOP <= 4.2:
                continue
            E1, Da, Db = sc[2], sc[3], sc[4]
            gcB = gc_c.unsqueeze(2).to_broadcast([64, 4, 64])
            f3 = lambda t_: t_[0:64, 0:256].rearrange("p (h i) -> p h i", h=4)
            bm = lambda name: self.cs(name + "_" + ms, 0, 64).unsqueeze(1).to_broadcast([64, 4, 64])
            self.colop(E1, f3(E1), pGv, gc_c, ALU.subtract, [pG, cT])
            if STOP <= 4.4:
                continue
            S.op("pool", lambda e: e.tensor_scalar(Da[0:64, 0:256], E1[0:64, 0:256], 0.0, None, ALU.min), rd=[E1], wr=[Da])
            S.op("pool", lambda e: e.tensor_scalar(Db[0:64, 0:256], E1[0:64, 0:256], -1.0, 0.0, ALU.mult, ALU.min), rd=[E1], wr=[Db])
            if STOP <= 4.5:
                continue
            S.op("act", lambda e: e.activation(out=Da[0:64, 0:256], in_=Da[0:64, 0:256], func=AF.Exp), rd=[Da], wr=[Da])
            S.op("act", lambda e: e.activation(out=Db[0:64, 0:256], in_=Db[0:64, 0:256], func=AF.Exp), rd=[Db], wr=[Db])
            if STOP <= 4.6:
                continue
            S.op("dve", lambda e: e.tensor_tensor(out=f3(E1), in0=pGv, in1=self.cs("last_" + ms, 0, 64).unsqueeze(1).to_broadcast([64, 4, 64]),
                                                  op=ALU.mult), rd=[pG, cst], wr=[E1])
            S.op("dve", lambda e: e.tensor_reduce(out=cT[0:64, 16:20], in_=f3(E1), axis=AX.X, op=ALU.add), rd=[E1], wr=[cT])
            if STOP <= 4.8:
                continue
            S.op("dve", lambda e: e.tensor_tensor(out=cT[0:64, 20:24], in0=cT[0:64, 16:20], in1=gc_c, op=ALU.subtract), rd=[cT], wr=[cT])
            S.op("act", lambda e: e.activation(out=cT[0:64, 20:24], in_=cT[0:64, 20:24], func=AF.Exp), rd=[cT], wr=[cT])
            S.op("act", lambda e: e.activation(out=cT[0:64, 24:28], in_=gc_c, func=AF.Exp), rd=[cT], wr=[cT])
            S.op("dve", lambda e: e.scalar_tensor_tensor(out=cT[0:64, 28:32], in0=cT[0:64, 24:28], scalar=-1.0, in1=beta_c,
                                                         op0=ALU.mult, op1=ALU.mult), rd=[cT], wr=[cT])
            dec_c, nbg_c = cT[0:64, 20:24], cT[0:64, 28:32]
            if STOP <= 5:
                continue
            bkT = [sc[5], sc[6]]
            for pr in range(2):
                pb = mp[0]
                S.op("pe", lambda e: e.matmul(pb[:, 64:128], lhsT=selpair[0:8, pr, :], rhs=bg[0:8, cols], start=True, stop=True),
                     rd=[cst, bg], wr=[pb])
                S.op("dve", lambda e: e.tensor_tensor(out=bkT[pr][:, 0:64], in0=pb[:, 64:128], in1=kT[pr][:, cols], op=ALU.mult),
                     rd=[pb, kT[pr]], wr=[bkT[pr]])
            pA, pB = mp[2], mp[3]
            for h in range(4):
                pr, r0 = h // 2, (h % 2) * 64
                kh, bkh, qh = kT[pr][r0:r0 + 64, cols], bkT[pr][r0:r0 + 64, 0:64], qT[pr][r0:r0 + 64, cols]
                S.op("pe", lambda e: e.matmul(pA[0:64, h * 128:h * 128 + 64], lhsT=kh, rhs=bkh, start=True, stop=True),
                     rd=[kT[pr], bkT[pr]], wr=[pA])
                S.op("pe", lambda e: e.matmul(pA[0:64, h * 128 + 64:h * 128 + 128], lhsT=bkh, rhs=kh, start=True, stop=True),
                     rd=[kT[pr], bkT[pr]], wr=[pA])
                S.op("pe", lambda e: e.matmul(pB[0:64, h * 64:h * 64 + 64], lhsT=kh, rhs=qh, start=True, stop=True),
                     rd=[kT[pr], qT[pr]], wr=[pB])
            NN, AQ = self.nn_t[0], sc[8]
            pAv = pA[0:64, 0:512].rearrange("p (h n) -> p h n", h=4)
            nnv = NN[0:64, 0:512].rearrange("p (h n) -> p h n", h=4)
            S.op("dve", lambda e: e.tensor_tensor(out=f3(E1), in0=f3(Da), in1=bm("triU"), op=ALU.mult), rd=[Da, cst], wr=[E1])
            S.op("dve", lambda e: e.scalar_tensor_tensor(out=R_(nnv[:, :, 0:64]), in0=pAv[:, :, 0:64], scalar=-1.0, in1=f3(E1),
                                                         op0=ALU.mult, op1=ALU.mult), rd=[pA, E1], wr=[NN])
            S.op("dve", lambda e: e.tensor_tensor(out=f3(Db), in0=f3(Db), in1=bm("triL"), op=ALU.mult), rd=[Db, cst], wr=[Db])
            S.op("dve", lambda e: e.scalar_tensor_tensor(out=R_(nnv[:, :, 64:128]), in0=pAv[:, :, 64:128], scalar=-1.0, in1=f3(Db),
                                                         op0=ALU.mult, op1=ALU.mult), rd=[pA, Db], wr=[NN])
            S.op("dve", lambda e: e.tensor_tensor(out=f3(Da), in0=f3(Da), in1=bm("incU"), op=ALU.mult), rd=[Da, cst], wr=[Da])
            S.op("dve", lambda e: e.tensor_tensor(out=AQ[0:64, 0:256], in0=pB[0:64, 0:256], in1=Da[0:64, 0:256], op=ALU.mult),
                 rd=[pB, Da], wr=[AQ])
            if STOP <= 6:
                continue
            pT_ = mp[0]
            for pr in range(2):
                S.op("pe", lambda e: e.transpose(pT_[0:64, pr * 128:(pr + 1) * 128], vT[pr][:, cols], self.ident[:]),
                     rd=[vT[pr], self.ident], wr=[pT_])
                S.op("pe", lambda e: e.transpose(pT_[0:64, 256 + pr * 128:256 + (pr + 1) * 128], kT[pr][:, cols], self.ident[:]),
                     rd=[kT[pr], self.ident], wr=[pT_])
            Vb, Kd = sc[9], sc[10]
            self.colop(Vb, f3(Vb), pT_[0:64, 0:256].rearrange("p (h v) -> p h v", h=4), beta_c, ALU.mult, [pT_, cT])
            self.colop(Kd, f3(Kd), pT_[0:64, 256:512].rearrange("p (h v) -> p h v", h=4), dec_c, ALU.mult, [pT_, cT])
            if STOP <= 7:
                continue
            if ns > 1:
                kpad = [(sc[20], sc[21]), (sc[22], sc[23])]
                qpad = [(sc[24], sc[25]), (tmp[6], tmp[7])]
                padv = {}
                for pr in range(2):
                    for nm, src_t, tl in (("k", kT[pr], kpad[pr]), ("q", qT[pr], qpad[pr])):
                        for half in range(2):
                            eye = self.cs("eye16").rearrange("p (a b) -> p a b", a=16)[:, half * 8:(half + 1) * 8, :].unsqueeze(3).to_broadcast([128, 8, 16, 4])
                            in0 = src_t[:, cols].rearrange("p (b i) -> p b i", i=4).unsqueeze(1).to_broadcast([128, 8, 16, 4])
                            S.op("pool", lambda e: e.tensor_tensor(out=tl[half][:, 0:512].rearrange("p (a b i) -> p a b i", a=8, b=16),
                                                                   in0=in0, in1=eye, op=ALU.mult), rd=[src_t, cst], wr=[tl[half]])
                        padv[(nm, pr)] = tl
                lk = lambda nm, pr, r0, si: (padv[(nm, pr)][si // 8][r0:r0 + 64, (si % 8) * 64:(si % 8) * 64 + 64], [padv[(nm, pr)][si // 8]])
            else:
                lk = lambda nm, pr, r0, si: ((kT if nm == "k" else qT)[pr][r0:r0 + 64, cols], [(kT if nm == "k" else qT)[pr]])
            pK = mp[1]
            for h in range(4):
                pr, r0 = h // 2, (h % 2) * 64
                for si, s_ in enumerate(seqs):
                    a_, t_ = lk("k", pr, r0, si)
                    S.op("pe", lambda e: e.matmul(pK[0:64, h * 64:(h + 1) * 64], lhsT=a_, rhs=Hv[r0:r0 + 64, pr, s_, :],
                                                  start=(si == 0), stop=(si == ns - 1)), rd=t_ + [Ht], wr=[pK])
            X = self.nx_t[0]
            self.colop(E1, f3(E1), pK[0:64, 0:256].rearrange("p (h v) -> p h v", h=4), nbg_c, ALU.mult, [pK, cT])
            S.op("dve", lambda e: e.tensor_tensor(out=R_(X[0:64, 0:256]), in0=E1[0:64, 0:256], in1=Vb[0:64, 0:256], op=ALU.add),
                 rd=[E1, Vb], wr=[X])
            if STOP <= 8:
                continue
            uh = self.neumann(NN, X, levels)
            if STOP <= 9:
                continue
            pY1, pY2 = mp[0], mp[1]
            for h in range(4):
                pr, r0 = h // 2, (h % 2) * 64
                for si, s_ in enumerate(seqs):
                    a_, t_ = lk("q", pr, r0, si)
                    S.op("pe", lambda e: e.matmul(pY1[r0:r0 + 64, pr * 64:(pr + 1) * 64], lhsT=Hv[r0:r0 + 64, pr, s_, :], rhs=a_,
                                                  start=(si == 0), stop=(si == ns - 1)), rd=t_ + [Ht], wr=[pY1])
                S.op("pe", lambda e: e.matmul(pY2[r0:r0 + 64, pr * 64:(pr + 1) * 64], lhsT=uh(h)[0],
                                              rhs=AQ[0:64, h * 64:(h + 1) * 64], start=True, stop=True), rd=[uh(h)[1], AQ], wr=[pY2])
            for h in range(4):
                pr, r0 = h // 2, (h % 2) * 64
                S.op("dve", lambda e: e.tensor_tensor(out=oall[pr][r0:r0 + 64, cols], in0=pY1[r0:r0 + 64, pr * 64:(pr + 1) * 64],
                                                      in1=exGv[r0:r0 + 64, h, :], op=ALU.mult), rd=[pY1, exG], wr=[oall[pr]])
            for pr in range(2):
                S.op("dve", lambda e: e.tensor_tensor(out=oall[pr][:, cols], in0=pY2[:, pr * 64:(pr + 1) * 64], in1=oall[pr][:, cols], op=ALU.add),
                     rd=[pY2, oall[pr]], wr=[oall[pr]])
            if STOP <= 10:
                continue
            for pr in range(2):
                pH = [mp[2], mp[3]]
                for hh in range(2):
                    h = pr * 2 + hh
                    r0 = hh * 64
                    if ns > 1:
                        Up = (sc[5], sc[6]) if hh == 0 else (sc[9], sc[2])
                        for half in range(2):
                            S.op("pool", lambda e: e.tensor_tensor(
                                out=Up[half][0:64, 0:512].rearrange("p (s v) -> p s v", s=8),
                                in0=uh(h)[0].unsqueeze(1).to_broadcast([64, 8, 64]),
                                in1=self.cs("bsel", 0, 64)[:, half * 8:(half + 1) * 8].unsqueeze(2).to_broadcast([64, 8, 64]), op=ALU.mult),
                                rd=[uh(h)[1], cst], wr=[Up[half]])
                            S.op("pe", lambda e: e.matmul(pH[half][r0:r0 + 64, 0:512], lhsT=Kd[0:64, h * 64:(h + 1) * 64], rhs=Up[half][0:64, 0:512],
                                                          start=True, stop=True), rd=[Kd, Up[half]], wr=[pH[half]])
                    else:
                        S.op("pe", lambda e: e.matmul(pH[0][r0:r0 + 64, 0:64], lhsT=Kd[0:64, h * 64:(h + 1) * 64], rhs=uh(h)[0],
                                                      start=True, stop=True), rd=[Kd, uh(h)[1]], wr=[pH[0]])
                    gC = exGv[r0:r0 + 64, h, :].rearrange("p (s i) -> p s i", i=C)[:, :, C - 1]
                    if ns > 1:
                        for si in range(ns):
                            hs1 = Hv[r0:r0 + 64, pr, si, :]
                            S.op("dve", lambda e: e.tensor_scalar(hs1, hs1, gC[:, si:si + 1], None, ALU.mult), rd=[Ht, exG], wr=[Ht])
                        for half in range(2):
                            hs2 = Hv[r0:r0 + 64, pr, half * 8:(half + 1) * 8, :]
                            S.op("dve", lambda e: e.tensor_tensor(out=hs2, in0=hs2, in1=pH[half][r0:r0 + 64, 0:512].rearrange("p (s v) -> p s v", s=8),
                                                                  op=ALU.add), rd=[Ht, pH[half]], wr=[Ht])
                    else:
                        hsl = Hv[r0:r0 + 64, pr, seqs[0], :]
                        S.op("dve", lambda e: e.scalar_tensor_tensor(out=hsl, in0=hsl, scalar=exGv[r0:r0 + 64, h, 63:64], in1=pH[0][r0:r0 + 64, 0:64],
                                                                     op0=ALU.mult, op1=ALU.add), rd=[Ht, exG, pH[0]], wr=[Ht])
        if last:
            og = self.o_gdn_s if samp else self.o_gdn_p
            S.dma("act", og[l], Hv, rd=[Ht])
        if "gdn" not in self.mixers or STOP < 99:
            return
        for pr in range(2):
            self.head_rms(oall[pr], NT, 1.0 / 64, EPS, rs)
            S.op("dve", lambda e: e.scalar_tensor_tensor(out=oall[pr][:, 0:NT], in0=oall[pr][:, 0:NT], scalar=pp[:, PPO["gdn_nw"]:PPO["gdn_nw"] + 1],
                                                         in1=rs[:, 0:NT], op0=ALU.mult, op1=ALU.mult), rd=[oall[pr], pp, rs], wr=[oall[pr]])
            S.op("act", lambda e: e.activation(out=sc[0][:, 0:NT], in_=self.p[17 + pr][:, 0:NT], func=AF.Silu), rd=[self.p[17 + pr]], wr=[sc[0]])
            S.op("dve", lambda e: e.tensor_tensor(out=self.mix[4 + pr][:, 0:NT], in0=oall[pr][:, 0:NT], in1=sc[0][:, 0:NT], op=ALU.mult),
                 rd=[oall[pr], sc[0]], wr=[self.mix[4 + pr]])

    def s5_setup(self, l):
        S = self.S
        k = self.s5k[l]
        pp = self.pp[l]
        col = lambda i: k[:, i * 8:(i + 1) * 8]
        ppv = lambda n: pp[:, PPO[n]:PPO[n] + 8]
        D_, T1, MAG, TH, KF, SIN, COS, ABR, ABI, DEN, NR, CFR, CFI, NCFR, T2, T3 = [col(i) for i in range(16)]
        self.S5 = dict(MAG=2, TH=3, CFR=11, CFI=12, NCFR=13)
        ki = self.s5i[:, 0:8]
        r, w = [k, pp], [k]
        A = lambda eng, fn, rd=r, wr=w: S.op(eng, fn, rd=rd, wr=wr)
        A("act", lambda e: e.activation(out=D_, in_=ppv("s5_log_dt"), func=AF.Exp))
        A("dve", lambda e: e.tensor_tensor(out=T1, in0=D_, in1=ppv("s5_a_re"), op=ALU.mult))
        A("act", lambda e: e.activation(out=MAG, in_=T1, func=AF.Exp))
        A("dve", lambda e: e.tensor_tensor(out=TH, in0=D_, in1=ppv("s5_a_im"), op=ALU.mult))
        A("dve", lambda e: e.tensor_scalar(KF, TH, 1.0 / (2 * math.pi), None, ALU.mult))
        A("dve", lambda e: e.tensor_copy(ki, KF), rd=[k], wr=[self.s5i])
        A("dve", lambda e: e.tensor_copy(KF, ki), rd=[self.s5i], wr=[k])
        A("dve", lambda e: e.scalar_tensor_tensor(out=TH, in0=KF, scalar=-2 * math.pi, in1=TH, op0=ALU.mult, op1=ALU.add))
        A("dve", lambda e: e.tensor_scalar(TH, TH, -3.1415925, 3.1415925, ALU.max, ALU.min))
        A("act", lambda e: e.activation(out=SIN, in_=TH, func=AF.Sin))
        A("act", lambda e: e.activation(out=T2, in_=TH, func=AF.Abs))
        A("dve", lambda e: e.tensor_scalar(T2, T2, -1.0, math.pi / 2, ALU.mult, ALU.add))
        A("act", lambda e: e.activation(out=COS, in_=T2, func=AF.Sin))
        A("dve", lambda e: e.tensor_tensor(out=ABR, in0=MAG, in1=COS, op=ALU.mult))
        A("dve", lambda e: e.tensor_tensor(out=ABI, in0=MAG, in1=SIN, op=ALU.mult))
        A("dve", lambda e: e.tensor_tensor(out=DEN, in0=ppv("s5_a_re"), in1=ppv("s5_a_re"), op=ALU.mult))
        A("dve", lambda e: e.tensor_tensor(out=T2, in0=ppv("s5_a_im"), in1=ppv("s5_a_im"), op=ALU.mult))
        A("dve", lambda e: e.tensor_tensor(out=DEN, in0=DEN, in1=T2, op=ALU.add))
        A("dve", lambda e: e.reciprocal(DEN, DEN))
        A("dve", lambda e: e.tensor_scalar(NR, ABR, -1.0, None, ALU.add))
        A("dve", lambda e: e.tensor_tensor(out=T2, in0=NR, in1=ppv("s5_a_re"), op=ALU.mult))
        A("dve", lambda e: e.tensor_tensor(out=T3, in0=ABI, in1=ppv("s5_a_im"), op=ALU.mult))
        A("dve", lambda e: e.tensor_tensor(out=T2, in0=T2, in1=T3, op=ALU.add))
        A("dve", lambda e: e.tensor_tensor(out=CFR, in0=T2, in1=DEN, op=ALU.mult))
        A("dve", lambda e: e.tensor_tensor(out=T2, in0=ABI, in1=ppv("s5_a_re"), op=ALU.mult))
        A("dve", lambda e: e.tensor_tensor(out=T3, in0=NR, in1=ppv("s5_a_im"), op=ALU.mult))
        A("dve", lambda e: e.tensor_tensor(out=T2, in0=T2, in1=T3, op=ALU.subtract))
        A("dve", lambda e: e.tensor_tensor(out=CFI, in0=T2, in1=DEN, op=ALU.mult))
        A("dve", lambda e: e.tensor_scalar(NCFR, CFR, -1.0, None, ALU.mult))

    def s5(self, g, st, l, NT, nseq, T, last):
        S = self.S
        samp = (g == "s")
        mp, sc, cst = self.mp, self.sc, self.cst
        k = self.s5k[l]
        kc = lambda name, j: k[:, self.S5[name] * 8 + j:self.S5[name] * 8 + j + 1]
        v3 = lambda ap: ap.rearrange("p (s t) -> p s t", t=T)
        stgB, stgC = self.stage[0], self.stage[1]
        S.dma("sp", stgB[:, 0:2048], self.s5B[l], wr=[stgB])
        S.dma("sp", stgC[:, 0:2048], self.s5C[l], wr=[stgC])
        Bm = stgB[:, 0:2048].rearrange("p (a j n) -> p a j n", a=2, j=8)
        Cm = stgC[:, 0:2048].rearrange("p (a j n) -> p a j n", a=2, j=8)
        S.op("dve", lambda e: e.tensor_scalar(stgC[:, 1024:2048], stgC[:, 1024:2048], -1.0, None, ALU.mult), rd=[stgC], wr=[stgC])
        wg = sc[19]
        S.dma("sp", wg[:, 0:512].rearrange("p (k n) -> p k n", k=2), self.s5glu[l].rearrange("(k p) n -> p k n", p=128), wr=[wg])
        if samp:
            hre, him = sc[17], sc[18]
            S.dma("sp", hre[:, 0:8 * nseq].rearrange("p (j s) -> p j s", j=8), self.s5s_re[l], wr=[hre])
            S.dma("sp", him[:, 0:8 * nseq].rearrange("p (j s) -> p j s", j=8), self.s5s_im[l], wr=[him])
        else:
            hre, him = self.s5h[l]
        hv = lambda t: t[:, 0:8 * nseq].rearrange("p (j s) -> p j s", j=8)
        setA = [sc[0], sc[1], sc[2], sc[3], sc[4], sc[5], sc[6], sc[7], sc[8]]
        setB = [sc[12], sc[13], sc[14], sc[15], sc[16], sc[20], sc[21], sc[22], sc[23]]
        ei = self.s5i[:, 0:T]
        bc = lambda ap: ap.unsqueeze(1).to_broadcast([128, nseq, T])
        u = [self.p[8], self.p[9]]
        Y = [mp[2], mp[3]]
        zt = [sc[9], sc[10]]
        for j in range(8):
            ET, F, Z1, Z2, ZR, ZI, DK, HR, HI = setA if j % 2 == 0 else setB
            ER, EI = ET[:, 0:T], ET[:, 256:256 + T]
            S.op("act", lambda e: e.activation(out=Z1[:, 0:T], in_=self.cs("iota")[:, 0:T], func=AF.Copy, scale=kc("TH", j)),
                 rd=[cst, k], wr=[Z1])
            S.op("pool", lambda e: e.tensor_scalar(Z2[:, 0:T], Z1[:, 0:T], 1.0 / (2 * math.pi), None, ALU.mult), rd=[Z1], wr=[Z2])
            S.op("dve", lambda e: e.tensor_copy(ei, Z2[:, 0:T]), rd=[Z2], wr=[self.s5i])
            S.op("dve", lambda e: e.tensor_copy(Z2[:, 0:T], ei), rd=[self.s5i], wr=[Z2])
            S.op("dve", lambda e: e.scalar_tensor_tensor(out=Z1[:, 0:T], in0=Z2[:, 0:T], scalar=-2 * math.pi, in1=Z1[:, 0:T],
                                                          op0=ALU.mult, op1=ALU.add), rd=[Z1, Z2], wr=[Z1])
            S.op("dve", lambda e: e.tensor_scalar(Z1[:, 0:T], Z1[:, 0:T], -3.1415925, 3.1415925, ALU.max, ALU.min), rd=[Z1], wr=[Z1])
            S.op("act", lambda e: e.activation(out=EI, in_=Z1[:, 0:T], func=AF.Sin), rd=[Z1], wr=[ET])
            S.op("act", lambda e: e.activation(out=Z2[:, 0:T], in_=Z1[:, 0:T], func=AF.Abs), rd=[Z1], wr=[Z2])
            S.op("pool", lambda e: e.tensor_scalar(Z2[:, 0:T], Z2[:, 0:T], -1.0, math.pi / 2, ALU.mult, ALU.add), rd=[Z2], wr=[Z2])
            S.op("act", lambda e: e.activation(out=ER, in_=Z2[:, 0:T], func=AF.Sin), rd=[Z2], wr=[ET])
            FR, FI = F[:, 0:T], F[:, 256:256 + T]
            S.op("act", lambda e: e.activation(out=FR, in_=ER, func=AF.Copy, scale=kc("CFR", j)), rd=[ET, k], wr=[F])
            S.op("dve", lambda e: e.scalar_tensor_tensor(out=FR, in0=EI, scalar=kc("CFI", j), in1=FR, op0=ALU.mult, op1=ALU.add),
                 rd=[ET, k, F], wr=[F])
            S.op("act", lambda e: e.activation(out=FI, in_=ER, func=AF.Copy, scale=kc("CFI", j)), rd=[ET, k], wr=[F])
            S.op("dve", lambda e: e.scalar_tensor_tensor(out=FI, in0=EI, scalar=kc("NCFR", j), in1=FI, op0=ALU.mult, op1=ALU.add),
                 rd=[ET, k, F], wr=[F])
            pA, pB = (mp[0], mp[1]) if j % 2 == 0 else (mp[4], self.ps_lin[0])
            S.op("pe", lambda e: e.matmul(pA[:, 0:NT], lhsT=Bm[:, 0, j, :], rhs=u[j // 4][:, 0:NT], start=True, stop=True),
                 rd=[stgB, u[j // 4]], wr=[pA])
            S.op("pe", lambda e: e.matmul(pB[:, 0:NT], lhsT=Bm[:, 1, j, :], rhs=u[j // 4][:, 0:NT], start=True, stop=True),
                 rd=[stgB, u[j // 4]], wr=[pB])
            S.op("dve", lambda e: e.tensor_tensor(out=v3(Z1[:, 0:NT]), in0=v3(pA[:, 0:NT]), in1=bc(FR), op=ALU.mult), rd=[pA, F], wr=[Z1])
            S.op("dve", lambda e: e.tensor_tensor(out=v3(Z2[:, 0:NT]), in0=v3(pB[:, 0:NT]), in1=bc(FI), op=ALU.mult), rd=[pB, F], wr=[Z2])
            S.op("dve", lambda e: e.tensor_tensor(out=ZR[:, 0:NT], in0=Z1[:, 0:NT], in1=Z2[:, 0:NT], op=ALU.subtract), rd=[Z1, Z2], wr=[ZR])
            S.op("dve", lambda e: e.tensor_tensor(out=v3(Z1[:, 0:NT]), in0=v3(pB[:, 0:NT]), in1=bc(FR), op=ALU.mult), rd=[pB, F], wr=[Z1])
            S.op("dve", lambda e: e.tensor_tensor(out=v3(Z2[:, 0:NT]), in0=v3(pA[:, 0:NT]), in1=bc(FI), op=ALU.mult), rd=[pA, F], wr=[Z2])
            S.op("dve", lambda e: e.tensor_tensor(out=ZI[:, 0:NT], in0=Z1[:, 0:NT], in1=Z2[:, 0:NT], op=ALU.add), rd=[Z1, Z2], wr=[ZI])
            S.op("dve", lambda e: e.scalar_tensor_tensor(out=v3(ZR[:, 0:NT])[:, :, 0], in0=hv(hre)[:, j, :], scalar=kc("MAG", j),
                                                          in1=v3(ZR[:, 0:NT])[:, :, 0], op0=ALU.mult, op1=ALU.add),
                 rd=[hre, k, ZR], wr=[ZR])
            S.op("dve", lambda e: e.scalar_tensor_tensor(out=v3(ZI[:, 0:NT])[:, :, 0], in0=hv(him)[:, j, :], scalar=kc("MAG", j),
                                                          in1=v3(ZI[:, 0:NT])[:, :, 0], op0=ALU.mult, op1=ALU.add),
                 rd=[him, k, ZI], wr=[ZI])
            S.op("pool", lambda e: e.tensor_scalar(DK[:, 0:NT], self.cs("rmask_p")[:, 0:NT], 0.0, kc("MAG", j), ALU.mult, ALU.add),
                 rd=[cst, k], wr=[DK])
            S.op("pool", lambda e: e.tensor_scalar(v3(DK[:, 0:NT])[:, :, 0], v3(DK[:, 0:NT])[:, :, 0], 0.0, None, ALU.mult), rd=[DK], wr=[DK])
            S.op("dve", lambda e: e.tensor_tensor_scan(Z1[:, 0:NT], DK[:, 0:NT], ZR[:, 0:NT], 0.0, ALU.mult, ALU.add), rd=[DK, ZR], wr=[Z1])
            S.op("dve", lambda e: e.tensor_tensor_scan(Z2[:, 0:NT], DK[:, 0:NT], ZI[:, 0:NT], 0.0, ALU.mult, ALU.add), rd=[DK, ZI], wr=[Z2])
            S.op("dve", lambda e: e.tensor_tensor(out=v3(ZR[:, 0:NT]), in0=v3(Z1[:, 0:NT]), in1=bc(ER), op=ALU.mult), rd=[Z1, ET], wr=[ZR])
            S.op("pool", lambda e: e.tensor_tensor(out=v3(ZI[:, 0:NT]), in0=v3(Z2[:, 0:NT]), in1=bc(EI), op=ALU.mult), rd=[Z2, ET], wr=[ZI])
            S.op("dve", lambda e: e.tensor_tensor(out=HR[:, 0:NT], in0=ZR[:, 0:NT], in1=ZI[:, 0:NT], op=ALU.subtract), rd=[ZR, ZI], wr=[HR])
            S.op("dve", lambda e: e.tensor_tensor(out=v3(ZR[:, 0:NT]), in0=v3(Z2[:, 0:NT]), in1=bc(ER), op=ALU.mult), rd=[Z2, ET], wr=[ZR])
            S.op("pool", lambda e: e.tensor_tensor(out=v3(ZI[:, 0:NT]), in0=v3(Z1[:, 0:NT]), in1=bc(EI), op=ALU.mult), rd=[Z1, ET], wr=[ZI])
            S.op("dve", lambda e: e.tensor_tensor(out=HI[:, 0:NT], in0=ZR[:, 0:NT], in1=ZI[:, 0:NT], op=ALU.add), rd=[ZR, ZI], wr=[HI])
            S.op("pool", lambda e: e.tensor_copy(hv(hre)[:, j, :], v3(HR[:, 0:NT])[:, :, T - 1]), rd=[HR], wr=[hre])
            S.op("pool", lambda e: e.tensor_copy(hv(him)[:, j, :], v3(HI[:, 0:NT])[:, :, T - 1]), rd=[HI], wr=[him])
            yy = Y[j // 4]
            S.op("pe", lambda e: e.matmul(yy[:, 0:NT], lhsT=Cm[:, 0, j, :], rhs=HR[:, 0:NT], start=(j % 4 == 0), stop=False),
                 rd=[stgC, HR], wr=[yy])
            S.op("pe", lambda e: e.matmul(yy[:, 0:NT], lhsT=Cm[:, 1, j, :], rhs=HI[:, 0:NT], start=False, stop=(j % 4 == 3)),
                 rd=[stgC, HI], wr=[yy])
        if last:
            ore, oim = (self.o_s5re_s, self.o_s5im_s) if samp else (self.o_s5re_p, self.o_s5im_p)
            S.dma("act", ore[l], hv(hre), rd=[hre])
            S.dma("act", oim[l], hv(him), rd=[him])
        if "s5" not in self.mixers:
            return
        for oc in range(2):
            S.op("dve", lambda e: e.scalar_tensor_tensor(out=sc[11][:, 0:NT], in0=u[oc][:, 0:NT], scalar=self.ppc(l, "s5_d", oc),
                                                         in1=Y[oc][:, 0:NT], op0=ALU.mult, op1=ALU.add),
                 rd=[u[oc], self.pp[l], Y[oc]], wr=[sc[11]])
            S.op("act", lambda e: e.activation(out=zt[oc][:, 0:NT], in_=sc[11][:, 0:NT], func=AF.Gelu_apprx_tanh), rd=[sc[11]], wr=[zt[oc]])
        wgv = wg[:, 0:512].rearrange("p (k n) -> p k n", k=2)
        for oc in range(2):
            pg = mp[oc]
            for kk_ in range(2):
                S.op("pe", lambda e: e.matmul(pg[:, 0:NT], lhsT=wgv[:, kk_, oc * 128:(oc + 1) * 128], rhs=zt[kk_][:, 0:NT],
                                              start=(kk_ == 0), stop=(kk_ == 1)), rd=[wg, zt[kk_]], wr=[pg])
            S.op("act", lambda e: e.activation(out=sc[11][:, 0:NT], in_=pg[:, 0:NT], func=AF.Sigmoid, bias=self.ppc(l, "s5_b_glu", oc)),
                 rd=[pg, self.pp[l]], wr=[sc[11]])
            S.op("dve", lambda e: e.tensor_tensor(out=self.mix[2 + oc][:, 0:NT], in0=zt[oc][:, 0:NT], in1=sc[11][:, 0:NT], op=ALU.mult),
                 rd=[zt[oc], sc[11]], wr=[self.mix[2 + oc]])

    def swa(self, g, st, l, NT, nseq, T, last):
        S = self.S
        samp = (g == "s")
        first = (not samp) and st == 0
        mp, sc, cst = self.mp, self.sc, self.cst
        do_mix = "swa" in self.mixers
        rt = sc[0]
        rsrc = self.ropeS if samp else self.ropeP[:, :, st * T:(st + 1) * T]
        S.dma("sp", rt[0:64, 0:2 * T].rearrange("p (a t) -> p a t", a=2), rsrc, wr=[rt])
        rtv = rt[0:64, 0:2 * T].rearrange("p (a t) -> p a t", a=2)
        cosb = rtv[:, 0, :].unsqueeze(1).to_broadcast([64, nseq, T])
        sinb = rtv[:, 1, :].unsqueeze(1).to_broadcast([64, nseq, T])
        selm = self.cs("selm").rearrange("p (a m) -> p a m", a=4)
        v3 = lambda ap: ap.rearrange("p (s t) -> p s t", t=T)
        ones64 = self.cs("ones")[:, 0:64]

        def rot(src, gsel, dst_ap, dst_t):
            pa, pb = mp[0], mp[1]
            S.op("pe", lambda e: e.matmul(pa[0:64, 0:NT], lhsT=selm[:, gsel, :], rhs=src[:, 0:NT], start=True, stop=True),
                 rd=[src, cst], wr=[pa])
            S.op("pe", lambda e: e.matmul(pb[0:64, 0:NT], lhsT=selm[:, 2 + gsel, :], rhs=src[:, 0:NT], start=True, stop=True),
                 rd=[src, cst], wr=[pb])
            S.op("dve", lambda e: e.tensor_tensor(out=v3(sc[1][0:64, 0:NT]), in0=v3(pa[0:64, 0:NT]), in1=cosb, op=ALU.mult),
                 rd=[pa, rt], wr=[sc[1]])
            S.op("dve", lambda e: e.tensor_tensor(out=v3(sc[2][0:64, 0:NT]), in0=v3(pb[0:64, 0:NT]), in1=sinb, op=ALU.mult),
                 rd=[pb, rt], wr=[sc[2]])
            S.op("dve", lambda e: e.tensor_tensor(out=dst_ap, in0=sc[1][0:64, 0:NT], in1=sc[2][0:64, 0:NT], op=ALU.add),
                 rd=[sc[1], sc[2]], wr=dst_t)

        vtok = [sc[3], sc[4]]
        if samp:
            stg = self.stage[0]
            vhis = stg[:, 0:2048].rearrange("p (s f) -> p s f", f=128)
            S.dma("sp", vhis, self.vc_s[l].rearrange("s j f -> j s f"), wr=[stg])
            S.op("pe", lambda e: e.transpose(mp[0][0:NT, 0:128], self.p[22][:, 0:NT], self.ident[:]),
                 rd=[self.p[22], self.ident], wr=[mp[0]])
            S.op("act", lambda e: e.activation(out=sc[3][0:NT, 0:128], in_=mp[0][0:NT, 0:128], func=AF.Copy),
                 rd=[mp[0]], wr=[sc[3]])
            if last:
                S.dma("act", self.o_swav_s[l, :, 0:124, :], self.vc_s[l, :, 4:128, :])
                for s_ in range(nseq):
                    S.dma("act", self.o_swav_s[l, s_, 124:128, :], sc[3][s_ * 4:s_ * 4 + 4, 0:128], rd=[sc[3]])
        else:
            def vblk(s_, b_):
                i = s_ * 3 + b_
                return vtok[i // 4][:, (i % 4) * 128:(i % 4 + 1) * 128], vtok[i // 4]
            for s_ in range(nseq):
                if not first:
                    a, t = vblk(s_, 0)
                    S.op("pool", lambda e: e.tensor_copy(a, self.vhist[l][:, s_ * 128:(s_ + 1) * 128]),
                         rd=[self.vhist[l]], wr=[t])
                for b_ in range(T // 128):
                    a, t = vblk(s_, 1 + b_)
                    c0 = s_ * T + b_ * 128
                    S.op("pe", lambda e: e.transpose(mp[0][:, 0:128], self.p[22][:, c0:c0 + 128], self.ident[:]),
                         rd=[self.p[22], self.ident], wr=[mp[0]])
                    S.op("act", lambda e: e.activation(out=a, in_=mp[0][:, 0:128], func=AF.Copy), rd=[mp[0]], wr=[t])
                a, t = vblk(s_, T // 128)
                S.op("pool", lambda e: e.tensor_copy(self.vhist[l][:, s_ * 128:(s_ + 1) * 128], a),
                     rd=[t], wr=[self.vhist[l]])
                if last:
                    S.dma("act", self.o_swav_p[l, s_, :, :], a, rd=[t])

        for kvh in range(2):
            qrot = [sc[5], sc[6]]
            knew = sc[7]
            for gg in range(2):
                rot(self.p[19 + kvh], gg, qrot[gg][0:64, 0:NT], [qrot[gg]])
            rot(self.p[21], kvh, knew[0:64, 0:NT], [knew])
            num_t = sc[10]
            if samp:
                stg_k = self.stage[1]
                khis = stg_k[0:64, 0:2048].rearrange("p (s j) -> p s j", j=128)
                S.dma("sp", khis, self.kc_s[l, :, kvh, :, :].rearrange("s d j -> d s j"), wr=[stg_k])
                if last:
                    S.dma("act", self.o_swak_s[l, :, kvh, :, 0:124], self.kc_s[l, :, kvh, :, 4:128])
                    S.dma("act", self.o_swak_s[l, :, kvh, :, 124:128].rearrange("s d t -> d s t"),
                          v3(knew[0:64, 0:NT]), rd=[knew])
                if not do_mix:
                    continue
                psS = mp[2]
                for s_ in range(nseq):
                    for gg in range(2):
                        S.op("pe", lambda e: e.matmul(psS[:, s_ * 8 + gg * 4:s_ * 8 + gg * 4 + 4], lhsT=khis[:, s_, :],
                                                      rhs=qrot[gg][0:64, s_ * 4:s_ * 4 + 4], start=True, stop=True),
                             rd=[stg_k, qrot[gg]], wr=[psS])
                for gg in range(2):
                    S.op("pe", lambda e: e.matmul(psS[0:64, 128 + gg * 64:128 + gg * 64 + 64], lhsT=knew[0:64, 0:NT],
                                                  rhs=qrot[gg][0:64, 0:NT], start=True, stop=True),
                         rd=[knew, qrot[gg]], wr=[psS])
                pT = sc[8]
                S.op("act", lambda e: e.activation(out=pT[:, 0:128], in_=psS[:, 0:128], func=AF.Exp, scale=0.125),
                     rd=[psS], wr=[pT])
                S.op("act", lambda e: e.activation(out=pT[0:64, 128:256], in_=psS[0:64, 128:256], func=AF.Exp, scale=0.125),
                     rd=[psS], wr=[pT])
                S.op("dve", lambda e: e.tensor_tensor(out=pT[:, 0:128], in0=pT[:, 0:128], in1=self.cs("mask_sh"), op=ALU.mult),
                     rd=[pT, cst], wr=[pT])
                S.op("dve", lambda e: e.tensor_tensor(out=pT[0:64, 128:256], in0=pT[0:64, 128:256], in1=self.cs("mask_sn", 0, 64),
                                                      op=ALU.mult), rd=[pT, cst], wr=[pT])
                psA = mp[3]
                for s_ in range(nseq):
                    for gg in range(2):
                        r_ = pT[:, s_ * 8 + gg * 4:s_ * 8 + gg * 4 + 4]
                        S.op("pe", lambda e: e.matmul(psA[gg * 64:gg * 64 + 64, s_ * 4:s_ * 4 + 4],
                                                      lhsT=vhis[:, s_, kvh * 64:kvh * 64 + 64], rhs=r_, start=True, stop=True),
                             rd=[stg, pT], wr=[psA])
                        S.op("pe", lambda e: e.matmul(psA[gg * 64:gg * 64 + 64, 64 + s_ * 4:64 + s_ * 4 + 4],
                                                      lhsT=ones64, rhs=r_, start=True, stop=True),
                             rd=[cst, pT], wr=[psA])
                for gg in range(2):
                    r_ = pT[0:64, 128 + gg * 64:128 + gg * 64 + 64]
                    S.op("pe", lambda e: e.matmul(psA[gg * 64:gg * 64 + 64, 128:192], lhsT=sc[3][0:64, kvh * 64:kvh * 64 + 64],
                                                  rhs=r_, start=True, stop=True), rd=[sc[3], pT], wr=[psA])
                    S.op("pe", lambda e: e.matmul(psA[gg * 64:gg * 64 + 64, 192:256], lhsT=ones64[0:64, :], rhs=r_,
                                                  start=True, stop=True), rd=[cst, pT], wr=[psA])
                S.op("act", lambda e: e.activation(out=sc[9][:, 0:128], in_=psA[:, 128:256], func=AF.Copy), rd=[psA], wr=[sc[9]])
                S.op("dve", lambda e: e.tensor_tensor(out=num_t[:, 0:128], in0=psA[:, 0:128], in1=sc[9][:, 0:128], op=ALU.add),
                     rd=[psA, sc[9]], wr=[num_t])
                num_ap, den_ap, nd_t = num_t[:, 0:64], num_t[:, 64:128], [num_t]
            else:
                kall = [sc[11], sc[12]]
                for s_ in range(nseq):
                    if not first:
                        S.op("pool", lambda e: e.tensor_copy(kall[s_][0:64, 0:128], self.khist[l][kvh][:, s_ * 128:(s_ + 1) * 128]),
                             rd=[self.khist[l][kvh]], wr=[kall[s_]])
                    S.op("pool", lambda e: e.tensor_copy(kall[s_][0:64, 128:128 + T], knew[0:64, s_ * T:(s_ + 1) * T]),
                         rd=[knew], wr=[kall[s_]])
                    S.op("pool", lambda e: e.tensor_copy(self.khist[l][kvh][:, s_ * 128:(s_ + 1) * 128], kall[s_][0:64, T:T + 128]),
                         rd=[kall[s_]], wr=[self.khist[l][kvh]])
                    if last:
                        S.dma("act", self.o_swak_p[l, s_, kvh, :, :], kall[s_][0:64, T:T + 128], rd=[kall[s_]])
                if not do_mix:
                    continue
                psN, psD = mp[3], mp[4]
                mask_p = self.cs("mask_p")
                for s_ in range(nseq):
                    for b_ in range(T // 128):
                        tok0 = s_ * T + b_ * 128
                        kts = []
                        if not (first and b_ == 0):
                            kts.append((0, b_ * 128, vblk(s_, b_)))
                        kts.append((1, (b_ + 1) * 128, vblk(s_, b_ + 1)))
                        psS = mp[2]
                        for (mi, k0, _) in kts:
                            for gg in range(2):
                                S.op("pe", lambda e: e.matmul(psS[:, mi * 256 + gg * 128:mi * 256 + gg * 128 + 128],
                                                              lhsT=kall[s_][0:64, k0:k0 + 128], rhs=qrot[gg][0:64, tok0:tok0 + 128],
                                                              start=True, stop=True), rd=[kall[s_], qrot[gg]], wr=[psS])
                        c0 = kts[0][0] * 256
                        pT = sc[8 + (b_ % 2)]
                        S.op("act", lambda e: e.activation(out=pT[:, c0:512], in_=psS[:, c0:512], func=AF.Exp, scale=0.125),
                             rd=[psS], wr=[pT])
                        nm_ = (512 - c0) // 256
                        pTv = pT[:, c0:512].rearrange("p (m g i) -> p m g i", g=2, i=128)
                        mkv = mask_p[:, c0 // 2:256].rearrange("p (m i) -> p m i", i=128).unsqueeze(2).to_broadcast([128, nm_, 2, 128])
                        S.op("dve", lambda e: e.tensor_tensor(out=pTv, in0=pTv, in1=mkv, op=ALU.mult), rd=[pT, cst], wr=[pT])
                        for gg in range(2):
                            for ki, (mi, k0, (va, vt)) in enumerate(kts):
                                r_ = pT[:, mi * 256 + gg * 128:mi * 256 + gg * 128 + 128]
                                S.op("pe", lambda e: e.matmul(psN[gg * 64:gg * 64 + 64, tok0:tok0 + 128], lhsT=va[:, kvh * 64:kvh * 64 + 64],
                                                              rhs=r_, start=(ki == 0), stop=(ki == len(kts) - 1)), rd=[vt, pT], wr=[psN])
                            for ki, (mi, k0, (va, vt)) in enumerate(kts):
                                r_ = pT[:, mi * 256 + gg * 128:mi * 256 + gg * 128 + 128]
                                S.op("pe", lambda e: e.matmul(psD[gg * 64:gg * 64 + 64, tok0:tok0 + 128], lhsT=ones64, rhs=r_,
                                                              start=(ki == 0), stop=(ki == len(kts) - 1)), rd=[cst, pT], wr=[psD])
                num_ap, den_ap, nd_t = psN[:, 0:NT], psD[:, 0:NT], [psN, psD]
            dt_ = sc[13]
            S.op("dve", lambda e: e.tensor_scalar(dt_[:, 0:NT], den_ap, self.esink[l][:, kvh:kvh + 1], None, ALU.add),
                 rd=nd_t + [self.esink[l]], wr=[dt_])
            S.op("dve", lambda e: e.reciprocal(dt_[:, 0:NT], dt_[:, 0:NT]), rd=[dt_], wr=[dt_])
            S.op("dve", lambda e: e.tensor_tensor(out=self.mix[6 + kvh][:, 0:NT], in0=num_ap, in1=dt_[:, 0:NT], op=ALU.mult),
                 rd=nd_t + [dt_], wr=[self.mix[6 + kvh]])

    def run_group(self, g, st):
        S = self.S
        if g == "s":
            NT = self.NSS * 4
            src = self.xT_s
            cols = [(0, NT, 0)]
            dst = self.yT_s
        else:
            NT = self.NSP * TSTEP
            src = self.xT_p
            dst = self.yT_p
            cols = [(q * self.SEQ + st * TSTEP, TSTEP, q * TSTEP) for q in range(self.NSP)]
        for c in range(8):
            for (d0, n, s0) in cols:
                S.dma("sp", self.x[c][:, s0:s0 + n], src[c * 128:(c + 1) * 128, d0:d0 + n], wr=[self.x[c]])
        for l in range(DEPTH):
            self.layer(g, st, l, NT)
        for c in range(8):
            for (d0, n, s0) in cols:
                S.dma("act", dst[c * 128:(c + 1) * 128, d0:d0 + n], self.x[c][:, s0:s0 + n], rd=[self.x[c]])

    def layer(self, g, st, l, NT):
        S = self.S
        nseq = self.NSS if g == "s" else self.NSP
        T = 4 if g == "s" else TSTEP
        last = (g == "s") or (st == self.nsteps - 1)
        self.prenorm(l, "g_mix_pre", NT)

        def cons_p(tag, pst, m):
            S.op("act", lambda e: e.activation(out=self.p[tag][0:m, 0:NT], in_=pst[0:m, 0:NT], func=AF.Copy),
                 rd=[pst], wr=[self.p[tag]])

        self.linear(self.w_in[l], 8, WIN_TILES, lambda k: (self.xn[k][:, 0:NT], [self.xn[k]]), NT, cons_p, wkey=(l, "in"))
        if last:
            o = self.o_shift_s if g == "s" else self.o_shift_p
            for c in range(8):
                src = self.p[c][:, 0:NT].rearrange("p (s t) -> p s t", t=T)[:, :, T - 1]
                S.dma("act", o[l, c * 128:(c + 1) * 128, :], src, rd=[self.p[c]])
        for nm, cs_ in (("rw", (0, 1)), ("s5", (2, 3)), ("gdn", (4, 5)), ("swa", (6, 7))):
            if nm not in self.mixers:
                for c in cs_:
                    S.op("pool", lambda e: e.memset(self.mix[c][:, 0:NT], 0.0), wr=[self.mix[c]])
        import os
        skip = os.environ.get("KSKIP", "").split(",")
        if "rw" not in skip:
            self.rwkv(g, st, l, NT, nseq, T, last)
        if "gdn" not in skip:
            self.gdn(g, st, l, NT, nseq, T, last)
        if "s5" not in skip:
            self.s5(g, st, l, NT, nseq, T, last)
        if "swa" not in skip:
            self.swa(g, st, l, NT, nseq, T, last)
        def cons_t(tag, pst, m):
            S.op("act", lambda e: e.activation(out=self.tmp[tag][:, 0:NT], in_=pst[:, 0:NT], func=AF.Copy),
                 rd=[pst], wr=[self.tmp[tag]])

        t_out = [(0, 512, [(i * 128, (i + 1) * 128, i) for i in range(4)]),
                 (512, 1024, [(i * 128, (i + 1) * 128, i) for i in range(4, 8)])]
        self.linear(self.w_out[l], 8, t_out, lambda k: (self.mix[k][:, 0:NT], [self.mix[k]]), NT, cons_t, wkey=(l, "out"))
        self.postnorm_residual(l, "g_mix_post", NT)
        self.prenorm(l, "g_mlp_pre", NT)
        hb = lambda i: self.p[i // 2][:, :].bitcast(BF16)[:, (i % 2) * 512:(i % 2) * 512 + NT]

        def cons_h(tag, pst, m):
            S.op("act", lambda e: e.activation(out=self.sq[0][:, 0:NT], in_=pst[:, 0:NT], func=AF.Relu),
                 rd=[pst], wr=[self.sq[0]])
            S.op("dve", lambda e: e.tensor_tensor(out=hb(tag), in0=self.sq[0][:, 0:NT], in1=self.sq[0][:, 0:NT],
                                                  op=ALU.mult), rd=[self.sq[0]], wr=[self.p[tag // 2]])

        t_up = [(j * 512, (j + 1) * 512, [(j * 512 + i * 128, j * 512 + (i + 1) * 128, j * 4 + i) for i in range(4)])
                for j in range(8)]
        self.linear(self.w_up[l], 8, t_up, lambda k: (self.xn[k][:, 0:NT], [self.xn[k]]), NT, cons_h, wkey=(l, "up"))
        t_dn = [(i * 128, (i + 1) * 128, [(i * 128, (i + 1) * 128, i)]) for i in range(8)]
        self.linear(self.w_down[l], 32, t_dn, lambda k: (hb(k), [self.p[k // 2]]), NT, cons_t, wkey=(l, "down"))
        self.postnorm_residual(l, "g_mlp_post", NT)


PPO = {}
NPP = 0


def _ppdef(name, n):
    global NPP
    PPO[name] = NPP
    NPP += n


for _n in ("g_mix_pre", "g_mix_post", "g_mlp_pre", "g_mlp_post", "rw_mu"):
    _ppdef(_n, 8)
_ppdef("sink", 2)
for _n in ("s5_a_re", "s5_a_im", "s5_log_dt"):
    _ppdef(_n, 8)
for _n in ("rw_w0", "rw_a0", "rw_kk", "rw_ka", "rw_rk", "rw_ln_w", "rw_ln_b"):
    _ppdef(_n, 2)
_ppdef("gdn_conv_w", 24)
_ppdef("gdn_nw", 1)
_ppdef("gdn_dtb", 1)
_ppdef("gdn_alog", 1)
_ppdef("s5_d", 2)
_ppdef("s5_b_glu", 2)


def colmajor(v):
    v = np.asarray(v, np.float32).reshape(-1, 128)
    return np.ascontiguousarray(v.T)


def pack_pp(inp):
    pp = np.zeros((DEPTH, 128, NPP), np.float32)
    for l in range(DEPTH):
        for n in ("g_mix_pre", "g_mix_post", "g_mlp_pre", "g_mlp_post", "rw_mu"):
            pp[l, :, PPO[n]:PPO[n] + 8] = colmajor(inp[n][l])
        g2 = lambda a: np.asarray(a, np.float32).reshape(8, 2, 64).transpose(1, 2, 0).reshape(128, 8)
        pp[l, :, PPO["s5_a_re"]:PPO["s5_a_re"] + 8] = g2(inp["s5_a_re"][l])
        pp[l, :, PPO["s5_a_im"]:PPO["s5_a_im"] + 8] = g2(inp["s5_a_im"][l])
        pp[l, :, PPO["s5_log_dt"]:PPO["s5_log_dt"] + 8] = g2(np.repeat(np.asarray(inp["s5_log_dt"][l])[:, None], 64, 1))
        pp[l, :, PPO["s5_d"]:PPO["s5_d"] + 2] = colmajor(inp["s5_d"][l])
        pp[l, :, PPO["s5_b_glu"]:PPO["s5_b_glu"] + 2] = colmajor(inp["s5_b_glu"][l])
        for n in ("rw_w0", "rw_a0", "rw_kk", "rw_ka", "rw_rk", "rw_ln_w", "rw_ln_b"):
            pp[l, :, PPO[n]:PPO[n] + 2] = colmajor(np.asarray(inp[n][l], np.float32).reshape(-1))
        cw = np.asarray(inp["gdn_conv_w"][l], np.float32)
        for i_ in range(4):
            pp[l, :, PPO["gdn_conv_w"] + i_ * 6:PPO["gdn_conv_w"] + i_ * 6 + 6] = colmajor(cw[i_])
        pp[l, :, PPO["gdn_nw"]] = np.tile(np.asarray(inp["gdn_norm_w"][l], np.float32), 2)
        pp[l, 4:8, PPO["gdn_dtb"]] = np.asarray(inp["gdn_dt_bias"][l], np.float32)
        pp[l, 4:8, PPO["gdn_alog"]] = np.asarray(inp["gdn_a_log"][l], np.float32)
        sk = np.asarray(inp["swa_sinks"][l], np.float32)
        pp[l, :, PPO["sink"]:PPO["sink"] + 2] = np.repeat(sk.reshape(2, 2, 1), 64, axis=2).reshape(2, 128).T
    return pp


CSO = {}
NCST = 0


def _cdef(name, n):
    global NCST
    CSO[name] = (NCST, n)
    NCST += n


for _n, _k in (("selm", 256), ("blk64", 128), ("mask_p", 256), ("mask_sh", 128), ("mask_sn", 128), ("ones", 128), ("iota", 256),
               ("triL_p", 64), ("triU_p", 64), ("incU_p", 64), ("triL_s", 64), ("triU_s", 64), ("incU_s", 64),
               ("last_p", 64), ("last_s", 64), ("eye16", 256), ("bsel", 16), ("rmask_p", 512), ("rmask_s", 64),
               ("selrow", 512), ("selpair", 256)):
    _cdef(_n, _k)


def build_cst():
    c = np.zeros((128, NCST), np.float32)

    def put(name, arr):
        o, n = CSO[name]
        arr = np.asarray(arr, np.float32).reshape(arr.shape[0], -1)
        assert arr.shape[1] == n, (name, arr.shape, n)
        c[:arr.shape[0], o:o + n] = arr

    selm = np.zeros((128, 4, 64), np.float32)
    for g in range(2):
        for m in range(64):
            selm[g * 64 + m, g, m] = 1.0
            if m < 8:
                selm[g * 64 + m + 8, 2 + g, m] = -1.0
            elif m < 16:
                selm[g * 64 + m - 8, 2 + g, m] = 1.0
    put("selm", selm)
    blk = np.zeros((128, 128), np.float32)
    blk[:64, :64] = 1
    blk[64:, 64:] = 1
    put("blk64", blk)
    j = np.arange(128)[:, None]
    i = np.arange(128)[None, :]
    mp = np.zeros((128, 2, 128), np.float32)
    mp[:, 0, :] = (j > i)
    mp[:, 1, :] = (j <= i)
    put("mask_p", mp)
    msh = np.zeros((128, 16, 2, 4), np.float32)
    msh[:] = (np.arange(128)[:, None, None, None] > np.arange(4)[None, None, None, :])
    put("mask_sh", msh)
    msn = np.zeros((64, 2, 16, 4), np.float32)
    for sp in range(16):
        for jp in range(4):
            for ii in range(4):
                if jp <= ii:
                    msn[sp * 4 + jp, :, sp, ii] = 1.0
    put("mask_sn", msn)
    put("ones", np.ones((128, 128), np.float32))
    put("iota", np.tile(np.arange(1, 257, dtype=np.float32)[None, :], (128, 1)))
    a64 = np.arange(64)
    for sfx, C in (("p", 64), ("s", 4)):
        same = (a64[:, None] // C) == (a64[None, :] // C)
        put("triL_" + sfx, (same & (a64[None, :] < a64[:, None])).astype(np.float32))
        put("triU_" + sfx, (same & (a64[:, None] < a64[None, :])).astype(np.float32))
        put("incU_" + sfx, (same & (a64[:, None] <= a64[None, :])).astype(np.float32))
        lastm = np.zeros((128, 64), np.float32)
        lastm[:, :] = ((a64[None, :] % C) == C - 1)
        lastm[:64] *= same.astype(np.float32)
        lastm[64:] *= same.astype(np.float32)
        put("last_" + sfx, lastm)
    put("eye16", np.tile(np.eye(16, dtype=np.float32).reshape(1, 256), (128, 1)))
    bs = np.zeros((128, 16), np.float32)
    for pp_ in range(64):
        bs[pp_, pp_ // 4] = 1.0
    put("bsel", bs)
    rm = np.ones((128, 512), np.float32)
    rm[:, ::64] = 0.0
    put("rmask_p", rm)
    rm = np.ones((128, 64), np.float32)
    rm[:, ::4] = 0.0
    put("rmask_s", rm)
    sr = np.zeros((128, 4, 128), np.float32)
    for h in range(4):
        sr[4 + h, h, :] = 1.0
    put("selrow", sr)
    spr = np.zeros((128, 2, 128), np.float32)
    for pr in range(2):
        spr[2 * pr, pr, 0:64] = 1.0
        spr[2 * pr + 1, pr, 64:128] = 1.0
    put("selpair", spr)
    return c


def rope_tables(pos):
    inv = (np.float32(500000.0) ** (-np.arange(0, 16, 2, dtype=np.float32) / np.float32(16))).astype(np.float32)
    ang = pos.astype(np.float32)[None, :] * inv[:, None]
    t = np.zeros((64, 2, len(pos)), np.float32)
    t[:, 0, :] = 1.0
    t[0:8, 0, :] = np.cos(ang)
    t[8:16, 0, :] = np.cos(ang)
    t[0:8, 1, :] = np.sin(ang)
    t[8:16, 1, :] = np.sin(ang)
    return t


_CACHE = {}


def run(inp, n_cores, nsp, seq, nss, mixers=("rw", "s5", "gdn", "swa")):
    key = (n_cores, nsp, seq, nss, tuple(mixers))
    if key not in _CACHE:
        k = Kern(nsp, seq, nss, mixers)
        k.build()
        _CACHE[key] = k
    k = _CACHE[key]
    f = lambda a: np.ascontiguousarray(np.asarray(a, np.float32))
    pp = pack_pp(inp)
    cst = build_cst()
    ropeP = rope_tables(np.arange(seq))
    ropeS = rope_tables(PAST + np.arange(4))
    s5B = np.zeros((DEPTH, 128, 2, 8, 128), np.float32)
    s5C = np.zeros((DEPTH, 128, 2, 8, 128), np.float32)
    for ri, (bn, cn) in enumerate((("s5_b_re", "s5_c_re"), ("s5_b_im", "s5_c_im"))):
        bb = f(inp[bn])
        cc = f(inp[cn])
        for gi in range(16):
            j, r0, c0 = gi // 2, (gi % 8) * 16, (gi % 2) * 64
            s5B[:, r0:r0 + 16, ri, j, c0:c0 + 64] = bb[:, gi].transpose(0, 2, 1)
            s5C[:, c0:c0 + 64, ri, j, r0:r0 + 16] = cc[:, gi].transpose(0, 2, 1)
    s5B = s5B.reshape(DEPTH, 128, -1)
    s5C = s5C.reshape(DEPTH, 128, -1)
    s5lay = lambda a: np.ascontiguousarray(a.reshape(DEPTH, -1, 8, 2, 64).transpose(0, 3, 4, 2, 1).reshape(DEPTH, 128, 8, -1))
    s5inv = lambda a: a.reshape(DEPTH, 2, 64, 8, -1).transpose(0, 4, 3, 1, 2).reshape(DEPTH, -1, 16, 64)
    hlay = lambda a: np.ascontiguousarray(a.reshape(DEPTH, -1, 2, 2, 64, 64).transpose(0, 3, 5, 2, 1, 4).reshape(DEPTH, 128, 2, -1, 64))
    hinv = lambda a: a.reshape(DEPTH, 2, 64, 2, -1, 64).transpose(0, 4, 3, 1, 5, 2).reshape(DEPTH, -1, 4, 64, 64)
    in_maps = []
    for c in range(n_cores):
        xp = f(inp["x_prompt"][c * nsp:(c + 1) * nsp]).reshape(nsp * seq, D)
        xs = f(inp["x_sample"][c * nss:(c + 1) * nss]).reshape(nss * 4, D)
        in_maps.append({
            "xT_p": np.ascontiguousarray(xp.T), "xT_s": np.ascontiguousarray(xs.T),
            "w_in": f(inp["w_in"]), "w_out": f(inp["w_out"]), "w_up": f(inp["w_up"]), "w_down": f(inp["w_down"]),
            "rw_s": hlay(f(inp["state_rwkv"][:, c * nss:(c + 1) * nss])),
            "rwsh_s": np.ascontiguousarray(f(inp["state_rwkv_shift"][:, c * nss:(c + 1) * nss]).reshape(DEPTH, nss, 8, 128).transpose(0, 3, 2, 1)),
            "rw_w2": f(inp["rw_w2"]), "rw_a2": f(inp["rw_a2"]), "rw_g2": f(inp["rw_g2"]),
            "gdn_s": hlay(f(inp["state_gdn"][:, c * nss:(c + 1) * nss])),
            "conv_s": np.ascontiguousarray(f(inp["state_gdn_conv"][:, c * nss:(c + 1) * nss]).reshape(DEPTH, nss, 3, 6, 128).transpose(0, 4, 3, 1, 2)),
            "s5B": s5B, "s5C": s5C, "s5glu": f(inp["s5_w_glu"]),
            "s5s_re": s5lay(f(inp["state_s5_re"][:, c * nss:(c + 1) * nss])),
            "s5s_im": s5lay(f(inp["state_s5_im"][:, c * nss:(c + 1) * nss])),
            "pp": pp, "cst": cst, "ropeP": ropeP, "ropeS": ropeS,
            "kc_s": np.ascontiguousarray(f(inp["cache_swa_k"][:, c * nss:(c + 1) * nss]).transpose(0, 1, 3, 4, 2)),
            "vc_s": f(inp["cache_swa_v"][:, c * nss:(c + 1) * nss]).reshape(DEPTH, nss, 128, 128),
        })
    import os
    if os.environ.get("KTRACE"):
        res = run_bass_kernel_spmd(k.nc, in_maps, core_ids=list(range(n_cores)), trace=True)
        print("EXEC_TIME_NS", res.exec_time_ns)
    else:
        res = run_bass_kernel_spmd(k.nc, in_maps, core_ids=list(range(n_cores)))
    R = res.results
    cat = lambda fn: np.concatenate([fn(r) for r in R], axis=0)
    y_p = cat(lambda r: r["yT_p"].T.reshape(nsp, seq, D))
    y_s = cat(lambda r: r["yT_s"].T.reshape(nss, 4, D))
    catb = lambda fn: np.concatenate([fn(r) for r in R], axis=1)
    shift_p = catb(lambda r: r["o_shift_p"].transpose(0, 2, 1)[:, :, None, :])
    shift_s = catb(lambda r: r["o_shift_s"].transpose(0, 2, 1)[:, :, None, :])
    swak_p = catb(lambda r: r["o_swak_p"].transpose(0, 1, 4, 2, 3))
    swak_s = catb(lambda r: r["o_swak_s"].transpose(0, 1, 4, 2, 3))
    swav_p = catb(lambda r: r["o_swav_p"].reshape(DEPTH, nsp, 128, 2, 64))
    swav_s = catb(lambda r: r["o_swav_s"].reshape(DEPTH, nss, 128, 2, 64))
    s5o = {n: catb(lambda r: s5inv(r["o_" + n])) for n in ("s5re_p", "s5im_p", "s5re_s", "s5im_s")}
    rwo = dict(rw_p=catb(lambda r: hinv(r["o_rw_p"])), rw_s=catb(lambda r: hinv(r["o_rw_s"])))
    gdo = dict(**rwo, gdn_p=catb(lambda r: hinv(r["o_gdn_p"])), gdn_s=catb(lambda r: hinv(r["o_gdn_s"])),
               conv_p=catb(lambda r: r["o_conv_p"]), conv_s=catb(lambda r: r["o_conv_s"]))
    return dict(y_p=y_p, y_s=y_s, shift_p=shift_p, shift_s=shift_s, swak_p=swak_p, **s5o, **gdo, swak_s=swak_s,
                swav_p=swav_p, swav_s=swav_s)


OUT_NAMES = ["y_p", "y_s", "rw_p", "rw_s", "shift_p", "shift_s", "s5re_p", "s5re_s", "s5im_p", "s5im_s",
             "gdn_p", "gdn_s", "conv_p", "conv_s", "swak_p", "swak_s", "swav_p", "swav_s"]


def out_shapes(B, BS):
    L = DEPTH
    return [(B, 2048, D), (BS, 4, D), (L, B, 4, 64, 64), (L, BS, 4, 64, 64), (L, B, 1, 1024), (L, BS, 1, 1024),
            (L, B, 16, 64), (L, BS, 16, 64), (L, B, 16, 64), (L, BS, 16, 64), (L, B, 4, 64, 64), (L, BS, 4, 64, 64),
            (L, B, 3, 768), (L, BS, 3, 768), (L, B, 128, 2, 64), (L, BS, 128, 2, 64), (L, B, 128, 2, 64),
            (L, BS, 128, 2, 64)]


def kernel(**inp):
    o = run(inp, 8, 2, 2048, 16)
    outs = []
    for n, shp in zip(OUT_NAMES, out_shapes(16, 128)):
        if n in o:
            outs.append(np.ascontiguousarray(o[n], dtype=np.float32).reshape(shp))
        else:
            outs.append(np.zeros(shp, np.float32))
    return tuple(outs)
```

```python
import math
from contextlib import ExitStack
import numpy as np
import ml_dtypes
import concourse.bass as bass
import concourse.mybir as mybir
from concourse.bass_utils import run_bass_kernel_spmd

F32 = mybir.dt.float32
BF16 = mybir.dt.bfloat16
F32R = mybir.dt.float32r


def R_(ap):
    return ap.bitcast(F32R)
AF = mybir.ActivationFunctionType
ALU = mybir.AluOpType
AX = mybir.AxisListType

D = 1024
DEPTH = 2
HD = 64
PROJ = 2824
DFF = 4096
EPS = 1e-6
PAST = 16384
TSTEP = 256


class Tl:
    def __init__(self, h):
        self.h = h
        self.w = None
        self.r = {}

    def __getitem__(self, k):
        return self.h[k]


class _Rec:
    def __init__(self):
        self.call = None

    def __getattr__(self, name):
        def f(*a, **k):
            self.call = (name, a, k)
            return None
        return f


class Sched:
    ENGS = ("pe", "dve", "act", "pool", "sp")

    def __init__(self, nc, es):
        self.nc = nc
        self.es = es
        self.eng = {"pe": nc.tensor, "dve": nc.vector, "act": nc.scalar, "pool": nc.gpsimd, "sp": nc.sync}
        self.prog = []
        self.ninst = 0

    def op(self, e, fn, rd=(), wr=()):
        r = _Rec()
        fn(r)
        assert r.call is not None
        self.prog.append(("op", e, r.call, list(rd), list(wr)))

    def dma(self, q, out, in_, rd=(), wr=()):
        self.prog.append(("dma", q, (out, in_), list(rd), list(wr)))

    def finish(self, q="sp"):
        import os
        SERIAL = int(os.environ.get("KSERIAL", "0"))
        last_ps = None
        nc, es = self.nc, self.es
        prog = self.prog
        n = len(prog)
        deps = [None] * n
        needs = [False] * n
        local = [0] * n
        lcnt = {e: 0 for e in self.ENGS}
        def rowrng(call):
            name, a, k = call
            ap = k.get("lhsT") if name == "matmul" else (a[1] if len(a) > 1 else k.get("in_"))
            b = ap.base_partition()
            kk = ap.shape[0]
            sz = 32 if kk <= 32 else (64 if kk <= 64 else 128)
            b = (b // sz) * sz
            return (b, b + sz)

        def rows_disjoint(r1, r2):
            return r1[1] <= r2[0] or r2[1] <= r1[0]

        for i, (kind, e, call, rd, wr) in enumerate(prog):
            lcnt[e] += 1
            local[i] = lcnt[e]
            d = set()
            for t in rd:
                if t.w is not None:
                    d.add(t.w)
                if getattr(t, "psum", False):
                    for k_, j in t.r.items():
                        if k_ != e:
                            d.add(j)
            for t in wr:
                if t.w is not None:
                    d.add(t.w)
                for j in t.r.values():
                    d.add(j)
            keep = set()
            for j in d:
                kj, ej = prog[j][0], prog[j][1]
                if kind == "op" and kj == "op" and ej == e:
                    if e == "pe":
                        if rows_disjoint(rowrng(call), rowrng(prog[j][2])):
                            keep.add(j)
                        continue
                keep.add(j)
            if SERIAL == 1 and i > 0:
                keep.add(i - 1)
            if SERIAL == 2 and any(getattr(t, "psum", False) for t in list(rd) + list(wr)):
                if last_ps is not None and not (e == "pe" and prog[last_ps][1] == "pe"):
                    keep.add(last_ps)
                last_ps = i
            if SERIAL == 3 and e == "pool" and i > 0:
                keep.add(i - 1)
            if SERIAL == 3 and i > 0 and prog[i - 1][1] == "pool":
                keep.add(i - 1)
            deps[i] = keep
            for j in keep:
                needs[j] = True
            rk = e if kind == "op" else ("dma", i)
            for t in rd:
                t.r[rk] = i
            for t in wr:
                t.w = i
                t.r = {}
        semh, cnt, epoch = {}, {}, {}
        for e in ("pe", "dve", "act", "pool"):
            epoch[e], cnt[e] = 0, 0
            semh[(e, 0)] = es.enter_context(nc.semaphore("s_%s_0" % e))
        ndma = 24
        dsem = [es.enter_context(nc.semaphore("s_dma_%d" % i)) for i in range(ndma)]
        for i in range(ndma):
            semh[("dma", i)] = dsem[i]
        dcnt = [0] * ndma
        dnext = 0
        waited = {e: {} for e in self.ENGS}
        tokn = [None] * n

        def wait(e, need):
            for k, v in need.items():
                if waited[e].get(k, 0) < v:
                    self.eng[e].wait_ge(semh[k], v)
                    waited[e][k] = v
                    self.ninst += 1

        for i, (kind, e, call, rd, wr) in enumerate(prog):
            need = {}
            for j in deps[i]:
                k, v = tokn[j]
                if need.get(k, 0) < v:
                    need[k] = v
            if kind == "op":
                wait(e, need)
                ins = getattr(self.eng[e], call[0])(*call[1], **call[2])
                if needs[i]:
                    if cnt[e] >= 30000:
                        epoch[e] += 1
                        cnt[e] = 0
                        semh[(e, epoch[e])] = es.enter_context(nc.semaphore("s_%s_%d" % (e, epoch[e])))
                    cnt[e] += 1
                    key = (e, epoch[e])
                    ins.then_inc(semh[key], 1)
                    tokn[i] = (key, cnt[e])
            else:
                j = dnext
                dnext = (dnext + 1) % ndma
                if dcnt[j] > 0:
                    need[("dma", j)] = max(need.get(("dma", j), 0), dcnt[j])
                wait(e, need)
                ins = self.eng[e].dma_start(out=call[0], in_=call[1])
                dcnt[j] += 16
                ins.then_inc(dsem[j], 16)
                tokn[i] = (("dma", j), dcnt[j])
            self.ninst += 1
        need = {("dma", j): dcnt[j] for j in range(ndma) if dcnt[j] > 0}
        wait(q, need)
        self.ninc = sum(needs)


RW0, S50, GD0, SW0 = 0, 1024, 1280, 2312
WIN_TILES = [
    (0, 512, [(0, 128, 0), (128, 256, 1), (256, 384, 2), (384, 512, 3)]),
    (512, 1024, [(512, 640, 4), (640, 768, 5), (768, 896, 6), (896, 1024, 7)]),
    (1024, 1536, [(1024, 1152, 8), (1152, 1280, 9), (1280, 1408, 10), (1408, 1536, 11)]),
    (1536, 2048, [(1536, 1664, 12), (1664, 1792, 13), (1792, 1920, 14), (1920, 2048, 15)]),
    (2048, 2312, [(2048, 2056, 16), (2056, 2184, 17), (2184, 2312, 18)]),
    (2312, 2824, [(2312, 2440, 19), (2440, 2568, 20), (2568, 2696, 21), (2696, 2824, 22)]),
]
NPCH = 23


class Kern:
    def __init__(self, n_seq_p, seq_len, n_seq_s, mixers=("rw", "s5", "gdn", "swa"), dbg=False):
        self.NSP, self.SEQ, self.NSS = n_seq_p, seq_len, n_seq_s
        self.mixers = mixers
        self.nsteps = seq_len // TSTEP
        self.nc = bass.Bass("TRN2", target_bir_lowering=False)
        self.dbg = dbg

    def din(self, name, shape, dt=F32):
        return self.nc.dram_tensor(name, list(shape), dt, kind="ExternalInput").ap()

    def dout(self, name, shape, dt=F32):
        return self.nc.dram_tensor(name, list(shape), dt, kind="ExternalOutput").ap()

    def sb(self, name, shape, dt=F32):
        return Tl(self.es.enter_context(self.nc.sbuf_tensor(name, list(shape), dt)))

    def ps(self, name, shape, dt=F32):
        t = Tl(self.es.enter_context(self.nc.psum_tensor(name, list(shape), dt)))
        t.psum = True
        return t

    def build(self):
        nc = self.nc
        NSP, SEQ, NSS = self.NSP, self.SEQ, self.NSS
        NTP = NSP * SEQ
        NTS = NSS * 4
        self.xT_p = self.din("xT_p", [D, NTP])
        self.xT_s = self.din("xT_s", [D, NTS])
        self.w_in = self.din("w_in", [DEPTH, D, PROJ])
        self.w_out = self.din("w_out", [DEPTH, D, D])
        self.w_up = self.din("w_up", [DEPTH, D, DFF])
        self.w_down = self.din("w_down", [DEPTH, DFF, D])
        self.pp_d = self.din("pp", [DEPTH, 128, NPP])
        self.yT_p = self.dout("yT_p", [D, NTP])
        self.yT_s = self.dout("yT_s", [D, NTS])
        self.o_shift_p = self.dout("o_shift_p", [DEPTH, 1024, NSP])
        self.o_shift_s = self.dout("o_shift_s", [DEPTH, 1024, NSS])
        self.cst_d = self.din("cst", [128, NCST])
        self.rw_s = self.din("rw_s", [DEPTH, 128, 2, NSS, 64])
        self.rwsh_s = self.din("rwsh_s", [DEPTH, 128, 8, NSS])
        self.rw_w2 = self.din("rw_w2", [DEPTH, 64, 256])
        self.rw_a2 = self.din("rw_a2", [DEPTH, 64, 256])
        self.rw_g2 = self.din("rw_g2", [DEPTH, 128, 256])
        self.o_rw_p = self.dout("o_rw_p", [DEPTH, 128, 2, NSP, 64])
        self.o_rw_s = self.dout("o_rw_s", [DEPTH, 128, 2, NSS, 64])
        self.gdn_s = self.din("gdn_s", [DEPTH, 128, 2, NSS, 64])
        self.conv_s = self.din("conv_s", [DEPTH, 128, 6, NSS, 3])
        self.o_gdn_p = self.dout("o_gdn_p", [DEPTH, 128, 2, NSP, 64])
        self.o_gdn_s = self.dout("o_gdn_s", [DEPTH, 128, 2, NSS, 64])
        self.o_conv_p = self.dout("o_conv_p", [DEPTH, NSP, 3, 768])
        self.o_conv_s = self.dout("o_conv_s", [DEPTH, NSS, 3, 768])
        self.s5B = self.din("s5B", [DEPTH, 128, 2 * 8 * 128])
        self.s5C = self.din("s5C", [DEPTH, 128, 2 * 8 * 128])
        self.s5glu = self.din("s5glu", [DEPTH, 256, 256])
        self.s5s_re = self.din("s5s_re", [DEPTH, 128, 8, NSS])
        self.s5s_im = self.din("s5s_im", [DEPTH, 128, 8, NSS])
        self.o_s5re_p = self.dout("o_s5re_p", [DEPTH, 128, 8, NSP])
        self.o_s5im_p = self.dout("o_s5im_p", [DEPTH, 128, 8, NSP])
        self.o_s5re_s = self.dout("o_s5re_s", [DEPTH, 128, 8, NSS])
        self.o_s5im_s = self.dout("o_s5im_s", [DEPTH, 128, 8, NSS])
        self.ropeP = self.din("ropeP", [64, 2, SEQ])
        self.ropeS = self.din("ropeS", [64, 2, 4])
        self.kc_s = self.din("kc_s", [DEPTH, NSS, 2, 64, 128])
        self.vc_s = self.din("vc_s", [DEPTH, NSS, 128, 128])
        self.o_swak_p = self.dout("o_swak_p", [DEPTH, NSP, 2, 64, 128])
        self.o_swak_s = self.dout("o_swak_s", [DEPTH, NSS, 2, 64, 128])
        self.o_swav_p = self.dout("o_swav_p", [DEPTH, NSP, 128, 128])
        self.o_swav_s = self.dout("o_swav_s", [DEPTH, NSS, 128, 128])

        import os
        self.wcache = {}
        self.wring = 0
        self.wscr_off = 0
        self.wscr = None
        if os.environ.get("KNOWSCR", "0") != "1":
            nel = DEPTH * (D * PROJ + D * D + D * DFF + DFF * D) // 128
            self.wscr = self.nc.dram_tensor("wscr", [128, nel], BF16, kind="Internal").ap()
        with ExitStack() as es:
            self.es = es
            es.enter_context(nc.allow_non_contiguous_dma(reason="small strided state/param io"))
            self.S = Sched(nc, es)
            self.alloc()
            self.setup_consts()
            import os
            grp = os.environ.get("KGROUPS", "sp")
            if "s" in grp:
                self.run_group("s", 0)
            if "p" in grp:
                for st in range(self.nsteps):
                    self.run_group("p", st)
            self.S.finish("sp")
        return nc

    def alloc(self):
        NT = 512
        self.x = [self.sb("x%d" % c, [128, NT]) for c in range(8)]
        self.xn = [self.sb("xn%d" % c, [128, NT], BF16) for c in range(8)]
        self.p = [self.sb("p%d" % c, [128, NT]) for c in range(NPCH)]
        self.mix = self.xn
        self.tmp = [self.sb("tmp%d" % c, [128, NT]) for c in range(8)]
        self.stage = [self.sb("stage%d" % i, [128, 2048]) for i in range(2)]
        self.wbf = [self.sb("wbf%d" % i, [128, 2048], BF16) for i in range(2)]
        self.wi = 0
        self.sq = [self.sb("sq%d" % i, [128, NT], BF16) for i in range(2)]
        self.rstd = self.sb("rstd", [128, NT])
        self.pp = [self.sb("ppsb%d" % l, [128, NPP]) for l in range(DEPTH)]
        self.ones_bf = self.sb("ones_bf", [128, 128], BF16)
        self.ident = self.sb("ident", [128, 128])
        self.ps_lin = [self.ps("ps_lin%d" % i, [128, 512]) for i in range(2)]
        self.pli = 0
        self.ps_ss = self.ps("ps_ss", [128, 512])
        self.mp = [self.ps("mp%d" % i, [128, 512]) for i in range(5)]
        self.cst = self.sb("cst_sb", [128, NCST])
        self.NSC = 26
        self.sc = [self.sb("sc%d" % i, [128, 512]) for i in range(self.NSC)]
        self.esink = [self.sb("esink%d" % l, [128, 2]) for l in range(DEPTH)]
        self.nn_t = [self.sb("nn0", [64, 512])]
        self.nx_t = [self.sb("nx0", [64, 256])]
        self.nnp = [[self.sb("nnp%d_%d" % (pr, i), [64, 256]) for i in range(2)] for pr in range(2)]
        self.nxp = [[self.sb("nxp%d_%d" % (pr, i), [64, 128]) for i in range(2)] for pr in range(2)]
        self.s5k = [self.sb("s5k%d" % l, [128, 16 * 8]) for l in range(DEPTH)]
        self.s5h = [[self.sb("s5h%d_%d" % (l, i), [128, 8 * self.NSP]) for i in range(2)] for l in range(DEPTH)]
        self.s5w = self.sb("s5w", [128, 512])
        self.gH = [self.sb("gH%d" % l, [128, 2 * self.NSP * 64]) for l in range(DEPTH)]
        self.rH = [self.sb("rH%d" % l, [128, 2 * self.NSP * 64]) for l in range(DEPTH)]
        self.rsh = [self.sb("rsh%d" % l, [128, 8 * self.NSP]) for l in range(DEPTH)]
        self.gcb = [self.sb("gcb%d" % l, [128, 6 * self.NSP * 3]) for l in range(DEPTH)]
        self.gnea = [self.sb("gnea%d" % l, [128, 1]) for l in range(DEPTH)]
        self.s5i = self.sb("s5i", [128, 256], mybir.dt.int32)
        self.khist = [[self.sb("khist%d_%d" % (l, k), [64, self.NSP * 128]) for k in range(2)] for l in range(DEPTH)]
        self.vhist = [self.sb("vhist%d" % l, [128, self.NSP * 128]) for l in range(DEPTH)]

    def setup_consts(self):
        S = self.S
        S.op("pool", lambda e: e.memset(self.ones_bf[:], 1.0), wr=[self.ones_bf])
        S.op("pool", lambda e: e.memset(self.ident[:], 0.0), wr=[self.ident])
        S.op("pool", lambda e: e.affine_select(out=self.ident[:], in_=self.ident[:], pattern=[[-1, 128]], base=0,
                                               channel_multiplier=1, compare_op=ALU.not_equal, fill=1.0),
             rd=[self.ident], wr=[self.ident])
        S.dma("sp", self.cst[:], self.cst_d, wr=[self.cst])
        for l in range(DEPTH):
            S.dma("sp", self.pp[l][:], self.pp_d[l], wr=[self.pp[l]])
        for l in range(DEPTH):
            for i in range(2):
                S.op("pool", lambda e: e.memset(self.s5h[l][i][:], 0.0), wr=[self.s5h[l][i]])
            self.s5_setup(l)
            S.op("pool", lambda e: e.memset(self.gH[l][:], 0.0), wr=[self.gH[l]])
            S.op("pool", lambda e: e.memset(self.rH[l][:], 0.0), wr=[self.rH[l]])
            S.op("pool", lambda e: e.memset(self.rsh[l][:], 0.0), wr=[self.rsh[l]])
            S.op("pool", lambda e: e.memset(self.gcb[l][:], 0.0), wr=[self.gcb[l]])
            o_ = PPO["gdn_alog"]
            S.op("act", lambda e: e.activation(out=self.gnea[l][:], in_=self.pp[l][:, o_:o_ + 1], func=AF.Exp),
                 rd=[self.pp[l]], wr=[self.gnea[l]])
            S.op("dve", lambda e: e.tensor_scalar(self.gnea[l][:], self.gnea[l][:], -1.0, None, ALU.mult),
                 rd=[self.gnea[l]], wr=[self.gnea[l]])
        for l in range(DEPTH):
            o = PPO["sink"]
            S.op("act", lambda e: e.activation(out=self.esink[l][:], in_=self.pp[l][:, o:o + 2], func=AF.Exp),
                 rd=[self.pp[l]], wr=[self.esink[l]])

    def cs(self, name, p0=0, p1=128):
        o, n = CSO[name]
        return self.cst[p0:p1, o:o + n]

    def ppc(self, l, name, c=0):
        o = PPO[name] + c
        return self.pp[l][:, o:o + 1]

    def linear(self, w2d, kc, tiles, rhs, NT, consume, wkey=None):
        S = self.S
        groups = [g_ for (_, _, gs) in tiles for g_ in gs]

        def load(k0, k1, c0, c1):
            b = self.wi
            self.wi ^= 1
            nk, ncol = k1 - k0, c1 - c0
            st, wb = self.stage[b], self.wbf[b]
            ck = (wkey, k0, k1, c0, c1)
            if wkey is not None and ck in self.wcache:
                self.wi ^= 1
                r_ = self.wring
                self.wring = (self.wring + 1) % 4
                if r_ < 2:
                    wb = self.wbf[r_]
                    wap = wb[:, 0:nk * ncol]
                else:
                    wb = self.stage[r_ - 2]
                    wap = wb[:, :].bitcast(BF16)[:, 0:nk * ncol]
                off, dep = self.wcache[ck]
                S.dma("sp", wap, self.wscr[:, off:off + nk * ncol], rd=[dep], wr=[wb])
                return wb, wap.rearrange("p (k n) -> p k n", k=nk)
            S.dma("sp", st[:, 0:nk * ncol].rearrange("p (k n) -> p k n", k=nk),
                  w2d[k0 * 128:k1 * 128, c0:c1].rearrange("(k p) n -> p k n", p=128), wr=[st])
            S.op("act", lambda e: e.activation(out=wb[:, 0:nk * ncol], in_=st[:, 0:nk * ncol], func=AF.Copy), rd=[st], wr=[wb])
            if wkey is not None and self.wscr is not None:
                off = self.wscr_off
                self.wscr_off += nk * ncol
                dep = Tl(None)
                S.dma("act", self.wscr[:, off:off + nk * ncol], wb[:, 0:nk * ncol], rd=[wb], wr=[dep])
                self.wcache[ck] = (off, dep)
            return wb, wb[:, 0:nk * ncol].rearrange("p (k n) -> p k n", k=nk)

        loads = []
        if kc * 256 <= 2048:
            i = 0
            while i < len(groups):
                batch = [groups[i]]
                if i + 1 < len(groups) and groups[i + 1][0] == groups[i][1] and groups[i + 1][1] - groups[i][0] <= 2048 // kc:
                    batch.append(groups[i + 1])
                i += len(batch)
                loads.append((0, kc, batch[0][0], batch[-1][1], [(m0, m1, tag, True, True) for (m0, m1, tag) in batch]))
        else:
            kseg = 2048 // 128
            for (m0, m1, tag) in groups:
                for k0 in range(0, kc, kseg):
                    loads.append((k0, k0 + kseg, m0, m1, [(m0, m1, tag, k0 == 0, k0 + kseg == kc)]))
        cached = wkey is not None and all((wkey, l_[0], l_[1], l_[2], l_[3]) in self.wcache for l_ in loads)
        depth = 3 if cached else 1
        q = [load(*loads[j][0:4]) for j in range(min(depth, len(loads)))]
        pst = None
        for li, (k0, k1, c0, c1, grp) in enumerate(loads):
            cur = q.pop(0)
            if li + depth < len(loads):
                q.append(load(*loads[li + depth][0:4]))
            wb, wv = cur
            for (m0, m1, tag, first, lastk) in grp:
                if first:
                    pst = self.ps_lin[self.pli]
                    self.pli ^= 1
                m = m1 - m0
                for k in range(k0, k1):
                    r_ap, r_t = rhs(k)
                    S.op("pe", lambda e: e.matmul(pst[0:m, 0:NT], lhsT=wv[:, k - k0, m0 - c0:m1 - c0], rhs=r_ap,
                                                  start=(k == 0), stop=(k == kc - 1)), rd=[wb] + r_t, wr=[pst])
                if lastk:
                    consume(tag, pst, m)

    def norm_stats(self, src, nch, NT):
        S = self.S
        for c in range(nch):
            a, t = src(c)
            sq = self.sq[c % 2]
            if c % 2 == 0:
                S.op("act", lambda e: e.activation(out=sq[:, 0:NT], in_=a, func=AF.Square), rd=t, wr=[sq])
            else:
                S.op("dve", lambda e: e.tensor_tensor(out=sq[:, 0:NT], in0=a, in1=a, op=ALU.mult), rd=t, wr=[sq])
            S.op("pe", lambda e: e.matmul(self.ps_ss[:, 0:NT], lhsT=self.ones_bf[:], rhs=sq[:, 0:NT],
                                          start=(c == 0), stop=(c == nch - 1)), rd=[sq, self.ones_bf], wr=[self.ps_ss])
        S.op("act", lambda e: e.activation(out=self.rstd[:, 0:NT], in_=self.ps_ss[:, 0:NT], func=AF.Sqrt,
                                           bias=EPS, scale=1.0 / (nch * 128)), rd=[self.ps_ss], wr=[self.rstd])
        S.op("dve", lambda e: e.reciprocal(self.rstd[:, 0:NT], self.rstd[:, 0:NT]), rd=[self.rstd], wr=[self.rstd])

    def prenorm(self, l, gname, NT):
        S = self.S
        self.norm_stats(lambda c: (self.x[c][:, 0:NT], [self.x[c]]), 8, NT)
        for c in range(8):
            S.op("dve", lambda e: e.scalar_tensor_tensor(out=self.xn[c][:, 0:NT], in0=self.x[c][:, 0:NT],
                                                         scalar=self.ppc(l, gname, c), in1=self.rstd[:, 0:NT],
                                                         op0=ALU.mult, op1=ALU.mult),
                 rd=[self.x[c], self.rstd, self.pp[l]], wr=[self.xn[c]])

    def postnorm_residual(self, l, gname, NT):
        S = self.S
        self.norm_stats(lambda c: (self.tmp[c][:, 0:NT], [self.tmp[c]]), 8, NT)
        for c in range(8):
            S.op("dve", lambda e: e.scalar_tensor_tensor(out=self.tmp[c][:, 0:NT], in0=self.tmp[c][:, 0:NT],
                                                         scalar=self.ppc(l, gname, c), in1=self.rstd[:, 0:NT],
                                                         op0=ALU.mult, op1=ALU.mult),
                 rd=[self.tmp[c], self.rstd, self.pp[l]], wr=[self.tmp[c]])
            S.op("dve", lambda e: e.tensor_tensor(out=self.x[c][:, 0:NT], in0=self.x[c][:, 0:NT],
                                                   in1=self.tmp[c][:, 0:NT], op=ALU.add),
                 rd=[self.x[c], self.tmp[c]], wr=[self.x[c]])


    def head_rms(self, src, NT, scale, bias_eps, rs_t):
        S = self.S
        sq = self.sc[19]
        S.op("act", lambda e: e.activation(out=sq[:, 0:NT], in_=src[:, 0:NT], func=AF.Square), rd=[src], wr=[sq])
        ps = self.mp[4]
        S.op("pe", lambda e: e.matmul(ps[:, 0:NT], lhsT=self.cs("blk64"), rhs=sq[:, 0:NT], start=True, stop=True),
             rd=[self.cst, sq], wr=[ps])
        if bias_eps is None:
            S.op("dve", lambda e: e.tensor_scalar(rs_t[:, 0:NT], ps[:, 0:NT], 1e-12, None, ALU.max), rd=[ps], wr=[rs_t])
            S.op("act", lambda e: e.activation(out=rs_t[:, 0:NT], in_=rs_t[:, 0:NT], func=AF.Sqrt), rd=[rs_t], wr=[rs_t])
        else:
            S.op("act", lambda e: e.activation(out=rs_t[:, 0:NT], in_=ps[:, 0:NT], func=AF.Sqrt, bias=bias_eps, scale=scale),
                 rd=[ps], wr=[rs_t])
        S.op("dve", lambda e: e.reciprocal(rs_t[:, 0:NT], rs_t[:, 0:NT]), rd=[rs_t], wr=[rs_t])

    def neumann(self, NN0, X0, levels):
        S = self.S
        psq = [self.mp[2], self.ps_lin[0]]
        psx = [self.mp[3], self.ps_lin[1]]
        for k in range(levels):
            for pr in range(2):
                if k == 0:
                    nt, xt = NN0, X0
                    nv = R_(NN0[0:64, 0:512]).rearrange("p (h n) -> p h n", h=4)[:, pr * 2:pr * 2 + 2, :]
                    xin = X0[0:64, pr * 128:(pr + 1) * 128]
                else:
                    nt, xt = self.nnp[pr][(k - 1) % 2], self.nxp[pr][(k - 1) % 2]
                    nv = R_(nt[0:64, 0:256]).rearrange("p (h n) -> p h n", h=2)
                    xin = xt[0:64, 0:128]
                xo = self.nxp[pr][k % 2]
                px = psx[pr]
                for hh in range(2):
                    S.op("pe", lambda e: e.matmul(px[0:64, hh * 64:(hh + 1) * 64], lhsT=nv[:, hh, 0:64], rhs=R_(xin[:, hh * 64:(hh + 1) * 64]),
                                                  start=True, stop=True), rd=[nt, xt], wr=[px])
                S.op("dve", lambda e: e.tensor_tensor(out=R_(xo[0:64, 0:128]), in0=px[0:64, 0:128], in1=xin, op=ALU.add),
                     rd=[px, xt], wr=[xo])
                if k < levels - 1:
                    no = self.nnp[pr][k % 2]
                    pq = psq[pr]
                    for hh in range(2):
                        S.op("pe", lambda e: e.matmul(pq[0:64, hh * 128:hh * 128 + 64], lhsT=nv[:, hh, 64:128], rhs=nv[:, hh, 0:64],
                                                      start=True, stop=True), rd=[nt], wr=[pq])
                        S.op("pe", lambda e: e.matmul(pq[0:64, hh * 128 + 64:hh * 128 + 128], lhsT=nv[:, hh, 0:64], rhs=nv[:, hh, 64:128],
                                                      start=True, stop=True), rd=[nt], wr=[pq])
                    S.op("act", lambda e: e.activation(out=R_(no[0:64, 0:256]), in_=pq[0:64, 0:256], func=AF.Copy), rd=[pq], wr=[no])
        fin = (levels - 1) % 2

        def uh(h):
            t = self.nxp[h // 2][fin]
            return t[0:64, (h % 2) * 64:(h % 2) * 64 + 64], t
        return uh

    def colop(self, out_t, out3, in3, col4, op, rd):
        for h in range(4):
            self.S.op("dve", lambda e: e.tensor_scalar(out3[:, h, :], in3[:, h, :], col4[:, h:h + 1], None, op), rd=rd, wr=[out_t])

    def instances(self, g, nseq, T):
        if g == "s":
            return [(0, list(range(nseq)), 4, "s", 2)]
        return [(s_ * T + ch * 64, [s_], 64, "p", 6) for ch in range(T // 64) for s_ in range(nseq)]

    def pad_seq(self, dst, src_ap, rd):
        S = self.S
        eye = self.cs("eye16").rearrange("p (a b) -> p a b", a=16).unsqueeze(3).to_broadcast([128, 16, 16, 4])
        in0 = src_ap.rearrange("p (b i) -> p b i", i=4).unsqueeze(1).to_broadcast([128, 16, 16, 4])
        S.op("pool", lambda e: e.tensor_tensor(out=dst.rearrange("p (a b i) -> p a b i", a=16, b=16), in0=in0, in1=eye, op=ALU.mult),
             rd=rd + [self.cst], wr=[])


    def rwkv(self, g, st, l, NT, nseq, T, last):
        S = self.S
        samp = (g == "s")
        mp, sc, cst, tmp, pp, p = self.mp, self.sc, self.cst, self.tmp, self.pp[l], self.p
        sfx = "s" if samp else "p"
        v3 = lambda ap: ap.rearrange("p (s t) -> p s t", t=T)
        ppc = lambda n, c: pp[:, PPO[n] + c:PPO[n] + c + 1]
        if samp:
            sh = sc[24]
            S.dma("sp", sh[:, 0:8 * nseq].rearrange("p (c s) -> p c s", c=8), self.rwsh_s[l], wr=[sh])
            stH = self.stage[self.wi]
            self.wi ^= 1
            S.dma("sp", stH[:, 0:2 * nseq * 64].rearrange("p (a s v) -> p a s v", a=2, s=nseq), self.rw_s[l], wr=[stH])
            Ht = stH
        else:
            sh = self.rsh[l]
            Ht = self.rH[l]
        shv = sh[:, 0:8 * nseq].rearrange("p (c s) -> p c s", c=8)
        Hv = Ht[:, 0:2 * nseq * 64].rearrange("p (a s v) -> p a s v", a=2, s=nseq)
        wm = sc[25]
        for c in range(8):
            pv, tv = v3(p[c][:, 0:NT]), v3(tmp[c][:, 0:NT])
            if T > 1:
                S.op("dve", lambda e: e.tensor_tensor(out=tv[:, :, 1:T], in0=pv[:, :, 0:T - 1], in1=pv[:, :, 1:T], op=ALU.subtract),
                     rd=[p[c]], wr=[tmp[c]])
            S.op("pool", lambda e: e.tensor_tensor(out=tv[:, :, 0], in0=shv[:, c, :], in1=pv[:, :, 0], op=ALU.subtract),
                 rd=[p[c], sh], wr=[tmp[c]])
            S.op("dve", lambda e: e.scalar_tensor_tensor(out=tmp[c][:, 0:NT], in0=tmp[c][:, 0:NT], scalar=ppc("rw_mu", c), in1=p[c][:, 0:NT],
                                                         op0=ALU.mult, op1=ALU.add), rd=[tmp[c], pp, p[c]], wr=[tmp[c]])
            if not samp:
                S.op("pool", lambda e: e.tensor_copy(shv[:, c, :], pv[:, :, T - 1]), rd=[p[c]], wr=[sh])
        rT, kT, vT = tmp[0:2], tmp[2:4], tmp[4:6]
        S.op("act", lambda e: e.activation(out=tmp[6][0:64, 0:NT], in_=tmp[6][0:64, 0:NT], func=AF.Tanh), rd=[tmp[6]], wr=[tmp[6]])
        S.op("act", lambda e: e.activation(out=tmp[7][:, 0:NT], in_=tmp[7][:, 0:NT], func=AF.Sigmoid), rd=[tmp[7]], wr=[tmp[7]])
        lw, aa, KK, gate, Wi = sc[0:2], sc[2:4], sc[4:6], sc[6:8], sc[8:10]
        At, Rt, Bt, Kt = sc[10:12], sc[12:14], sc[14:16], sc[16:18]
        bonus, yall = sc[20:22], sc[22:24]
        rs = sc[25]
        wmv = wm[:, 0:384].rearrange("p (a n) -> p a n", a=3)
        for pr in range(2):
            cs_ = slice(pr * 128, (pr + 1) * 128)
            S.dma("sp", wmv[0:64, 0, :], self.rw_w2[l][:, cs_], wr=[wm])
            S.dma("sp", wmv[64:128, 1, :], self.rw_a2[l][:, cs_], wr=[wm])
            S.dma("sp", wmv[:, 2, :], self.rw_g2[l][:, cs_], wr=[wm])
            ps = mp[0]
            S.op("pe", lambda e: e.matmul(ps[:, 0:NT], lhsT=wmv[0:64, 0, :], rhs=tmp[6][0:64, 0:NT], start=True, stop=True),
                 rd=[wm, tmp[6]], wr=[ps])
            S.op("act", lambda e: e.activation(out=lw[pr][:, 0:NT], in_=ps[:, 0:NT], func=AF.Sigmoid, bias=ppc("rw_w0", pr)),
                 rd=[ps, pp], wr=[lw[pr]])
            S.op("dve", lambda e: e.tensor_scalar(lw[pr][:, 0:NT], lw[pr][:, 0:NT], -math.exp(-0.5), None, ALU.mult), rd=[lw[pr]], wr=[lw[pr]])
            ps = mp[1]
            S.op("pe", lambda e: e.matmul(ps[:, 0:NT], lhsT=wmv[64:128, 1, :], rhs=tmp[6][64:128, 0:NT], start=True, stop=True),
                 rd=[wm, tmp[6]], wr=[ps])
            S.op("act", lambda e: e.activation(out=aa[pr][:, 0:NT], in_=ps[:, 0:NT], func=AF.Sigmoid, bias=ppc("rw_a0", pr)),
                 rd=[ps, pp], wr=[aa[pr]])
            ps = mp[2]
            S.op("pe", lambda e: e.matmul(ps[:, 0:NT], lhsT=wmv[:, 2, :], rhs=tmp[7][:, 0:NT], start=True, stop=True),
                 rd=[wm, tmp[7]], wr=[ps])
            S.op("act", lambda e: e.activation(out=gate[pr][:, 0:NT], in_=ps[:, 0:NT], func=AF.Copy), rd=[ps], wr=[gate[pr]])
            S.op("dve", lambda e: e.tensor_scalar(KK[pr][:, 0:NT], kT[pr][:, 0:NT], ppc("rw_kk", pr), None, ALU.mult), rd=[kT[pr], pp], wr=[KK[pr]])
            self.head_rms(KK[pr], NT, None, None, rs)
            S.op("dve", lambda e: e.tensor_tensor(out=KK[pr][:, 0:NT], in0=KK[pr][:, 0:NT], in1=rs[:, 0:NT], op=ALU.mult), rd=[KK[pr], rs], wr=[KK[pr]])
            S.op("dve", lambda e: e.tensor_scalar(rs[:, 0:NT], aa[pr][:, 0:NT], -1.0, None, ALU.add), rd=[aa[pr]], wr=[rs])
            S.op("dve", lambda e: e.tensor_scalar(rs[:, 0:NT], rs[:, 0:NT], ppc("rw_ka", pr), 1.0, ALU.mult, ALU.add), rd=[rs, pp], wr=[rs])
            S.op("dve", lambda e: e.tensor_tensor(out=kT[pr][:, 0:NT], in0=kT[pr][:, 0:NT], in1=rs[:, 0:NT], op=ALU.mult), rd=[kT[pr], rs], wr=[kT[pr]])
            cum = rs
            S.op("dve", lambda e: e.tensor_tensor_scan(cum[:, 0:NT], self.cs("rmask_" + sfx)[:, 0:NT], lw[pr][:, 0:NT], 0.0, ALU.mult, ALU.add),
                 rd=[lw[pr], cst], wr=[cum])
            S.op("act", lambda e: e.activation(out=Wi[pr][:, 0:NT], in_=cum[:, 0:NT], func=AF.Exp), rd=[cum], wr=[Wi[pr]])
            S.op("dve", lambda e: e.tensor_tensor(out=lw[pr][:, 0:NT], in0=cum[:, 0:NT], in1=lw[pr][:, 0:NT], op=ALU.subtract), rd=[cum, lw[pr]], wr=[lw[pr]])
            S.op("act", lambda e: e.activation(out=lw[pr][:, 0:NT], in_=lw[pr][:, 0:NT], func=AF.Exp), rd=[lw[pr]], wr=[lw[pr]])
            S.op("act", lambda e: e.activation(out=cum[:, 0:NT], in_=cum[:, 0:NT], func=AF.Exp, scale=-1.0), rd=[cum], wr=[cum])
            S.op("dve", lambda e: e.scalar_tensor_tensor(out=At[pr][:, 0:NT], in0=KK[pr][:, 0:NT], scalar=-1.0, in1=lw[pr][:, 0:NT],
                                                         op0=ALU.mult, op1=ALU.mult), rd=[KK[pr], lw[pr]], wr=[At[pr]])
            S.op("dve", lambda e: e.tensor_tensor(out=Rt[pr][:, 0:NT], in0=rT[pr][:, 0:NT], in1=Wi[pr][:, 0:NT], op=ALU.mult), rd=[rT[pr], Wi[pr]], wr=[Rt[pr]])
            S.op("dve", lambda e: e.tensor_tensor(out=Bt[pr][:, 0:NT], in0=KK[pr][:, 0:NT], in1=aa[pr][:, 0:NT], op=ALU.mult), rd=[KK[pr], aa[pr]], wr=[Bt[pr]])
            S.op("dve", lambda e: e.tensor_tensor(out=Bt[pr][:, 0:NT], in0=Bt[pr][:, 0:NT], in1=cum[:, 0:NT], op=ALU.mult), rd=[Bt[pr], cum], wr=[Bt[pr]])
            S.op("dve", lambda e: e.tensor_tensor(out=Kt[pr][:, 0:NT], in0=kT[pr][:, 0:NT], in1=cum[:, 0:NT], op=ALU.mult), rd=[kT[pr], cum], wr=[Kt[pr]])
            S.op("dve", lambda e: e.scalar_tensor_tensor(out=bonus[pr][:, 0:NT], in0=rT[pr][:, 0:NT], scalar=ppc("rw_rk", pr), in1=kT[pr][:, 0:NT],
                                                         op0=ALU.mult, op1=ALU.mult), rd=[rT[pr], pp, kT[pr]], wr=[bonus[pr]])
            ps = mp[3]
            S.op("pe", lambda e: e.matmul(ps[:, 0:NT], lhsT=self.cs("blk64"), rhs=bonus[pr][:, 0:NT], start=True, stop=True),
                 rd=[cst, bonus[pr]], wr=[ps])
            S.op("dve", lambda e: e.tensor_tensor(out=bonus[pr][:, 0:NT], in0=ps[:, 0:NT], in1=vT[pr][:, 0:NT], op=ALU.mult), rd=[ps, vT[pr]], wr=[bonus[pr]])
        f3 = lambda t_: t_[0:64, 0:256].rearrange("p (h i) -> p h i", h=4)
        for (c0, seqs, C, ms, levels) in self.instances(g, nseq, T):
            cols = slice(c0, c0 + 64)
            ns = len(seqs)
            bm = lambda name: self.cs(name + "_" + ms, 0, 64).unsqueeze(1).to_broadcast([64, 4, 64])
            pA, pB, pC = mp[2], mp[3], mp[4]
            for h in range(4):
                pr, r0 = h // 2, (h % 2) * 64
                a_, r_, b_, k_ = (t_[pr][r0:r0 + 64, cols] for t_ in (At, Rt, Bt, Kt))
                rd_ = [At[pr], Rt[pr], Bt[pr], Kt[pr]]
                for (dst, o_, l_, rr_) in ((pA, h * 128, b_, a_), (pA, h * 128 + 64, a_, b_), (pB, h * 128, b_, r_),
                                           (pB, h * 128 + 64, k_, a_), (pC, h * 64, k_, r_)):
                    S.op("pe", lambda e: e.matmul(dst[0:64, o_:o_ + 64], lhsT=l_, rhs=rr_, start=True, stop=True), rd=rd_, wr=[dst])
            NN, AT2, AT3 = self.nn_t[0], sc[25], sc[0]
            v4 = lambda t_: t_[0:64, 0:512].rearrange("p (h n) -> p h n", h=4)
            S.op("dve", lambda e: e.tensor_tensor(out=R_(v4(NN)[:, :, 0:64]), in0=v4(pA)[:, :, 0:64], in1=bm("triU"), op=ALU.mult), rd=[pA, cst], wr=[NN])
            S.op("dve", lambda e: e.tensor_tensor(out=R_(v4(NN)[:, :, 64:128]), in0=v4(pA)[:, :, 64:128], in1=bm("triL"), op=ALU.mult), rd=[pA, cst], wr=[NN])
            S.op("dve", lambda e: e.tensor_tensor(out=v4(AT2)[:, :, 0:64], in0=v4(pB)[:, :, 0:64], in1=bm("incU"), op=ALU.mult), rd=[pB, cst], wr=[AT2])
            S.op("dve", lambda e: e.tensor_tensor(out=v4(AT2)[:, :, 64:128], in0=v4(pB)[:, :, 64:128], in1=bm("triU"), op=ALU.mult), rd=[pB, cst], wr=[AT2])
            S.op("dve", lambda e: e.tensor_tensor(out=f3(AT3), in0=f3(pC), in1=bm("incU"), op=ALU.mult), rd=[pC, cst], wr=[AT3])
            ATrb = lambda h: v4(AT2)[:, h, 0:64]
            ATak = lambda h: v4(AT2)[:, h, 64:128]
            ATrk = lambda h: f3(AT3)[:, h, :]
            pT0, pT1 = mp[0], mp[1]
            for pr in range(2):
                S.op("pe", lambda e: e.transpose(pT0[0:64, pr * 128:(pr + 1) * 128], vT[pr][:, cols], self.ident[:]), rd=[vT[pr], self.ident], wr=[pT0])
                S.op("pe", lambda e: e.transpose(pT0[0:64, 256 + pr * 128:256 + (pr + 1) * 128], Bt[pr][:, cols], self.ident[:]), rd=[Bt[pr], self.ident], wr=[pT0])
                S.op("pe", lambda e: e.transpose(pT1[0:64, pr * 128:(pr + 1) * 128], Kt[pr][:, cols], self.ident[:]), rd=[Kt[pr], self.ident], wr=[pT1])
            Vtok, Btok, Ktok = sc[1], sc[2], sc[3]
            S.op("act", lambda e: e.activation(out=Vtok[0:64, 0:256], in_=pT0[0:64, 0:256], func=AF.Copy), rd=[pT0], wr=[Vtok])
            S.op("act", lambda e: e.activation(out=Btok[0:64, 0:256], in_=pT0[0:64, 256:512], func=AF.Copy), rd=[pT0], wr=[Btok])
            S.op("act", lambda e: e.activation(out=Ktok[0:64, 0:256], in_=pT1[0:64, 0:256], func=AF.Copy), rd=[pT1], wr=[Ktok])
            if ns > 1:
                apad = [(p[0], p[1]), (p[2], p[3])]
                rpad = [(p[4], p[5]), (p[6], p[7])]
                padv = {}
                for pr in range(2):
                    for nm, src_t, tl in (("a", At[pr], apad[pr]), ("r", Rt[pr], rpad[pr])):
                        for half in range(2):
                            eye = self.cs("eye16").rearrange("p (a b) -> p a b", a=16)[:, half * 8:(half + 1) * 8, :].unsqueeze(3).to_broadcast([128, 8, 16, 4])
                            in0 = src_t[:, cols].rearrange("p (b i) -> p b i", i=4).unsqueeze(1).to_broadcast([128, 8, 16, 4])
                            S.op("pool", lambda e: e.tensor_tensor(out=tl[half][:, 0:512].rearrange("p (a b i) -> p a b i", a=8, b=16),
                                                                   in0=in0, in1=eye, op=ALU.mult), rd=[src_t, cst], wr=[tl[half]])
                        padv[(nm, pr)] = tl
                lk = lambda nm, pr, r0, si: (padv[(nm, pr)][si // 8][r0:r0 + 64, (si % 8) * 64:(si % 8) * 64 + 64], [padv[(nm, pr)][si // 8]])
            else:
                lk = lambda nm, pr, r0, si: ((At if nm == "a" else Rt)[pr][r0:r0 + 64, cols], [(At if nm == "a" else Rt)[pr]])
            pK = mp[1]
            for h in range(4):
                pr, r0 = h // 2, (h % 2) * 64
                for si, s_ in enumerate(seqs):
                    a_, t_ = lk("a", pr, r0, si)
                    S.op("pe", lambda e: e.matmul(pK[0:64, 256 + h * 64:256 + (h + 1) * 64], lhsT=a_, rhs=Hv[r0:r0 + 64, pr, s_, :],
                                                  start=(si == 0), stop=False), rd=t_ + [Ht], wr=[pK])
                S.op("pe", lambda e: e.matmul(pK[0:64, 256 + h * 64:256 + (h + 1) * 64], lhsT=ATak(h), rhs=Vtok[0:64, h * 64:(h + 1) * 64],
                                              start=False, stop=True), rd=[AT2, Vtok], wr=[pK])
            X = self.nx_t[0]
            S.op("act", lambda e: e.activation(out=R_(X[0:64, 0:256]), in_=pK[0:64, 256:512], func=AF.Copy), rd=[pK], wr=[X])
            uh = self.neumann(NN, X, levels)
            pY = mp[0]
            for h in range(4):
                pr, r0 = h // 2, (h % 2) * 64
                o_ = pY[r0:r0 + 64, pr * 64:(pr + 1) * 64]
                for si, s_ in enumerate(seqs):
                    a_, t_ = lk("r", pr, r0, si)
                    S.op("pe", lambda e: e.matmul(o_, lhsT=Hv[r0:r0 + 64, pr, s_, :], rhs=a_, start=(si == 0), stop=False), rd=t_ + [Ht], wr=[pY])
                S.op("pe", lambda e: e.matmul(o_, lhsT=uh(h)[0], rhs=ATrb(h), start=False, stop=False), rd=[uh(h)[1], AT2], wr=[pY])
                S.op("pe", lambda e: e.matmul(o_, lhsT=Vtok[0:64, h * 64:(h + 1) * 64], rhs=ATrk(h), start=False, stop=True), rd=[Vtok, AT3], wr=[pY])
            for pr in range(2):
                S.op("act", lambda e: e.activation(out=yall[pr][:, cols], in_=pY[:, pr * 64:(pr + 1) * 64], func=AF.Copy), rd=[pY], wr=[yall[pr]])
            for pr in range(2):
                pH = [mp[2], mp[3]]
                for hh in range(2):
                    h = pr * 2 + hh
                    r0 = hh * 64
                    if ns > 1:
                        Up = (tmp[0], tmp[1])
                        Vp = (tmp[2], tmp[3])
                        for half in range(2):
                            bs_ = self.cs("bsel", 0, 64)[:, half * 8:(half + 1) * 8].unsqueeze(2).to_broadcast([64, 8, 64])
                            for (dst_, src_ap, src_t) in ((Up, uh(h)[0], uh(h)[1]), (Vp, Vtok[0:64, h * 64:(h + 1) * 64], Vtok)):
                                S.op("pool", lambda e: e.tensor_tensor(
                                    out=dst_[half][0:64, 0:512].rearrange("p (s v) -> p s v", s=8),
                                    in0=src_ap.unsqueeze(1).to_broadcast([64, 8, 64]), in1=bs_, op=ALU.mult),
                                    rd=[src_t, cst], wr=[dst_[half]])
                            S.op("pe", lambda e: e.matmul(pH[half][r0:r0 + 64, 0:512], lhsT=Btok[0:64, h * 64:(h + 1) * 64], rhs=Up[half][0:64, 0:512],
                                                          start=True, stop=False), rd=[Btok, Up[half]], wr=[pH[half]])
                            S.op("pe", lambda e: e.matmul(pH[half][r0:r0 + 64, 0:512], lhsT=Ktok[0:64, h * 64:(h + 1) * 64], rhs=Vp[half][0:64, 0:512],
                                                          start=False, stop=True), rd=[Ktok, Vp[half]], wr=[pH[half]])
                    else:
                        S.op("pe", lambda e: e.matmul(pH[0][r0:r0 + 64, 0:64], lhsT=Btok[0:64, h * 64:(h + 1) * 64], rhs=uh(h)[0],
                                                      start=True, stop=False), rd=[Btok, uh(h)[1]], wr=[pH[0]])
                        S.op("pe", lambda e: e.matmul(pH[0][r0:r0 + 64, 0:64], lhsT=Ktok[0:64, h * 64:(h + 1) * 64], rhs=Vtok[0:64, h * 64:(h + 1) * 64],
                                                      start=False, stop=True), rd=[Ktok, Vtok], wr=[pH[0]])
                    wC = Wi[pr][r0:r0 + 64, cols].rearrange("p (s i) -> p s i", i=C)[:, :, C - 1]
                    if ns > 1:
                        for half in range(2):
                            hs2 = Hv[r0:r0 + 64, pr, half * 8:(half + 1) * 8, :]
                            S.op("dve", lambda e: e.tensor_tensor(out=hs2, in0=hs2, in1=pH[half][r0:r0 + 64, 0:512].rearrange("p (s v) -> p s v", s=8),
                                                                  op=ALU.add), rd=[Ht, pH[half]], wr=[Ht])
                        for si in range(ns):
                            hs1 = Hv[r0:r0 + 64, pr, si, :]
                            S.op("dve", lambda e: e.tensor_scalar(hs1, hs1, wC[:, si:si + 1], None, ALU.mult), rd=[Ht, Wi[pr]], wr=[Ht])
                    else:
                        hsl = Hv[r0:r0 + 64, pr, seqs[0], :]
                        S.op("dve", lambda e: e.tensor_tensor(out=hsl, in0=hsl, in1=pH[0][r0:r0 + 64, 0:64], op=ALU.add), rd=[Ht, pH[0]], wr=[Ht])
                        S.op("dve", lambda e: e.tensor_scalar(hsl, hsl, Wi[pr][r0:r0 + 64, c0 + 63:c0 + 64], None, ALU.mult), rd=[Ht, Wi[pr]], wr=[Ht])
        if last:
            og = self.o_rw_s if samp else self.o_rw_p
            S.dma("act", og[l], Hv, rd=[Ht])
        if "rw" not in self.mixers:
            return
        for pr in range(2):
            y = yall[pr]
            psM, psV = mp[0], mp[1]
            sq = sc[19]
            S.op("pe", lambda e: e.matmul(psM[:, 0:NT], lhsT=self.cs("blk64"), rhs=y[:, 0:NT], start=True, stop=True), rd=[cst, y], wr=[psM])
            S.op("act", lambda e: e.activation(out=sq[:, 0:NT], in_=y[:, 0:NT], func=AF.Square), rd=[y], wr=[sq])
            S.op("pe", lambda e: e.matmul(psV[:, 0:NT], lhsT=self.cs("blk64"), rhs=sq[:, 0:NT], start=True, stop=True), rd=[cst, sq], wr=[psV])
            mean, var = sc[0], sc[1]
            S.op("act", lambda e: e.activation(out=mean[:, 0:NT], in_=psM[:, 0:NT], func=AF.Copy, scale=1.0 / 64), rd=[psM], wr=[mean])
            S.op("dve", lambda e: e.tensor_tensor(out=var[:, 0:NT], in0=mean[:, 0:NT], in1=mean[:, 0:NT], op=ALU.mult), rd=[mean], wr=[var])
            S.op("dve", lambda e: e.scalar_tensor_tensor(out=var[:, 0:NT], in0=psV[:, 0:NT], scalar=1.0 / 64, in1=var[:, 0:NT],
                                                         op0=ALU.mult, op1=ALU.subtract), rd=[psV, var], wr=[var])
            S.op("act", lambda e: e.activation(out=var[:, 0:NT], in_=var[:, 0:NT], func=AF.Sqrt, bias=64e-5), rd=[var], wr=[var])
            S.op("dve", lambda e: e.reciprocal(var[:, 0:NT], var[:, 0:NT]), rd=[var], wr=[var])
            S.op("dve", lambda e: e.tensor_tensor(out=y[:, 0:NT], in0=y[:, 0:NT], in1=mean[:, 0:NT], op=ALU.subtract), rd=[y, mean], wr=[y])
            S.op("dve", lambda e: e.tensor_tensor(out=y[:, 0:NT], in0=y[:, 0:NT], in1=var[:, 0:NT], op=ALU.mult), rd=[y, var], wr=[y])
            S.op("dve", lambda e: e.tensor_scalar(y[:, 0:NT], y[:, 0:NT], ppc("rw_ln_w", pr), ppc("rw_ln_b", pr), ALU.mult, ALU.add), rd=[y, pp], wr=[y])
            S.op("dve", lambda e: e.tensor_tensor(out=y[:, 0:NT], in0=y[:, 0:NT], in1=bonus[pr][:, 0:NT], op=ALU.add), rd=[y, bonus[pr]], wr=[y])
            S.op("dve", lambda e: e.tensor_tensor(out=self.mix[pr][:, 0:NT], in0=y[:, 0:NT], in1=gate[pr][:, 0:NT], op=ALU.mult),
                 rd=[y, gate[pr]], wr=[self.mix[pr]])

    def gdn(self, g, st, l, NT, nseq, T, last):
        S = self.S
        samp = (g == "s")
        mp, sc, cst, tmp, pp = self.mp, self.sc, self.cst, self.tmp, self.pp[l]
        sfx = "s" if samp else "p"
        import os
        STOP = float(os.environ.get("KGDN_STOP", "99"))
        v3 = lambda ap: ap.rearrange("p (s t) -> p s t", t=T)
        if samp:
            cb = sc[17]
            S.dma("sp", cb[:, 0:6 * nseq * 3].rearrange("p (c s t) -> p c s t", c=6, s=nseq), self.conv_s[l], wr=[cb])
            stH = self.stage[self.wi]
            self.wi ^= 1
            S.dma("sp", stH[:, 0:2 * nseq * 64].rearrange("p (a s v) -> p a s v", a=2, s=nseq), self.gdn_s[l], wr=[stH])
            Ht = stH
        else:
            cb = self.gcb[l]
            Ht = self.gH[l]
        cbv = cb[:, 0:6 * nseq * 3].rearrange("p (c s t) -> p c s t", c=6, s=nseq)
        Hv = Ht[:, 0:2 * nseq * 64].rearrange("p (a s v) -> p a s v", a=2, s=nseq)
        for c in range(6):
            x = self.p[10 + c]
            xv = v3(x[:, 0:NT])
            acc = tmp[c]
            av = v3(acc[:, 0:NT])
            w = lambda i: pp[:, PPO["gdn_conv_w"] + i * 6 + c:PPO["gdn_conv_w"] + i * 6 + c + 1]
            S.op("dve", lambda e: e.tensor_scalar(acc[:, 0:NT], x[:, 0:NT], w(3), None, ALU.mult), rd=[x, pp], wr=[acc])
            for i in (1, 2, 3):
                S.op("dve", lambda e: e.scalar_tensor_tensor(out=av[:, :, i:T], in0=xv[:, :, 0:T - i], scalar=w(3 - i), in1=av[:, :, i:T],
                                                             op0=ALU.mult, op1=ALU.add), rd=[x, pp, acc], wr=[acc])
                S.op("dve", lambda e: e.scalar_tensor_tensor(out=av[:, :, 0:i], in0=cbv[:, c, :, 3 - i:3], scalar=w(3 - i), in1=av[:, :, 0:i],
                                                             op0=ALU.mult, op1=ALU.add), rd=[cb, pp, acc], wr=[acc])
            if last:
                oc_ = self.o_conv_s if samp else self.o_conv_p
                for t_ in range(3):
                    S.dma("act", oc_[l][:, t_, c * 128:(c + 1) * 128].rearrange("s p -> p s"), xv[:, :, T - 3 + t_], rd=[x])
            if not samp:
                S.op("pool", lambda e: e.tensor_copy(cbv[:, c, :, :], xv[:, :, T - 3:T]), rd=[x], wr=[cb])
            S.op("act", lambda e: e.activation(out=acc[:, 0:NT], in_=acc[:, 0:NT], func=AF.Silu), rd=[acc], wr=[acc])
        if STOP <= 1:
            return
        rs = sc[16]
        for c in range(4):
            self.head_rms(tmp[c], NT, None, None, rs)
            if c < 2:
                S.op("dve", lambda e: e.scalar_tensor_tensor(out=tmp[c][:, 0:NT], in0=tmp[c][:, 0:NT], scalar=0.125, in1=rs[:, 0:NT],
                                                             op0=ALU.mult, op1=ALU.mult), rd=[tmp[c], rs], wr=[tmp[c]])
            else:
                S.op("dve", lambda e: e.tensor_tensor(out=tmp[c][:, 0:NT], in0=tmp[c][:, 0:NT], in1=rs[:, 0:NT], op=ALU.mult),
                     rd=[tmp[c], rs], wr=[tmp[c]])
        qT, kT, vT = tmp[0:2], tmp[2:4], tmp[4:6]
        if STOP <= 2:
            return
        bg = sc[15]
        bg2 = sc[14]
        S.op("act", lambda e: e.activation(out=bg[0:8, 0:NT], in_=self.p[16][0:8, 0:NT], func=AF.Sigmoid), rd=[self.p[16]], wr=[bg])
        S.op("act", lambda e: e.activation(out=bg2[0:8, 0:NT], in_=self.p[16][0:8, 0:NT], func=AF.Exp,
                                           bias=pp[0:8, PPO["gdn_dtb"]:PPO["gdn_dtb"] + 1]), rd=[self.p[16], pp], wr=[bg2])
        S.op("act", lambda e: e.activation(out=bg2[0:8, 0:NT], in_=bg2[0:8, 0:NT], func=AF.Ln, bias=1.0), rd=[bg2], wr=[bg2])
        S.op("dve", lambda e: e.tensor_scalar(bg2[0:8, 0:NT], bg2[0:8, 0:NT], self.gnea[l][0:8, 0:1], None, ALU.mult),
             rd=[bg2, self.gnea[l]], wr=[bg2])
        S.op("dve", lambda e: e.tensor_tensor_scan(bg2[0:8, 0:NT], self.cs("rmask_" + sfx)[0:8, 0:NT], bg2[0:8, 0:NT], 0.0, ALU.mult, ALU.add),
             rd=[bg2, cst], wr=[bg2])
        oall = [sc[12], sc[13]]
        if STOP <= 3:
            return
        selrow = self.cs("selrow").rearrange("p (h n) -> p h n", h=4)
        selpair = self.cs("selpair").rearrange("p (a n) -> p a n", a=2)
        for (c0, seqs, C, ms, levels) in self.instances(g, nseq, T):
            cols = slice(c0, c0 + 64)
            ns = len(seqs)
            pt = mp[0]
            S.op("pe", lambda e: e.transpose(pt[0:64, 0:8], bg[0:8, cols], self.ident[0:8, 0:8]), rd=[bg, self.ident], wr=[pt])
            S.op("pe", lambda e: e.transpose(pt[0:64, 8:16], bg2[0:8, cols], self.ident[0:8, 0:8]), rd=[bg2, self.ident], wr=[pt])
            cT = sc[0]
            S.op("act", lambda e: e.activation(out=cT[0:64, 0:16], in_=pt[0:64, 0:16], func=AF.Copy), rd=[pt], wr=[cT])
            beta_c, gc_c = cT[0:64, 0:4], cT[0:64, 12:16]
            if STOP <= 4:
                continue
            pG = mp[1]
            for h in range(4):
                S.op("pe", lambda e: e.matmul(pG[:, h * 64:(h + 1) * 64], lhsT=selrow[0:8, h, :], rhs=bg2[0:8, cols], start=True, stop=True),
                     rd=[cst, bg2], wr=[pG])
            pGv = pG[0:64, 0:256].rearrange("p (h i) -> p h i", h=4)
            exG = sc[1]
            S.op("act", lambda e: e.activation(out=exG[:, 0:256], in_=pG[:, 0:256], func=AF.Exp), rd=[pG], wr=[exG])
            exGv = exG[:, 0:256].rearrange("p (h i) -> p h i", h=4)
            if STOP <= 4.2:
                continue
            E1, Da, Db = sc[2], sc[3], sc[4]
            gcB = gc_c.unsqueeze(2).to_broadcast([64, 4, 64])
            f3 = lambda t_: t_[0:64, 0:256].rearrange("p (h i) -> p h i", h=4)
            bm = lambda name: self.cs(name + "_" + ms, 0, 64).unsqueeze(1).to_broadcast([64, 4, 64])
            self.colop(E1, f3(E1), pGv, gc_c, ALU.subtract, [pG, cT])
            if STOP <= 4.4:
                continue
            S.op("dve", lambda e: e.tensor_scalar(Da[0:64, 0:256], E1[0:64, 0:256], 0.0, None, ALU.min), rd=[E1], wr=[Da])
            S.op("dve", lambda e: e.tensor_scalar(Db[0:64, 0:256], E1[0:64, 0:256], -1.0, 0.0, ALU.mult, ALU.min), rd=[E1], wr=[Db])
            if STOP <= 4.5:
                continue
            S.op("act", lambda e: e.activation(out=Da[0:64, 0:256], in_=Da[0:64, 0:256], func=AF.Exp), rd=[Da], wr=[Da])
            S.op("act", lambda e: e.activation(out=Db[0:64, 0:256], in_=Db[0:64, 0:256], func=AF.Exp), rd=[Db], wr=[Db])
            if STOP <= 4.6:
                continue
            S.op("dve", lambda e: e.tensor_tensor(out=f3(E1), in0=pGv, in1=self.cs("last_" + ms, 0, 64).unsqueeze(1).to_broadcast([64, 4, 64]),
                                                  op=ALU.mult), rd=[pG, cst], wr=[E1])
            S.op("dve", lambda e: e.tensor_reduce(out=cT[0:64, 16:20], in_=f3(E1), axis=AX.X, op=ALU.add), rd=[E1], wr=[cT])
            if STOP <= 4.8:
                continue
            S.op("dve", lambda e: e.tensor_tensor(out=cT[0:64, 20:24], in0=cT[0:64, 16:20], in1=gc_c, op=ALU.subtract), rd=[cT], wr=[cT])
            S.op("act", lambda e: e.activation(out=cT[0:64, 20:24], in_=cT[0:64, 20:24], func=AF.Exp), rd=[cT], wr=[cT])
            S.op("act", lambda e: e.activation(out=cT[0:64, 24:28], in_=gc_c, func=AF.Exp), rd=[cT], wr=[cT])
            S.op("dve", lambda e: e.scalar_tensor_tensor(out=cT[0:64, 28:32], in0=cT[0:64, 24:28], scalar=-1.0, in1=beta_c,
                                                         op0=ALU.mult, op1=ALU.mult), rd=[cT], wr=[cT])
            dec_c, nbg_c = cT[0:64, 20:24], cT[0:64, 28:32]
            if STOP <= 5:
                continue
            bkT = [sc[5], sc[6]]
            for pr in range(2):
                pb = mp[0]
                S.op("pe", lambda e: e.matmul(pb[:, 64:128], lhsT=selpair[0:8, pr, :], rhs=bg[0:8, cols], start=True, stop=True),
                     rd=[cst, bg], wr=[pb])
                S.op("dve", lambda e: e.tensor_tensor(out=bkT[pr][:, 0:64], in0=pb[:, 64:128], in1=kT[pr][:, cols], op=ALU.mult),
                     rd=[pb, kT[pr]], wr=[bkT[pr]])
            pA, pB = mp[2], mp[3]
            for h in range(4):
                pr, r0 = h // 2, (h % 2) * 64
                kh, bkh, qh = kT[pr][r0:r0 + 64, cols], bkT[pr][r0:r0 + 64, 0:64], qT[pr][r0:r0 + 64, cols]
                S.op("pe", lambda e: e.matmul(pA[0:64, h * 128:h * 128 + 64], lhsT=kh, rhs=bkh, start=True, stop=True),
                     rd=[kT[pr], bkT[pr]], wr=[pA])
                S.op("pe", lambda e: e.matmul(pA[0:64, h * 128 + 64:h * 128 + 128], lhsT=bkh, rhs=kh, start=True, stop=True),
                     rd=[kT[pr], bkT[pr]], wr=[pA])
                S.op("pe", lambda e: e.matmul(pB[0:64, h * 64:h * 64 + 64], lhsT=kh, rhs=qh, start=True, stop=True),
                     rd=[kT[pr], qT[pr]], wr=[pB])
            NN, AQ = self.nn_t[0], sc[8]
            pAv = pA[0:64, 0:512].rearrange("p (h n) -> p h n", h=4)
            nnv = NN[0:64, 0:512].rearrange("p (h n) -> p h n", h=4)
            S.op("dve", lambda e: e.tensor_tensor(out=f3(E1), in0=f3(Da), in1=bm("triU"), op=ALU.mult), rd=[Da, cst], wr=[E1])
            S.op("dve", lambda e: e.scalar_tensor_tensor(out=R_(nnv[:, :, 0:64]), in0=pAv[:, :, 0:64], scalar=-1.0, in1=f3(E1),
                                                         op0=ALU.mult, op1=ALU.mult), rd=[pA, E1], wr=[NN])
            S.op("dve", lambda e: e.tensor_tensor(out=f3(Db), in0=f3(Db), in1=bm("triL"), op=ALU.mult), rd=[Db, cst], wr=[Db])
            S.op("dve", lambda e: e.scalar_tensor_tensor(out=R_(nnv[:, :, 64:128]), in0=pAv[:, :, 64:128], scalar=-1.0, in1=f3(Db),
                                                         op0=ALU.mult, op1=ALU.mult), rd=[pA, Db], wr=[NN])
            S.op("dve", lambda e: e.tensor_tensor(out=f3(Da), in0=f3(Da), in1=bm("incU"), op=ALU.mult), rd=[Da, cst], wr=[Da])
            S.op("dve", lambda e: e.tensor_tensor(out=AQ[0:64, 0:256], in0=pB[0:64, 0:256], in1=Da[0:64, 0:256], op=ALU.mult),
                 rd=[pB, Da], wr=[AQ])
            if STOP <= 6:
                continue
            pT_ = mp[0]
            for pr in range(2):
                S.op("pe", lambda e: e.transpose(pT_[0:64, pr * 128:(pr + 1) * 128], vT[pr][:, cols], self.ident[:]),
                     rd=[vT[pr], self.ident], wr=[pT_])
                S.op("pe", lambda e: e.transpose(pT_[0:64, 256 + pr * 128:256 + (pr + 1) * 128], kT[pr][:, cols], self.ident[:]),
                     rd=[kT[pr], self.ident], wr=[pT_])
            Vb, Kd = sc[9], sc[10]
            self.colop(Vb, f3(Vb), pT_[0:64, 0:256].rearrange("p (h v) -> p h v", h=4), beta_c, ALU.mult, [pT_, cT])
            self.colop(Kd, f3(Kd), pT_[0:64, 256:512].rearrange("p (h v) -> p h v", h=4), dec_c, ALU.mult, [pT_, cT])
            if STOP <= 7:
                continue
            if ns > 1:
                kpad = [(sc[20], sc[21]), (sc[22], sc[23])]
                qpad = [(sc[24], sc[25]), (tmp[6], tmp[7])]
                padv = {}
                for pr in range(2):
                    for nm, src_t, tl in (("k", kT[pr], kpad[pr]), ("q", qT[pr], qpad[pr])):
                        for half in range(2):
                            eye = self.cs("eye16").rearrange("p (a b) -> p a b", a=16)[:, half * 8:(half + 1) * 8, :].unsqueeze(3).to_broadcast([128, 8, 16, 4])
                            in0 = src_t[:, cols].rearrange("p (b i) -> p b i", i=4).unsqueeze(1).to_broadcast([128, 8, 16, 4])
                            S.op("pool", lambda e: e.tensor_tensor(out=tl[half][:, 0:512].rearrange("p (a b i) -> p a b i", a=8, b=16),
                                                                   in0=in0, in1=eye, op=ALU.mult), rd=[src_t, cst], wr=[tl[half]])
                        padv[(nm, pr)] = tl
                lk = lambda nm, pr, r0, si: (padv[(nm, pr)][si // 8][r0:r0 + 64, (si % 8) * 64:(si % 8) * 64 + 64], [padv[(nm, pr)][si // 8]])
            else:
                lk = lambda nm, pr, r0, si: ((kT if nm == "k" else qT)[pr][r0:r0 + 64, cols], [(kT if nm == "k" else qT)[pr]])
            pK = mp[1]
            for h in range(4):
                pr, r0 = h // 2, (h % 2) * 64
                for si, s_ in enumerate(seqs):
                    a_, t_ = lk("k", pr, r0, si)
                    S.op("pe", lambda e: e.matmul(pK[0:64, h * 64:(h + 1) * 64], lhsT=a_, rhs=Hv[r0:r0 + 64, pr, s_, :],
                                                  start=(si == 0), stop=(si == ns - 1)), rd=t_ + [Ht], wr=[pK])
            X = self.nx_t[0]
            self.colop(E1, f3(E1), pK[0:64, 0:256].rearrange("p (h v) -> p h v", h=4), nbg_c, ALU.mult, [pK, cT])
            S.op("dve", lambda e: e.tensor_tensor(out=R_(X[0:64, 0:256]), in0=E1[0:64, 0:256], in1=Vb[0:64, 0:256], op=ALU.add),
                 rd=[E1, Vb], wr=[X])
            if STOP <= 8:
                continue
            uh = self.neumann(NN, X, levels)
            if STOP <= 9:
                continue
            pY1, pY2 = mp[0], mp[1]
            for h in range(4):
                pr, r0 = h // 2, (h % 2) * 64
                for si, s_ in enumerate(seqs):
                    a_, t_ = lk("q", pr, r0, si)
                    S.op("pe", lambda e: e.matmul(pY1[r0:r0 + 64, pr * 64:(pr + 1) * 64], lhsT=Hv[r0:r0 + 64, pr, s_, :], rhs=a_,
                                                  start=(si == 0), stop=(si == ns - 1)), rd=t_ + [Ht], wr=[pY1])
                S.op("pe", lambda e: e.matmul(pY2[r0:r0 + 64, pr * 64:(pr + 1) * 64], lhsT=uh(h)[0],
                                              rhs=AQ[0:64, h * 64:(h + 1) * 64], start=True, stop=True), rd=[uh(h)[1], AQ], wr=[pY2])
            for h in range(4):
                pr, r0 = h // 2, (h % 2) * 64
                S.op("dve", lambda e: e.tensor_tensor(out=oall[pr][r0:r0 + 64, cols], in0=pY1[r0:r0 + 64, pr * 64:(pr + 1) * 64],
                                                      in1=exGv[r0:r0 + 64, h, :], op=ALU.mult), rd=[pY1, exG], wr=[oall[pr]])
            for pr in range(2):
                S.op("dve", lambda e: e.tensor_tensor(out=oall[pr][:, cols], in0=pY2[:, pr * 64:(pr + 1) * 64], in1=oall[pr][:, cols], op=ALU.add),
                     rd=[pY2, oall[pr]], wr=[oall[pr]])
            if STOP <= 10:
                continue
            for pr in range(2):
                pH = [mp[2], mp[3]]
                for hh in range(2):
                    h = pr * 2 + hh
                    r0 = hh * 64
                    if ns > 1:
                        Up = (sc[5], sc[6]) if hh == 0 else (sc[9], sc[2])
                        for half in range(2):
                            S.op("pool", lambda e: e.tensor_tensor(
                                out=Up[half][0:64, 0:512].rearrange("p (s v) -> p s v", s=8),
                                in0=uh(h)[0].unsqueeze(1).to_broadcast([64, 8, 64]),
                                in1=self.cs("bsel", 0, 64)[:, half * 8:(half + 1) * 8].unsqueeze(2).to_broadcast([64, 8, 64]), op=ALU.mult),
                                rd=[uh(h)[1], cst], wr=[Up[half]])
                            S.op("pe", lambda e: e.matmul(pH[half][r0:r0 + 64, 0:512], lhsT=Kd[0:64, h * 64:(h + 1) * 64], rhs=Up[half][0:64, 0:512],
                                                          start=True, stop=True), rd=[Kd, Up[half]], wr=[pH[half]])
                    else:
                        S.op("pe", lambda e: e.matmul(pH[0][r0:r0 + 64, 0:64], lhsT=Kd[0:64, h * 64:(h + 1) * 64], rhs=uh(h)[0],
                                                      start=True, stop=True), rd=[Kd, uh(h)[1]], wr=[pH[0]])
                    gC = exGv[r0:r0 + 64, h, :].rearrange("p (s i) -> p s i", i=C)[:, :, C - 1]
                    if ns > 1:
                        for si in range(ns):
                            hs1 = Hv[r0:r0 + 64, pr, si, :]
                            S.op("dve", lambda e: e.tensor_scalar(hs1, hs1, gC[:, si:si + 1], None, ALU.mult), rd=[Ht, exG], wr=[Ht])
                        for half in range(2):
                            hs2 = Hv[r0:r0 + 64, pr, half * 8:(half + 1) * 8, :]
                            S.op("dve", lambda e: e.tensor_tensor(out=hs2, in0=hs2, in1=pH[half][r0:r0 + 64, 0:512].rearrange("p (s v) -> p s v", s=8),
                                                                  op=ALU.add), rd=[Ht, pH[half]], wr=[Ht])
                    else:
                        hsl = Hv[r0:r0 + 64, pr, seqs[0], :]
                        S.op("dve", lambda e: e.scalar_tensor_tensor(out=hsl, in0=hsl, scalar=exGv[r0:r0 + 64, h, 63:64], in1=pH[0][r0:r0 + 64, 0:64],
                                                                     op0=ALU.mult, op1=ALU.add), rd=[Ht, exG, pH[0]], wr=[Ht])
        if last:
            og = self.o_gdn_s if samp else self.o_gdn_p
            S.dma("act", og[l], Hv, rd=[Ht])
        if "gdn" not in self.mixers or STOP < 99:
            return
        for pr in range(2):
            self.head_rms(oall[pr], NT, 1.0 / 64, EPS, rs)
            S.op("dve", lambda e: e.scalar_tensor_tensor(out=oall[pr][:, 0:NT], in0=oall[pr][:, 0:NT], scalar=pp[:, PPO["gdn_nw"]:PPO["gdn_nw"] + 1],
                                                         in1=rs[:, 0:NT], op0=ALU.mult, op1=ALU.mult), rd=[oall[pr], pp, rs], wr=[oall[pr]])
            S.op("act", lambda e: e.activation(out=sc[0][:, 0:NT], in_=self.p[17 + pr][:, 0:NT], func=AF.Silu), rd=[self.p[17 + pr]], wr=[sc[0]])
            S.op("dve", lambda e: e.tensor_tensor(out=self.mix[4 + pr][:, 0:NT], in0=oall[pr][:, 0:NT], in1=sc[0][:, 0:NT], op=ALU.mult),
                 rd=[oall[pr], sc[0]], wr=[self.mix[4 + pr]])

    def s5_setup(self, l):
        S = self.S
        k = self.s5k[l]
        pp = self.pp[l]
        col = lambda i: k[:, i * 8:(i + 1) * 8]
        ppv = lambda n: pp[:, PPO[n]:PPO[n] + 8]
        D_, T1, MAG, TH, KF, SIN, COS, ABR, ABI, DEN, NR, CFR, CFI, NCFR, T2, T3 = [col(i) for i in range(16)]
        self.S5 = dict(MAG=2, TH=3, CFR=11, CFI=12, NCFR=13)
        ki = self.s5i[:, 0:8]
        r, w = [k, pp], [k]
        A = lambda eng, fn, rd=r, wr=w: S.op(eng, fn, rd=rd, wr=wr)
        A("act", lambda e: e.activation(out=D_, in_=ppv("s5_log_dt"), func=AF.Exp))
        A("dve", lambda e: e.tensor_tensor(out=T1, in0=D_, in1=ppv("s5_a_re"), op=ALU.mult))
        A("act", lambda e: e.activation(out=MAG, in_=T1, func=AF.Exp))
        A("dve", lambda e: e.tensor_tensor(out=TH, in0=D_, in1=ppv("s5_a_im"), op=ALU.mult))
        A("dve", lambda e: e.tensor_scalar(KF, TH, 1.0 / (2 * math.pi), None, ALU.mult))
        A("dve", lambda e: e.tensor_copy(ki, KF), rd=[k], wr=[self.s5i])
        A("dve", lambda e: e.tensor_copy(KF, ki), rd=[self.s5i], wr=[k])
        A("dve", lambda e: e.scalar_tensor_tensor(out=TH, in0=KF, scalar=-2 * math.pi, in1=TH, op0=ALU.mult, op1=ALU.add))
        A("dve", lambda e: e.tensor_scalar(TH, TH, -3.1415925, 3.1415925, ALU.max, ALU.min))
        A("act", lambda e: e.activation(out=SIN, in_=TH, func=AF.Sin))
        A("act", lambda e: e.activation(out=T2, in_=TH, func=AF.Abs))
        A("dve", lambda e: e.tensor_scalar(T2, T2, -1.0, math.pi / 2, ALU.mult, ALU.add))
        A("act", lambda e: e.activation(out=COS, in_=T2, func=AF.Sin))
        A("dve", lambda e: e.tensor_tensor(out=ABR, in0=MAG, in1=COS, op=ALU.mult))
        A("dve", lambda e: e.tensor_tensor(out=ABI, in0=MAG, in1=SIN, op=ALU.mult))
        A("dve", lambda e: e.tensor_tensor(out=DEN, in0=ppv("s5_a_re"), in1=ppv("s5_a_re"), op=ALU.mult))
        A("dve", lambda e: e.tensor_tensor(out=T2, in0=ppv("s5_a_im"), in1=ppv("s5_a_im"), op=ALU.mult))
        A("dve", lambda e: e.tensor_tensor(out=DEN, in0=DEN, in1=T2, op=ALU.add))
        A("dve", lambda e: e.reciprocal(DEN, DEN))
        A("dve", lambda e: e.tensor_scalar(NR, ABR, -1.0, None, ALU.add))
        A("dve", lambda e: e.tensor_tensor(out=T2, in0=NR, in1=ppv("s5_a_re"), op=ALU.mult))
        A("dve", lambda e: e.tensor_tensor(out=T3, in0=ABI, in1=ppv("s5_a_im"), op=ALU.mult))
        A("dve", lambda e: e.tensor_tensor(out=T2, in0=T2, in1=T3, op=ALU.add))
        A("dve", lambda e: e.tensor_tensor(out=CFR, in0=T2, in1=DEN, op=ALU.mult))
        A("dve", lambda e: e.tensor_tensor(out=T2, in0=ABI, in1=ppv("s5_a_re"), op=ALU.mult))
        A("dve", lambda e: e.tensor_tensor(out=T3, in0=NR, in1=ppv("s5_a_im"), op=ALU.mult))
        A("dve", lambda e: e.tensor_tensor(out=T2, in0=T2, in1=T3, op=ALU.subtract))
        A("dve", lambda e: e.tensor_tensor(out=CFI, in0=T2, in1=DEN, op=ALU.mult))
        A("dve", lambda e: e.tensor_scalar(NCFR, CFR, -1.0, None, ALU.mult))

    def s5(self, g, st, l, NT, nseq, T, last):
        S = self.S
        samp = (g == "s")
        mp, sc, cst = self.mp, self.sc, self.cst
        k = self.s5k[l]
        kc = lambda name, j: k[:, self.S5[name] * 8 + j:self.S5[name] * 8 + j + 1]
        v3 = lambda ap: ap.rearrange("p (s t) -> p s t", t=T)
        stgB, stgC = self.stage[0], self.stage[1]
        S.dma("sp", stgB[:, 0:2048], self.s5B[l], wr=[stgB])
        S.dma("sp", stgC[:, 0:2048], self.s5C[l], wr=[stgC])
        Bm = stgB[:, 0:2048].rearrange("p (a j n) -> p a j n", a=2, j=8)
        Cm = stgC[:, 0:2048].rearrange("p (a j n) -> p a j n", a=2, j=8)
        S.op("dve", lambda e: e.tensor_scalar(stgC[:, 1024:2048], stgC[:, 1024:2048], -1.0, None, ALU.mult), rd=[stgC], wr=[stgC])
        wg = sc[19]
        S.dma("sp", wg[:, 0:512].rearrange("p (k n) -> p k n", k=2), self.s5glu[l].rearrange("(k p) n -> p k n", p=128), wr=[wg])
        if samp:
            hre, him = sc[17], sc[18]
            S.dma("sp", hre[:, 0:8 * nseq].rearrange("p (j s) -> p j s", j=8), self.s5s_re[l], wr=[hre])
            S.dma("sp", him[:, 0:8 * nseq].rearrange("p (j s) -> p j s", j=8), self.s5s_im[l], wr=[him])
        else:
            hre, him = self.s5h[l]
        hv = lambda t: t[:, 0:8 * nseq].rearrange("p (j s) -> p j s", j=8)
        setA = [sc[0], sc[1], sc[2], sc[3], sc[4], sc[5], sc[6], sc[7], sc[8]]
        setB = [sc[12], sc[13], sc[14], sc[15], sc[16], sc[20], sc[21], sc[22], sc[23]]
        ei = self.s5i[:, 0:T]
        bc = lambda ap: ap.unsqueeze(1).to_broadcast([128, nseq, T])
        u = [self.p[8], self.p[9]]
        Y = [mp[2], mp[3]]
        zt = [sc[9], sc[10]]
        for j in range(8):
            ET, F, Z1, Z2, ZR, ZI, DK, HR, HI = setA if j % 2 == 0 else setB
            ER, EI = ET[:, 0:T], ET[:, 256:256 + T]
            S.op("act", lambda e: e.activation(out=Z1[:, 0:T], in_=self.cs("iota")[:, 0:T], func=AF.Copy, scale=kc("TH", j)),
                 rd=[cst, k], wr=[Z1])
            S.op("dve", lambda e: e.tensor_scalar(Z2[:, 0:T], Z1[:, 0:T], 1.0 / (2 * math.pi), None, ALU.mult), rd=[Z1], wr=[Z2])
            S.op("dve", lambda e: e.tensor_copy(ei, Z2[:, 0:T]), rd=[Z2], wr=[self.s5i])
            S.op("dve", lambda e: e.tensor_copy(Z2[:, 0:T], ei), rd=[self.s5i], wr=[Z2])
            S.op("dve", lambda e: e.scalar_tensor_tensor(out=Z1[:, 0:T], in0=Z2[:, 0:T], scalar=-2 * math.pi, in1=Z1[:, 0:T],
                                                          op0=ALU.mult, op1=ALU.add), rd=[Z1, Z2], wr=[Z1])
            S.op("dve", lambda e: e.tensor_scalar(Z1[:, 0:T], Z1[:, 0:T], -3.1415925, 3.1415925, ALU.max, ALU.min), rd=[Z1], wr=[Z1])
            S.op("act", lambda e: e.activation(out=EI, in_=Z1[:, 0:T], func=AF.Sin), rd=[Z1], wr=[ET])
            S.op("act", lambda e: e.activation(out=Z2[:, 0:T], in_=Z1[:, 0:T], func=AF.Abs), rd=[Z1], wr=[Z2])
            S.op("pool", lambda e: e.tensor_scalar(Z2[:, 0:T], Z2[:, 0:T], -1.0, math.pi / 2, ALU.mult, ALU.add), rd=[Z2], wr=[Z2])
            S.op("act", lambda e: e.activation(out=ER, in_=Z2[:, 0:T], func=AF.Sin), rd=[Z2], wr=[ET])
            FR, FI = F[:, 0:T], F[:, 256:256 + T]
            S.op("act", lambda e: e.activation(out=FR, in_=ER, func=AF.Copy, scale=kc("CFR", j)), rd=[ET, k], wr=[F])
            S.op("dve", lambda e: e.scalar_tensor_tensor(out=FR, in0=EI, scalar=kc("CFI", j), in1=FR, op0=ALU.mult, op1=ALU.add),
                 rd=[ET, k, F], wr=[F])
            S.op("act", lambda e: e.activation(out=FI, in_=ER, func=AF.Copy, scale=kc("CFI", j)), rd=[ET, k], wr=[F])
            S.op("dve", lambda e: e.scalar_tensor_tensor(out=FI, in0=EI, scalar=kc("NCFR", j), in1=FI, op0=ALU.mult, op1=ALU.add),
                 rd=[ET, k, F], wr=[F])
            pA, pB = (mp[0], mp[1]) if j % 2 == 0 else (mp[4], self.ps_lin[0])
            S.op("pe", lambda e: e.matmul(pA[:, 0:NT], lhsT=Bm[:, 0, j, :], rhs=u[j // 4][:, 0:NT], start=True, stop=True),
                 rd=[stgB, u[j // 4]], wr=[pA])
            S.op("pe", lambda e: e.matmul(pB[:, 0:NT], lhsT=Bm[:, 1, j, :], rhs=u[j // 4][:, 0:NT], start=True, stop=True),
                 rd=[stgB, u[j // 4]], wr=[pB])
            S.op("dve", lambda e: e.tensor_tensor(out=v3(Z1[:, 0:NT]), in0=v3(pA[:, 0:NT]), in1=bc(FR), op=ALU.mult), rd=[pA, F], wr=[Z1])
            S.op("dve", lambda e: e.tensor_tensor(out=v3(Z2[:, 0:NT]), in0=v3(pB[:, 0:NT]), in1=bc(FI), op=ALU.mult), rd=[pB, F], wr=[Z2])
            S.op("dve", lambda e: e.tensor_tensor(out=ZR[:, 0:NT], in0=Z1[:, 0:NT], in1=Z2[:, 0:NT], op=ALU.subtract), rd=[Z1, Z2], wr=[ZR])
            S.op("dve", lambda e: e.tensor_tensor(out=v3(Z1[:, 0:NT]), in0=v3(pB[:, 0:NT]), in1=bc(FR), op=ALU.mult), rd=[pB, F], wr=[Z1])
            S.op("dve", lambda e: e.tensor_tensor(out=v3(Z2[:, 0:NT]), in0=v3(pA[:, 0:NT]), in1=bc(FI), op=ALU.mult), rd=[pA, F], wr=[Z2])
            S.op("dve", lambda e: e.tensor_tensor(out=ZI[:, 0:NT], in0=Z1[:, 0:NT], in1=Z2[:, 0:NT], op=ALU.add), rd=[Z1, Z2], wr=[ZI])
            S.op("dve", lambda e: e.scalar_tensor_tensor(out=v3(ZR[:, 0:NT])[:, :, 0], in0=hv(hre)[:, j, :], scalar=kc("MAG", j),
                                                          in1=v3(ZR[:, 0:NT])[:, :, 0], op0=ALU.mult, op1=ALU.add),
                 rd=[hre, k, ZR], wr=[ZR])
            S.op("dve", lambda e: e.scalar_tensor_tensor(out=v3(ZI[:, 0:NT])[:, :, 0], in0=hv(him)[:, j, :], scalar=kc("MAG", j),
                                                          in1=v3(ZI[:, 0:NT])[:, :, 0], op0=ALU.mult, op1=ALU.add),
                 rd=[him, k, ZI], wr=[ZI])
            S.op("act", lambda e: e.activation(out=DK[:, 0:NT], in_=self.cs("rmask_p")[:, 0:NT], func=AF.Identity, scale=0.0, bias=kc("MAG", j)),
                 rd=[cst, k], wr=[DK])
            S.op("dve", lambda e: e.tensor_scalar(v3(DK[:, 0:NT])[:, :, 0], v3(DK[:, 0:NT])[:, :, 0], 0.0, None, ALU.mult), rd=[DK], wr=[DK])
            S.op("dve", lambda e: e.tensor_tensor_scan(Z1[:, 0:NT], DK[:, 0:NT], ZR[:, 0:NT], 0.0, ALU.mult, ALU.add), rd=[DK, ZR], wr=[Z1])
            S.op("dve", lambda e: e.tensor_tensor_scan(Z2[:, 0:NT], DK[:, 0:NT], ZI[:, 0:NT], 0.0, ALU.mult, ALU.add), rd=[DK, ZI], wr=[Z2])
            S.op("dve", lambda e: e.tensor_tensor(out=v3(ZR[:, 0:NT]), in0=v3(Z1[:, 0:NT]), in1=bc(ER), op=ALU.mult), rd=[Z1, ET], wr=[ZR])
            S.op("pool", lambda e: e.tensor_tensor(out=v3(ZI[:, 0:NT]), in0=v3(Z2[:, 0:NT]), in1=bc(EI), op=ALU.mult), rd=[Z2, ET], wr=[ZI])
            S.op("dve", lambda e: e.tensor_tensor(out=HR[:, 0:NT], in0=ZR[:, 0:NT], in1=ZI[:, 0:NT], op=ALU.subtract), rd=[ZR, ZI], wr=[HR])
            S.op("dve", lambda e: e.tensor_tensor(out=v3(ZR[:, 0:NT]), in0=v3(Z2[:, 0:NT]), in1=bc(ER), op=ALU.mult), rd=[Z2, ET], wr=[ZR])
            S.op("pool", lambda e: e.tensor_tensor(out=v3(ZI[:, 0:NT]), in0=v3(Z1[:, 0:NT]), in1=bc(EI), op=ALU.mult), rd=[Z1, ET], wr=[ZI])
            S.op("dve", lambda e: e.tensor_tensor(out=HI[:, 0:NT], in0=ZR[:, 0:NT], in1=ZI[:, 0:NT], op=ALU.add), rd=[ZR, ZI], wr=[HI])
            S.op("dve", lambda e: e.tensor_copy(hv(hre)[:, j, :], v3(HR[:, 0:NT])[:, :, T - 1]), rd=[HR], wr=[hre])
            S.op("dve", lambda e: e.tensor_copy(hv(him)[:, j, :], v3(HI[:, 0:NT])[:, :, T - 1]), rd=[HI], wr=[him])
            yy = Y[j // 4]
            S.op("pe", lambda e: e.matmul(yy[:, 0:NT], lhsT=Cm[:, 0, j, :], rhs=HR[:, 0:NT], start=(j % 4 == 0), stop=False),
                 rd=[stgC, HR], wr=[yy])
            S.op("pe", lambda e: e.matmul(yy[:, 0:NT], lhsT=Cm[:, 1, j, :], rhs=HI[:, 0:NT], start=False, stop=(j % 4 == 3)),
                 rd=[stgC, HI], wr=[yy])
        if last:
            ore, oim = (self.o_s5re_s, self.o_s5im_s) if samp else (self.o_s5re_p, self.o_s5im_p)
            S.dma("act", ore[l], hv(hre), rd=[hre])
            S.dma("act", oim[l], hv(him), rd=[him])
        if "s5" not in self.mixers:
            return
        for oc in range(2):
            S.op("dve", lambda e: e.scalar_tensor_tensor(out=sc[11][:, 0:NT], in0=u[oc][:, 0:NT], scalar=self.ppc(l, "s5_d", oc),
                                                         in1=Y[oc][:, 0:NT], op0=ALU.mult, op1=ALU.add),
                 rd=[u[oc], self.pp[l], Y[oc]], wr=[sc[11]])
            S.op("act", lambda e: e.activation(out=zt[oc][:, 0:NT], in_=sc[11][:, 0:NT], func=AF.Gelu_apprx_tanh), rd=[sc[11]], wr=[zt[oc]])
        wgv = wg[:, 0:512].rearrange("p (k n) -> p k n", k=2)
        for oc in range(2):
            pg = mp[oc]
            for kk_ in range(2):
                S.op("pe", lambda e: e.matmul(pg[:, 0:NT], lhsT=wgv[:, kk_, oc * 128:(oc + 1) * 128], rhs=zt[kk_][:, 0:NT],
                                              start=(kk_ == 0), stop=(kk_ == 1)), rd=[wg, zt[kk_]], wr=[pg])
            S.op("act", lambda e: e.activation(out=sc[11][:, 0:NT], in_=pg[:, 0:NT], func=AF.Sigmoid, bias=self.ppc(l, "s5_b_glu", oc)),
                 rd=[pg, self.pp[l]], wr=[sc[11]])
            S.op("dve", lambda e: e.tensor_tensor(out=self.mix[2 + oc][:, 0:NT], in0=zt[oc][:, 0:NT], in1=sc[11][:, 0:NT], op=ALU.mult),
                 rd=[zt[oc], sc[11]], wr=[self.mix[2 + oc]])

    def swa(self, g, st, l, NT, nseq, T, last):
        S = self.S
        samp = (g == "s")
        first = (not samp) and st == 0
        mp, sc, cst = self.mp, self.sc, self.cst
        do_mix = "swa" in self.mixers
        rt = sc[0]
        rsrc = self.ropeS if samp else self.ropeP[:, :, st * T:(st + 1) * T]
        S.dma("sp", rt[0:64, 0:2 * T].rearrange("p (a t) -> p a t", a=2), rsrc, wr=[rt])
        rtv = rt[0:64, 0:2 * T].rearrange("p (a t) -> p a t", a=2)
        cosb = rtv[:, 0, :].unsqueeze(1).to_broadcast([64, nseq, T])
        sinb = rtv[:, 1, :].unsqueeze(1).to_broadcast([64, nseq, T])
        selm = self.cs("selm").rearrange("p (a m) -> p a m", a=4)
        v3 = lambda ap: ap.rearrange("p (s t) -> p s t", t=T)
        ones64 = self.cs("ones")[:, 0:64]

        def rot(src, gsel, dst_ap, dst_t):
            pa, pb = mp[0], mp[1]
            S.op("pe", lambda e: e.matmul(pa[0:64, 0:NT], lhsT=selm[:, gsel, :], rhs=src[:, 0:NT], start=True, stop=True),
                 rd=[src, cst], wr=[pa])
            S.op("pe", lambda e: e.matmul(pb[0:64, 0:NT], lhsT=selm[:, 2 + gsel, :], rhs=src[:, 0:NT], start=True, stop=True),
                 rd=[src, cst], wr=[pb])
            S.op("dve", lambda e: e.tensor_tensor(out=v3(sc[1][0:64, 0:NT]), in0=v3(pa[0:64, 0:NT]), in1=cosb, op=ALU.mult),
                 rd=[pa, rt], wr=[sc[1]])
            S.op("dve", lambda e: e.tensor_tensor(out=v3(sc[2][0:64, 0:NT]), in0=v3(pb[0:64, 0:NT]), in1=sinb, op=ALU.mult),
                 rd=[pb, rt], wr=[sc[2]])
            S.op("dve", lambda e: e.tensor_tensor(out=dst_ap, in0=sc[1][0:64, 0:NT], in1=sc[2][0:64, 0:NT], op=ALU.add),
                 rd=[sc[1], sc[2]], wr=dst_t)

        vtok = [sc[3], sc[4]]
        if samp:
            stg = self.stage[0]
            vhis = stg[:, 0:2048].rearrange("p (s f) -> p s f", f=128)
            S.dma("sp", vhis, self.vc_s[l].rearrange("s j f -> j s f"), wr=[stg])
            S.op("pe", lambda e: e.transpose(mp[0][0:NT, 0:128], self.p[22][:, 0:NT], self.ident[:]),
                 rd=[self.p[22], self.ident], wr=[mp[0]])
            S.op("act", lambda e: e.activation(out=sc[3][0:NT, 0:128], in_=mp[0][0:NT, 0:128], func=AF.Copy),
                 rd=[mp[0]], wr=[sc[3]])
            if last:
                S.dma("act", self.o_swav_s[l, :, 0:124, :], self.vc_s[l, :, 4:128, :])
                for s_ in range(nseq):
                    S.dma("act", self.o_swav_s[l, s_, 124:128, :], sc[3][s_ * 4:s_ * 4 + 4, 0:128], rd=[sc[3]])
        else:
            def vblk(s_, b_):
                i = s_ * 3 + b_
                return vtok[i // 4][:, (i % 4) * 128:(i % 4 + 1) * 128], vtok[i // 4]
            for s_ in range(nseq):
                if not first:
                    a, t = vblk(s_, 0)
                    S.op("pool", lambda e: e.tensor_copy(a, self.vhist[l][:, s_ * 128:(s_ + 1) * 128]),
                         rd=[self.vhist[l]], wr=[t])
                for b_ in range(T // 128):
                    a, t = vblk(s_, 1 + b_)
                    c0 = s_ * T + b_ * 128
                    S.op("pe", lambda e: e.transpose(mp[0][:, 0:128], self.p[22][:, c0:c0 + 128], self.ident[:]),
                         rd=[self.p[22], self.ident], wr=[mp[0]])
                    S.op("act", lambda e: e.activation(out=a, in_=mp[0][:, 0:128], func=AF.Copy), rd=[mp[0]], wr=[t])
                a, t = vblk(s_, T // 128)
                S.op("pool", lambda e: e.tensor_copy(self.vhist[l][:, s_ * 128:(s_ + 1) * 128], a),
                     rd=[t], wr=[self.vhist[l]])
                if last:
                    S.dma("act", self.o_swav_p[l, s_, :, :], a, rd=[t])

        for kvh in range(2):
            qrot = [sc[5], sc[6]]
            knew = sc[7]
            for gg in range(2):
                rot(self.p[19 + kvh], gg, qrot[gg][0:64, 0:NT], [qrot[gg]])
            rot(self.p[21], kvh, knew[0:64, 0:NT], [knew])
            num_t = sc[10]
            if samp:
                stg_k = self.stage[1]
                khis = stg_k[0:64, 0:2048].rearrange("p (s j) -> p s j", j=128)
                S.dma("sp", khis, self.kc_s[l, :, kvh, :, :].rearrange("s d j -> d s j"), wr=[stg_k])
                if last:
                    S.dma("act", self.o_swak_s[l, :, kvh, :, 0:124], self.kc_s[l, :, kvh, :, 4:128])
                    S.dma("act", self.o_swak_s[l, :, kvh, :, 124:128].rearrange("s d t -> d s t"),
                          v3(knew[0:64, 0:NT]), rd=[knew])
                if not do_mix:
                    continue
                psS = mp[2]
                for s_ in range(nseq):
                    for gg in range(2):
                        S.op("pe", lambda e: e.matmul(psS[:, s_ * 8 + gg * 4:s_ * 8 + gg * 4 + 4], lhsT=khis[:, s_, :],
                                                      rhs=qrot[gg][0:64, s_ * 4:s_ * 4 + 4], start=True, stop=True),
                             rd=[stg_k, qrot[gg]], wr=[psS])
                for gg in range(2):
                    S.op("pe", lambda e: e.matmul(psS[0:64, 128 + gg * 64:128 + gg * 64 + 64], lhsT=knew[0:64, 0:NT],
                                                  rhs=qrot[gg][0:64, 0:NT], start=True, stop=True),
                         rd=[knew, qrot[gg]], wr=[psS])
                pT = sc[8]
                S.op("act", lambda e: e.activation(out=pT[:, 0:128], in_=psS[:, 0:128], func=AF.Exp, scale=0.125),
                     rd=[psS], wr=[pT])
                S.op("act", lambda e: e.activation(out=pT[0:64, 128:256], in_=psS[0:64, 128:256], func=AF.Exp, scale=0.125),
                     rd=[psS], wr=[pT])
                S.op("dve", lambda e: e.tensor_tensor(out=pT[:, 0:128], in0=pT[:, 0:128], in1=self.cs("mask_sh"), op=ALU.mult),
                     rd=[pT, cst], wr=[pT])
                S.op("dve", lambda e: e.tensor_tensor(out=pT[0:64, 128:256], in0=pT[0:64, 128:256], in1=self.cs("mask_sn", 0, 64),
                                                      op=ALU.mult), rd=[pT, cst], wr=[pT])
                psA = mp[3]
                for s_ in range(nseq):
                    for gg in range(2):
                        r_ = pT[:, s_ * 8 + gg * 4:s_ * 8 + gg * 4 + 4]
                        S.op("pe", lambda e: e.matmul(psA[gg * 64:gg * 64 + 64, s_ * 4:s_ * 4 + 4],
                                                      lhsT=vhis[:, s_, kvh * 64:kvh * 64 + 64], rhs=r_, start=True, stop=True),
                             rd=[stg, pT], wr=[psA])
                        S.op("pe", lambda e: e.matmul(psA[gg * 64:gg * 64 + 64, 64 + s_ * 4:64 + s_ * 4 + 4],
                                                      lhsT=ones64, rhs=r_, start=True, stop=True),
                             rd=[cst, pT], wr=[psA])
                for gg in range(2):
                    r_ = pT[0:64, 128 + gg * 64:128 + gg * 64 + 64]
                    S.op("pe", lambda e: e.matmul(psA[gg * 64:gg * 64 + 64, 128:192], lhsT=sc[3][0:64, kvh * 64:kvh * 64 + 64],
                                                  rhs=r_, start=True, stop=True), rd=[sc[3], pT], wr=[psA])
                    S.op("pe", lambda e: e.matmul(psA[gg * 64:gg * 64 + 64, 192:256], lhsT=ones64[0:64, :], rhs=r_,
                                                  start=True, stop=True), rd=[cst, pT], wr=[psA])
                S.op("act", lambda e: e.activation(out=sc[9][:, 0:128], in_=psA[:, 128:256], func=AF.Copy), rd=[psA], wr=[sc[9]])
                S.op("dve", lambda e: e.tensor_tensor(out=num_t[:, 0:128], in0=psA[:, 0:128], in1=sc[9][:, 0:128], op=ALU.add),
                     rd=[psA, sc[9]], wr=[num_t])
                num_ap, den_ap, nd_t = num_t[:, 0:64], num_t[:, 64:128], [num_t]
            else:
                kall = [sc[11], sc[12]]
                for s_ in range(nseq):
                    if not first:
                        S.op("pool", lambda e: e.tensor_copy(kall[s_][0:64, 0:128], self.khist[l][kvh][:, s_ * 128:(s_ + 1) * 128]),
                             rd=[self.khist[l][kvh]], wr=[kall[s_]])
                    S.op("pool", lambda e: e.tensor_copy(kall[s_][0:64, 128:128 + T], knew[0:64, s_ * T:(s_ + 1) * T]),
                         rd=[knew], wr=[kall[s_]])
                    S.op("pool", lambda e: e.tensor_copy(self.khist[l][kvh][:, s_ * 128:(s_ + 1) * 128], kall[s_][0:64, T:T + 128]),
                         rd=[kall[s_]], wr=[self.khist[l][kvh]])
                    if last:
                        S.dma("act", self.o_swak_p[l, s_, kvh, :, :], kall[s_][0:64, T:T + 128], rd=[kall[s_]])
                if not do_mix:
                    continue
                psN, psD = mp[3], mp[4]
                mask_p = self.cs("mask_p")
                for s_ in range(nseq):
                    for b_ in range(T // 128):
                        tok0 = s_ * T + b_ * 128
                        kts = []
                        if not (first and b_ == 0):
                            kts.append((0, b_ * 128, vblk(s_, b_)))
                        kts.append((1, (b_ + 1) * 128, vblk(s_, b_ + 1)))
                        psS = mp[2]
                        for (mi, k0, _) in kts:
                            for gg in range(2):
                                S.op("pe", lambda e: e.matmul(psS[:, mi * 256 + gg * 128:mi * 256 + gg * 128 + 128],
                                                              lhsT=kall[s_][0:64, k0:k0 + 128], rhs=qrot[gg][0:64, tok0:tok0 + 128],
                                                              start=True, stop=True), rd=[kall[s_], qrot[gg]], wr=[psS])
                        c0 = kts[0][0] * 256
                        pT = sc[8 + (b_ % 2)]
                        S.op("act", lambda e: e.activation(out=pT[:, c0:512], in_=psS[:, c0:512], func=AF.Exp, scale=0.125),
                             rd=[psS], wr=[pT])
                        nm_ = (512 - c0) // 256
                        pTv = pT[:, c0:512].rearrange("p (m g i) -> p m g i", g=2, i=128)
                        mkv = mask_p[:, c0 // 2:256].rearrange("p (m i) -> p m i", i=128).unsqueeze(2).to_broadcast([128, nm_, 2, 128])
                        S.op("dve", lambda e: e.tensor_tensor(out=pTv, in0=pTv, in1=mkv, op=ALU.mult), rd=[pT, cst], wr=[pT])
                        for gg in range(2):
                            for ki, (mi, k0, (va, vt)) in enumerate(kts):
                                r_ = pT[:, mi * 256 + gg * 128:mi * 256 + gg * 128 + 128]
                                S.op("pe", lambda e: e.matmul(psN[gg * 64:gg * 64 + 64, tok0:tok0 + 128], lhsT=va[:, kvh * 64:kvh * 64 + 64],
                                                              rhs=r_, start=(ki == 0), stop=(ki == len(kts) - 1)), rd=[vt, pT], wr=[psN])
                            for ki, (mi, k0, (va, vt)) in enumerate(kts):
                                r_ = pT[:, mi * 256 + gg * 128:mi * 256 + gg * 128 + 128]
                                S.op("pe", lambda e: e.matmul(psD[gg * 64:gg * 64 + 64, tok0:tok0 + 128], lhsT=ones64, rhs=r_,
                                                              start=(ki == 0), stop=(ki == len(kts) - 1)), rd=[cst, pT], wr=[psD])
                num_ap, den_ap, nd_t = psN[:, 0:NT], psD[:, 0:NT], [psN, psD]
            dt_ = sc[13]
            S.op("dve", lambda e: e.tensor_scalar(dt_[:, 0:NT], den_ap, self.esink[l][:, kvh:kvh + 1], None, ALU.add),
                 rd=nd_t + [self.esink[l]], wr=[dt_])
            S.op("dve", lambda e: e.reciprocal(dt_[:, 0:NT], dt_[:, 0:NT]), rd=[dt_], wr=[dt_])
            S.op("dve", lambda e: e.tensor_tensor(out=self.mix[6 + kvh][:, 0:NT], in0=num_ap, in1=dt_[:, 0:NT], op=ALU.mult),
                 rd=nd_t + [dt_], wr=[self.mix[6 + kvh]])

    def run_group(self, g, st):
        S = self.S
        if g == "s":
            NT = self.NSS * 4
            src = self.xT_s
            cols = [(0, NT, 0)]
            dst = self.yT_s
        else:
            NT = self.NSP * TSTEP
            src = self.xT_p
            dst = self.yT_p
            cols = [(q * self.SEQ + st * TSTEP, TSTEP, q * TSTEP) for q in range(self.NSP)]
        for c in range(8):
            for (d0, n, s0) in cols:
                S.dma("sp", self.x[c][:, s0:s0 + n], src[c * 128:(c + 1) * 128, d0:d0 + n], wr=[self.x[c]])
        for l in range(DEPTH):
            self.layer(g, st, l, NT)
        for c in range(8):
            for (d0, n, s0) in cols:
                S.dma("act", dst[c * 128:(c + 1) * 128, d0:d0 + n], self.x[c][:, s0:s0 + n], rd=[self.x[c]])

    def layer(self, g, st, l, NT):
        S = self.S
        nseq = self.NSS if g == "s" else self.NSP
        T = 4 if g == "s" else TSTEP
        last = (g == "s") or (st == self.nsteps - 1)
        self.prenorm(l, "g_mix_pre", NT)

        def cons_p(tag, pst, m):
            S.op("act", lambda e: e.activation(out=self.p[tag][0:m, 0:NT], in_=pst[0:m, 0:NT], func=AF.Copy),
                 rd=[pst], wr=[self.p[tag]])

        self.linear(self.w_in[l], 8, WIN_TILES, lambda k: (self.xn[k][:, 0:NT], [self.xn[k]]), NT, cons_p, wkey=(l, "in"))
        if last:
            o = self.o_shift_s if g == "s" else self.o_shift_p
            for c in range(8):
                src = self.p[c][:, 0:NT].rearrange("p (s t) -> p s t", t=T)[:, :, T - 1]
                S.dma("act", o[l, c * 128:(c + 1) * 128, :], src, rd=[self.p[c]])
        for nm, cs_ in (("rw", (0, 1)), ("s5", (2, 3)), ("gdn", (4, 5)), ("swa", (6, 7))):
            if nm not in self.mixers:
                for c in cs_:
                    S.op("pool", lambda e: e.memset(self.mix[c][:, 0:NT], 0.0), wr=[self.mix[c]])
        import os
        skip = os.environ.get("KSKIP", "").split(",")
        if "rw" not in skip:
            self.rwkv(g, st, l, NT, nseq, T, last)
        if "gdn" not in skip:
            self.gdn(g, st, l, NT, nseq, T, last)
        if "s5" not in skip:
            self.s5(g, st, l, NT, nseq, T, last)
        if "swa" not in skip:
            self.swa(g, st, l, NT, nseq, T, last)
        def cons_t(tag, pst, m):
            S.op("act", lambda e: e.activation(out=self.tmp[tag][:, 0:NT], in_=pst[:, 0:NT], func=AF.Copy),
                 rd=[pst], wr=[self.tmp[tag]])

        t_out = [(0, 512, [(i * 128, (i + 1) * 128, i) for i in range(4)]),
                 (512, 1024, [(i * 128, (i + 1) * 128, i) for i in range(4, 8)])]
        self.linear(self.w_out[l], 8, t_out, lambda k: (self.mix[k][:, 0:NT], [self.mix[k]]), NT, cons_t, wkey=(l, "out"))
        self.postnorm_residual(l, "g_mix_post", NT)
        self.prenorm(l, "g_mlp_pre", NT)
        hb = lambda i: self.p[i // 2][:, :].bitcast(BF16)[:, (i % 2) * 512:(i % 2) * 512 + NT]

        def cons_h(tag, pst, m):
            S.op("act", lambda e: e.activation(out=self.sq[0][:, 0:NT], in_=pst[:, 0:NT], func=AF.Relu),
                 rd=[pst], wr=[self.sq[0]])
            S.op("dve", lambda e: e.tensor_tensor(out=hb(tag), in0=self.sq[0][:, 0:NT], in1=self.sq[0][:, 0:NT],
                                                  op=ALU.mult), rd=[self.sq[0]], wr=[self.p[tag // 2]])

        t_up = [(j * 512, (j + 1) * 512, [(j * 512 + i * 128, j * 512 + (i + 1) * 128, j * 4 + i) for i in range(4)])
                for j in range(8)]
        self.linear(self.w_up[l], 8, t_up, lambda k: (self.xn[k][:, 0:NT], [self.xn[k]]), NT, cons_h, wkey=(l, "up"))
        t_dn = [(i * 128, (i + 1) * 128, [(i * 128, (i + 1) * 128, i)]) for i in range(8)]
        self.linear(self.w_down[l], 32, t_dn, lambda k: (hb(k), [self.p[k // 2]]), NT, cons_t, wkey=(l, "down"))
        self.postnorm_residual(l, "g_mlp_post", NT)


PPO = {}
NPP = 0


def _ppdef(name, n):
    global NPP
    PPO[name] = NPP
    NPP += n


for _n in ("g_mix_pre", "g_mix_post", "g_mlp_pre", "g_mlp_post", "rw_mu"):
    _ppdef(_n, 8)
_ppdef("sink", 2)
for _n in ("s5_a_re", "s5_a_im", "s5_log_dt"):
    _ppdef(_n, 8)
for _n in ("rw_w0", "rw_a0", "rw_kk", "rw_ka", "rw_rk", "rw_ln_w", "rw_ln_b"):
    _ppdef(_n, 2)
_ppdef("gdn_conv_w", 24)
_ppdef("gdn_nw", 1)
_ppdef("gdn_dtb", 1)
_ppdef("gdn_alog", 1)
_ppdef("s5_d", 2)
_ppdef("s5_b_glu", 2)


def colmajor(v):
    v = np.asarray(v, np.float32).reshape(-1, 128)
    return np.ascontiguousarray(v.T)


def pack_pp(inp):
    pp = np.zeros((DEPTH, 128, NPP), np.float32)
    for l in range(DEPTH):
        for n in ("g_mix_pre", "g_mix_post", "g_mlp_pre", "g_mlp_post", "rw_mu"):
            pp[l, :, PPO[n]:PPO[n] + 8] = colmajor(inp[n][l])
        g2 = lambda a: np.asarray(a, np.float32).reshape(8, 2, 64).transpose(1, 2, 0).reshape(128, 8)
        pp[l, :, PPO["s5_a_re"]:PPO["s5_a_re"] + 8] = g2(inp["s5_a_re"][l])
        pp[l, :, PPO["s5_a_im"]:PPO["s5_a_im"] + 8] = g2(inp["s5_a_im"][l])
        pp[l, :, PPO["s5_log_dt"]:PPO["s5_log_dt"] + 8] = g2(np.repeat(np.asarray(inp["s5_log_dt"][l])[:, None], 64, 1))
        pp[l, :, PPO["s5_d"]:PPO["s5_d"] + 2] = colmajor(inp["s5_d"][l])
        pp[l, :, PPO["s5_b_glu"]:PPO["s5_b_glu"] + 2] = colmajor(inp["s5_b_glu"][l])
        for n in ("rw_w0", "rw_a0", "rw_kk", "rw_ka", "rw_rk", "rw_ln_w", "rw_ln_b"):
            pp[l, :, PPO[n]:PPO[n] + 2] = colmajor(np.asarray(inp[n][l], np.float32).reshape(-1))
        cw = np.asarray(inp["gdn_conv_w"][l], np.float32)
        for i_ in range(4):
            pp[l, :, PPO["gdn_conv_w"] + i_ * 6:PPO["gdn_conv_w"] + i_ * 6 + 6] = colmajor(cw[i_])
        pp[l, :, PPO["gdn_nw"]] = np.tile(np.asarray(inp["gdn_norm_w"][l], np.float32), 2)
        pp[l, 4:8, PPO["gdn_dtb"]] = np.asarray(inp["gdn_dt_bias"][l], np.float32)
        pp[l, 4:8, PPO["gdn_alog"]] = np.asarray(inp["gdn_a_log"][l], np.float32)
        sk = np.asarray(inp["swa_sinks"][l], np.float32)
        pp[l, :, PPO["sink"]:PPO["sink"] + 2] = np.repeat(sk.reshape(2, 2, 1), 64, axis=2).reshape(2, 128).T
    return pp


CSO = {}
NCST = 0


def _cdef(name, n):
    global NCST
    CSO[name] = (NCST, n)
    NCST += n


for _n, _k in (("selm", 256), ("blk64", 128), ("mask_p", 256), ("mask_sh", 128), ("mask_sn", 128), ("ones", 128), ("iota", 256),
               ("triL_p", 64), ("triU_p", 64), ("incU_p", 64), ("triL_s", 64), ("triU_s", 64), ("incU_s", 64),
               ("last_p", 64), ("last_s", 64), ("eye16", 256), ("bsel", 16), ("rmask_p", 512), ("rmask_s", 64),
               ("selrow", 512), ("selpair", 256)):
    _cdef(_n, _k)


def build_cst():
    c = np.zeros((128, NCST), np.float32)

    def put(name, arr):
        o, n = CSO[name]
        arr = np.asarray(arr, np.float32).reshape(arr.shape[0], -1)
        assert arr.shape[1] == n, (name, arr.shape, n)
        c[:arr.shape[0], o:o + n] = arr

    selm = np.zeros((128, 4, 64), np.float32)
    for g in range(2):
        for m in range(64):
            selm[g * 64 + m, g, m] = 1.0
            if m < 8:
                selm[g * 64 + m + 8, 2 + g, m] = -1.0
            elif m < 16:
                selm[g * 64 + m - 8, 2 + g, m] = 1.0
    put("selm", selm)
    blk = np.zeros((128, 128), np.float32)
    blk[:64, :64] = 1
    blk[64:, 64:] = 1
    put("blk64", blk)
    j = np.arange(128)[:, None]
    i = np.arange(128)[None, :]
    mp = np.zeros((128, 2, 128), np.float32)
    mp[:, 0, :] = (j > i)
    mp[:, 1, :] = (j <= i)
    put("mask_p", mp)
    msh = np.zeros((128, 16, 2, 4), np.float32)
    msh[:] = (np.arange(128)[:, None, None, None] > np.arange(4)[None, None, None, :])
    put("mask_sh", msh)
    msn = np.zeros((64, 2, 16, 4), np.float32)
    for sp in range(16):
        for jp in range(4):
            for ii in range(4):
                if jp <= ii:
                    msn[sp * 4 + jp, :, sp, ii] = 1.0
    put("mask_sn", msn)
    put("ones", np.ones((128, 128), np.float32))
    put("iota", np.tile(np.arange(1, 257, dtype=np.float32)[None, :], (128, 1)))
    a64 = np.arange(64)
    for sfx, C in (("p", 64), ("s", 4)):
        same = (a64[:, None] // C) == (a64[None, :] // C)
        put("triL_" + sfx, (same & (a64[None, :] < a64[:, None])).astype(np.float32))
        put("triU_" + sfx, (same & (a64[:, None] < a64[None, :])).astype(np.float32))
        put("incU_" + sfx, (same & (a64[:, None] <= a64[None, :])).astype(np.float32))
        lastm = np.zeros((128, 64), np.float32)
        lastm[:, :] = ((a64[None, :] % C) == C - 1)
        lastm[:64] *= same.astype(np.float32)
        lastm[64:] *= same.astype(np.float32)
        put("last_" + sfx, lastm)
    put("eye16", np.tile(np.eye(16, dtype=np.float32).reshape(1, 256), (128, 1)))
    bs = np.zeros((128, 16), np.float32)
    for pp_ in range(64):
        bs[pp_, pp_ // 4] = 1.0
    put("bsel", bs)
    rm = np.ones((128, 512), np.float32)
    rm[:, ::64] = 0.0
    put("rmask_p", rm)
    rm = np.ones((128, 64), np.float32)
    rm[:, ::4] = 0.0
    put("rmask_s", rm)
    sr = np.zeros((128, 4, 128), np.float32)
    for h in range(4):
        sr[4 + h, h, :] = 1.0
    put("selrow", sr)
    spr = np.zeros((128, 2, 128), np.float32)
    for pr in range(2):
        spr[2 * pr, pr, 0:64] = 1.0
        spr[2 * pr + 1, pr, 64:128] = 1.0
    put("selpair", spr)
    return c


def rope_tables(pos):
    inv = (np.float32(500000.0) ** (-np.arange(0, 16, 2, dtype=np.float32) / np.float32(16))).astype(np.float32)
    ang = pos.astype(np.float32)[None, :] * inv[:, None]
    t = np.zeros((64, 2, len(pos)), np.float32)
    t[:, 0, :] = 1.0
    t[0:8, 0, :] = np.cos(ang)
    t[8:16, 0, :] = np.cos(ang)
    t[0:8, 1, :] = np.sin(ang)
    t[8:16, 1, :] = np.sin(ang)
    return t


_CACHE = {}


def run(inp, n_cores, nsp, seq, nss, mixers=("rw", "s5", "gdn", "swa")):
    key = (n_cores, nsp, seq, nss, tuple(mixers))
    if key not in _CACHE:
        k = Kern(nsp, seq, nss, mixers)
        k.build()
        _CACHE[key] = k
    k = _CACHE[key]
    f = lambda a: np.ascontiguousarray(np.asarray(a, np.float32))
    pp = pack_pp(inp)
    cst = build_cst()
    ropeP = rope_tables(np.arange(seq))
    ropeS = rope_tables(PAST + np.arange(4))
    s5B = np.zeros((DEPTH, 128, 2, 8, 128), np.float32)
    s5C = np.zeros((DEPTH, 128, 2, 8, 128), np.float32)
    for ri, (bn, cn) in enumerate((("s5_b_re", "s5_c_re"), ("s5_b_im", "s5_c_im"))):
        bb = f(inp[bn])
        cc = f(inp[cn])
        for gi in range(16):
            j, r0, c0 = gi // 2, (gi % 8) * 16, (gi % 2) * 64
            s5B[:, r0:r0 + 16, ri, j, c0:c0 + 64] = bb[:, gi].transpose(0, 2, 1)
            s5C[:, c0:c0 + 64, ri, j, r0:r0 + 16] = cc[:, gi].transpose(0, 2, 1)
    s5B = s5B.reshape(DEPTH, 128, -1)
    s5C = s5C.reshape(DEPTH, 128, -1)
    s5lay = lambda a: np.ascontiguousarray(a.reshape(DEPTH, -1, 8, 2, 64).transpose(0, 3, 4, 2, 1).reshape(DEPTH, 128, 8, -1))
    s5inv = lambda a: a.reshape(DEPTH, 2, 64, 8, -1).transpose(0, 4, 3, 1, 2).reshape(DEPTH, -1, 16, 64)
    hlay = lambda a: np.ascontiguousarray(a.reshape(DEPTH, -1, 2, 2, 64, 64).transpose(0, 3, 5, 2, 1, 4).reshape(DEPTH, 128, 2, -1, 64))
    hinv = lambda a: a.reshape(DEPTH, 2, 64, 2, -1, 64).transpose(0, 4, 3, 1, 5, 2).reshape(DEPTH, -1, 4, 64, 64)
    in_maps = []
    for c in range(n_cores):
        xp = f(inp["x_prompt"][c * nsp:(c + 1) * nsp]).reshape(nsp * seq, D)
        xs = f(inp["x_sample"][c * nss:(c + 1) * nss]).reshape(nss * 4, D)
        in_maps.append({
            "xT_p": np.ascontiguousarray(xp.T), "xT_s": np.ascontiguousarray(xs.T),
            "w_in": f(inp["w_in"]), "w_out": f(inp["w_out"]), "w_up": f(inp["w_up"]), "w_down": f(inp["w_down"]),
            "rw_s": hlay(f(inp["state_rwkv"][:, c * nss:(c + 1) * nss])),
            "rwsh_s": np.ascontiguousarray(f(inp["state_rwkv_shift"][:, c * nss:(c + 1) * nss]).reshape(DEPTH, nss, 8, 128).transpose(0, 3, 2, 1)),
            "rw_w2": f(inp["rw_w2"]), "rw_a2": f(inp["rw_a2"]), "rw_g2": f(inp["rw_g2"]),
            "gdn_s": hlay(f(inp["state_gdn"][:, c * nss:(c + 1) * nss])),
            "conv_s": np.ascontiguousarray(f(inp["state_gdn_conv"][:, c * nss:(c + 1) * nss]).reshape(DEPTH, nss, 3, 6, 128).transpose(0, 4, 3, 1, 2)),
            "s5B": s5B, "s5C": s5C, "s5glu": f(inp["s5_w_glu"]),
            "s5s_re": s5lay(f(inp["state_s5_re"][:, c * nss:(c + 1) * nss])),
            "s5s_im": s5lay(f(inp["state_s5_im"][:, c * nss:(c + 1) * nss])),
            "pp": pp, "cst": cst, "ropeP": ropeP, "ropeS": ropeS,
            "kc_s": np.ascontiguousarray(f(inp["cache_swa_k"][:, c * nss:(c + 1) * nss]).transpose(0, 1, 3, 4, 2)),
            "vc_s": f(inp["cache_swa_v"][:, c * nss:(c + 1) * nss]).reshape(DEPTH, nss, 128, 128),
        })
    import os
    if os.environ.get("KTRACE"):
        res = run_bass_kernel_spmd(k.nc, in_maps, core_ids=list(range(n_cores)), trace=True)
        print("EXEC_TIME_NS", res.exec_time_ns)
    else:
        res = run_bass_kernel_spmd(k.nc, in_maps, core_ids=list(range(n_cores)))
    R = res.results
    cat = lambda fn: np.concatenate([fn(r) for r in R], axis=0)
    y_p = cat(lambda r: r["yT_p"].T.reshape(nsp, seq, D))
    y_s = cat(lambda r: r["yT_s"].T.reshape(nss, 4, D))
    catb = lambda fn: np.concatenate([fn(r) for r in R], axis=1)
    shift_p = catb(lambda r: r["o_shift_p"].transpose(0, 2, 1)[:, :, None, :])
    shift_s = catb(lambda r: r["o_shift_s"].transpose(0, 2, 1)[:, :, None, :])
    swak_p = catb(lambda r: r["o_swak_p"].transpose(0, 1, 4, 2, 3))
    swak_s = catb(lambda r: r["o_swak_s"].transpose(0, 1, 4, 2, 3))
    swav_p = catb(lambda r: r["o_swav_p"].reshape(DEPTH, nsp, 128, 2, 64))
    swav_s = catb(lambda r: r["o_swav_s"].reshape(DEPTH, nss, 128, 2, 64))
    s5o = {n: catb(lambda r: s5inv(r["o_" + n])) for n in ("s5re_p", "s5im_p", "s5re_s", "s5im_s")}
    rwo = dict(rw_p=catb(lambda r: hinv(r["o_rw_p"])), rw_s=catb(lambda r: hinv(r["o_rw_s"])))
    gdo = dict(**rwo, gdn_p=catb(lambda r: hinv(r["o_gdn_p"])), gdn_s=catb(lambda r: hinv(r["o_gdn_s"])),
               conv_p=catb(lambda r: r["o_conv_p"]), conv_s=catb(lambda r: r["o_conv_s"]))
    return dict(y_p=y_p, y_s=y_s, shift_p=shift_p, shift_s=shift_s, swak_p=swak_p, **s5o, **gdo, swak_s=swak_s,
                swav_p=swav_p, swav_s=swav_s)


OUT_NAMES = ["y_p", "y_s", "rw_p", "rw_s", "shift_p", "shift_s", "s5re_p", "s5re_s", "s5im_p", "s5im_s",
             "gdn_p", "gdn_s", "conv_p", "conv_s", "swak_p", "swak_s", "swav_p", "swav_s"]


def out_shapes(B, BS):
    L = DEPTH
    return [(B, 2048, D), (BS, 4, D), (L, B, 4, 64, 64), (L, BS, 4, 64, 64), (L, B, 1, 1024), (L, BS, 1, 1024),
            (L, B, 16, 64), (L, BS, 16, 64), (L, B, 16, 64), (L, BS, 16, 64), (L, B, 4, 64, 64), (L, BS, 4, 64, 64),
            (L, B, 3, 768), (L, BS, 3, 768), (L, B, 128, 2, 64), (L, BS, 128, 2, 64), (L, B, 128, 2, 64),
            (L, BS, 128, 2, 64)]


def kernel(**inp):
    o = run(inp, 8, 2, 2048, 16)
    outs = []
    for n, shp in zip(OUT_NAMES, out_shapes(16, 128)):
        if n in o:
            outs.append(np.ascontiguousarray(o[n], dtype=np.float32).reshape(shp))
        else:
            outs.append(np.zeros(shp, np.float32))
    return tuple(outs)
```

```python
import math
from contextlib import ExitStack
import numpy as np
import ml_dtypes
import concourse.bass as bass
import concourse.mybir as mybir
from concourse.bass_utils import run_bass_kernel_spmd

F32 = mybir.dt.float32
BF16 = mybir.dt.bfloat16
F32R = mybir.dt.float32r


def R_(ap):
    return ap.bitcast(F32R)
AF = mybir.ActivationFunctionType
ALU = mybir.AluOpType
AX = mybir.AxisListType

D = 1024
DEPTH = 2
HD = 64
PROJ = 2824
DFF = 4096
EPS = 1e-6
PAST = 16384
TSTEP = 256


class Tl:
    def __init__(self, h):
        self.h = h
        self.w = None
        self.r = {}

    def __getitem__(self, k):
        return self.h[k]


class _Rec:
    def __init__(self):
        self.call = None

    def __getattr__(self, name):
        def f(*a, **k):
            self.call = (name, a, k)
            return None
        return f


class Sched:
    ENGS = ("pe", "dve", "act", "pool", "sp")

    def __init__(self, nc, es):
        self.nc = nc
        self.es = es
        self.eng = {"pe": nc.tensor, "dve": nc.vector, "act": nc.scalar, "pool": nc.gpsimd, "sp": nc.sync}
        self.prog = []
        self.ninst = 0

    def op(self, e, fn, rd=(), wr=()):
        r = _Rec()
        fn(r)
        assert r.call is not None
        self.prog.append(("op", e, r.call, list(rd), list(wr)))

    def dma(self, q, out, in_, rd=(), wr=()):
        self.prog.append(("dma", q, (out, in_), list(rd), list(wr)))

    def finish(self, q="sp"):
        import os
        SERIAL = int(os.environ.get("KSERIAL", "0"))
        last_ps = None
        nc, es = self.nc, self.es
        prog = self.prog
        n = len(prog)
        deps = [None] * n
        needs = [False] * n
        local = [0] * n
        lcnt = {e: 0 for e in self.ENGS}
        def rowrng(call):
            name, a, k = call
            ap = k.get("lhsT") if name == "matmul" else (a[1] if len(a) > 1 else k.get("in_"))
            b = ap.base_partition()
            kk = ap.shape[0]
            sz = 32 if kk <= 32 else (64 if kk <= 64 else 128)
            b = (b // sz) * sz
            return (b, b + sz)

        def rows_disjoint(r1, r2):
            return r1[1] <= r2[0] or r2[1] <= r1[0]

        for i, (kind, e, call, rd, wr) in enumerate(prog):
            lcnt[e] += 1
            local[i] = lcnt[e]
            d = set()
            for t in rd:
                if t.w is not None:
                    d.add(t.w)
                if getattr(t, "psum", False):
                    for k_, j in t.r.items():
                        if k_ != e:
                            d.add(j)
            for t in wr:
                if t.w is not None:
                    d.add(t.w)
                for j in t.r.values():
                    d.add(j)
            keep = set()
            for j in d:
                kj, ej = prog[j][0], prog[j][1]
                if kind == "op" and kj == "op" and ej == e:
                    if e == "pe":
                        if rows_disjoint(rowrng(call), rowrng(prog[j][2])):
                            keep.add(j)
                        continue
                keep.add(j)
            if SERIAL == 1 and i > 0:
                keep.add(i - 1)
            if SERIAL == 2 and any(getattr(t, "psum", False) for t in list(rd) + list(wr)):
                if last_ps is not None and not (e == "pe" and prog[last_ps][1] == "pe"):
                    keep.add(last_ps)
                last_ps = i
            if SERIAL == 3 and e == "pool" and i > 0:
                keep.add(i - 1)
            if SERIAL == 3 and i > 0 and prog[i - 1][1] == "pool":
                keep.add(i - 1)
            deps[i] = keep
            for j in keep:
                needs[j] = True
            rk = e if kind == "op" else ("dma", i)
            for t in rd:
                t.r[rk] = i
            for t in wr:
                t.w = i
                t.r = {}
        semh, cnt, epoch = {}, {}, {}
        for e in ("pe", "dve", "act", "pool"):
            epoch[e], cnt[e] = 0, 0
            semh[(e, 0)] = es.enter_context(nc.semaphore("s_%s_0" % e))
        ndma = 24
        dsem = [es.enter_context(nc.semaphore("s_dma_%d" % i)) for i in range(ndma)]
        for i in range(ndma):
            semh[("dma", i)] = dsem[i]
        dcnt = [0] * ndma
        dnext = 0
        waited = {e: {} for e in self.ENGS}
        tokn = [None] * n

        def wait(e, need):
            for k, v in need.items():
                if waited[e].get(k, 0) < v:
                    self.eng[e].wait_ge(semh[k], v)
                    waited[e][k] = v
                    self.ninst += 1

        for i, (kind, e, call, rd, wr) in enumerate(prog):
            need = {}
            for j in deps[i]:
                k, v = tokn[j]
                if need.get(k, 0) < v:
                    need[k] = v
            if kind == "op":
                wait(e, need)
                ins = getattr(self.eng[e], call[0])(*call[1], **call[2])
                if needs[i]:
                    if cnt[e] >= 30000:
                        epoch[e] += 1
                        cnt[e] = 0
                        semh[(e, epoch[e])] = es.enter_context(nc.semaphore("s_%s_%d" % (e, epoch[e])))
                    cnt[e] += 1
                    key = (e, epoch[e])
                    ins.then_inc(semh[key], 1)
                    tokn[i] = (key, cnt[e])
            else:
                j = dnext
                dnext = (dnext + 1) % ndma
                if dcnt[j] > 0:
                    need[("dma", j)] = max(need.get(("dma", j), 0), dcnt[j])
                wait(e, need)
                ins = self.eng[e].dma_start(out=call[0], in_=call[1])
                dcnt[j] += 16
                ins.then_inc(dsem[j], 16)
                tokn[i] = (("dma", j), dcnt[j])
            self.ninst += 1
        need = {("dma", j): dcnt[j] for j in range(ndma) if dcnt[j] > 0}
        wait(q, need)
        self.ninc = sum(needs)


RW0, S50, GD0, SW0 = 0, 1024, 1280, 2312
WIN_TILES = [
    (0, 512, [(0, 128, 0), (128, 256, 1), (256, 384, 2), (384, 512, 3)]),
    (512, 1024, [(512, 640, 4), (640, 768, 5), (768, 896, 6), (896, 1024, 7)]),
    (1024, 1536, [(1024, 1152, 8), (1152, 1280, 9), (1280, 1408, 10), (1408, 1536, 11)]),
    (1536, 2048, [(1536, 1664, 12), (1664, 1792, 13), (1792, 1920, 14), (1920, 2048, 15)]),
    (2048, 2312, [(2048, 2056, 16), (2056, 2184, 17), (2184, 2312, 18)]),
    (2312, 2824, [(2312, 2440, 19), (2440, 2568, 20), (2568, 2696, 21), (2696, 2824, 22)]),
]
NPCH = 23


class Kern:
    def __init__(self, n_seq_p, seq_len, n_seq_s, mixers=("rw", "s5", "gdn", "swa"), dbg=False):
        self.NSP, self.SEQ, self.NSS = n_seq_p, seq_len, n_seq_s
        self.mixers = mixers
        self.nsteps = seq_len // TSTEP
        self.nc = bass.Bass("TRN2", target_bir_lowering=False)
        self.dbg = dbg

    def din(self, name, shape, dt=F32):
        return self.nc.dram_tensor(name, list(shape), dt, kind="ExternalInput").ap()

    def dout(self, name, shape, dt=F32):
        return self.nc.dram_tensor(name, list(shape), dt, kind="ExternalOutput").ap()

    def sb(self, name, shape, dt=F32):
        return Tl(self.es.enter_context(self.nc.sbuf_tensor(name, list(shape), dt)))

    def ps(self, name, shape, dt=F32):
        t = Tl(self.es.enter_context(self.nc.psum_tensor(name, list(shape), dt)))
        t.psum = True
        return t

    def build(self):
        nc = self.nc
        NSP, SEQ, NSS = self.NSP, self.SEQ, self.NSS
        NTP = NSP * SEQ
        NTS = NSS * 4
        self.xT_p = self.din("xT_p", [D, NTP])
        self.xT_s = self.din("xT_s", [D, NTS])
        self.w_in = self.din("w_in", [DEPTH, D, PROJ])
        self.w_out = self.din("w_out", [DEPTH, D, D])
        self.w_up = self.din("w_up", [DEPTH, D, DFF])
        self.w_down = self.din("w_down", [DEPTH, DFF, D])
        self.pp_d = self.din("pp", [DEPTH, 128, NPP])
        self.yT_p = self.dout("yT_p", [D, NTP])
        self.yT_s = self.dout("yT_s", [D, NTS])
        self.o_shift_p = self.dout("o_shift_p", [DEPTH, 1024, NSP])
        self.o_shift_s = self.dout("o_shift_s", [DEPTH, 1024, NSS])
        self.cst_d = self.din("cst", [128, NCST])
        self.rw_s = self.din("rw_s", [DEPTH, 128, 2, NSS, 64])
        self.rwsh_s = self.din("rwsh_s", [DEPTH, 128, 8, NSS])
        self.rw_w2 = self.din("rw_w2", [DEPTH, 64, 256])
        self.rw_a2 = self.din("rw_a2", [DEPTH, 64, 256])
        self.rw_g2 = self.din("rw_g2", [DEPTH, 128, 256])
        self.o_rw_p = self.dout("o_rw_p", [DEPTH, 128, 2, NSP, 64])
        self.o_rw_s = self.dout("o_rw_s", [DEPTH, 128, 2, NSS, 64])
        self.gdn_s = self.din("gdn_s", [DEPTH, 128, 2, NSS, 64])
        self.conv_s = self.din("conv_s", [DEPTH, 128, 6, NSS, 3])
        self.o_gdn_p = self.dout("o_gdn_p", [DEPTH, 128, 2, NSP, 64])
        self.o_gdn_s = self.dout("o_gdn_s", [DEPTH, 128, 2, NSS, 64])
        self.o_conv_p = self.dout("o_conv_p", [DEPTH, NSP, 3, 768])
        self.o_conv_s = self.dout("o_conv_s", [DEPTH, NSS, 3, 768])
        self.s5B = self.din("s5B", [DEPTH, 128, 2 * 8 * 128])
        self.s5C = self.din("s5C", [DEPTH, 128, 2 * 8 * 128])
        self.s5glu = self.din("s5glu", [DEPTH, 256, 256])
        self.s5s_re = self.din("s5s_re", [DEPTH, 128, 8, NSS])
        self.s5s_im = self.din("s5s_im", [DEPTH, 128, 8, NSS])
        self.o_s5re_p = self.dout("o_s5re_p", [DEPTH, 128, 8, NSP])
        self.o_s5im_p = self.dout("o_s5im_p", [DEPTH, 128, 8, NSP])
        self.o_s5re_s = self.dout("o_s5re_s", [DEPTH, 128, 8, NSS])
        self.o_s5im_s = self.dout("o_s5im_s", [DEPTH, 128, 8, NSS])
        self.ropeP = self.din("ropeP", [64, 2, SEQ])
        self.ropeS = self.din("ropeS", [64, 2, 4])
        self.kc_s = self.din("kc_s", [DEPTH, NSS, 2, 64, 128])
        self.vc_s = self.din("vc_s", [DEPTH, NSS, 128, 128])
        self.o_swak_p = self.dout("o_swak_p", [DEPTH, NSP, 2, 64, 128])
        self.o_swak_s = self.dout("o_swak_s", [DEPTH, NSS, 2, 64, 128])
        self.o_swav_p = self.dout("o_swav_p", [DEPTH, NSP, 128, 128])
        self.o_swav_s = self.dout("o_swav_s", [DEPTH, NSS, 128, 128])

        import os
        self.wcache = {}
        self.wring = 0
        self.wscr_off = 0
        self.wscr = None
        if os.environ.get("KNOWSCR", "0") != "1":
            nel = DEPTH * (D * PROJ + D * D + D * DFF + DFF * D) // 128
            self.wscr = self.nc.dram_tensor("wscr", [128, nel], BF16, kind="Internal").ap()
        with ExitStack() as es:
            self.es = es
            es.enter_context(nc.allow_non_contiguous_dma(reason="small strided state/param io"))
            self.S = Sched(nc, es)
            self.alloc()
            self.setup_consts()
            import os
            grp = os.environ.get("KGROUPS", "sp")
            if "s" in grp:
                self.run_group("s", 0)
            if "p" in grp:
                for st in range(self.nsteps):
                    self.run_group("p", st)
            self.S.finish("sp")
        return nc

    def alloc(self):
        NT = 512
        self.x = [self.sb("x%d" % c, [128, NT]) for c in range(8)]
        self.xn = [self.sb("xn%d" % c, [128, NT], BF16) for c in range(8)]
        self.p = [self.sb("p%d" % c, [128, NT]) for c in range(NPCH)]
        self.mix = self.xn
        self.tmp = [self.sb("tmp%d" % c, [128, NT]) for c in range(8)]
        self.stage = [self.sb("stage%d" % i, [128, 2048]) for i in range(2)]
        self.wbf = [self.sb("wbf%d" % i, [128, 2048], BF16) for i in range(2)]
        self.wi = 0
        self.sq = [self.sb("sq%d" % i, [128, NT], BF16) for i in range(2)]
        self.rstd = self.sb("rstd", [128, NT])
        self.pp = [self.sb("ppsb%d" % l, [128, NPP]) for l in range(DEPTH)]
        self.ones_bf = self.sb("ones_bf", [128, 128], BF16)
        self.ident = self.sb("ident", [128, 128])
        self.ps_lin = [self.ps("ps_lin%d" % i, [128, 512]) for i in range(2)]
        self.pli = 0
        self.ps_ss = self.ps("ps_ss", [128, 512])
        self.mp = [self.ps("mp%d" % i, [128, 512]) for i in range(5)]
        self.cst = self.sb("cst_sb", [128, NCST])
        self.NSC = 26
        self.sc = [self.sb("sc%d" % i, [128, 512]) for i in range(self.NSC)]
        self.esink = [self.sb("esink%d" % l, [128, 2]) for l in range(DEPTH)]
        self.nn_t = [self.sb("nn0", [64, 512])]
        self.nx_t = [self.sb("nx0", [64, 256])]
        self.nnp = [[self.sb("nnp%d_%d" % (pr, i), [64, 256]) for i in range(2)] for pr in range(2)]
        self.nxp = [[self.sb("nxp%d_%d" % (pr, i), [64, 128]) for i in range(2)] for pr in range(2)]
        self.s5k = [self.sb("s5k%d" % l, [128, 16 * 8]) for l in range(DEPTH)]
        self.s5h = [[self.sb("s5h%d_%d" % (l, i), [128, 8 * self.NSP]) for i in range(2)] for l in range(DEPTH)]
        self.s5w = self.sb("s5w", [128, 512])
        self.gH = [self.sb("gH%d" % l, [128, 2 * self.NSP * 64]) for l in range(DEPTH)]
        self.rH = [self.sb("rH%d" % l, [128, 2 * self.NSP * 64]) for l in range(DEPTH)]
        self.rsh = [self.sb("rsh%d" % l, [128, 8 * self.NSP]) for l in range(DEPTH)]
        self.gcb = [self.sb("gcb%d" % l, [128, 6 * self.NSP * 3]) for l in range(DEPTH)]
        self.gnea = [self.sb("gnea%d" % l, [128, 1]) for l in range(DEPTH)]
        self.s5i = self.sb("s5i", [128, 256], mybir.dt.int32)
        self.khist = [[self.sb("khist%d_%d" % (l, k), [64, self.NSP * 128]) for k in range(2)] for l in range(DEPTH)]
        self.vhist = [self.sb("vhist%d" % l, [128, self.NSP * 128]) for l in range(DEPTH)]

    def setup_consts(self):
        S = self.S
        S.op("pool", lambda e: e.memset(self.ones_bf[:], 1.0), wr=[self.ones_bf])
        S.op("pool", lambda e: e.memset(self.ident[:], 0.0), wr=[self.ident])
        S.op("pool", lambda e: e.affine_select(out=self.ident[:], in_=self.ident[:], pattern=[[-1, 128]], base=0,
                                               channel_multiplier=1, compare_op=ALU.not_equal, fill=1.0),
             rd=[self.ident], wr=[self.ident])
        S.dma("sp", self.cst[:], self.cst_d, wr=[self.cst])
        for l in range(DEPTH):
            S.dma("sp", self.pp[l][:], self.pp_d[l], wr=[self.pp[l]])
        for l in range(DEPTH):
            for i in range(2):
                S.op("pool", lambda e: e.memset(self.s5h[l][i][:], 0.0), wr=[self.s5h[l][i]])
            self.s5_setup(l)
            S.op("pool", lambda e: e.memset(self.gH[l][:], 0.0), wr=[self.gH[l]])
            S.op("pool", lambda e: e.memset(self.rH[l][:], 0.0), wr=[self.rH[l]])
            S.op("pool", lambda e: e.memset(self.rsh[l][:], 0.0), wr=[self.rsh[l]])
            S.op("pool", lambda e: e.memset(self.gcb[l][:], 0.0), wr=[self.gcb[l]])
            o_ = PPO["gdn_alog"]
            S.op("act", lambda e: e.activation(out=self.gnea[l][:], in_=self.pp[l][:, o_:o_ + 1], func=AF.Exp),
                 rd=[self.pp[l]], wr=[self.gnea[l]])
            S.op("dve", lambda e: e.tensor_scalar(self.gnea[l][:], self.gnea[l][:], -1.0, None, ALU.mult),
                 rd=[self.gnea[l]], wr=[self.gnea[l]])
        for l in range(DEPTH):
            o = PPO["sink"]
            S.op("act", lambda e: e.activation(out=self.esink[l][:], in_=self.pp[l][:, o:o + 2], func=AF.Exp),
                 rd=[self.pp[l]], wr=[self.esink[l]])

    def cs(self, name, p0=0, p1=128):
        o, n = CSO[name]
        return self.cst[p0:p1, o:o + n]

    def ppc(self, l, name, c=0):
        o = PPO[name] + c
        return self.pp[l][:, o:o + 1]

    def linear(self, w2d, kc, tiles, rhs, NT, consume, wkey=None):
        S = self.S
        groups = [g_ for (_, _, gs) in tiles for g_ in gs]

        def load(k0, k1, c0, c1):
            b = self.wi
            self.wi ^= 1
            nk, ncol = k1 - k0, c1 - c0
            st, wb = self.stage[b], self.wbf[b]
            ck = (wkey, k0, k1, c0, c1)
            if wkey is not None and ck in self.wcache:
                self.wi ^= 1
                r_ = self.wring
                self.wring = (self.wring + 1) % 4
                if r_ < 2:
                    wb = self.wbf[r_]
                    wap = wb[:, 0:nk * ncol]
                else:
                    wb = self.stage[r_ - 2]
                    wap = wb[:, :].bitcast(BF16)[:, 0:nk * ncol]
                off, dep = self.wcache[ck]
                S.dma("sp", wap, self.wscr[:, off:off + nk * ncol], rd=[dep], wr=[wb])
                return wb, wap.rearrange("p (k n) -> p k n", k=nk)
            S.dma("sp", st[:, 0:nk * ncol].rearrange("p (k n) -> p k n", k=nk),
                  w2d[k0 * 128:k1 * 128, c0:c1].rearrange("(k p) n -> p k n", p=128), wr=[st])
            S.op("act", lambda e: e.activation(out=wb[:, 0:nk * ncol], in_=st[:, 0:nk * ncol], func=AF.Copy), rd=[st], wr=[wb])
            if wkey is not None and self.wscr is not None:
                off = self.wscr_off
                self.wscr_off += nk * ncol
                dep = Tl(None)
                S.dma("act", self.wscr[:, off:off + nk * ncol], wb[:, 0:nk * ncol], rd=[wb], wr=[dep])
                self.wcache[ck] = (off, dep)
            return wb, wb[:, 0:nk * ncol].rearrange("p (k n) -> p k n", k=nk)

        loads = []
        if kc * 256 <= 2048:
            i = 0
            while i < len(groups):
                batch = [groups[i]]
                if i + 1 < len(groups) and groups[i + 1][0] == groups[i][1] and groups[i + 1][1] - groups[i][0] <= 2048 // kc:
                    batch.append(groups[i + 1])
                i += len(batch)
                loads.append((0, kc, batch[0][0], batch[-1][1], [(m0, m1, tag, True, True) for (m0, m1, tag) in batch]))
        else:
            kseg = 2048 // 128
            for (m0, m1, tag) in groups:
                for k0 in range(0, kc, kseg):
                    loads.append((k0, k0 + kseg, m0, m1, [(m0, m1, tag, k0 == 0, k0 + kseg == kc)]))
        cached = wkey is not None and all((wkey, l_[0], l_[1], l_[2], l_[3]) in self.wcache for l_ in loads)
        depth = 3 if cached else 1
        q = [load(*loads[j][0:4]) for j in range(min(depth, len(loads)))]
        pst = None
        for li, (k0, k1, c0, c1, grp) in enumerate(loads):
            cur = q.pop(0)
            if li + depth < len(loads):
                q.append(load(*loads[li + depth][0:4]))
            wb, wv = cur
            for (m0, m1, tag, first, lastk) in grp:
                if first:
                    pst = self.ps_lin[self.pli]
                    self.pli ^= 1
                m = m1 - m0
                for k in range(k0, k1):
                    r_ap, r_t = rhs(k)
                    S.op("pe", lambda e: e.matmul(pst[0:m, 0:NT], lhsT=wv[:, k - k0, m0 - c0:m1 - c0], rhs=r_ap,
                                                  start=(k == 0), stop=(k == kc - 1)), rd=[wb] + r_t, wr=[pst])
                if lastk:
                    consume(tag, pst, m)

    def norm_stats(self, src, nch, NT):
        S = self.S
        for c in range(nch):
            a, t = src(c)
            sq = self.sq[c % 2]
            if c % 2 == 0:
                S.op("act", lambda e: e.activation(out=sq[:, 0:NT], in_=a, func=AF.Square), rd=t, wr=[sq])
            else:
                S.op("dve", lambda e: e.tensor_tensor(out=sq[:, 0:NT], in0=a, in1=a, op=ALU.mult), rd=t, wr=[sq])
            S.op("pe", lambda e: e.matmul(self.ps_ss[:, 0:NT], lhsT=self.ones_bf[:], rhs=sq[:, 0:NT],
                                          start=(c == 0), stop=(c == nch - 1)), rd=[sq, self.ones_bf], wr=[self.ps_ss])
        S.op("act", lambda e: e.activation(out=self.rstd[:, 0:NT], in_=self.ps_ss[:, 0:NT], func=AF.Sqrt,
                                           bias=EPS, scale=1.0 / (nch * 128)), rd=[self.ps_ss], wr=[self.rstd])
        S.op("dve", lambda e: e.reciprocal(self.rstd[:, 0:NT], self.rstd[:, 0:NT]), rd=[self.rstd], wr=[self.rstd])

    def prenorm(self, l, gname, NT):
        S = self.S
        self.norm_stats(lambda c: (self.x[c][:, 0:NT], [self.x[c]]), 8, NT)
        for c in range(8):
            S.op("dve", lambda e: e.scalar_tensor_tensor(out=self.xn[c][:, 0:NT], in0=self.x[c][:, 0:NT],
                                                         scalar=self.ppc(l, gname, c), in1=self.rstd[:, 0:NT],
                                                         op0=ALU.mult, op1=ALU.mult),
                 rd=[self.x[c], self.rstd, self.pp[l]], wr=[self.xn[c]])

    def postnorm_residual(self, l, gname, NT):
        S = self.S
        self.norm_stats(lambda c: (self.tmp[c][:, 0:NT], [self.tmp[c]]), 8, NT)
        for c in range(8):
            S.op("dve", lambda e: e.scalar_tensor_tensor(out=self.tmp[c][:, 0:NT], in0=self.tmp[c][:, 0:NT],
                                                         scalar=self.ppc(l, gname, c), in1=self.rstd[:, 0:NT],
                                                         op0=ALU.mult, op1=ALU.mult),
                 rd=[self.tmp[c], self.rstd, self.pp[l]], wr=[self.tmp[c]])
            S.op("dve", lambda e: e.tensor_tensor(out=self.x[c][:, 0:NT], in0=self.x[c][:, 0:NT],
                                                   in1=self.tmp[c][:, 0:NT], op=ALU.add),
                 rd=[self.x[c], self.tmp[c]], wr=[self.x[c]])


    def head_rms(self, src, NT, scale, bias_eps, rs_t):
        S = self.S
        sq = self.sc[19]
        S.op("act", lambda e: e.activation(out=sq[:, 0:NT], in_=src[:, 0:NT], func=AF.Square), rd=[src], wr=[sq])
        ps = self.mp[4]
        S.op("pe", lambda e: e.matmul(ps[:, 0:NT], lhsT=self.cs("blk64"), rhs=sq[:, 0:NT], start=True, stop=True),
             rd=[self.cst, sq], wr=[ps])
        if bias_eps is None:
            S.op("dve", lambda e: e.tensor_scalar(rs_t[:, 0:NT], ps[:, 0:NT], 1e-12, None, ALU.max), rd=[ps], wr=[rs_t])
            S.op("act", lambda e: e.activation(out=rs_t[:, 0:NT], in_=rs_t[:, 0:NT], func=AF.Sqrt), rd=[rs_t], wr=[rs_t])
        else:
            S.op("act", lambda e: e.activation(out=rs_t[:, 0:NT], in_=ps[:, 0:NT], func=AF.Sqrt, bias=bias_eps, scale=scale),
                 rd=[ps], wr=[rs_t])
        S.op("dve", lambda e: e.reciprocal(rs_t[:, 0:NT], rs_t[:, 0:NT]), rd=[rs_t], wr=[rs_t])

    def neumann(self, NN0, X0, levels):
        S = self.S
        psq = [self.mp[2], self.ps_lin[0]]
        psx = [self.mp[3], self.ps_lin[1]]
        for k in range(levels):
            for pr in range(2):
                if k == 0:
                    nt, xt = NN0, X0
                    nv = R_(NN0[0:64, 0:512]).rearrange("p (h n) -> p h n", h=4)[:, pr * 2:pr * 2 + 2, :]
                    xin = X0[0:64, pr * 128:(pr + 1) * 128]
                else:
                    nt, xt = self.nnp[pr][(k - 1) % 2], self.nxp[pr][(k - 1) % 2]
                    nv = R_(nt[0:64, 0:256]).rearrange("p (h n) -> p h n", h=2)
                    xin = xt[0:64, 0:128]
                xo = self.nxp[pr][k % 2]
                px = psx[pr]
                for hh in range(2):
                    S.op("pe", lambda e: e.matmul(px[0:64, hh * 64:(hh + 1) * 64], lhsT=nv[:, hh, 0:64], rhs=R_(xin[:, hh * 64:(hh + 1) * 64]),
                                                  start=True, stop=True), rd=[nt, xt], wr=[px])
                S.op("dve", lambda e: e.tensor_tensor(out=R_(xo[0:64, 0:128]), in0=px[0:64, 0:128], in1=xin, op=ALU.add),
                     rd=[px, xt], wr=[xo])
                if k < levels - 1:
                    no = self.nnp[pr][k % 2]
                    pq = psq[pr]
                    for hh in range(2):
                        S.op("pe", lambda e: e.matmul(pq[0:64, hh * 128:hh * 128 + 64], lhsT=nv[:, hh, 64:128], rhs=nv[:, hh, 0:64],
                                                      start=True, stop=True), rd=[nt], wr=[pq])
                        S.op("pe", lambda e: e.matmul(pq[0:64, hh * 128 + 64:hh * 128 + 128], lhsT=nv[:, hh, 0:64], rhs=nv[:, hh, 64:128],
                                                      start=True, stop=True), rd=[nt], wr=[pq])
                    S.op("act", lambda e: e.activation(out=R_(no[0:64, 0:256]), in_=pq[0:64, 0:256], func=AF.Copy), rd=[pq], wr=[no])
        fin = (levels - 1) % 2

        def uh(h):
            t = self.nxp[h // 2][fin]
            return t[0:64, (h % 2) * 64:(h % 2) * 64 + 64], t
        return uh

    def colop(self, out_t, out3, in3, col4, op, rd):
        for h in range(4):
            self.S.op("dve", lambda e: e.tensor_scalar(out3[:, h, :], in3[:, h, :], col4[:, h:h + 1], None, op), rd=rd, wr=[out_t])

    def instances(self, g, nseq, T):
        if g == "s":
            return [(0, list(range(nseq)), 4, "s", 2)]
        return [(s_ * T + ch * 64, [s_], 64, "p", 6) for ch in range(T // 64) for s_ in range(nseq)]

    def pad_seq(self, dst, src_ap, rd):
        S = self.S
        eye = self.cs("eye16").rearrange("p (a b) -> p a b", a=16).unsqueeze(3).to_broadcast([128, 16, 16, 4])
        in0 = src_ap.rearrange("p (b i) -> p b i", i=4).unsqueeze(1).to_broadcast([128, 16, 16, 4])
        S.op("pool", lambda e: e.tensor_tensor(out=dst.rearrange("p (a b i) -> p a b i", a=16, b=16), in0=in0, in1=eye, op=ALU.mult),
             rd=rd + [self.cst], wr=[])


    def rwkv(self, g, st, l, NT, nseq, T, last):
        S = self.S
        samp = (g == "s")
        mp, sc, cst, tmp, pp, p = self.mp, self.sc, self.cst, self.tmp, self.pp[l], self.p
        sfx = "s" if samp else "p"
        v3 = lambda ap: ap.rearrange("p (s t) -> p s t", t=T)
        ppc = lambda n, c: pp[:, PPO[n] + c:PPO[n] + c + 1]
        if samp:
            sh = sc[24]
            S.dma("sp", sh[:, 0:8 * nseq].rearrange("p (c s) -> p c s", c=8), self.rwsh_s[l], wr=[sh])
            stH = self.stage[self.wi]
            self.wi ^= 1
            S.dma("sp", stH[:, 0:2 * nseq * 64].rearrange("p (a s v) -> p a s v", a=2, s=nseq), self.rw_s[l], wr=[stH])
            Ht = stH
        else:
            sh = self.rsh[l]
            Ht = self.rH[l]
        shv = sh[:, 0:8 * nseq].rearrange("p (c s) -> p c s", c=8)
        Hv = Ht[:, 0:2 * nseq * 64].rearrange("p (a s v) -> p a s v", a=2, s=nseq)
        wm = sc[25]
        for c in range(8):
            pv, tv = v3(p[c][:, 0:NT]), v3(tmp[c][:, 0:NT])
            if T > 1:
                S.op("dve", lambda e: e.tensor_tensor(out=tv[:, :, 1:T], in0=pv[:, :, 0:T - 1], in1=pv[:, :, 1:T], op=ALU.subtract),
                     rd=[p[c]], wr=[tmp[c]])
            S.op("dve", lambda e: e.tensor_tensor(out=tv[:, :, 0], in0=shv[:, c, :], in1=pv[:, :, 0], op=ALU.subtract),
                 rd=[p[c], sh], wr=[tmp[c]])
            S.op("dve", lambda e: e.scalar_tensor_tensor(out=tmp[c][:, 0:NT], in0=tmp[c][:, 0:NT], scalar=ppc("rw_mu", c), in1=p[c][:, 0:NT],
                                                         op0=ALU.mult, op1=ALU.add), rd=[tmp[c], pp, p[c]], wr=[tmp[c]])
            if not samp:
                S.op("dve", lambda e: e.tensor_copy(shv[:, c, :], pv[:, :, T - 1]), rd=[p[c]], wr=[sh])
        rT, kT, vT = tmp[0:2], tmp[2:4], tmp[4:6]
        S.op("act", lambda e: e.activation(out=tmp[6][0:64, 0:NT], in_=tmp[6][0:64, 0:NT], func=AF.Tanh), rd=[tmp[6]], wr=[tmp[6]])
        S.op("act", lambda e: e.activation(out=tmp[7][:, 0:NT], in_=tmp[7][:, 0:NT], func=AF.Sigmoid), rd=[tmp[7]], wr=[tmp[7]])
        lw, aa, KK, gate, Wi = sc[0:2], sc[2:4], sc[4:6], sc[6:8], sc[8:10]
        At, Rt, Bt, Kt = sc[10:12], sc[12:14], sc[14:16], sc[16:18]
        bonus, yall = sc[20:22], sc[22:24]
        rs = sc[25]
        wmv = wm[:, 0:384].rearrange("p (a n) -> p a n", a=3)
        for pr in range(2):
            cs_ = slice(pr * 128, (pr + 1) * 128)
            S.dma("sp", wmv[0:64, 0, :], self.rw_w2[l][:, cs_], wr=[wm])
            S.dma("sp", wmv[64:128, 1, :], self.rw_a2[l][:, cs_], wr=[wm])
            S.dma("sp", wmv[:, 2, :], self.rw_g2[l][:, cs_], wr=[wm])
            ps = mp[0]
            S.op("pe", lambda e: e.matmul(ps[:, 0:NT], lhsT=wmv[0:64, 0, :], rhs=tmp[6][0:64, 0:NT], start=True, stop=True),
                 rd=[wm, tmp[6]], wr=[ps])
            S.op("act", lambda e: e.activation(out=lw[pr][:, 0:NT], in_=ps[:, 0:NT], func=AF.Sigmoid, bias=ppc("rw_w0", pr)),
                 rd=[ps, pp], wr=[lw[pr]])
            S.op("dve", lambda e: e.tensor_scalar(lw[pr][:, 0:NT], lw[pr][:, 0:NT], -math.exp(-0.5), None, ALU.mult), rd=[lw[pr]], wr=[lw[pr]])
            ps = mp[1]
            S.op("pe", lambda e: e.matmul(ps[:, 0:NT], lhsT=wmv[64:128, 1, :], rhs=tmp[6][64:128, 0:NT], start=True, stop=True),
                 rd=[wm, tmp[6]], wr=[ps])
            S.op("act", lambda e: e.activation(out=aa[pr][:, 0:NT], in_=ps[:, 0:NT], func=AF.Sigmoid, bias=ppc("rw_a0", pr)),
                 rd=[ps, pp], wr=[aa[pr]])
            ps = mp[2]
            S.op("pe", lambda e: e.matmul(ps[:, 0:NT], lhsT=wmv[:, 2, :], rhs=tmp[7][:, 0:NT], start=True, stop=True),
                 rd=[wm, tmp[7]], wr=[ps])
            S.op("act", lambda e: e.activation(out=gate[pr][:, 0:NT], in_=ps[:, 0:NT], func=AF.Copy), rd=[ps], wr=[gate[pr]])
            S.op("dve", lambda e: e.tensor_scalar(KK[pr][:, 0:NT], kT[pr][:, 0:NT], ppc("rw_kk", pr), None, ALU.mult), rd=[kT[pr], pp], wr=[KK[pr]])
            self.head_rms(KK[pr], NT, None, None, rs)
            S.op("dve", lambda e: e.tensor_tensor(out=KK[pr][:, 0:NT], in0=KK[pr][:, 0:NT], in1=rs[:, 0:NT], op=ALU.mult), rd=[KK[pr], rs], wr=[KK[pr]])
            S.op("dve", lambda e: e.tensor_scalar(rs[:, 0:NT], aa[pr][:, 0:NT], -1.0, None, ALU.add), rd=[aa[pr]], wr=[rs])
            S.op("dve", lambda e: e.tensor_scalar(rs[:, 0:NT], rs[:, 0:NT], ppc("rw_ka", pr), 1.0, ALU.mult, ALU.add), rd=[rs, pp], wr=[rs])
            S.op("dve", lambda e: e.tensor_tensor(out=kT[pr][:, 0:NT], in0=kT[pr][:, 0:NT], in1=rs[:, 0:NT], op=ALU.mult), rd=[kT[pr], rs], wr=[kT[pr]])
            cum = rs
            S.op("dve", lambda e: e.tensor_tensor_scan(cum[:, 0:NT], self.cs("rmask_" + sfx)[:, 0:NT], lw[pr][:, 0:NT], 0.0, ALU.mult, ALU.add),
                 rd=[lw[pr], cst], wr=[cum])
            S.op("act", lambda e: e.activation(out=Wi[pr][:, 0:NT], in_=cum[:, 0:NT], func=AF.Exp), rd=[cum], wr=[Wi[pr]])
            S.op("dve", lambda e: e.tensor_tensor(out=lw[pr][:, 0:NT], in0=cum[:, 0:NT], in1=lw[pr][:, 0:NT], op=ALU.subtract), rd=[cum, lw[pr]], wr=[lw[pr]])
            S.op("act", lambda e: e.activation(out=lw[pr][:, 0:NT], in_=lw[pr][:, 0:NT], func=AF.Exp), rd=[lw[pr]], wr=[lw[pr]])
            S.op("act", lambda e: e.activation(out=cum[:, 0:NT], in_=cum[:, 0:NT], func=AF.Exp, scale=-1.0), rd=[cum], wr=[cum])
            S.op("dve", lambda e: e.scalar_tensor_tensor(out=At[pr][:, 0:NT], in0=KK[pr][:, 0:NT], scalar=-1.0, in1=lw[pr][:, 0:NT],
                                                         op0=ALU.mult, op1=ALU.mult), rd=[KK[pr], lw[pr]], wr=[At[pr]])
            S.op("dve", lambda e: e.tensor_tensor(out=Rt[pr][:, 0:NT], in0=rT[pr][:, 0:NT], in1=Wi[pr][:, 0:NT], op=ALU.mult), rd=[rT[pr], Wi[pr]], wr=[Rt[pr]])
            S.op("dve", lambda e: e.tensor_tensor(out=Bt[pr][:, 0:NT], in0=KK[pr][:, 0:NT], in1=aa[pr][:, 0:NT], op=ALU.mult), rd=[KK[pr], aa[pr]], wr=[Bt[pr]])
            S.op("dve", lambda e: e.tensor_tensor(out=Bt[pr][:, 0:NT], in0=Bt[pr][:, 0:NT], in1=cum[:, 0:NT], op=ALU.mult), rd=[Bt[pr], cum], wr=[Bt[pr]])
            S.op("dve", lambda e: e.tensor_tensor(out=Kt[pr][:, 0:NT], in0=kT[pr][:, 0:NT], in1=cum[:, 0:NT], op=ALU.mult), rd=[kT[pr], cum], wr=[Kt[pr]])
            S.op("dve", lambda e: e.scalar_tensor_tensor(out=bonus[pr][:, 0:NT], in0=rT[pr][:, 0:NT], scalar=ppc("rw_rk", pr), in1=kT[pr][:, 0:NT],
                                                         op0=ALU.mult, op1=ALU.mult), rd=[rT[pr], pp, kT[pr]], wr=[bonus[pr]])
            ps = mp[3]
            S.op("pe", lambda e: e.matmul(ps[:, 0:NT], lhsT=self.cs("blk64"), rhs=bonus[pr][:, 0:NT], start=True, stop=True),
                 rd=[cst, bonus[pr]], wr=[ps])
            S.op("dve", lambda e: e.tensor_tensor(out=bonus[pr][:, 0:NT], in0=ps[:, 0:NT], in1=vT[pr][:, 0:NT], op=ALU.mult), rd=[ps, vT[pr]], wr=[bonus[pr]])
        f3 = lambda t_: t_[0:64, 0:256].rearrange("p (h i) -> p h i", h=4)
        for (c0, seqs, C, ms, levels) in self.instances(g, nseq, T):
            cols = slice(c0, c0 + 64)
            ns = len(seqs)
            bm = lambda name: self.cs(name + "_" + ms, 0, 64).unsqueeze(1).to_broadcast([64, 4, 64])
            pA, pB, pC = mp[2], mp[3], mp[4]
            for h in range(4):
                pr, r0 = h // 2, (h % 2) * 64
                a_, r_, b_, k_ = (t_[pr][r0:r0 + 64, cols] for t_ in (At, Rt, Bt, Kt))
                rd_ = [At[pr], Rt[pr], Bt[pr], Kt[pr]]
                for (dst, o_, l_, rr_) in ((pA, h * 128, b_, a_), (pA, h * 128 + 64, a_, b_), (pB, h * 128, b_, r_),
                                           (pB, h * 128 + 64, k_, a_), (pC, h * 64, k_, r_)):
                    S.op("pe", lambda e: e.matmul(dst[0:64, o_:o_ + 64], lhsT=l_, rhs=rr_, start=True, stop=True), rd=rd_, wr=[dst])
            NN, AT2, AT3 = self.nn_t[0], sc[25], sc[0]
            v4 = lambda t_: t_[0:64, 0:512].rearrange("p (h n) -> p h n", h=4)
            S.op("dve", lambda e: e.tensor_tensor(out=R_(v4(NN)[:, :, 0:64]), in0=v4(pA)[:, :, 0:64], in1=bm("triU"), op=ALU.mult), rd=[pA, cst], wr=[NN])
            S.op("dve", lambda e: e.tensor_tensor(out=R_(v4(NN)[:, :, 64:128]), in0=v4(pA)[:, :, 64:128], in1=bm("triL"), op=ALU.mult), rd=[pA, cst], wr=[NN])
            S.op("dve", lambda e: e.tensor_tensor(out=v4(AT2)[:, :, 0:64], in0=v4(pB)[:, :, 0:64], in1=bm("incU"), op=ALU.mult), rd=[pB, cst], wr=[AT2])
            S.op("dve", lambda e: e.tensor_tensor(out=v4(AT2)[:, :, 64:128], in0=v4(pB)[:, :, 64:128], in1=bm("triU"), op=ALU.mult), rd=[pB, cst], wr=[AT2])
            S.op("dve", lambda e: e.tensor_tensor(out=f3(AT3), in0=f3(pC), in1=bm("incU"), op=ALU.mult), rd=[pC, cst], wr=[AT3])
            ATrb = lambda h: v4(AT2)[:, h, 0:64]
            ATak = lambda h: v4(AT2)[:, h, 64:128]
            ATrk = lambda h: f3(AT3)[:, h, :]
            pT0, pT1 = mp[0], mp[1]
            for pr in range(2):
                S.op("pe", lambda e: e.transpose(pT0[0:64, pr * 128:(pr + 1) * 128], vT[pr][:, cols], self.ident[:]), rd=[vT[pr], self.ident], wr=[pT0])
                S.op("pe", lambda e: e.transpose(pT0[0:64, 256 + pr * 128:256 + (pr + 1) * 128], Bt[pr][:, cols], self.ident[:]), rd=[Bt[pr], self.ident], wr=[pT0])
                S.op("pe", lambda e: e.transpose(pT1[0:64, pr * 128:(pr + 1) * 128], Kt[pr][:, cols], self.ident[:]), rd=[Kt[pr], self.ident], wr=[pT1])
            Vtok, Btok, Ktok = sc[1], sc[2], sc[3]
            S.op("act", lambda e: e.activation(out=Vtok[0:64, 0:256], in_=pT0[0:64, 0:256], func=AF.Copy), rd=[pT0], wr=[Vtok])
            S.op("act", lambda e: e.activation(out=Btok[0:64, 0:256], in_=pT0[0:64, 256:512], func=AF.Copy), rd=[pT0], wr=[Btok])
            S.op("act", lambda e: e.activation(out=Ktok[0:64, 0:256], in_=pT1[0:64, 0:256], func=AF.Copy), rd=[pT1], wr=[Ktok])
            if ns > 1:
                apad = [(p[0], p[1]), (p[2], p[3])]
                rpad = [(p[4], p[5]), (p[6], p[7])]
                padv = {}
                for pr in range(2):
                    for nm, src_t, tl in (("a", At[pr], apad[pr]), ("r", Rt[pr], rpad[pr])):
                        for half in range(2):
                            eye = self.cs("eye16").rearrange("p (a b) -> p a b", a=16)[:, half * 8:(half + 1) * 8, :].unsqueeze(3).to_broadcast([128, 8, 16, 4])
                            in0 = src_t[:, cols].rearrange("p (b i) -> p b i", i=4).unsqueeze(1).to_broadcast([128, 8, 16, 4])
                            S.op("pool", lambda e: e.tensor_tensor(out=tl[half][:, 0:512].rearrange("p (a b i) -> p a b i", a=8, b=16),
                                                                   in0=in0, in1=eye, op=ALU.mult), rd=[src_t, cst], wr=[tl[half]])
                        padv[(nm, pr)] = tl
                lk = lambda nm, pr, r0, si: (padv[(nm, pr)][si // 8][r0:r0 + 64, (si % 8) * 64:(si % 8) * 64 + 64], [padv[(nm, pr)][si // 8]])
            else:
                lk = lambda nm, pr, r0, si: ((At if nm == "a" else Rt)[pr][r0:r0 + 64, cols], [(At if nm == "a" else Rt)[pr]])
            pK = mp[1]
            for h in range(4):
                pr, r0 = h // 2, (h % 2) * 64
                for si, s_ in enumerate(seqs):
                    a_, t_ = lk("a", pr, r0, si)
                    S.op("pe", lambda e: e.matmul(pK[0:64, 256 + h * 64:256 + (h + 1) * 64], lhsT=a_, rhs=Hv[r0:r0 + 64, pr, s_, :],
                                                  start=(si == 0), stop=False), rd=t_ + [Ht], wr=[pK])
                S.op("pe", lambda e: e.matmul(pK[0:64, 256 + h * 64:256 + (h + 1) * 64], lhsT=ATak(h), rhs=Vtok[0:64, h * 64:(h + 1) * 64],
                                              start=False, stop=True), rd=[AT2, Vtok], wr=[pK])
            X = self.nx_t[0]
            S.op("act", lambda e: e.activation(out=R_(X[0:64, 0:256]), in_=pK[0:64, 256:512], func=AF.Copy), rd=[pK], wr=[X])
            uh = self.neumann(NN, X, levels)
            pY = mp[0]
            for h in range(4):
                pr, r0 = h // 2, (h % 2) * 64
                o_ = pY[r0:r0 + 64, pr * 64:(pr + 1) * 64]
                for si, s_ in enumerate(seqs):
                    a_, t_ = lk("r", pr, r0, si)
                    S.op("pe", lambda e: e.matmul(o_, lhsT=Hv[r0:r0 + 64, pr, s_, :], rhs=a_, start=(si == 0), stop=False), rd=t_ + [Ht], wr=[pY])
                S.op("pe", lambda e: e.matmul(o_, lhsT=uh(h)[0], rhs=ATrb(h), start=False, stop=False), rd=[uh(h)[1], AT2], wr=[pY])
                S.op("pe", lambda e: e.matmul(o_, lhsT=Vtok[0:64, h * 64:(h + 1) * 64], rhs=ATrk(h), start=False, stop=True), rd=[Vtok, AT3], wr=[pY])
            for pr in range(2):
                S.op("act", lambda e: e.activation(out=yall[pr][:, cols], in_=pY[:, pr * 64:(pr + 1) * 64], func=AF.Copy), rd=[pY], wr=[yall[pr]])
            for pr in range(2):
                pH = [mp[2], mp[3]]
                for hh in range(2):
                    h = pr * 2 + hh
                    r0 = hh * 64
                    if ns > 1:
                        Up = (tmp[0], tmp[1])
                        Vp = (tmp[2], tmp[3])
                        for half in range(2):
                            bs_ = self.cs("bsel", 0, 64)[:, half * 8:(half + 1) * 8].unsqueeze(2).to_broadcast([64, 8, 64])
                            for (dst_, src_ap, src_t) in ((Up, uh(h)[0], uh(h)[1]), (Vp, Vtok[0:64, h * 64:(h + 1) * 64], Vtok)):
                                S.op("pool", lambda e: e.tensor_tensor(
                                    out=dst_[half][0:64, 0:512].rearrange("p (s v) -> p s v", s=8),
                                    in0=src_ap.unsqueeze(1).to_broadcast([64, 8, 64]), in1=bs_, op=ALU.mult),
                                    rd=[src_t, cst], wr=[dst_[half]])
                            S.op("pe", lambda e: e.matmul(pH[half][r0:r0 + 64, 0:512], lhsT=Btok[0:64, h * 64:(h + 1) * 64], rhs=Up[half][0:64, 0:512],
                                                          start=True, stop=False), rd=[Btok, Up[half]], wr=[pH[half]])
                            S.op("pe", lambda e: e.matmul(pH[half][r0:r0 + 64, 0:512], lhsT=Ktok[0:64, h * 64:(h + 1) * 64], rhs=Vp[half][0:64, 0:512],
                                                          start=False, stop=True), rd=[Ktok, Vp[half]], wr=[pH[half]])
                    else:
                        S.op("pe", lambda e: e.matmul(pH[0][r0:r0 + 64, 0:64], lhsT=Btok[0:64, h * 64:(h + 1) * 64], rhs=uh(h)[0],
                                                      start=True, stop=False), rd=[Btok, uh(h)[1]], wr=[pH[0]])
                        S.op("pe", lambda e: e.matmul(pH[0][r0:r0 + 64, 0:64], lhsT=Ktok[0:64, h * 64:(h + 1) * 64], rhs=Vtok[0:64, h * 64:(h + 1) * 64],
                                                      start=False, stop=True), rd=[Ktok, Vtok], wr=[pH[0]])
                    wC = Wi[pr][r0:r0 + 64, cols].rearrange("p (s i) -> p s i", i=C)[:, :, C - 1]
                    if ns > 1:
                        for half in range(2):
                            hs2 = Hv[r0:r0 + 64, pr, half * 8:(half + 1) * 8, :]
                            S.op("dve", lambda e: e.tensor_tensor(out=hs2, in0=hs2, in1=pH[half][r0:r0 + 64, 0:512].rearrange("p (s v) -> p s v", s=8),
                                                                  op=ALU.add), rd=[Ht, pH[half]], wr=[Ht])
                        for si in range(ns):
                            hs1 = Hv[r0:r0 + 64, pr, si, :]
                            S.op("dve", lambda e: e.tensor_scalar(hs1, hs1, wC[:, si:si + 1], None, ALU.mult), rd=[Ht, Wi[pr]], wr=[Ht])
                    else:
                        hsl = Hv[r0:r0 + 64, pr, seqs[0], :]
                        S.op("dve", lambda e: e.tensor_tensor(out=hsl, in0=hsl, in1=pH[0][r0:r0 + 64, 0:64], op=ALU.add), rd=[Ht, pH[0]], wr=[Ht])
                        S.op("dve", lambda e: e.tensor_scalar(hsl, hsl, Wi[pr][r0:r0 + 64, c0 + 63:c0 + 64], None, ALU.mult), rd=[Ht, Wi[pr]], wr=[Ht])
        if last:
            og = self.o_rw_s if samp else self.o_rw_p
            S.dma("act", og[l], Hv, rd=[Ht])
        if "rw" not in self.mixers:
            return
        for pr in range(2):
            y = yall[pr]
            psM, psV = mp[0], mp[1]
            sq = sc[19]
            S.op("pe", lambda e: e.matmul(psM[:, 0:NT], lhsT=self.cs("blk64"), rhs=y[:, 0:NT], start=True, stop=True), rd=[cst, y], wr=[psM])
            S.op("act", lambda e: e.activation(out=sq[:, 0:NT], in_=y[:, 0:NT], func=AF.Square), rd=[y], wr=[sq])
            S.op("pe", lambda e: e.matmul(psV[:, 0:NT], lhsT=self.cs("blk64"), rhs=sq[:, 0:NT], start=True, stop=True), rd=[cst, sq], wr=[psV])
            mean, var = sc[0], sc[1]
            S.op("act", lambda e: e.activation(out=mean[:, 0:NT], in_=psM[:, 0:NT], func=AF.Copy, scale=1.0 / 64), rd=[psM], wr=[mean])
            S.op("dve", lambda e: e.tensor_tensor(out=var[:, 0:NT], in0=mean[:, 0:NT], in1=mean[:, 0:NT], op=ALU.mult), rd=[mean], wr=[var])
            S.op("dve", lambda e: e.scalar_tensor_tensor(out=var[:, 0:NT], in0=psV[:, 0:NT], scalar=1.0 / 64, in1=var[:, 0:NT],
                                                         op0=ALU.mult, op1=ALU.subtract), rd=[psV, var], wr=[var])
            S.op("act", lambda e: e.activation(out=var[:, 0:NT], in_=var[:, 0:NT], func=AF.Sqrt, bias=64e-5), rd=[var], wr=[var])
            S.op("dve", lambda e: e.reciprocal(var[:, 0:NT], var[:, 0:NT]), rd=[var], wr=[var])
            S.op("dve", lambda e: e.tensor_tensor(out=y[:, 0:NT], in0=y[:, 0:NT], in1=mean[:, 0:NT], op=ALU.subtract), rd=[y, mean], wr=[y])
            S.op("dve", lambda e: e.tensor_tensor(out=y[:, 0:NT], in0=y[:, 0:NT], in1=var[:, 0:NT], op=ALU.mult), rd=[y, var], wr=[y])
            S.op("dve", lambda e: e.tensor_scalar(y[:, 0:NT], y[:, 0:NT], ppc("rw_ln_w", pr), ppc("rw_ln_b", pr), ALU.mult, ALU.add), rd=[y, pp], wr=[y])
            S.op("dve", lambda e: e.tensor_tensor(out=y[:, 0:NT], in0=y[:, 0:NT], in1=bonus[pr][:, 0:NT], op=ALU.add), rd=[y, bonus[pr]], wr=[y])
            S.op("dve", lambda e: e.tensor_tensor(out=self.mix[pr][:, 0:NT], in0=y[:, 0:NT], in1=gate[pr][:, 0:NT], op=ALU.mult),
                 rd=[y, gate[pr]], wr=[self.mix[pr]])

    def gdn(self, g, st, l, NT, nseq, T, last):
        S = self.S
        samp = (g == "s")
        mp, sc, cst, tmp, pp = self.mp, self.sc, self.cst, self.tmp, self.pp[l]
        sfx = "s" if samp else "p"
        import os
        STOP = float(os.environ.get("KGDN_STOP", "99"))
        v3 = lambda ap: ap.rearrange("p (s t) -> p s t", t=T)
        if samp:
            cb = sc[17]
            S.dma("sp", cb[:, 0:6 * nseq * 3].rearrange("p (c s t) -> p c s t", c=6, s=nseq), self.conv_s[l], wr=[cb])
            stH = self.stage[self.wi]
            self.wi ^= 1
            S.dma("sp", stH[:, 0:2 * nseq * 64].rearrange("p (a s v) -> p a s v", a=2, s=nseq), self.gdn_s[l], wr=[stH])
            Ht = stH
        else:
            cb = self.gcb[l]
            Ht = self.gH[l]
        cbv = cb[:, 0:6 * nseq * 3].rearrange("p (c s t) -> p c s t", c=6, s=nseq)
        Hv = Ht[:, 0:2 * nseq * 64].rearrange("p (a s v) -> p a s v", a=2, s=nseq)
        for c in range(6):
            x = self.p[10 + c]
            xv = v3(x[:, 0:NT])
            acc = tmp[c]
            av = v3(acc[:, 0:NT])
            w = lambda i: pp[:, PPO["gdn_conv_w"] + i * 6 + c:PPO["gdn_conv_w"] + i * 6 + c + 1]
            S.op("dve", lambda e: e.tensor_scalar(acc[:, 0:NT], x[:, 0:NT], w(3), None, ALU.mult), rd=[x, pp], wr=[acc])
            for i in (1, 2, 3):
                S.op("dve", lambda e: e.scalar_tensor_tensor(out=av[:, :, i:T], in0=xv[:, :, 0:T - i], scalar=w(3 - i), in1=av[:, :, i:T],
                                                             op0=ALU.mult, op1=ALU.add), rd=[x, pp, acc], wr=[acc])
                S.op("dve", lambda e: e.scalar_tensor_tensor(out=av[:, :, 0:i], in0=cbv[:, c, :, 3 - i:3], scalar=w(3 - i), in1=av[:, :, 0:i],
                                                             op0=ALU.mult, op1=ALU.add), rd=[cb, pp, acc], wr=[acc])
            if last:
                oc_ = self.o_conv_s if samp else self.o_conv_p
                for t_ in range(3):
                    S.dma("act", oc_[l][:, t_, c * 128:(c + 1) * 128].rearrange("s p -> p s"), xv[:, :, T - 3 + t_], rd=[x])
            if not samp:
                S.op("dve", lambda e: e.tensor_copy(cbv[:, c, :, :], xv[:, :, T - 3:T]), rd=[x], wr=[cb])
            S.op("act", lambda e: e.activation(out=acc[:, 0:NT], in_=acc[:, 0:NT], func=AF.Silu), rd=[acc], wr=[acc])
        if STOP <= 1:
            return
        rs = sc[16]
        for c in range(4):
            self.head_rms(tmp[c], NT, None, None, rs)
            if c < 2:
                S.op("dve", lambda e: e.scalar_tensor_tensor(out=tmp[c][:, 0:NT], in0=tmp[c][:, 0:NT], scalar=0.125, in1=rs[:, 0:NT],
                                                             op0=ALU.mult, op1=ALU.mult), rd=[tmp[c], rs], wr=[tmp[c]])
            else:
                S.op("dve", lambda e: e.tensor_tensor(out=tmp[c][:, 0:NT], in0=tmp[c][:, 0:NT], in1=rs[:, 0:NT], op=ALU.mult),
                     rd=[tmp[c], rs], wr=[tmp[c]])
        qT, kT, vT = tmp[0:2], tmp[2:4], tmp[4:6]
        if STOP <= 2:
            return
        bg = sc[15]
        bg2 = sc[14]
        S.op("act", lambda e: e.activation(out=bg[0:8, 0:NT], in_=self.p[16][0:8, 0:NT], func=AF.Sigmoid), rd=[self.p[16]], wr=[bg])
        S.op("act", lambda e: e.activation(out=bg2[0:8, 0:NT], in_=self.p[16][0:8, 0:NT], func=AF.Exp,
                                           bias=pp[0:8, PPO["gdn_dtb"]:PPO["gdn_dtb"] + 1]), rd=[self.p[16], pp], wr=[bg2])
        S.op("act", lambda e: e.activation(out=bg2[0:8, 0:NT], in_=bg2[0:8, 0:NT], func=AF.Ln, bias=1.0), rd=[bg2], wr=[bg2])
        S.op("dve", lambda e: e.tensor_scalar(bg2[0:8, 0:NT], bg2[0:8, 0:NT], self.gnea[l][0:8, 0:1], None, ALU.mult),
             rd=[bg2, self.gnea[l]], wr=[bg2])
        S.op("dve", lambda e: e.tensor_tensor_scan(bg2[0:8, 0:NT], self.cs("rmask_" + sfx)[0:8, 0:NT], bg2[0:8, 0:NT], 0.0, ALU.mult, ALU.add),
             rd=[bg2, cst], wr=[bg2])
        oall = [sc[12], sc[13]]
        if STOP <= 3:
            return
        selrow = self.cs("selrow").rearrange("p (h n) -> p h n", h=4)
        selpair = self.cs("selpair").rearrange("p (a n) -> p a n", a=2)
        for (c0, seqs, C, ms, levels) in self.instances(g, nseq, T):
            cols = slice(c0, c0 + 64)
            ns = len(seqs)
            pt = mp[0]
            S.op("pe", lambda e: e.transpose(pt[0:64, 0:8], bg[0:8, cols], self.ident[0:8, 0:8]), rd=[bg, self.ident], wr=[pt])
            S.op("pe", lambda e: e.transpose(pt[0:64, 8:16], bg2[0:8, cols], self.ident[0:8, 0:8]), rd=[bg2, self.ident], wr=[pt])
            cT = sc[0]
            S.op("act", lambda e: e.activation(out=cT[0:64, 0:16], in_=pt[0:64, 0:16], func=AF.Copy), rd=[pt], wr=[cT])
            beta_c, gc_c = cT[0:64, 0:4], cT[0:64, 12:16]
            if STOP <= 4:
                continue
            pG = mp[1]
            for h in range(4):
                S.op("pe", lambda e: e.matmul(pG[:, h * 64:(h + 1) * 64], lhsT=selrow[0:8, h, :], rhs=bg2[0:8, cols], start=True, stop=True),
                     rd=[cst, bg2], wr=[pG])
            pGv = pG[0:64, 0:256].rearrange("p (h i) -> p h i", h=4)
            exG = sc[1]
            S.op("act", lambda e: e.activation(out=exG[:, 0:256], in_=pG[:, 0:256], func=AF.Exp), rd=[pG], wr=[exG])
            exGv = exG[:, 0:256].rearrange("p (h i) -> p h i", h=4)
            if STOP <= 4.2:
                continue
            E1, Da, Db = sc[2], sc[3], sc[4]
            gcB = gc_c.unsqueeze(2).to_broadcast([64, 4, 64])
            f3 = lambda t_: t_[0:64, 0:256].rearrange("p (h i) -> p h i", h=4)
            bm = lambda name: self.cs(name + "_" + ms, 0, 64).unsqueeze(1).to_broadcast([64, 4, 64])
            self.colop(E1, f3(E1), pGv, gc_c, ALU.subtract, [pG, cT])
            if STOP <= 4.4:
                continue
            S.op("dve", lambda e: e.tensor_scalar(Da[0:64, 0:256], E1[0:64, 0:256], 0.0, None, ALU.min), rd=[E1], wr=[Da])
            S.op("dve", lambda e: e.tensor_scalar(Db[0:64, 0:256], E1[0:64, 0:256], -1.0, 0.0, ALU.mult, ALU.min), rd=[E1], wr=[Db])
            if STOP <= 4.5:
                continue
            S.op("act", lambda e: e.activation(out=Da[0:64, 0:256], in_=Da[0:64, 0:256], func=AF.Exp), rd=[Da], wr=[Da])
            S.op("act", lambda e: e.activation(out=Db[0:64, 0:256], in_=Db[0:64, 0:256], func=AF.Exp), rd=[Db], wr=[Db])
            if STOP <= 4.6:
                continue
            S.op("dve", lambda e: e.tensor_tensor(out=f3(E1), in0=pGv, in1=self.cs("last_" + ms, 0, 64).unsqueeze(1).to_broadcast([64, 4, 64]),
                                                  op=ALU.mult), rd=[pG, cst], wr=[E1])
            S.op("dve", lambda e: e.tensor_reduce(out=cT[0:64, 16:20], in_=f3(E1), axis=AX.X, op=ALU.add), rd=[E1], wr=[cT])
            if STOP <= 4.8:
                continue
            S.op("dve", lambda e: e.tensor_tensor(out=cT[0:64, 20:24], in0=cT[0:64, 16:20], in1=gc_c, op=ALU.subtract), rd=[cT], wr=[cT])
            S.op("act", lambda e: e.activation(out=cT[0:64, 20:24], in_=cT[0:64, 20:24], func=AF.Exp), rd=[cT], wr=[cT])
            S.op("act", lambda e: e.activation(out=cT[0:64, 24:28], in_=gc_c, func=AF.Exp), rd=[cT], wr=[cT])
            S.op("dve", lambda e: e.scalar_tensor_tensor(out=cT[0:64, 28:32], in0=cT[0:64, 24:28], scalar=-1.0, in1=beta_c,
                                                         op0=ALU.mult, op1=ALU.mult), rd=[cT], wr=[cT])
            dec_c, nbg_c = cT[0:64, 20:24], cT[0:64, 28:32]
            if STOP <= 5:
                continue
            bkT = [sc[5], sc[6]]
            for pr in range(2):
                pb = mp[0]
                S.op("pe", lambda e: e.matmul(pb[:, 64:128], lhsT=selpair[0:8, pr, :], rhs=bg[0:8, cols], start=True, stop=True),
                     rd=[cst, bg], wr=[pb])
                S.op("dve", lambda e: e.tensor_tensor(out=bkT[pr][:, 0:64], in0=pb[:, 64:128], in1=kT[pr][:, cols], op=ALU.mult),
                     rd=[pb, kT[pr]], wr=[bkT[pr]])
            pA, pB = mp[2], mp[3]
            for h in range(4):
                pr, r0 = h // 2, (h % 2) * 64
                kh, bkh, qh = kT[pr][r0:r0 + 64, cols], bkT[pr][r0:r0 + 64, 0:64], qT[pr][r0:r0 + 64, cols]
                S.op("pe", lambda e: e.matmul(pA[0:64, h * 128:h * 128 + 64], lhsT=kh, rhs=bkh, start=True, stop=True),
                     rd=[kT[pr], bkT[pr]], wr=[pA])
                S.op("pe", lambda e: e.matmul(pA[0:64, h * 128 + 64:h * 128 + 128], lhsT=bkh, rhs=kh, start=True, stop=True),
                     rd=[kT[pr], bkT[pr]], wr=[pA])
                S.op("pe", lambda e: e.matmul(pB[0:64, h * 64:h * 64 + 64], lhsT=kh, rhs=qh, start=True, stop=True),
                     rd=[kT[pr], qT[pr]], wr=[pB])
            NN, AQ = self.nn_t[0], sc[8]
            pAv = pA[0:64, 0:512].rearrange("p (h n) -> p h n", h=4)
            nnv = NN[0:64, 0:512].rearrange("p (h n) -> p h n", h=4)
            S.op("dve", lambda e: e.tensor_tensor(out=f3(E1), in0=f3(Da), in1=bm("triU"), op=ALU.mult), rd=[Da, cst], wr=[E1])
            S.op("dve", lambda e: e.scalar_tensor_tensor(out=R_(nnv[:, :, 0:64]), in0=pAv[:, :, 0:64], scalar=-1.0, in1=f3(E1),
                                                         op0=ALU.mult, op1=ALU.mult), rd=[pA, E1], wr=[NN])
            S.op("dve", lambda e: e.tensor_tensor(out=f3(Db), in0=f3(Db), in1=bm("triL"), op=ALU.mult), rd=[Db, cst], wr=[Db])
            S.op("dve", lambda e: e.scalar_tensor_tensor(out=R_(nnv[:, :, 64:128]), in0=pAv[:, :, 64:128], scalar=-1.0, in1=f3(Db),
                                                         op0=ALU.mult, op1=ALU.mult), rd=[pA, Db], wr=[NN])
            S.op("dve", lambda e: e.tensor_tensor(out=f3(Da), in0=f3(Da), in1=bm("incU"), op=ALU.mult), rd=[Da, cst], wr=[Da])
            S.op("dve", lambda e: e.tensor_tensor(out=AQ[0:64, 0:256], in0=pB[0:64, 0:256], in1=Da[0:64, 0:256], op=ALU.mult),
                 rd=[pB, Da], wr=[AQ])
            if STOP <= 6:
                continue
            pT_ = mp[0]
            for pr in range(2):
                S.op("pe", lambda e: e.transpose(pT_[0:64, pr * 128:(pr + 1) * 128], vT[pr][:, cols], self.ident[:]),
                     rd=[vT[pr], self.ident], wr=[pT_])
                S.op("pe", lambda e: e.transpose(pT_[0:64, 256 + pr * 128:256 + (pr + 1) * 128], kT[pr][:, cols], self.ident[:]),
                     rd=[kT[pr], self.ident], wr=[pT_])
            Vb, Kd = sc[9], sc[10]
            self.colop(Vb, f3(Vb), pT_[0:64, 0:256].rearrange("p (h v) -> p h v", h=4), beta_c, ALU.mult, [pT_, cT])
            self.colop(Kd, f3(Kd), pT_[0:64, 256:512].rearrange("p (h v) -> p h v", h=4), dec_c, ALU.mult, [pT_, cT])
            if STOP <= 7:
                continue
            if ns > 1:
                kpad = [(sc[20], sc[21]), (sc[22], sc[23])]
                qpad = [(sc[24], sc[25]), (tmp[6], tmp[7])]
                padv = {}
                for pr in range(2):
                    for nm, src_t, tl in (("k", kT[pr], kpad[pr]), ("q", qT[pr], qpad[pr])):
                        for half in range(2):
                            eye = self.cs("eye16").rearrange("p (a b) -> p a b", a=16)[:, half * 8:(half + 1) * 8, :].unsqueeze(3).to_broadcast([128, 8, 16, 4])
                            in0 = src_t[:, cols].rearrange("p (b i) -> p b i", i=4).unsqueeze(1).to_broadcast([128, 8, 16, 4])
                            S.op("pool", lambda e: e.tensor_tensor(out=tl[half][:, 0:512].rearrange("p (a b i) -> p a b i", a=8, b=16),
                                                                   in0=in0, in1=eye, op=ALU.mult), rd=[src_t, cst], wr=[tl[half]])
                        padv[(nm, pr)] = tl
                lk = lambda nm, pr, r0, si: (padv[(nm, pr)][si // 8][r0:r0 + 64, (si % 8) * 64:(si % 8) * 64 + 64], [padv[(nm, pr)][si // 8]])
            else:
                lk = lambda nm, pr, r0, si: ((kT if nm == "k" else qT)[pr][r0:r0 + 64, cols], [(kT if nm == "k" else qT)[pr]])
            pK = mp[1]
            for h in range(4):
                pr, r0 = h // 2, (h % 2) * 64
                for si, s_ in enumerate(seqs):
                    a_, t_ = lk("k", pr, r0, si)
                    S.op("pe", lambda e: e.matmul(pK[0:64, h * 64:(h + 1) * 64], lhsT=a_, rhs=Hv[r0:r0 + 64, pr, s_, :],
                                                  start=(si == 0), stop=(si == ns - 1)), rd=t_ + [Ht], wr=[pK])
            X = self.nx_t[0]
            self.colop(E1, f3(E1), pK[0:64, 0:256].rearrange("p (h v) -> p h v", h=4), nbg_c, ALU.mult, [pK, cT])
            S.op("dve", lambda e: e.tensor_tensor(out=R_(X[0:64, 0:256]), in0=E1[0:64, 0:256], in1=Vb[0:64, 0:256], op=ALU.add),
                 rd=[E1, Vb], wr=[X])
            if STOP <= 8:
                continue
            uh = self.neumann(NN, X, levels)
            if STOP <= 9:
                continue
            pY1, pY2 = mp[0], mp[1]
            for h in range(4):
                pr, r0 = h // 2, (h % 2) * 64
                for si, s_ in enumerate(seqs):
                    a_, t_ = lk("q", pr, r0, si)
                    S.op("pe", lambda e: e.matmul(pY1[r0:r0 + 64, pr * 64:(pr + 1) * 64], lhsT=Hv[r0:r0 + 64, pr, s_, :], rhs=a_,
                                                  start=(si == 0), stop=(si == ns - 1)), rd=t_ + [Ht], wr=[pY1])
                S.op("pe", lambda e: e.matmul(pY2[r0:r0 + 64, pr * 64:(pr + 1) * 64], lhsT=uh(h)[0],
                                              rhs=AQ[0:64, h * 64:(h + 1) * 64], start=True, stop=True), rd=[uh(h)[1], AQ], wr=[pY2])
            for h in range(4):
                pr, r0 = h // 2, (h % 2) * 64
                S.op("dve", lambda e: e.tensor_tensor(out=oall[pr][r0:r0 + 64, cols], in0=pY1[r0:r0 + 64, pr * 64:(pr + 1) * 64],
                                                      in1=exGv[r0:r0 + 64, h, :], op=ALU.mult), rd=[pY1, exG], wr=[oall[pr]])
            for pr in range(2):
                S.op("dve", lambda e: e.tensor_tensor(out=oall[pr][:, cols], in0=pY2[:, pr * 64:(pr + 1) * 64], in1=oall[pr][:, cols], op=ALU.add),
                     rd=[pY2, oall[pr]], wr=[oall[pr]])
            if STOP <= 10:
                continue
            for pr in range(2):
                pH = [mp[2], mp[3]]
                for hh in range(2):
                    h = pr * 2 + hh
                    r0 = hh * 64
                    if ns > 1:
                        Up = (sc[5], sc[6]) if hh == 0 else (sc[9], sc[2])
                        for half in range(2):
                            S.op("pool", lambda e: e.tensor_tensor(
                                out=Up[half][0:64, 0:512].rearrange("p (s v) -> p s v", s=8),
                                in0=uh(h)[0].unsqueeze(1).to_broadcast([64, 8, 64]),
                                in1=self.cs("bsel", 0, 64)[:, half * 8:(half + 1) * 8].unsqueeze(2).to_broadcast([64, 8, 64]), op=ALU.mult),
                                rd=[uh(h)[1], cst], wr=[Up[half]])
                            S.op("pe", lambda e: e.matmul(pH[half][r0:r0 + 64, 0:512], lhsT=Kd[0:64, h * 64:(h + 1) * 64], rhs=Up[half][0:64, 0:512],
                                                          start=True, stop=True), rd=[Kd, Up[half]], wr=[pH[half]])
                    else:
                        S.op("pe", lambda e: e.matmul(pH[0][r0:r0 + 64, 0:64], lhsT=Kd[0:64, h * 64:(h + 1) * 64], rhs=uh(h)[0],
                                                      start=True, stop=True), rd=[Kd, uh(h)[1]], wr=[pH[0]])
                    gC = exGv[r0:r0 + 64, h, :].rearrange("p (s i) -> p s i", i=C)[:, :, C - 1]
                    if ns > 1:
                        for si in range(ns):
                            hs1 = Hv[r0:r0 + 64, pr, si, :]
                            S.op("dve", lambda e: e.tensor_scalar(hs1, hs1, gC[:, si:si + 1], None, ALU.mult), rd=[Ht, exG], wr=[Ht])
                        for half in range(2):
                            hs2 = Hv[r0:r0 + 64, pr, half * 8:(half + 1) * 8, :]
                            S.op("dve", lambda e: e.tensor_tensor(out=hs2, in0=hs2, in1=pH[half][r0:r0 + 64, 0:512].rearrange("p (s v) -> p s v", s=8),
                                                                  op=ALU.add), rd=[Ht, pH[half]], wr=[Ht])
                    else:
                        hsl = Hv[r0:r0 + 64, pr, seqs[0], :]
                        S.op("dve", lambda e: e.scalar_tensor_tensor(out=hsl, in0=hsl, scalar=exGv[r0:r0 + 64, h, 63:64], in1=pH[0][r0:r0 + 64, 0:64],
                                                                     op0=ALU.mult, op1=ALU.add), rd=[Ht, exG, pH[0]], wr=[Ht])
        if last:
            og = self.o_gdn_s if samp else self.o_gdn_p
            S.dma("act", og[l], Hv, rd=[Ht])
        if "gdn" not in self.mixers or STOP < 99:
            return
        for pr in range(2):
            self.head_rms(oall[pr], NT, 1.0 / 64, EPS, rs)
            S.op("dve", lambda e: e.scalar_tensor_tensor(out=oall[pr][:, 0:NT], in0=oall[pr][:, 0:NT], scalar=pp[:, PPO["gdn_nw"]:PPO["gdn_nw"] + 1],
                                                         in1=rs[:, 0:NT], op0=ALU.mult, op1=ALU.mult), rd=[oall[pr], pp, rs], wr=[oall[pr]])
            S.op("act", lambda e: e.activation(out=sc[0][:, 0:NT], in_=self.p[17 + pr][:, 0:NT], func=AF.Silu), rd=[self.p[17 + pr]], wr=[sc[0]])
            S.op("dve", lambda e: e.tensor_tensor(out=self.mix[4 + pr][:, 0:NT], in0=oall[pr][:, 0:NT], in1=sc[0][:, 0:NT], op=ALU.mult),
                 rd=[oall[pr], sc[0]], wr=[self.mix[4 + pr]])

    def s5_setup(self, l):
        S = self.S
        k = self.s5k[l]
        pp = self.pp[l]
        col = lambda i: k[:, i * 8:(i + 1) * 8]
        ppv = lambda n: pp[:, PPO[n]:PPO[n] + 8]
        D_, T1, MAG, TH, KF, SIN, COS, ABR, ABI, DEN, NR, CFR, CFI, NCFR, T2, T3 = [col(i) for i in range(16)]
        self.S5 = dict(MAG=2, TH=3, CFR=11, CFI=12, NCFR=13)
        ki = self.s5i[:, 0:8]
        r, w = [k, pp], [k]
        A = lambda eng, fn, rd=r, wr=w: S.op(eng, fn, rd=rd, wr=wr)
        A("act", lambda e: e.activation(out=D_, in_=ppv("s5_log_dt"), func=AF.Exp))
        A("dve", lambda e: e.tensor_tensor(out=T1, in0=D_, in1=ppv("s5_a_re"), op=ALU.mult))
        A("act", lambda e: e.activation(out=MAG, in_=T1, func=AF.Exp))
        A("dve", lambda e: e.tensor_tensor(out=TH, in0=D_, in1=ppv("s5_a_im"), op=ALU.mult))
        A("dve", lambda e: e.tensor_scalar(KF, TH, 1.0 / (2 * math.pi), None, ALU.mult))
        A("dve", lambda e: e.tensor_copy(ki, KF), rd=[k], wr=[self.s5i])
        A("dve", lambda e: e.tensor_copy(KF, ki), rd=[self.s5i], wr=[k])
        A("dve", lambda e: e.scalar_tensor_tensor(out=TH, in0=KF, scalar=-2 * math.pi, in1=TH, op0=ALU.mult, op1=ALU.add))
        A("dve", lambda e: e.tensor_scalar(TH, TH, -3.1415925, 3.1415925, ALU.max, ALU.min))
        A("act", lambda e: e.activation(out=SIN, in_=TH, func=AF.Sin))
        A("act", lambda e: e.activation(out=T2, in_=TH, func=AF.Abs))
        A("dve", lambda e: e.tensor_scalar(T2, T2, -1.0, math.pi / 2, ALU.mult, ALU.add))
        A("act", lambda e: e.activation(out=COS, in_=T2, func=AF.Sin))
        A("dve", lambda e: e.tensor_tensor(out=ABR, in0=MAG, in1=COS, op=ALU.mult))
        A("dve", lambda e: e.tensor_tensor(out=ABI, in0=MAG, in1=SIN, op=ALU.mult))
        A("dve", lambda e: e.tensor_tensor(out=DEN, in0=ppv("s5_a_re"), in1=ppv("s5_a_re"), op=ALU.mult))
        A("dve", lambda e: e.tensor_tensor(out=T2, in0=ppv("s5_a_im"), in1=ppv("s5_a_im"), op=ALU.mult))
        A("dve", lambda e: e.tensor_tensor(out=DEN, in0=DEN, in1=T2, op=ALU.add))
        A("dve", lambda e: e.reciprocal(DEN, DEN))
        A("dve", lambda e: e.tensor_scalar(NR, ABR, -1.0, None, ALU.add))
        A("dve", lambda e: e.tensor_tensor(out=T2, in0=NR, in1=ppv("s5_a_re"), op=ALU.mult))
        A("dve", lambda e: e.tensor_tensor(out=T3, in0=ABI, in1=ppv("s5_a_im"), op=ALU.mult))
        A("dve", lambda e: e.tensor_tensor(out=T2, in0=T2, in1=T3, op=ALU.add))
        A("dve", lambda e: e.tensor_tensor(out=CFR, in0=T2, in1=DEN, op=ALU.mult))
        A("dve", lambda e: e.tensor_tensor(out=T2, in0=ABI, in1=ppv("s5_a_re"), op=ALU.mult))
        A("dve", lambda e: e.tensor_tensor(out=T3, in0=NR, in1=ppv("s5_a_im"), op=ALU.mult))
        A("dve", lambda e: e.tensor_tensor(out=T2, in0=T2, in1=T3, op=ALU.subtract))
        A("dve", lambda e: e.tensor_tensor(out=CFI, in0=T2, in1=DEN, op=ALU.mult))
        A("dve", lambda e: e.tensor_scalar(NCFR, CFR, -1.0, None, ALU.mult))

    def s5(self, g, st, l, NT, nseq, T, last):
        S = self.S
        samp = (g == "s")
        mp, sc, cst = self.mp, self.sc, self.cst
        k = self.s5k[l]
        kc = lambda name, j: k[:, self.S5[name] * 8 + j:self.S5[name] * 8 + j + 1]
        v3 = lambda ap: ap.rearrange("p (s t) -> p s t", t=T)
        stgB, stgC = self.stage[0], self.stage[1]
        S.dma("sp", stgB[:, 0:2048], self.s5B[l], wr=[stgB])
        S.dma("sp", stgC[:, 0:2048], self.s5C[l], wr=[stgC])
        Bm = stgB[:, 0:2048].rearrange("p (a j n) -> p a j n", a=2, j=8)
        Cm = stgC[:, 0:2048].rearrange("p (a j n) -> p a j n", a=2, j=8)
        S.op("dve", lambda e: e.tensor_scalar(stgC[:, 1024:2048], stgC[:, 1024:2048], -1.0, None, ALU.mult), rd=[stgC], wr=[stgC])
        wg = sc[19]
        S.dma("sp", wg[:, 0:512].rearrange("p (k n) -> p k n", k=2), self.s5glu[l].rearrange("(k p) n -> p k n", p=128), wr=[wg])
        if samp:
            hre, him = sc[17], sc[18]
            S.dma("sp", hre[:, 0:8 * nseq].rearrange("p (j s) -> p j s", j=8), self.s5s_re[l], wr=[hre])
            S.dma("sp", him[:, 0:8 * nseq].rearrange("p (j s) -> p j s", j=8), self.s5s_im[l], wr=[him])
        else:
            hre, him = self.s5h[l]
        hv = lambda t: t[:, 0:8 * nseq].rearrange("p (j s) -> p j s", j=8)
        setA = [sc[0], sc[1], sc[2], sc[3], sc[4], sc[5], sc[6], sc[7], sc[8]]
        setB = [sc[12], sc[13], sc[14], sc[15], sc[16], sc[20], sc[21], sc[22], sc[23]]
        ei = self.s5i[:, 0:T]
        bc = lambda ap: ap.unsqueeze(1).to_broadcast([128, nseq, T])
        u = [self.p[8], self.p[9]]
        Y = [mp[2], mp[3]]
        zt = [sc[9], sc[10]]
        for j in range(8):
            ET, F, Z1, Z2, ZR, ZI, DK, HR, HI = setA if j % 2 == 0 else setB
            ER, EI = ET[:, 0:T], ET[:, 256:256 + T]
            S.op("act", lambda e: e.activation(out=Z1[:, 0:T], in_=self.cs("iota")[:, 0:T], func=AF.Copy, scale=kc("TH", j)),
                 rd=[cst, k], wr=[Z1])
            S.op("dve", lambda e: e.tensor_scalar(Z2[:, 0:T], Z1[:, 0:T], 1.0 / (2 * math.pi), None, ALU.mult), rd=[Z1], wr=[Z2])
            S.op("dve", lambda e: e.tensor_copy(ei, Z2[:, 0:T]), rd=[Z2], wr=[self.s5i])
            S.op("dve", lambda e: e.tensor_copy(Z2[:, 0:T], ei), rd=[self.s5i], wr=[Z2])
            S.op("dve", lambda e: e.scalar_tensor_tensor(out=Z1[:, 0:T], in0=Z2[:, 0:T], scalar=-2 * math.pi, in1=Z1[:, 0:T],
                                                          op0=ALU.mult, op1=ALU.add), rd=[Z1, Z2], wr=[Z1])
            S.op("dve", lambda e: e.tensor_scalar(Z1[:, 0:T], Z1[:, 0:T], -3.1415925, 3.1415925, ALU.max, ALU.min), rd=[Z1], wr=[Z1])
            S.op("act", lambda e: e.activation(out=EI, in_=Z1[:, 0:T], func=AF.Sin), rd=[Z1], wr=[ET])
            S.op("act", lambda e: e.activation(out=Z2[:, 0:T], in_=Z1[:, 0:T], func=AF.Abs), rd=[Z1], wr=[Z2])
            S.op("dve", lambda e: e.tensor_scalar(Z2[:, 0:T], Z2[:, 0:T], -1.0, math.pi / 2, ALU.mult, ALU.add), rd=[Z2], wr=[Z2])
            S.op("act", lambda e: e.activation(out=ER, in_=Z2[:, 0:T], func=AF.Sin), rd=[Z2], wr=[ET])
            FR, FI = F[:, 0:T], F[:, 256:256 + T]
            S.op("act", lambda e: e.activation(out=FR, in_=ER, func=AF.Copy, scale=kc("CFR", j)), rd=[ET, k], wr=[F])
            S.op("dve", lambda e: e.scalar_tensor_tensor(out=FR, in0=EI, scalar=kc("CFI", j), in1=FR, op0=ALU.mult, op1=ALU.add),
                 rd=[ET, k, F], wr=[F])
            S.op("act", lambda e: e.activation(out=FI, in_=ER, func=AF.Copy, scale=kc("CFI", j)), rd=[ET, k], wr=[F])
            S.op("dve", lambda e: e.scalar_tensor_tensor(out=FI, in0=EI, scalar=kc("NCFR", j), in1=FI, op0=ALU.mult, op1=ALU.add),
                 rd=[ET, k, F], wr=[F])
            pA, pB = (mp[0], mp[1]) if j % 2 == 0 else (mp[4], self.ps_lin[0])
            S.op("pe", lambda e: e.matmul(pA[:, 0:NT], lhsT=Bm[:, 0, j, :], rhs=u[j // 4][:, 0:NT], start=True, stop=True),
                 rd=[stgB, u[j // 4]], wr=[pA])
            S.op("pe", lambda e: e.matmul(pB[:, 0:NT], lhsT=Bm[:, 1, j, :], rhs=u[j // 4][:, 0:NT], start=True, stop=True),
                 rd=[stgB, u[j // 4]], wr=[pB])
            S.op("dve", lambda e: e.tensor_tensor(out=v3(Z1[:, 0:NT]), in0=v3(pA[:, 0:NT]), in1=bc(FR), op=ALU.mult), rd=[pA, F], wr=[Z1])
            S.op("dve", lambda e: e.tensor_tensor(out=v3(Z2[:, 0:NT]), in0=v3(pB[:, 0:NT]), in1=bc(FI), op=ALU.mult), rd=[pB, F], wr=[Z2])
            S.op("dve", lambda e: e.tensor_tensor(out=ZR[:, 0:NT], in0=Z1[:, 0:NT], in1=Z2[:, 0:NT], op=ALU.subtract), rd=[Z1, Z2], wr=[ZR])
            S.op("dve", lambda e: e.tensor_tensor(out=v3(Z1[:, 0:NT]), in0=v3(pB[:, 0:NT]), in1=bc(FR), op=ALU.mult), rd=[pB, F], wr=[Z1])
            S.op("dve", lambda e: e.tensor_tensor(out=v3(Z2[:, 0:NT]), in0=v3(pA[:, 0:NT]), in1=bc(FI), op=ALU.mult), rd=[pA, F], wr=[Z2])
            S.op("dve", lambda e: e.tensor_tensor(out=ZI[:, 0:NT], in0=Z1[:, 0:NT], in1=Z2[:, 0:NT], op=ALU.add), rd=[Z1, Z2], wr=[ZI])
            S.op("dve", lambda e: e.scalar_tensor_tensor(out=v3(ZR[:, 0:NT])[:, :, 0], in0=hv(hre)[:, j, :], scalar=kc("MAG", j),
                                                          in1=v3(ZR[:, 0:NT])[:, :, 0], op0=ALU.mult, op1=ALU.add),
                 rd=[hre, k, ZR], wr=[ZR])
            S.op("dve", lambda e: e.scalar_tensor_tensor(out=v3(ZI[:, 0:NT])[:, :, 0], in0=hv(him)[:, j, :], scalar=kc("MAG", j),
                                                          in1=v3(ZI[:, 0:NT])[:, :, 0], op0=ALU.mult, op1=ALU.add),
                 rd=[him, k, ZI], wr=[ZI])
            S.op("act", lambda e: e.activation(out=DK[:, 0:NT], in_=self.cs("rmask_p")[:, 0:NT], func=AF.Identity, scale=0.0, bias=kc("MAG", j)),
                 rd=[cst, k], wr=[DK])
            S.op("dve", lambda e: e.tensor_scalar(v3(DK[:, 0:NT])[:, :, 0], v3(DK[:, 0:NT])[:, :, 0], 0.0, None, ALU.mult), rd=[DK], wr=[DK])
            S.op("dve", lambda e: e.tensor_tensor_scan(Z1[:, 0:NT], DK[:, 0:NT], ZR[:, 0:NT], 0.0, ALU.mult, ALU.add), rd=[DK, ZR], wr=[Z1])
            S.op("dve", lambda e: e.tensor_tensor_scan(Z2[:, 0:NT], DK[:, 0:NT], ZI[:, 0:NT], 0.0, ALU.mult, ALU.add), rd=[DK, ZI], wr=[Z2])
            S.op("dve", lambda e: e.tensor_tensor(out=v3(ZR[:, 0:NT]), in0=v3(Z1[:, 0:NT]), in1=bc(ER), op=ALU.mult), rd=[Z1, ET], wr=[ZR])
            S.op("dve", lambda e: e.tensor_tensor(out=v3(ZI[:, 0:NT]), in0=v3(Z2[:, 0:NT]), in1=bc(EI), op=ALU.mult), rd=[Z2, ET], wr=[ZI])
            S.op("dve", lambda e: e.tensor_tensor(out=HR[:, 0:NT], in0=ZR[:, 0:NT], in1=ZI[:, 0:NT], op=ALU.subtract), rd=[ZR, ZI], wr=[HR])
            S.op("dve", lambda e: e.tensor_tensor(out=v3(ZR[:, 0:NT]), in0=v3(Z2[:, 0:NT]), in1=bc(ER), op=ALU.mult), rd=[Z2, ET], wr=[ZR])
            S.op("dve", lambda e: e.tensor_tensor(out=v3(ZI[:, 0:NT]), in0=v3(Z1[:, 0:NT]), in1=bc(EI), op=ALU.mult), rd=[Z1, ET], wr=[ZI])
            S.op("dve", lambda e: e.tensor_tensor(out=HI[:, 0:NT], in0=ZR[:, 0:NT], in1=ZI[:, 0:NT], op=ALU.add), rd=[ZR, ZI], wr=[HI])
            S.op("dve", lambda e: e.tensor_copy(hv(hre)[:, j, :], v3(HR[:, 0:NT])[:, :, T - 1]), rd=[HR], wr=[hre])
            S.op("dve", lambda e: e.tensor_copy(hv(him)[:, j, :], v3(HI[:, 0:NT])[:, :, T - 1]), rd=[HI], wr=[him])
            yy = Y[j // 4]
            S.op("pe", lambda e: e.matmul(yy[:, 0:NT], lhsT=Cm[:, 0, j, :], rhs=HR[:, 0:NT], start=(j % 4 == 0), stop=False),
                 rd=[stgC, HR], wr=[yy])
            S.op("pe", lambda e: e.matmul(yy[:, 0:NT], lhsT=Cm[:, 1, j, :], rhs=HI[:, 0:NT], start=False, stop=(j % 4 == 3)),
                 rd=[stgC, HI], wr=[yy])
        if last:
            ore, oim = (self.o_s5re_s, self.o_s5im_s) if samp else (self.o_s5re_p, self.o_s5im_p)
            S.dma("act", ore[l], hv(hre), rd=[hre])
            S.dma("act", oim[l], hv(him), rd=[him])
        if "s5" not in self.mixers:
            return
        for oc in range(2):
            S.op("dve", lambda e: e.scalar_tensor_tensor(out=sc[11][:, 0:NT], in0=u[oc][:, 0:NT], scalar=self.ppc(l, "s5_d", oc),
                                                         in1=Y[oc][:, 0:NT], op0=ALU.mult, op1=ALU.add),
                 rd=[u[oc], self.pp[l], Y[oc]], wr=[sc[11]])
            S.op("act", lambda e: e.activation(out=zt[oc][:, 0:NT], in_=sc[11][:, 0:NT], func=AF.Gelu_apprx_tanh), rd=[sc[11]], wr=[zt[oc]])
        wgv = wg[:, 0:512].rearrange("p (k n) -> p k n", k=2)
        for oc in range(2):
            pg = mp[oc]
            for kk_ in range(2):
                S.op("pe", lambda e: e.matmul(pg[:, 0:NT], lhsT=wgv[:, kk_, oc * 128:(oc + 1) * 128], rhs=zt[kk_][:, 0:NT],
                                              start=(kk_ == 0), stop=(kk_ == 1)), rd=[wg, zt[kk_]], wr=[pg])
            S.op("act", lambda e: e.activation(out=sc[11][:, 0:NT], in_=pg[:, 0:NT], func=AF.Sigmoid, bias=self.ppc(l, "s5_b_glu", oc)),
                 rd=[pg, self.pp[l]], wr=[sc[11]])
            S.op("dve", lambda e: e.tensor_tensor(out=self.mix[2 + oc][:, 0:NT], in0=zt[oc][:, 0:NT], in1=sc[11][:, 0:NT], op=ALU.mult),
                 rd=[zt[oc], sc[11]], wr=[self.mix[2 + oc]])

    def swa(self, g, st, l, NT, nseq, T, last):
        S = self.S
        samp = (g == "s")
        first = (not samp) and st == 0
        mp, sc, cst = self.mp, self.sc, self.cst
        do_mix = "swa" in self.mixers
        rt = sc[0]
        rsrc = self.ropeS if samp else self.ropeP[:, :, st * T:(st + 1) * T]
        S.dma("sp", rt[0:64, 0:2 * T].rearrange("p (a t) -> p a t", a=2), rsrc, wr=[rt])
        rtv = rt[0:64, 0:2 * T].rearrange("p (a t) -> p a t", a=2)
        cosb = rtv[:, 0, :].unsqueeze(1).to_broadcast([64, nseq, T])
        sinb = rtv[:, 1, :].unsqueeze(1).to_broadcast([64, nseq, T])
        selm = self.cs("selm").rearrange("p (a m) -> p a m", a=4)
        v3 = lambda ap: ap.rearrange("p (s t) -> p s t", t=T)
        ones64 = self.cs("ones")[:, 0:64]

        def rot(src, gsel, dst_ap, dst_t):
            pa, pb = mp[0], mp[1]
            S.op("pe", lambda e: e.matmul(pa[0:64, 0:NT], lhsT=selm[:, gsel, :], rhs=src[:, 0:NT], start=True, stop=True),
                 rd=[src, cst], wr=[pa])
            S.op("pe", lambda e: e.matmul(pb[0:64, 0:NT], lhsT=selm[:, 2 + gsel, :], rhs=src[:, 0:NT], start=True, stop=True),
                 rd=[src, cst], wr=[pb])
            S.op("dve", lambda e: e.tensor_tensor(out=v3(sc[1][0:64, 0:NT]), in0=v3(pa[0:64, 0:NT]), in1=cosb, op=ALU.mult),
                 rd=[pa, rt], wr=[sc[1]])
            S.op("dve", lambda e: e.tensor_tensor(out=v3(sc[2][0:64, 0:NT]), in0=v3(pb[0:64, 0:NT]), in1=sinb, op=ALU.mult),
                 rd=[pb, rt], wr=[sc[2]])
            S.op("dve", lambda e: e.tensor_tensor(out=dst_ap, in0=sc[1][0:64, 0:NT], in1=sc[2][0:64, 0:NT], op=ALU.add),
                 rd=[sc[1], sc[2]], wr=dst_t)

        vtok = [sc[3], sc[4]]
        if samp:
            stg = self.stage[0]
            vhis = stg[:, 0:2048].rearrange("p (s f) -> p s f", f=128)
            S.dma("sp", vhis, self.vc_s[l].rearrange("s j f -> j s f"), wr=[stg])
            S.op("pe", lambda e: e.transpose(mp[0][0:NT, 0:128], self.p[22][:, 0:NT], self.ident[:]),
                 rd=[self.p[22], self.ident], wr=[mp[0]])
            S.op("act", lambda e: e.activation(out=sc[3][0:NT, 0:128], in_=mp[0][0:NT, 0:128], func=AF.Copy),
                 rd=[mp[0]], wr=[sc[3]])
            if last:
                S.dma("act", self.o_swav_s[l, :, 0:124, :], self.vc_s[l, :, 4:128, :])
                for s_ in range(nseq):
                    S.dma("act", self.o_swav_s[l, s_, 124:128, :], sc[3][s_ * 4:s_ * 4 + 4, 0:128], rd=[sc[3]])
        else:
            def vblk(s_, b_):
                i = s_ * 3 + b_
                return vtok[i // 4][:, (i % 4) * 128:(i % 4 + 1) * 128], vtok[i // 4]
            for s_ in range(nseq):
                if not first:
                    a, t = vblk(s_, 0)
                    S.op("dve", lambda e: e.tensor_copy(a, self.vhist[l][:, s_ * 128:(s_ + 1) * 128]),
                         rd=[self.vhist[l]], wr=[t])
                for b_ in range(T // 128):
                    a, t = vblk(s_, 1 + b_)
                    c0 = s_ * T + b_ * 128
                    S.op("pe", lambda e: e.transpose(mp[0][:, 0:128], self.p[22][:, c0:c0 + 128], self.ident[:]),
                         rd=[self.p[22], self.ident], wr=[mp[0]])
                    S.op("act", lambda e: e.activation(out=a, in_=mp[0][:, 0:128], func=AF.Copy), rd=[mp[0]], wr=[t])
                a, t = vblk(s_, T // 128)
                S.op("dve", lambda e: e.tensor_copy(self.vhist[l][:, s_ * 128:(s_ + 1) * 128], a),
                     rd=[t], wr=[self.vhist[l]])
                if last:
                    S.dma("act", self.o_swav_p[l, s_, :, :], a, rd=[t])

        for kvh in range(2):
            qrot = [sc[5], sc[6]]
            knew = sc[7]
            for gg in range(2):
                rot(self.p[19 + kvh], gg, qrot[gg][0:64, 0:NT], [qrot[gg]])
            rot(self.p[21], kvh, knew[0:64, 0:NT], [knew])
            num_t = sc[10]
            if samp:
                stg_k = self.stage[1]
                khis = stg_k[0:64, 0:2048].rearrange("p (s j) -> p s j", j=128)
                S.dma("sp", khis, self.kc_s[l, :, kvh, :, :].rearrange("s d j -> d s j"), wr=[stg_k])
                if last:
                    S.dma("act", self.o_swak_s[l, :, kvh, :, 0:124], self.kc_s[l, :, kvh, :, 4:128])
                    S.dma("act", self.o_swak_s[l, :, kvh, :, 124:128].rearrange("s d t -> d s t"),
                          v3(knew[0:64, 0:NT]), rd=[knew])
                if not do_mix:
                    continue
                psS = mp[2]
                for s_ in range(nseq):
                    for gg in range(2):
                        S.op("pe", lambda e: e.matmul(psS[:, s_ * 8 + gg * 4:s_ * 8 + gg * 4 + 4], lhsT=khis[:, s_, :],
                                                      rhs=qrot[gg][0:64, s_ * 4:s_ * 4 + 4], start=True, stop=True),
                             rd=[stg_k, qrot[gg]], wr=[psS])
                for gg in range(2):
                    S.op("pe", lambda e: e.matmul(psS[0:64, 128 + gg * 64:128 + gg * 64 + 64], lhsT=knew[0:64, 0:NT],
                                                  rhs=qrot[gg][0:64, 0:NT], start=True, stop=True),
                         rd=[knew, qrot[gg]], wr=[psS])
                pT = sc[8]
                S.op("act", lambda e: e.activation(out=pT[:, 0:128], in_=psS[:, 0:128], func=AF.Exp, scale=0.125),
                     rd=[psS], wr=[pT])
                S.op("act", lambda e: e.activation(out=pT[0:64, 128:256], in_=psS[0:64, 128:256], func=AF.Exp, scale=0.125),
                     rd=[psS], wr=[pT])
                S.op("dve", lambda e: e.tensor_tensor(out=pT[:, 0:128], in0=pT[:, 0:128], in1=self.cs("mask_sh"), op=ALU.mult),
                     rd=[pT, cst], wr=[pT])
                S.op("dve", lambda e: e.tensor_tensor(out=pT[0:64, 128:256], in0=pT[0:64, 128:256], in1=self.cs("mask_sn", 0, 64),
                                                      op=ALU.mult), rd=[pT, cst], wr=[pT])
                psA = mp[3]
                for s_ in range(nseq):
                    for gg in range(2):
                        r_ = pT[:, s_ * 8 + gg * 4:s_ * 8 + gg * 4 + 4]
                        S.op("pe", lambda e: e.matmul(psA[gg * 64:gg * 64 + 64, s_ * 4:s_ * 4 + 4],
                                                      lhsT=vhis[:, s_, kvh * 64:kvh * 64 + 64], rhs=r_, start=True, stop=True),
                             rd=[stg, pT], wr=[psA])
                        S.op("pe", lambda e: e.matmul(psA[gg * 64:gg * 64 + 64, 64 + s_ * 4:64 + s_ * 4 + 4],
                                                      lhsT=ones64, rhs=r_, start=True, stop=True),
                             rd=[cst, pT], wr=[psA])
                for gg in range(2):
                    r_ = pT[0:64, 128 + gg * 64:128 + gg * 64 + 64]
                    S.op("pe", lambda e: e.matmul(psA[gg * 64:gg * 64 + 64, 128:192], lhsT=sc[3][0:64, kvh * 64:kvh * 64 + 64],
                                                  rhs=r_, start=True, stop=True), rd=[sc[3], pT], wr=[psA])
                    S.op("pe", lambda e: e.matmul(psA[gg * 64:gg * 64 + 64, 192:256], lhsT=ones64[0:64, :], rhs=r_,
                                                  start=True, stop=True), rd=[cst, pT], wr=[psA])
                S.op("act", lambda e: e.activation(out=sc[9][:, 0:128], in_=psA[:, 128:256], func=AF.Copy), rd=[psA], wr=[sc[9]])
                S.op("dve", lambda e: e.tensor_tensor(out=num_t[:, 0:128], in0=psA[:, 0:128], in1=sc[9][:, 0:128], op=ALU.add),
                     rd=[psA, sc[9]], wr=[num_t])
                num_ap, den_ap, nd_t = num_t[:, 0:64], num_t[:, 64:128], [num_t]
            else:
                kall = [sc[11], sc[12]]
                for s_ in range(nseq):
                    if not first:
                        S.op("dve", lambda e: e.tensor_copy(kall[s_][0:64, 0:128], self.khist[l][kvh][:, s_ * 128:(s_ + 1) * 128]),
                             rd=[self.khist[l][kvh]], wr=[kall[s_]])
                    S.op("dve", lambda e: e.tensor_copy(kall[s_][0:64, 128:128 + T], knew[0:64, s_ * T:(s_ + 1) * T]),
                         rd=[knew], wr=[kall[s_]])
                    S.op("dve", lambda e: e.tensor_copy(self.khist[l][kvh][:, s_ * 128:(s_ + 1) * 128], kall[s_][0:64, T:T + 128]),
                         rd=[kall[s_]], wr=[self.khist[l][kvh]])
                    if last:
                        S.dma("act", self.o_swak_p[l, s_, kvh, :, :], kall[s_][0:64, T:T + 128], rd=[kall[s_]])
                if not do_mix:
                    continue
                psN, psD = mp[3], mp[4]
                mask_p = self.cs("mask_p")
                for s_ in range(nseq):
                    for b_ in range(T // 128):
                        tok0 = s_ * T + b_ * 128
                        kts = []
                        if not (first and b_ == 0):
                            kts.append((0, b_ * 128, vblk(s_, b_)))
                        kts.append((1, (b_ + 1) * 128, vblk(s_, b_ + 1)))
                        psS = mp[2]
                        for (mi, k0, _) in kts:
                            for gg in range(2):
                                S.op("pe", lambda e: e.matmul(psS[:, mi * 256 + gg * 128:mi * 256 + gg * 128 + 128],
                                                              lhsT=kall[s_][0:64, k0:k0 + 128], rhs=qrot[gg][0:64, tok0:tok0 + 128],
                                                              start=True, stop=True), rd=[kall[s_], qrot[gg]], wr=[psS])
                        c0 = kts[0][0] * 256
                        pT = sc[8 + (b_ % 2)]
                        S.op("act", lambda e: e.activation(out=pT[:, c0:512], in_=psS[:, c0:512], func=AF.Exp, scale=0.125),
                             rd=[psS], wr=[pT])
                        nm_ = (512 - c0) // 256
                        pTv = pT[:, c0:512].rearrange("p (m g i) -> p m g i", g=2, i=128)
                        mkv = mask_p[:, c0 // 2:256].rearrange("p (m i) -> p m i", i=128).unsqueeze(2).to_broadcast([128, nm_, 2, 128])
                        S.op("dve", lambda e: e.tensor_tensor(out=pTv, in0=pTv, in1=mkv, op=ALU.mult), rd=[pT, cst], wr=[pT])
                        for gg in range(2):
                            for ki, (mi, k0, (va, vt)) in enumerate(kts):
                                r_ = pT[:, mi * 256 + gg * 128:mi * 256 + gg * 128 + 128]
                                S.op("pe", lambda e: e.matmul(psN[gg * 64:gg * 64 + 64, tok0:tok0 + 128], lhsT=va[:, kvh * 64:kvh * 64 + 64],
                                                              rhs=r_, start=(ki == 0), stop=(ki == len(kts) - 1)), rd=[vt, pT], wr=[psN])
                            for ki, (mi, k0, (va, vt)) in enumerate(kts):
                                r_ = pT[:, mi * 256 + gg * 128:mi * 256 + gg * 128 + 128]
                                S.op("pe", lambda e: e.matmul(psD[gg * 64:gg * 64 + 64, tok0:tok0 + 128], lhsT=ones64, rhs=r_,
                                                              start=(ki == 0), stop=(ki == len(kts) - 1)), rd=[cst, pT], wr=[psD])
                num_ap, den_ap, nd_t = psN[:, 0:NT], psD[:, 0:NT], [psN, psD]
            dt_ = sc[13]
            S.op("dve", lambda e: e.tensor_scalar(dt_[:, 0:NT], den_ap, self.esink[l][:, kvh:kvh + 1], None, ALU.add),
                 rd=nd_t + [self.esink[l]], wr=[dt_])
            S.op("dve", lambda e: e.reciprocal(dt_[:, 0:NT], dt_[:, 0:NT]), rd=[dt_], wr=[dt_])
            S.op("dve", lambda e: e.tensor_tensor(out=self.mix[6 + kvh][:, 0:NT], in0=num_ap, in1=dt_[:, 0:NT], op=ALU.mult),
                 rd=nd_t + [dt_], wr=[self.mix[6 + kvh]])

    def run_group(self, g, st):
        S = self.S
        if g == "s":
            NT = self.NSS * 4
            src = self.xT_s
            cols = [(0, NT, 0)]
            dst = self.yT_s
        else:
            NT = self.NSP * TSTEP
            src = self.xT_p
            dst = self.yT_p
            cols = [(q * self.SEQ + st * TSTEP, TSTEP, q * TSTEP) for q in range(self.NSP)]
        for c in range(8):
            for (d0, n, s0) in cols:
                S.dma("sp", self.x[c][:, s0:s0 + n], src[c * 128:(c + 1) * 128, d0:d0 + n], wr=[self.x[c]])
        for l in range(DEPTH):
            self.layer(g, st, l, NT)
        for c in range(8):
            for (d0, n, s0) in cols:
                S.dma("act", dst[c * 128:(c + 1) * 128, d0:d0 + n], self.x[c][:, s0:s0 + n], rd=[self.x[c]])

    def layer(self, g, st, l, NT):
        S = self.S
        nseq = self.NSS if g == "s" else self.NSP
        T = 4 if g == "s" else TSTEP
        last = (g == "s") or (st == self.nsteps - 1)
        self.prenorm(l, "g_mix_pre", NT)

        def cons_p(tag, pst, m):
            S.op("act", lambda e: e.activation(out=self.p[tag][0:m, 0:NT], in_=pst[0:m, 0:NT], func=AF.Copy),
                 rd=[pst], wr=[self.p[tag]])

        self.linear(self.w_in[l], 8, WIN_TILES, lambda k: (self.xn[k][:, 0:NT], [self.xn[k]]), NT, cons_p, wkey=(l, "in"))
        if last:
            o = self.o_shift_s if g == "s" else self.o_shift_p
            for c in range(8):
                src = self.p[c][:, 0:NT].rearrange("p (s t) -> p s t", t=T)[:, :, T - 1]
                S.dma("act", o[l, c * 128:(c + 1) * 128, :], src, rd=[self.p[c]])
        for nm, cs_ in (("rw", (0, 1)), ("s5", (2, 3)), ("gdn", (4, 5)), ("swa", (6, 7))):
            if nm not in self.mixers:
                for c in cs_:
                    S.op("pool", lambda e: e.memset(self.mix[c][:, 0:NT], 0.0), wr=[self.mix[c]])
        import os
        skip = os.environ.get("KSKIP", "").split(",")
        if "rw" not in skip:
            self.rwkv(g, st, l, NT, nseq, T, last)
        if "gdn" not in skip:
            self.gdn(g, st, l, NT, nseq, T, last)
        if "s5" not in skip:
            self.s5(g, st, l, NT, nseq, T, last)
        if "swa" not in skip:
            self.swa(g, st, l, NT, nseq, T, last)
        def cons_t(tag, pst, m):
            S.op("act", lambda e: e.activation(out=self.tmp[tag][:, 0:NT], in_=pst[:, 0:NT], func=AF.Copy),
                 rd=[pst], wr=[self.tmp[tag]])

        t_out = [(0, 512, [(i * 128, (i + 1) * 128, i) for i in range(4)]),
                 (512, 1024, [(i * 128, (i + 1) * 128, i) for i in range(4, 8)])]
        self.linear(self.w_out[l], 8, t_out, lambda k: (self.mix[k][:, 0:NT], [self.mix[k]]), NT, cons_t, wkey=(l, "out"))
        self.postnorm_residual(l, "g_mix_post", NT)
        self.prenorm(l, "g_mlp_pre", NT)
        hb = lambda i: self.p[i // 2][:, :].bitcast(BF16)[:, (i % 2) * 512:(i % 2) * 512 + NT]

        def cons_h(tag, pst, m):
            S.op("act", lambda e: e.activation(out=self.sq[0][:, 0:NT], in_=pst[:, 0:NT], func=AF.Relu),
                 rd=[pst], wr=[self.sq[0]])
            S.op("dve", lambda e: e.tensor_tensor(out=hb(tag), in0=self.sq[0][:, 0:NT], in1=self.sq[0][:, 0:NT],
                                                  op=ALU.mult), rd=[self.sq[0]], wr=[self.p[tag // 2]])

        t_up = [(j * 512, (j + 1) * 512, [(j * 512 + i * 128, j * 512 + (i + 1) * 128, j * 4 + i) for i in range(4)])
                for j in range(8)]
        self.linear(self.w_up[l], 8, t_up, lambda k: (self.xn[k][:, 0:NT], [self.xn[k]]), NT, cons_h, wkey=(l, "up"))
        t_dn = [(i * 128, (i + 1) * 128, [(i * 128, (i + 1) * 128, i)]) for i in range(8)]
        self.linear(self.w_down[l], 32, t_dn, lambda k: (hb(k), [self.p[k // 2]]), NT, cons_t, wkey=(l, "down"))
        self.postnorm_residual(l, "g_mlp_post", NT)


PPO = {}
NPP = 0


def _ppdef(name, n):
    global NPP
    PPO[name] = NPP
    NPP += n


for _n in ("g_mix_pre", "g_mix_post", "g_mlp_pre", "g_mlp_post", "rw_mu"):
    _ppdef(_n, 8)
_ppdef("sink", 2)
for _n in ("s5_a_re", "s5_a_im", "s5_log_dt"):
    _ppdef(_n, 8)
for _n in ("rw_w0", "rw_a0", "rw_kk", "rw_ka", "rw_rk", "rw_ln_w", "rw_ln_b"):
    _ppdef(_n, 2)
_ppdef("gdn_conv_w", 24)
_ppdef("gdn_nw", 1)
_ppdef("gdn_dtb", 1)
_ppdef("gdn_alog", 1)
_ppdef("s5_d", 2)
_ppdef("s5_b_glu", 2)


def colmajor(v):
    v = np.asarray(v, np.float32).reshape(-1, 128)
    return np.ascontiguousarray(v.T)


def pack_pp(inp):
    pp = np.zeros((DEPTH, 128, NPP), np.float32)
    for l in range(DEPTH):
        for n in ("g_mix_pre", "g_mix_post", "g_mlp_pre", "g_mlp_post", "rw_mu"):
            pp[l, :, PPO[n]:PPO[n] + 8] = colmajor(inp[n][l])
        g2 = lambda a: np.asarray(a, np.float32).reshape(8, 2, 64).transpose(1, 2, 0).reshape(128, 8)
        pp[l, :, PPO["s5_a_re"]:PPO["s5_a_re"] + 8] = g2(inp["s5_a_re"][l])
        pp[l, :, PPO["s5_a_im"]:PPO["s5_a_im"] + 8] = g2(inp["s5_a_im"][l])
        pp[l, :, PPO["s5_log_dt"]:PPO["s5_log_dt"] + 8] = g2(np.repeat(np.asarray(inp["s5_log_dt"][l])[:, None], 64, 1))
        pp[l, :, PPO["s5_d"]:PPO["s5_d"] + 2] = colmajor(inp["s5_d"][l])
        pp[l, :, PPO["s5_b_glu"]:PPO["s5_b_glu"] + 2] = colmajor(inp["s5_b_glu"][l])
        for n in ("rw_w0", "rw_a0", "rw_kk", "rw_ka", "rw_rk", "rw_ln_w", "rw_ln_b"):
            pp[l, :, PPO[n]:PPO[n] + 2] = colmajor(np.asarray(inp[n][l], np.float32).reshape(-1))
        cw = np.asarray(inp["gdn_conv_w"][l], np.float32)
        for i_ in range(4):
            pp[l, :, PPO["gdn_conv_w"] + i_ * 6:PPO["gdn_conv_w"] + i_ * 6 + 6] = colmajor(cw[i_])
        pp[l, :, PPO["gdn_nw"]] = np.tile(np.asarray(inp["gdn_norm_w"][l], np.float32), 2)
        pp[l, 4:8, PPO["gdn_dtb"]] = np.asarray(inp["gdn_dt_bias"][l], np.float32)
        pp[l, 4:8, PPO["gdn_alog"]] = np.asarray(inp["gdn_a_log"][l], np.float32)
        sk = np.asarray(inp["swa_sinks"][l], np.float32)
        pp[l, :, PPO["sink"]:PPO["sink"] + 2] = np.repeat(sk.reshape(2, 2, 1), 64, axis=2).reshape(2, 128).T
    return pp


CSO = {}
NCST = 0


def _cdef(name, n):
    global NCST
    CSO[name] = (NCST, n)
    NCST += n


for _n, _k in (("selm", 256), ("blk64", 128), ("mask_p", 256), ("mask_sh", 128), ("mask_sn", 128), ("ones", 128), ("iota", 256),
               ("triL_p", 64), ("triU_p", 64), ("incU_p", 64), ("triL_s", 64), ("triU_s", 64), ("incU_s", 64),
               ("last_p", 64), ("last_s", 64), ("eye16", 256), ("bsel", 16), ("rmask_p", 512), ("rmask_s", 64),
               ("selrow", 512), ("selpair", 256)):
    _cdef(_n, _k)


def build_cst():
    c = np.zeros((128, NCST), np.float32)

    def put(name, arr):
        o, n = CSO[name]
        arr = np.asarray(arr, np.float32).reshape(arr.shape[0], -1)
        assert arr.shape[1] == n, (name, arr.shape, n)
        c[:arr.shape[0], o:o + n] = arr

    selm = np.zeros((128, 4, 64), np.float32)
    for g in range(2):
        for m in range(64):
            selm[g * 64 + m, g, m] = 1.0
            if m < 8:
                selm[g * 64 + m + 8, 2 + g, m] = -1.0
            elif m < 16:
                selm[g * 64 + m - 8, 2 + g, m] = 1.0
    put("selm", selm)
    blk = np.zeros((128, 128), np.float32)
    blk[:64, :64] = 1
    blk[64:, 64:] = 1
    put("blk64", blk)
    j = np.arange(128)[:, None]
    i = np.arange(128)[None, :]
    mp = np.zeros((128, 2, 128), np.float32)
    mp[:, 0, :] = (j > i)
    mp[:, 1, :] = (j <= i)
    put("mask_p", mp)
    msh = np.zeros((128, 16, 2, 4), np.float32)
    msh[:] = (np.arange(128)[:, None, None, None] > np.arange(4)[None, None, None, :])
    put("mask_sh", msh)
    msn = np.zeros((64, 2, 16, 4), np.float32)
    for sp in range(16):
        for jp in range(4):
            for ii in range(4):
                if jp <= ii:
                    msn[sp * 4 + jp, :, sp, ii] = 1.0
    put("mask_sn", msn)
    put("ones", np.ones((128, 128), np.float32))
    put("iota", np.tile(np.arange(1, 257, dtype=np.float32)[None, :], (128, 1)))
    a64 = np.arange(64)
    for sfx, C in (("p", 64), ("s", 4)):
        same = (a64[:, None] // C) == (a64[None, :] // C)
        put("triL_" + sfx, (same & (a64[None, :] < a64[:, None])).astype(np.float32))
        put("triU_" + sfx, (same & (a64[:, None] < a64[None, :])).astype(np.float32))
        put("incU_" + sfx, (same & (a64[:, None] <= a64[None, :])).astype(np.float32))
        lastm = np.zeros((128, 64), np.float32)
        lastm[:, :] = ((a64[None, :] % C) == C - 1)
        lastm[:64] *= same.astype(np.float32)
        lastm[64:] *= same.astype(np.float32)
        put("last_" + sfx, lastm)
    put("eye16", np.tile(np.eye(16, dtype=np.float32).reshape(1, 256), (128, 1)))
    bs = np.zeros((128, 16), np.float32)
    for pp_ in range(64):
        bs[pp_, pp_ // 4] = 1.0
    put("bsel", bs)
    rm = np.ones((128, 512), np.float32)
    rm[:, ::64] = 0.0
    put("rmask_p", rm)
    rm = np.ones((128, 64), np.float32)
    rm[:, ::4] = 0.0
    put("rmask_s", rm)
    sr = np.zeros((128, 4, 128), np.float32)
    for h in range(4):
        sr[4 + h, h, :] = 1.0
    put("selrow", sr)
    spr = np.zeros((128, 2, 128), np.float32)
    for pr in range(2):
        spr[2 * pr, pr, 0:64] = 1.0
        spr[2 * pr + 1, pr, 64:128] = 1.0
    put("selpair", spr)
    return c


def rope_tables(pos):
    inv = (np.float32(500000.0) ** (-np.arange(0, 16, 2, dtype=np.float32) / np.float32(16))).astype(np.float32)
    ang = pos.astype(np.float32)[None, :] * inv[:, None]
    t = np.zeros((64, 2, len(pos)), np.float32)
    t[:, 0, :] = 1.0
    t[0:8, 0, :] = np.cos(ang)
    t[8:16, 0, :] = np.cos(ang)
    t[0:8, 1, :] = np.sin(ang)
    t[8:16, 1, :] = np.sin(ang)
    return t


_CACHE = {}


def run(inp, n_cores, nsp, seq, nss, mixers=("rw", "s5", "gdn", "swa")):
    key = (n_cores, nsp, seq, nss, tuple(mixers))
    if key not in _CACHE:
        k = Kern(nsp, seq, nss, mixers)
        k.build()
        _CACHE[key] = k
    k = _CACHE[key]
    f = lambda a: np.ascontiguousarray(np.asarray(a, np.float32))
    pp = pack_pp(inp)
    cst = build_cst()
    ropeP = rope_tables(np.arange(seq))
    ropeS = rope_tables(PAST + np.arange(4))
    s5B = np.zeros((DEPTH, 128, 2, 8, 128), np.float32)
    s5C = np.zeros((DEPTH, 128, 2, 8, 128), np.float32)
    for ri, (bn, cn) in enumerate((("s5_b_re", "s5_c_re"), ("s5_b_im", "s5_c_im"))):
        bb = f(inp[bn])
        cc = f(inp[cn])
        for gi in range(16):
            j, r0, c0 = gi // 2, (gi % 8) * 16, (gi % 2) * 64
            s5B[:, r0:r0 + 16, ri, j, c0:c0 + 64] = bb[:, gi].transpose(0, 2, 1)
            s5C[:, c0:c0 + 64, ri, j, r0:r0 + 16] = cc[:, gi].transpose(0, 2, 1)
    s5B = s5B.reshape(DEPTH, 128, -1)
    s5C = s5C.reshape(DEPTH, 128, -1)
    s5lay = lambda a: np.ascontiguousarray(a.reshape(DEPTH, -1, 8, 2, 64).transpose(0, 3, 4, 2, 1).reshape(DEPTH, 128, 8, -1))
    s5inv = lambda a: a.reshape(DEPTH, 2, 64, 8, -1).transpose(0, 4, 3, 1, 2).reshape(DEPTH, -1, 16, 64)
    hlay = lambda a: np.ascontiguousarray(a.reshape(DEPTH, -1, 2, 2, 64, 64).transpose(0, 3, 5, 2, 1, 4).reshape(DEPTH, 128, 2, -1, 64))
    hinv = lambda a: a.reshape(DEPTH, 2, 64, 2, -1, 64).transpose(0, 4, 3, 1, 5, 2).reshape(DEPTH, -1, 4, 64, 64)
    in_maps = []
    for c in range(n_cores):
        xp = f(inp["x_prompt"][c * nsp:(c + 1) * nsp]).reshape(nsp * seq, D)
        xs = f(inp["x_sample"][c * nss:(c + 1) * nss]).reshape(nss * 4, D)
        in_maps.append({
            "xT_p": np.ascontiguousarray(xp.T), "xT_s": np.ascontiguousarray(xs.T),
            "w_in": f(inp["w_in"]), "w_out": f(inp["w_out"]), "w_up": f(inp["w_up"]), "w_down": f(inp["w_down"]),
            "rw_s": hlay(f(inp["state_rwkv"][:, c * nss:(c + 1) * nss])),
            "rwsh_s": np.ascontiguousarray(f(inp["state_rwkv_shift"][:, c * nss:(c + 1) * nss]).reshape(DEPTH, nss, 8, 128).transpose(0, 3, 2, 1)),
            "rw_w2": f(inp["rw_w2"]), "rw_a2": f(inp["rw_a2"]), "rw_g2": f(inp["rw_g2"]),
            "gdn_s": hlay(f(inp["state_gdn"][:, c * nss:(c + 1) * nss])),
            "conv_s": np.ascontiguousarray(f(inp["state_gdn_conv"][:, c * nss:(c + 1) * nss]).reshape(DEPTH, nss, 3, 6, 128).transpose(0, 4, 3, 1, 2)),
            "s5B": s5B, "s5C": s5C, "s5glu": f(inp["s5_w_glu"]),
            "s5s_re": s5lay(f(inp["state_s5_re"][:, c * nss:(c + 1) * nss])),
            "s5s_im": s5lay(f(inp["state_s5_im"][:, c * nss:(c + 1) * nss])),
            "pp": pp, "cst": cst, "ropeP": ropeP, "ropeS": ropeS,
            "kc_s": np.ascontiguousarray(f(inp["cache_swa_k"][:, c * nss:(c + 1) * nss]).transpose(0, 1, 3, 4, 2)),
            "vc_s": f(inp["cache_swa_v"][:, c * nss:(c + 1) * nss]).reshape(DEPTH, nss, 128, 128),
        })
    import os
    if os.environ.get("KTRACE"):
        res = run_bass_kernel_spmd(k.nc, in_maps, core_ids=list(range(n_cores)), trace=True)
        print("EXEC_TIME_NS", res.exec_time_ns)
    else:
        res = run_bass_kernel_spmd(k.nc, in_maps, core_ids=list(range(n_cores)))
    R = res.results
    cat = lambda fn: np.concatenate([fn(r) for r in R], axis=0)
    y_p = cat(lambda r: r["yT_p"].T.reshape(nsp, seq, D))
    y_s = cat(lambda r: r["yT_s"].T.reshape(nss, 4, D))
    catb = lambda fn: np.concatenate([fn(r) for r in R], axis=1)
    shift_p = catb(lambda r: r["o_shift_p"].transpose(0, 2, 1)[:, :, None, :])
    shift_s = catb(lambda r: r["o_shift_s"].transpose(0, 2, 1)[:, :, None, :])
    swak_p = catb(lambda r: r["o_swak_p"].transpose(0, 1, 4, 2, 3))
    swak_s = catb(lambda r: r["o_swak_s"].transpose(0, 1, 4, 2, 3))
    swav_p = catb(lambda r: r["o_swav_p"].reshape(DEPTH, nsp, 128, 2, 64))
    swav_s = catb(lambda r: r["o_swav_s"].reshape(DEPTH, nss, 128, 2, 64))
    s5o = {n: catb(lambda r: s5inv(r["o_" + n])) for n in ("s5re_p", "s5im_p", "s5re_s", "s5im_s")}
    rwo = dict(rw_p=catb(lambda r: hinv(r["o_rw_p"])), rw_s=catb(lambda r: hinv(r["o_rw_s"])))
    gdo = dict(**rwo, gdn_p=catb(lambda r: hinv(r["o_gdn_p"])), gdn_s=catb(lambda r: hinv(r["o_gdn_s"])),
               conv_p=catb(lambda r: r["o_conv_p"]), conv_s=catb(lambda r: r["o_conv_s"]))
    return dict(y_p=y_p, y_s=y_s, shift_p=shift_p, shift_s=shift_s, swak_p=swak_p, **s5o, **gdo, swak_s=swak_s,
                swav_p=swav_p, swav_s=swav_s)


OUT_NAMES = ["y_p", "y_s", "rw_p", "rw_s", "shift_p", "shift_s", "s5re_p", "s5re_s", "s5im_p", "s5im_s",
             "gdn_p", "gdn_s", "conv_p", "conv_s", "swak_p", "swak_s", "swav_p", "swav_s"]


def out_shapes(B, BS):
    L = DEPTH
    return [(B, 2048, D), (BS, 4, D), (L, B, 4, 64, 64), (L, BS, 4, 64, 64), (L, B, 1, 1024), (L, BS, 1, 1024),
            (L, B, 16, 64), (L, BS, 16, 64), (L, B, 16, 64), (L, BS, 16, 64), (L, B, 4, 64, 64), (L, BS, 4, 64, 64),
            (L, B, 3, 768), (L, BS, 3, 768), (L, B, 128, 2, 64), (L, BS, 128, 2, 64), (L, B, 128, 2, 64),
            (L, BS, 128, 2, 64)]


def kernel(**inp):
    o = run(inp, 8, 2, 2048, 16)
    outs = []
    for n, shp in zip(OUT_NAMES, out_shapes(16, 128)):
        if n in o:
            outs.append(np.ascontiguousarray(o[n], dtype=np.float32).reshape(shp))
        else:
            outs.append(np.zeros(shp, np.float32))
    return tuple(outs)
```

```python
import math
from contextlib import ExitStack
import numpy as np
import ml_dtypes
import concourse.bass as bass
import concourse.mybir as mybir
from concourse.bass_utils import run_bass_kernel_spmd

F32 = mybir.dt.float32
BF16 = mybir.dt.bfloat16
F32R = mybir.dt.float32r


def R_(ap):
    return ap.bitcast(F32R)
AF = mybir.ActivationFunctionType
ALU = mybir.AluOpType
AX = mybir.AxisListType

D = 1024
DEPTH = 2
HD = 64
PROJ = 2824
DFF = 4096
EPS = 1e-6
PAST = 16384
TSTEP = 256


class Tl:
    def __init__(self, h):
        self.h = h
        self.w = None
        self.r = {}

    def __getitem__(self, k):
        return self.h[k]


class _Rec:
    def __init__(self):
        self.call = None

    def __getattr__(self, name):
        def f(*a, **k):
            self.call = (name, a, k)
            return None
        return f


class Sched:
    ENGS = ("pe", "dve", "act", "pool", "sp")

    def __init__(self, nc, es):
        self.nc = nc
        self.es = es
        self.eng = {"pe": nc.tensor, "dve": nc.vector, "act": nc.scalar, "pool": nc.gpsimd, "sp": nc.sync}
        self.prog = []
        self.ninst = 0

    def op(self, e, fn, rd=(), wr=()):
        r = _Rec()
        fn(r)
        assert r.call is not None
        self.prog.append(("op", e, r.call, list(rd), list(wr)))

    def dma(self, q, out, in_, rd=(), wr=()):
        self.prog.append(("dma", q, (out, in_), list(rd), list(wr)))

    def finish(self, q="sp"):
        import os
        SERIAL = int(os.environ.get("KSERIAL", "0"))
        last_ps = None
        nc, es = self.nc, self.es
        prog = self.prog
        n = len(prog)
        deps = [None] * n
        needs = [False] * n
        local = [0] * n
        lcnt = {e: 0 for e in self.ENGS}
        def rowrng(call):
            name, a, k = call
            ap = k.get("lhsT") if name == "matmul" else (a[1] if len(a) > 1 else k.get("in_"))
            b = ap.base_partition()
            kk = ap.shape[0]
            sz = 32 if kk <= 32 else (64 if kk <= 64 else 128)
            b = (b // sz) * sz
            return (b, b + sz)

        def rows_disjoint(r1, r2):
            return r1[1] <= r2[0] or r2[1] <= r1[0]

        for i, (kind, e, call, rd, wr) in enumerate(prog):
            lcnt[e] += 1
            local[i] = lcnt[e]
            d = set()
            for t in rd:
                if t.w is not None:
                    d.add(t.w)
                if getattr(t, "psum", False):
                    for k_, j in t.r.items():
                        if k_ != e:
                            d.add(j)
            for t in wr:
                if t.w is not None:
                    d.add(t.w)
                for j in t.r.values():
                    d.add(j)
            keep = set()
            for j in d:
                kj, ej = prog[j][0], prog[j][1]
                if kind == "op" and kj == "op" and ej == e:
                    if e == "pe":
                        if rows_disjoint(rowrng(call), rowrng(prog[j][2])):
                            keep.add(j)
                        continue
                keep.add(j)
            if SERIAL == 1 and i > 0:
                keep.add(i - 1)
            if SERIAL == 2 and any(getattr(t, "psum", False) for t in list(rd) + list(wr)):
                if last_ps is not None and not (e == "pe" and prog[last_ps][1] == "pe"):
                    keep.add(last_ps)
                last_ps = i
            if SERIAL == 3 and e == "pool" and i > 0:
                keep.add(i - 1)
            if SERIAL == 3 and i > 0 and prog[i - 1][1] == "pool":
                keep.add(i - 1)
            deps[i] = keep
            for j in keep:
                needs[j] = True
            rk = e if kind == "op" else ("dma", i)
            for t in rd:
                t.r[rk] = i
            for t in wr:
                t.w = i
                t.r = {}
        semh, cnt, epoch = {}, {}, {}
        for e in ("pe", "dve", "act", "pool"):
            epoch[e], cnt[e] = 0, 0
            semh[(e, 0)] = es.enter_context(nc.semaphore("s_%s_0" % e))
        ndma = 24
        dsem = [es.enter_context(nc.semaphore("s_dma_%d" % i)) for i in range(ndma)]
        for i in range(ndma):
            semh[("dma", i)] = dsem[i]
        dcnt = [0] * ndma
        dnext = 0
        waited = {e: {} for e in self.ENGS}
        tokn = [None] * n

        def wait(e, need):
            for k, v in need.items():
                if waited[e].get(k, 0) < v:
                    self.eng[e].wait_ge(semh[k], v)
                    waited[e][k] = v
                    self.ninst += 1

        for i, (kind, e, call, rd, wr) in enumerate(prog):
            need = {}
            for j in deps[i]:
                k, v = tokn[j]
                if need.get(k, 0) < v:
                    need[k] = v
            if kind == "op":
                wait(e, need)
                ins = getattr(self.eng[e], call[0])(*call[1], **call[2])
                if needs[i]:
                    if cnt[e] >= 30000:
                        epoch[e] += 1
                        cnt[e] = 0
                        semh[(e, epoch[e])] = es.enter_context(nc.semaphore("s_%s_%d" % (e, epoch[e])))
                    cnt[e] += 1
                    key = (e, epoch[e])
                    ins.then_inc(semh[key], 1)
                    tokn[i] = (key, cnt[e])
            else:
                j = dnext
                dnext = (dnext + 1) % ndma
                if dcnt[j] > 0:
                    need[("dma", j)] = max(need.get(("dma", j), 0), dcnt[j])
                wait(e, need)
                ins = self.eng[e].dma_start(out=call[0], in_=call[1])
                dcnt[j] += 16
                ins.then_inc(dsem[j], 16)
                tokn[i] = (("dma", j), dcnt[j])
            self.ninst += 1
        need = {("dma", j): dcnt[j] for j in range(ndma) if dcnt[j] > 0}
        wait(q, need)
        self.ninc = sum(needs)


RW0, S50, GD0, SW0 = 0, 1024, 1280, 2312
WIN_TILES = [
    (0, 512, [(0, 128, 0), (128, 256, 1), (256, 384, 2), (384, 512, 3)]),
    (512, 1024, [(512, 640, 4), (640, 768, 5), (768, 896, 6), (896, 1024, 7)]),
    (1024, 1536, [(1024, 1152, 8), (1152, 1280, 9), (1280, 1408, 10), (1408, 1536, 11)]),
    (1536, 2048, [(1536, 1664, 12), (1664, 1792, 13), (1792, 1920, 14), (1920, 2048, 15)]),
    (2048, 2312, [(2048, 2056, 16), (2056, 2184, 17), (2184, 2312, 18)]),
    (2312, 2824, [(2312, 2440, 19), (2440, 2568, 20), (2568, 2696, 21), (2696, 2824, 22)]),
]
NPCH = 23


class Kern:
    def __init__(self, n_seq_p, seq_len, n_seq_s, mixers=("rw", "s5", "gdn", "swa"), dbg=False):
        self.NSP, self.SEQ, self.NSS = n_seq_p, seq_len, n_seq_s
        self.mixers = mixers
        self.nsteps = seq_len // TSTEP
        self.nc = bass.Bass("TRN2", target_bir_lowering=False)
        self.dbg = dbg

    def din(self, name, shape, dt=F32):
        return self.nc.dram_tensor(name, list(shape), dt, kind="ExternalInput").ap()

    def dout(self, name, shape, dt=F32):
        return self.nc.dram_tensor(name, list(shape), dt, kind="ExternalOutput").ap()

    def sb(self, name, shape, dt=F32):
        return Tl(self.es.enter_context(self.nc.sbuf_tensor(name, list(shape), dt)))

    def ps(self, name, shape, dt=F32):
        t = Tl(self.es.enter_context(self.nc.psum_tensor(name, list(shape), dt)))
        t.psum = True
        return t

    def build(self):
        nc = self.nc
        NSP, SEQ, NSS = self.NSP, self.SEQ, self.NSS
        NTP = NSP * SEQ
        NTS = NSS * 4
        self.xT_p = self.din("xT_p", [D, NTP])
        self.xT_s = self.din("xT_s", [D, NTS])
        self.w_in = self.din("w_in", [DEPTH, D, PROJ])
        self.w_out = self.din("w_out", [DEPTH, D, D])
        self.w_up = self.din("w_up", [DEPTH, D, DFF])
        self.w_down = self.din("w_down", [DEPTH, DFF, D])
        self.pp_d = self.din("pp", [DEPTH, 128, NPP])
        self.yT_p = self.dout("yT_p", [D, NTP])
        self.yT_s = self.dout("yT_s", [D, NTS])
        self.o_shift_p = self.dout("o_shift_p", [DEPTH, 1024, NSP])
        self.o_shift_s = self.dout("o_shift_s", [DEPTH, 1024, NSS])
        self.cst_d = self.din("cst", [128, NCST])
        self.rw_s = self.din("rw_s", [DEPTH, 128, 2, NSS, 64])
        self.rwsh_s = self.din("rwsh_s", [DEPTH, 128, 8, NSS])
        self.rw_w2 = self.din("rw_w2", [DEPTH, 64, 256])
        self.rw_a2 = self.din("rw_a2", [DEPTH, 64, 256])
        self.rw_g2 = self.din("rw_g2", [DEPTH, 128, 256])
        self.o_rw_p = self.dout("o_rw_p", [DEPTH, 128, 2, NSP, 64])
        self.o_rw_s = self.dout("o_rw_s", [DEPTH, 128, 2, NSS, 64])
        self.gdn_s = self.din("gdn_s", [DEPTH, 128, 2, NSS, 64])
        self.conv_s = self.din("conv_s", [DEPTH, 128, 6, NSS, 3])
        self.o_gdn_p = self.dout("o_gdn_p", [DEPTH, 128, 2, NSP, 64])
        self.o_gdn_s = self.dout("o_gdn_s", [DEPTH, 128, 2, NSS, 64])
        self.o_conv_p = self.dout("o_conv_p", [DEPTH, NSP, 3, 768])
        self.o_conv_s = self.dout("o_conv_s", [DEPTH, NSS, 3, 768])
        self.s5B = self.din("s5B", [DEPTH, 128, 2 * 8 * 128])
        self.s5C = self.din("s5C", [DEPTH, 128, 2 * 8 * 128])
        self.s5glu = self.din("s5glu", [DEPTH, 256, 256])
        self.s5s_re = self.din("s5s_re", [DEPTH, 128, 8, NSS])
        self.s5s_im = self.din("s5s_im", [DEPTH, 128, 8, NSS])
        self.o_s5re_p = self.dout("o_s5re_p", [DEPTH, 128, 8, NSP])
        self.o_s5im_p = self.dout("o_s5im_p", [DEPTH, 128, 8, NSP])
        self.o_s5re_s = self.dout("o_s5re_s", [DEPTH, 128, 8, NSS])
        self.o_s5im_s = self.dout("o_s5im_s", [DEPTH, 128, 8, NSS])
        self.ropeP = self.din("ropeP", [64, 2, SEQ])
        self.ropeS = self.din("ropeS", [64, 2, 4])
        self.kc_s = self.din("kc_s", [DEPTH, NSS, 2, 64, 128])
        self.vc_s = self.din("vc_s", [DEPTH, NSS, 128, 128])
        self.o_swak_p = self.dout("o_swak_p", [DEPTH, NSP, 2, 64, 128])
        self.o_swak_s = self.dout("o_swak_s", [DEPTH, NSS, 2, 64, 128])
        self.o_swav_p = self.dout("o_swav_p", [DEPTH, NSP, 128, 128])
        self.o_swav_s = self.dout("o_swav_s", [DEPTH, NSS, 128, 128])

        import os
        self.wcache = {}
        self.wring = 0
        self.wscr_off = 0
        self.wscr = None
        if os.environ.get("KNOWSCR", "0") != "1":
            nel = DEPTH * (D * PROJ + D * D + D * DFF + DFF * D) // 128
            self.wscr = self.nc.dram_tensor("wscr", [128, nel], BF16, kind="Internal").ap()
        with ExitStack() as es:
            self.es = es
            es.enter_context(nc.allow_non_contiguous_dma(reason="small strided state/param io"))
            self.S = Sched(nc, es)
            self.alloc()
            self.setup_consts()
            import os
            grp = os.environ.get("KGROUPS", "sp")
            if "s" in grp:
                self.run_group("s", 0)
            if "p" in grp:
                for st in range(self.nsteps):
                    self.run_group("p", st)
            self.S.finish("sp")
        return nc

    def alloc(self):
        NT = 512
        self.x = [self.sb("x%d" % c, [128, NT]) for c in range(8)]
        self.xn = [self.sb("xn%d" % c, [128, NT], BF16) for c in range(8)]
        self.p = [self.sb("p%d" % c, [128, NT]) for c in range(NPCH)]
        self.mix = self.xn
        self.tmp = [self.sb("tmp%d" % c, [128, NT]) for c in range(8)]
        self.stage = [self.sb("stage%d" % i, [128, 2048]) for i in range(2)]
        self.wbf = [self.sb("wbf%d" % i, [128, 2048], BF16) for i in range(2)]
        self.wi = 0
        self.sq = [self.sb("sq%d" % i, [128, NT], BF16) for i in range(2)]
        self.rstd = self.sb("rstd", [128, NT])
        self.pp = [self.sb("ppsb%d" % l, [128, NPP]) for l in range(DEPTH)]
        self.ones_bf = self.sb("ones_bf", [128, 128], BF16)
        self.ident = self.sb("ident", [128, 128])
        self.ps_lin = [self.ps("ps_lin%d" % i, [128, 512]) for i in range(2)]
        self.pli = 0
        self.ps_ss = self.ps("ps_ss", [128, 512])
        self.mp = [self.ps("mp%d" % i, [128, 512]) for i in range(5)]
        self.cst = self.sb("cst_sb", [128, NCST])
        self.NSC = 26
        self.sc = [self.sb("sc%d" % i, [128, 512]) for i in range(self.NSC)]
        self.esink = [self.sb("esink%d" % l, [128, 2]) for l in range(DEPTH)]
        self.nn_t = [self.sb("nn0", [64, 512])]
        self.nx_t = [self.sb("nx0", [64, 256])]
        self.nnp = [[self.sb("nnp%d_%d" % (pr, i), [64, 256]) for i in range(2)] for pr in range(2)]
        self.nxp = [[self.sb("nxp%d_%d" % (pr, i), [64, 128]) for i in range(2)] for pr in range(2)]
        self.s5k = [self.sb("s5k%d" % l, [128, 16 * 8]) for l in range(DEPTH)]
        self.s5h = [[self.sb("s5h%d_%d" % (l, i), [128, 8 * self.NSP]) for i in range(2)] for l in range(DEPTH)]
        self.s5w = self.sb("s5w", [128, 512])
        self.gH = [self.sb("gH%d" % l, [128, 2 * self.NSP * 64]) for l in range(DEPTH)]
        self.rH = [self.sb("rH%d" % l, [128, 2 * self.NSP * 64]) for l in range(DEPTH)]
        self.rsh = [self.sb("rsh%d" % l, [128, 8 * self.NSP]) for l in range(DEPTH)]
        self.gcb = [self.sb("gcb%d" % l, [128, 6 * self.NSP * 3]) for l in range(DEPTH)]
        self.gnea = [self.sb("gnea%d" % l, [128, 1]) for l in range(DEPTH)]
        self.s5i = self.sb("s5i", [128, 256], mybir.dt.int32)
        self.khist = [[self.sb("khist%d_%d" % (l, k), [64, self.NSP * 128]) for k in range(2)] for l in range(DEPTH)]
        self.vhist = [self.sb("vhist%d" % l, [128, self.NSP * 128]) for l in range(DEPTH)]

    def setup_consts(self):
        S = self.S
        S.op("pool", lambda e: e.memset(self.ones_bf[:], 1.0), wr=[self.ones_bf])
        S.op("pool", lambda e: e.memset(self.ident[:], 0.0), wr=[self.ident])
        S.op("pool", lambda e: e.affine_select(out=self.ident[:], in_=self.ident[:], pattern=[[-1, 128]], base=0,
                                               channel_multiplier=1, compare_op=ALU.not_equal, fill=1.0),
             rd=[self.ident], wr=[self.ident])
        S.dma("sp", self.cst[:], self.cst_d, wr=[self.cst])
        for l in range(DEPTH):
            S.dma("sp", self.pp[l][:], self.pp_d[l], wr=[self.pp[l]])
        for l in range(DEPTH):
            for i in range(2):
                S.op("pool", lambda e: e.memset(self.s5h[l][i][:], 0.0), wr=[self.s5h[l][i]])
            self.s5_setup(l)
            S.op("pool", lambda e: e.memset(self.gH[l][:], 0.0), wr=[self.gH[l]])
            S.op("pool", lambda e: e.memset(self.rH[l][:], 0.0), wr=[self.rH[l]])
            S.op("pool", lambda e: e.memset(self.rsh[l][:], 0.0), wr=[self.rsh[l]])
            S.op("pool", lambda e: e.memset(self.gcb[l][:], 0.0), wr=[self.gcb[l]])
            o_ = PPO["gdn_alog"]
            S.op("act", lambda e: e.activation(out=self.gnea[l][:], in_=self.pp[l][:, o_:o_ + 1], func=AF.Exp),
                 rd=[self.pp[l]], wr=[self.gnea[l]])
            S.op("dve", lambda e: e.tensor_scalar(self.gnea[l][:], self.gnea[l][:], -1.0, None, ALU.mult),
                 rd=[self.gnea[l]], wr=[self.gnea[l]])
        for l in range(DEPTH):
            o = PPO["sink"]
            S.op("act", lambda e: e.activation(out=self.esink[l][:], in_=self.pp[l][:, o:o + 2], func=AF.Exp),
                 rd=[self.pp[l]], wr=[self.esink[l]])

    def cs(self, name, p0=0, p1=128):
        o, n = CSO[name]
        return self.cst[p0:p1, o:o + n]

    def ppc(self, l, name, c=0):
        o = PPO[name] + c
        return self.pp[l][:, o:o + 1]

    def linear(self, w2d, kc, tiles, rhs, NT, consume, wkey=None):
        S = self.S
        groups = [g_ for (_, _, gs) in tiles for g_ in gs]

        def load(k0, k1, c0, c1):
            b = self.wi
            self.wi ^= 1
            nk, ncol = k1 - k0, c1 - c0
            st, wb = self.stage[b], self.wbf[b]
            ck = (wkey, k0, k1, c0, c1)
            if wkey is not None and ck in self.wcache:
                self.wi ^= 1
                r_ = self.wring
                self.wring = (self.wring + 1) % 4
                if r_ < 2:
                    wb = self.wbf[r_]
                    wap = wb[:, 0:nk * ncol]
                else:
                    wb = self.stage[r_ - 2]
                    wap = wb[:, :].bitcast(BF16)[:, 0:nk * ncol]
                off, dep = self.wcache[ck]
                S.dma("sp", wap, self.wscr[:, off:off + nk * ncol], rd=[dep], wr=[wb])
                return wb, wap.rearrange("p (k n) -> p k n", k=nk)
            S.dma("sp", st[:, 0:nk * ncol].rearrange("p (k n) -> p k n", k=nk),
                  w2d[k0 * 128:k1 * 128, c0:c1].rearrange("(k p) n -> p k n", p=128), wr=[st])
            S.op("act", lambda e: e.activation(out=wb[:, 0:nk * ncol], in_=st[:, 0:nk * ncol], func=AF.Copy), rd=[st], wr=[wb])
            if wkey is not None and self.wscr is not None:
                off = self.wscr_off
                self.wscr_off += nk * ncol
                dep = Tl(None)
                S.dma("act", self.wscr[:, off:off + nk * ncol], wb[:, 0:nk * ncol], rd=[wb], wr=[dep])
                self.wcache[ck] = (off, dep)
            return wb, wb[:, 0:nk * ncol].rearrange("p (k n) -> p k n", k=nk)

        loads = []
        if kc * 256 <= 2048:
            i = 0
            while i < len(groups):
                batch = [groups[i]]
                if i + 1 < len(groups) and groups[i + 1][0] == groups[i][1] and groups[i + 1][1] - groups[i][0] <= 2048 // kc:
                    batch.append(groups[i + 1])
                i += len(batch)
                loads.append((0, kc, batch[0][0], batch[-1][1], [(m0, m1, tag, True, True) for (m0, m1, tag) in batch]))
        else:
            kseg = 2048 // 128
            for (m0, m1, tag) in groups:
                for k0 in range(0, kc, kseg):
                    loads.append((k0, k0 + kseg, m0, m1, [(m0, m1, tag, k0 == 0, k0 + kseg == kc)]))
        cached = wkey is not None and all((wkey, l_[0], l_[1], l_[2], l_[3]) in self.wcache for l_ in loads)
        depth = 3 if cached else 1
        q = [load(*loads[j][0:4]) for j in range(min(depth, len(loads)))]
        pst = None
        for li, (k0, k1, c0, c1, grp) in enumerate(loads):
            cur = q.pop(0)
            if li + depth < len(loads):
                q.append(load(*loads[li + depth][0:4]))
            wb, wv = cur
            for (m0, m1, tag, first, lastk) in grp:
                if first:
                    pst = self.ps_lin[self.pli]
                    self.pli ^= 1
                m = m1 - m0
                for k in range(k0, k1):
                    r_ap, r_t = rhs(k)
                    S.op("pe", lambda e: e.matmul(pst[0:m, 0:NT], lhsT=wv[:, k - k0, m0 - c0:m1 - c0], rhs=r_ap,
                                                  start=(k == 0), stop=(k == kc - 1)), rd=[wb] + r_t, wr=[pst])
                if lastk:
                    consume(tag, pst, m)

    def norm_stats(self, src, nch, NT):
        S = self.S
        for c in range(nch):
            a, t = src(c)
            sq = self.sq[c % 2]
            if c % 2 == 0:
                S.op("act", lambda e: e.activation(out=sq[:, 0:NT], in_=a, func=AF.Square), rd=t, wr=[sq])
            else:
                S.op("dve", lambda e: e.tensor_tensor(out=sq[:, 0:NT], in0=a, in1=a, op=ALU.mult), rd=t, wr=[sq])
            S.op("pe", lambda e: e.matmul(self.ps_ss[:, 0:NT], lhsT=self.ones_bf[:], rhs=sq[:, 0:NT],
                                          start=(c == 0), stop=(c == nch - 1)), rd=[sq, self.ones_bf], wr=[self.ps_ss])
        S.op("act", lambda e: e.activation(out=self.rstd[:, 0:NT], in_=self.ps_ss[:, 0:NT], func=AF.Ln,
                                           bias=EPS, scale=1.0 / (nch * 128)), rd=[self.ps_ss], wr=[self.rstd])
        S.op("act", lambda e: e.activation(out=self.rstd[:, 0:NT], in_=self.rstd[:, 0:NT], func=AF.Exp, scale=-0.5),
             rd=[self.rstd], wr=[self.rstd])

    def prenorm(self, l, gname, NT):
        S = self.S
        self.norm_stats(lambda c: (self.x[c][:, 0:NT], [self.x[c]]), 8, NT)
        for c in range(8):
            S.op("dve", lambda e: e.scalar_tensor_tensor(out=self.xn[c][:, 0:NT], in0=self.x[c][:, 0:NT],
                                                         scalar=self.ppc(l, gname, c), in1=self.rstd[:, 0:NT],
                                                         op0=ALU.mult, op1=ALU.mult),
                 rd=[self.x[c], self.rstd, self.pp[l]], wr=[self.xn[c]])

    def postnorm_residual(self, l, gname, NT):
        S = self.S
        self.norm_stats(lambda c: (self.tmp[c][:, 0:NT], [self.tmp[c]]), 8, NT)
        for c in range(8):
            S.op("dve", lambda e: e.scalar_tensor_tensor(out=self.tmp[c][:, 0:NT], in0=self.tmp[c][:, 0:NT],
                                                         scalar=self.ppc(l, gname, c), in1=self.rstd[:, 0:NT],
                                                         op0=ALU.mult, op1=ALU.mult),
                 rd=[self.tmp[c], self.rstd, self.pp[l]], wr=[self.tmp[c]])
            S.op("dve", lambda e: e.tensor_tensor(out=self.x[c][:, 0:NT], in0=self.x[c][:, 0:NT],
                                                   in1=self.tmp[c][:, 0:NT], op=ALU.add),
                 rd=[self.x[c], self.tmp[c]], wr=[self.x[c]])


    def head_rms(self, src, NT, scale, bias_eps, rs_t):
        S = self.S
        sq = self.sc[19]
        S.op("act", lambda e: e.activation(out=sq[:, 0:NT], in_=src[:, 0:NT], func=AF.Square), rd=[src], wr=[sq])
        ps = self.mp[4]
        S.op("pe", lambda e: e.matmul(ps[:, 0:NT], lhsT=self.cs("blk64"), rhs=sq[:, 0:NT], start=True, stop=True),
             rd=[self.cst, sq], wr=[ps])
        if bias_eps is None:
            S.op("dve", lambda e: e.tensor_scalar(rs_t[:, 0:NT], ps[:, 0:NT], 1e-12, None, ALU.max), rd=[ps], wr=[rs_t])
            S.op("act", lambda e: e.activation(out=rs_t[:, 0:NT], in_=rs_t[:, 0:NT], func=AF.Sqrt), rd=[rs_t], wr=[rs_t])
        else:
            S.op("act", lambda e: e.activation(out=rs_t[:, 0:NT], in_=ps[:, 0:NT], func=AF.Sqrt, bias=bias_eps, scale=scale),
                 rd=[ps], wr=[rs_t])
        S.op("dve", lambda e: e.reciprocal(rs_t[:, 0:NT], rs_t[:, 0:NT]), rd=[rs_t], wr=[rs_t])

    def neumann(self, NN0, X0, levels):
        S = self.S
        psq = [self.mp[2], self.ps_lin[0]]
        psx = [self.mp[3], self.ps_lin[1]]
        for k in range(levels):
            for pr in range(2):
                if k == 0:
                    nt, xt = NN0, X0
                    nv = R_(NN0[0:64, 0:512]).rearrange("p (h n) -> p h n", h=4)[:, pr * 2:pr * 2 + 2, :]
                    xin = X0[0:64, pr * 128:(pr + 1) * 128]
                else:
                    nt, xt = self.nnp[pr][(k - 1) % 2], self.nxp[pr][(k - 1) % 2]
                    nv = R_(nt[0:64, 0:256]).rearrange("p (h n) -> p h n", h=2)
                    xin = xt[0:64, 0:128]
                xo = self.nxp[pr][k % 2]
                px = psx[pr]
                for hh in range(2):
                    S.op("pe", lambda e: e.matmul(px[0:64, hh * 64:(hh + 1) * 64], lhsT=nv[:, hh, 0:64], rhs=R_(xin[:, hh * 64:(hh + 1) * 64]),
                                                  start=True, stop=True), rd=[nt, xt], wr=[px])
                S.op("dve", lambda e: e.tensor_tensor(out=R_(xo[0:64, 0:128]), in0=px[0:64, 0:128], in1=xin, op=ALU.add),
                     rd=[px, xt], wr=[xo])
                if k < levels - 1:
                    no = self.nnp[pr][k % 2]
                    pq = psq[pr]
                    for hh in range(2):
                        S.op("pe", lambda e: e.matmul(pq[0:64, hh * 128:hh * 128 + 64], lhsT=nv[:, hh, 64:128], rhs=nv[:, hh, 0:64],
                                                      start=True, stop=True), rd=[nt], wr=[pq])
                        S.op("pe", lambda e: e.matmul(pq[0:64, hh * 128 + 64:hh * 128 + 128], lhsT=nv[:, hh, 0:64], rhs=nv[:, hh, 64:128],
                                                      start=True, stop=True), rd=[nt], wr=[pq])
                    S.op("act", lambda e: e.activation(out=R_(no[0:64, 0:256]), in_=pq[0:64, 0:256], func=AF.Copy), rd=[pq], wr=[no])
        fin = (levels - 1) % 2

        def uh(h):
            t = self.nxp[h // 2][fin]
            return t[0:64, (h % 2) * 64:(h % 2) * 64 + 64], t
        return uh

    def colop(self, out_t, out3, in3, col4, op, rd):
        for h in range(4):
            self.S.op("dve", lambda e: e.tensor_scalar(out3[:, h, :], in3[:, h, :], col4[:, h:h + 1], None, op), rd=rd, wr=[out_t])

    def instances(self, g, nseq, T):
        if g == "s":
            return [(0, list(range(nseq)), 4, "s", 2)]
        return [(s_ * T + ch * 64, [s_], 64, "p", 6) for ch in range(T // 64) for s_ in range(nseq)]

    def pad_seq(self, dst, src_ap, rd):
        S = self.S
        eye = self.cs("eye16").rearrange("p (a b) -> p a b", a=16).unsqueeze(3).to_broadcast([128, 16, 16, 4])
        in0 = src_ap.rearrange("p (b i) -> p b i", i=4).unsqueeze(1).to_broadcast([128, 16, 16, 4])
        S.op("pool", lambda e: e.tensor_tensor(out=dst.rearrange("p (a b i) -> p a b i", a=16, b=16), in0=in0, in1=eye, op=ALU.mult),
             rd=rd + [self.cst], wr=[])


    def rwkv(self, g, st, l, NT, nseq, T, last):
        S = self.S
        samp = (g == "s")
        mp, sc, cst, tmp, pp, p = self.mp, self.sc, self.cst, self.tmp, self.pp[l], self.p
        sfx = "s" if samp else "p"
        v3 = lambda ap: ap.rearrange("p (s t) -> p s t", t=T)
        ppc = lambda n, c: pp[:, PPO[n] + c:PPO[n] + c + 1]
        if samp:
            sh = sc[24]
            S.dma("sp", sh[:, 0:8 * nseq].rearrange("p (c s) -> p c s", c=8), self.rwsh_s[l], wr=[sh])
            stH = self.stage[self.wi]
            self.wi ^= 1
            S.dma("sp", stH[:, 0:2 * nseq * 64].rearrange("p (a s v) -> p a s v", a=2, s=nseq), self.rw_s[l], wr=[stH])
            Ht = stH
        else:
            sh = self.rsh[l]
            Ht = self.rH[l]
        shv = sh[:, 0:8 * nseq].rearrange("p (c s) -> p c s", c=8)
        Hv = Ht[:, 0:2 * nseq * 64].rearrange("p (a s v) -> p a s v", a=2, s=nseq)
        wm = sc[25]
        for c in range(8):
            pv, tv = v3(p[c][:, 0:NT]), v3(tmp[c][:, 0:NT])
            if T > 1:
                S.op("dve", lambda e: e.tensor_tensor(out=tv[:, :, 1:T], in0=pv[:, :, 0:T - 1], in1=pv[:, :, 1:T], op=ALU.subtract),
                     rd=[p[c]], wr=[tmp[c]])
            S.op("dve", lambda e: e.tensor_tensor(out=tv[:, :, 0], in0=shv[:, c, :], in1=pv[:, :, 0], op=ALU.subtract),
                 rd=[p[c], sh], wr=[tmp[c]])
            S.op("dve", lambda e: e.scalar_tensor_tensor(out=tmp[c][:, 0:NT], in0=tmp[c][:, 0:NT], scalar=ppc("rw_mu", c), in1=p[c][:, 0:NT],
                                                         op0=ALU.mult, op1=ALU.add), rd=[tmp[c], pp, p[c]], wr=[tmp[c]])
            if not samp:
                S.op("dve", lambda e: e.tensor_copy(shv[:, c, :], pv[:, :, T - 1]), rd=[p[c]], wr=[sh])
        rT, kT, vT = tmp[0:2], tmp[2:4], tmp[4:6]
        S.op("act", lambda e: e.activation(out=tmp[6][0:64, 0:NT], in_=tmp[6][0:64, 0:NT], func=AF.Tanh), rd=[tmp[6]], wr=[tmp[6]])
        S.op("act", lambda e: e.activation(out=tmp[7][:, 0:NT], in_=tmp[7][:, 0:NT], func=AF.Sigmoid), rd=[tmp[7]], wr=[tmp[7]])
        lw, aa, KK, gate, Wi = sc[0:2], sc[2:4], sc[4:6], sc[6:8], sc[8:10]
        At, Rt, Bt, Kt = sc[10:12], sc[12:14], sc[14:16], sc[16:18]
        bonus, yall = sc[20:22], sc[22:24]
        rs = sc[25]
        wmv = wm[:, 0:384].rearrange("p (a n) -> p a n", a=3)
        for pr in range(2):
            cs_ = slice(pr * 128, (pr + 1) * 128)
            S.dma("sp", wmv[0:64, 0, :], self.rw_w2[l][:, cs_], wr=[wm])
            S.dma("sp", wmv[64:128, 1, :], self.rw_a2[l][:, cs_], wr=[wm])
            S.dma("sp", wmv[:, 2, :], self.rw_g2[l][:, cs_], wr=[wm])
            ps = mp[0]
            S.op("pe", lambda e: e.matmul(ps[:, 0:NT], lhsT=wmv[0:64, 0, :], rhs=tmp[6][0:64, 0:NT], start=True, stop=True),
                 rd=[wm, tmp[6]], wr=[ps])
            S.op("act", lambda e: e.activation(out=lw[pr][:, 0:NT], in_=ps[:, 0:NT], func=AF.Sigmoid, bias=ppc("rw_w0", pr)),
                 rd=[ps, pp], wr=[lw[pr]])
            S.op("dve", lambda e: e.tensor_scalar(lw[pr][:, 0:NT], lw[pr][:, 0:NT], -math.exp(-0.5), None, ALU.mult), rd=[lw[pr]], wr=[lw[pr]])
            ps = mp[1]
            S.op("pe", lambda e: e.matmul(ps[:, 0:NT], lhsT=wmv[64:128, 1, :], rhs=tmp[6][64:128, 0:NT], start=True, stop=True),
                 rd=[wm, tmp[6]], wr=[ps])
            S.op("act", lambda e: e.activation(out=aa[pr][:, 0:NT], in_=ps[:, 0:NT], func=AF.Sigmoid, bias=ppc("rw_a0", pr)),
                 rd=[ps, pp], wr=[aa[pr]])
            ps = mp[2]
            S.op("pe", lambda e: e.matmul(ps[:, 0:NT], lhsT=wmv[:, 2, :], rhs=tmp[7][:, 0:NT], start=True, stop=True),
                 rd=[wm, tmp[7]], wr=[ps])
            S.op("act", lambda e: e.activation(out=gate[pr][:, 0:NT], in_=ps[:, 0:NT], func=AF.Copy), rd=[ps], wr=[gate[pr]])
            S.op("dve", lambda e: e.tensor_scalar(KK[pr][:, 0:NT], kT[pr][:, 0:NT], ppc("rw_kk", pr), None, ALU.mult), rd=[kT[pr], pp], wr=[KK[pr]])
            self.head_rms(KK[pr], NT, None, None, rs)
            S.op("dve", lambda e: e.tensor_tensor(out=KK[pr][:, 0:NT], in0=KK[pr][:, 0:NT], in1=rs[:, 0:NT], op=ALU.mult), rd=[KK[pr], rs], wr=[KK[pr]])
            S.op("dve", lambda e: e.tensor_scalar(rs[:, 0:NT], aa[pr][:, 0:NT], -1.0, None, ALU.add), rd=[aa[pr]], wr=[rs])
            S.op("dve", lambda e: e.tensor_scalar(rs[:, 0:NT], rs[:, 0:NT], ppc("rw_ka", pr), 1.0, ALU.mult, ALU.add), rd=[rs, pp], wr=[rs])
            S.op("dve", lambda e: e.tensor_tensor(out=kT[pr][:, 0:NT], in0=kT[pr][:, 0:NT], in1=rs[:, 0:NT], op=ALU.mult), rd=[kT[pr], rs], wr=[kT[pr]])
            cum = rs
            S.op("dve", lambda e: e.tensor_tensor_scan(cum[:, 0:NT], self.cs("rmask_" + sfx)[:, 0:NT], lw[pr][:, 0:NT], 0.0, ALU.mult, ALU.add),
                 rd=[lw[pr], cst], wr=[cum])
            S.op("act", lambda e: e.activation(out=Wi[pr][:, 0:NT], in_=cum[:, 0:NT], func=AF.Exp), rd=[cum], wr=[Wi[pr]])
            S.op("dve", lambda e: e.tensor_tensor(out=lw[pr][:, 0:NT], in0=cum[:, 0:NT], in1=lw[pr][:, 0:NT], op=ALU.subtract), rd=[cum, lw[pr]], wr=[lw[pr]])
            S.op("act", lambda e: e.activation(out=lw[pr][:, 0:NT], in_=lw[pr][:, 0:NT], func=AF.Exp), rd=[lw[pr]], wr=[lw[pr]])
            S.op("act", lambda e: e.activation(out=cum[:, 0:NT], in_=cum[:, 0:NT], func=AF.Exp, scale=-1.0), rd=[cum], wr=[cum])
            S.op("dve", lambda e: e.scalar_tensor_tensor(out=At[pr][:, 0:NT], in0=KK[pr][:, 0:NT], scalar=-1.0, in1=lw[pr][:, 0:NT],
                                                         op0=ALU.mult, op1=ALU.mult), rd=[KK[pr], lw[pr]], wr=[At[pr]])
            S.op("dve", lambda e: e.tensor_tensor(out=Rt[pr][:, 0:NT], in0=rT[pr][:, 0:NT], in1=Wi[pr][:, 0:NT], op=ALU.mult), rd=[rT[pr], Wi[pr]], wr=[Rt[pr]])
            S.op("dve", lambda e: e.tensor_tensor(out=Bt[pr][:, 0:NT], in0=KK[pr][:, 0:NT], in1=aa[pr][:, 0:NT], op=ALU.mult), rd=[KK[pr], aa[pr]], wr=[Bt[pr]])
            S.op("dve", lambda e: e.tensor_tensor(out=Bt[pr][:, 0:NT], in0=Bt[pr][:, 0:NT], in1=cum[:, 0:NT], op=ALU.mult), rd=[Bt[pr], cum], wr=[Bt[pr]])
            S.op("dve", lambda e: e.tensor_tensor(out=Kt[pr][:, 0:NT], in0=kT[pr][:, 0:NT], in1=cum[:, 0:NT], op=ALU.mult), rd=[kT[pr], cum], wr=[Kt[pr]])
            S.op("dve", lambda e: e.scalar_tensor_tensor(out=bonus[pr][:, 0:NT], in0=rT[pr][:, 0:NT], scalar=ppc("rw_rk", pr), in1=kT[pr][:, 0:NT],
                                                         op0=ALU.mult, op1=ALU.mult), rd=[rT[pr], pp, kT[pr]], wr=[bonus[pr]])
            ps = mp[3]
            S.op("pe", lambda e: e.matmul(ps[:, 0:NT], lhsT=self.cs("blk64"), rhs=bonus[pr][:, 0:NT], start=True, stop=True),
                 rd=[cst, bonus[pr]], wr=[ps])
            S.op("dve", lambda e: e.tensor_tensor(out=bonus[pr][:, 0:NT], in0=ps[:, 0:NT], in1=vT[pr][:, 0:NT], op=ALU.mult), rd=[ps, vT[pr]], wr=[bonus[pr]])
        f3 = lambda t_: t_[0:64, 0:256].rearrange("p (h i) -> p h i", h=4)
        for (c0, seqs, C, ms, levels) in self.instances(g, nseq, T):
            cols = slice(c0, c0 + 64)
            ns = len(seqs)
            bm = lambda name: self.cs(name + "_" + ms, 0, 64).unsqueeze(1).to_broadcast([64, 4, 64])
            pA, pB, pC = mp[2], mp[3], mp[4]
            for h in range(4):
                pr, r0 = h // 2, (h % 2) * 64
                a_, r_, b_, k_ = (t_[pr][r0:r0 + 64, cols] for t_ in (At, Rt, Bt, Kt))
                rd_ = [At[pr], Rt[pr], Bt[pr], Kt[pr]]
                for (dst, o_, l_, rr_) in ((pA, h * 128, b_, a_), (pA, h * 128 + 64, a_, b_), (pB, h * 128, b_, r_),
                                           (pB, h * 128 + 64, k_, a_), (pC, h * 64, k_, r_)):
                    S.op("pe", lambda e: e.matmul(dst[0:64, o_:o_ + 64], lhsT=l_, rhs=rr_, start=True, stop=True), rd=rd_, wr=[dst])
            NN, AT2, AT3 = self.nn_t[0], sc[25], sc[0]
            v4 = lambda t_: t_[0:64, 0:512].rearrange("p (h n) -> p h n", h=4)
            S.op("dve", lambda e: e.tensor_tensor(out=R_(v4(NN)[:, :, 0:64]), in0=v4(pA)[:, :, 0:64], in1=bm("triU"), op=ALU.mult), rd=[pA, cst], wr=[NN])
            S.op("dve", lambda e: e.tensor_tensor(out=R_(v4(NN)[:, :, 64:128]), in0=v4(pA)[:, :, 64:128], in1=bm("triL"), op=ALU.mult), rd=[pA, cst], wr=[NN])
            S.op("dve", lambda e: e.tensor_tensor(out=v4(AT2)[:, :, 0:64], in0=v4(pB)[:, :, 0:64], in1=bm("incU"), op=ALU.mult), rd=[pB, cst], wr=[AT2])
            S.op("dve", lambda e: e.tensor_tensor(out=v4(AT2)[:, :, 64:128], in0=v4(pB)[:, :, 64:128], in1=bm("triU"), op=ALU.mult), rd=[pB, cst], wr=[AT2])
            S.op("dve", lambda e: e.tensor_tensor(out=f3(AT3), in0=f3(pC), in1=bm("incU"), op=ALU.mult), rd=[pC, cst], wr=[AT3])
            ATrb = lambda h: v4(AT2)[:, h, 0:64]
            ATak = lambda h: v4(AT2)[:, h, 64:128]
            ATrk = lambda h: f3(AT3)[:, h, :]
            pT0, pT1 = mp[0], mp[1]
            for pr in range(2):
                S.op("pe", lambda e: e.transpose(pT0[0:64, pr * 128:(pr + 1) * 128], vT[pr][:, cols], self.ident[:]), rd=[vT[pr], self.ident], wr=[pT0])
                S.op("pe", lambda e: e.transpose(pT0[0:64, 256 + pr * 128:256 + (pr + 1) * 128], Bt[pr][:, cols], self.ident[:]), rd=[Bt[pr], self.ident], wr=[pT0])
                S.op("pe", lambda e: e.transpose(pT1[0:64, pr * 128:(pr + 1) * 128], Kt[pr][:, cols], self.ident[:]), rd=[Kt[pr], self.ident], wr=[pT1])
            Vtok, Btok, Ktok = sc[1], sc[2], sc[3]
            S.op("act", lambda e: e.activation(out=Vtok[0:64, 0:256], in_=pT0[0:64, 0:256], func=AF.Copy), rd=[pT0], wr=[Vtok])
            S.op("act", lambda e: e.activation(out=Btok[0:64, 0:256], in_=pT0[0:64, 256:512], func=AF.Copy), rd=[pT0], wr=[Btok])
            S.op("act", lambda e: e.activation(out=Ktok[0:64, 0:256], in_=pT1[0:64, 0:256], func=AF.Copy), rd=[pT1], wr=[Ktok])
            if ns > 1:
                apad = [(p[0], p[1]), (p[2], p[3])]
                rpad = [(p[4], p[5]), (p[6], p[7])]
                padv = {}
                for pr in range(2):
                    for nm, src_t, tl in (("a", At[pr], apad[pr]), ("r", Rt[pr], rpad[pr])):
                        for half in range(2):
                            eye = self.cs("eye16").rearrange("p (a b) -> p a b", a=16)[:, half * 8:(half + 1) * 8, :].unsqueeze(3).to_broadcast([128, 8, 16, 4])
                            in0 = src_t[:, cols].rearrange("p (b i) -> p b i", i=4).unsqueeze(1).to_broadcast([128, 8, 16, 4])
                            S.op("pool", lambda e: e.tensor_tensor(out=tl[half][:, 0:512].rearrange("p (a b i) -> p a b i", a=8, b=16),
                                                                   in0=in0, in1=eye, op=ALU.mult), rd=[src_t, cst], wr=[tl[half]])
                        padv[(nm, pr)] = tl
                lk = lambda nm, pr, r0, si: (padv[(nm, pr)][si // 8][r0:r0 + 64, (si % 8) * 64:(si % 8) * 64 + 64], [padv[(nm, pr)][si // 8]])
            else:
                lk = lambda nm, pr, r0, si: ((At if nm == "a" else Rt)[pr][r0:r0 + 64, cols], [(At if nm == "a" else Rt)[pr]])
            pK = mp[1]
            for h in range(4):
                pr, r0 = h // 2, (h % 2) * 64
                for si, s_ in enumerate(seqs):
                    a_, t_ = lk("a", pr, r0, si)
                    S.op("pe", lambda e: e.matmul(pK[0:64, 256 + h * 64:256 + (h + 1) * 64], lhsT=a_, rhs=Hv[r0:r0 + 64, pr, s_, :],
                                                  start=(si == 0), stop=False), rd=t_ + [Ht], wr=[pK])
                S.op("pe", lambda e: e.matmul(pK[0:64, 256 + h * 64:256 + (h + 1) * 64], lhsT=ATak(h), rhs=Vtok[0:64, h * 64:(h + 1) * 64],
                                              start=False, stop=True), rd=[AT2, Vtok], wr=[pK])
            X = self.nx_t[0]
            S.op("act", lambda e: e.activation(out=R_(X[0:64, 0:256]), in_=pK[0:64, 256:512], func=AF.Copy), rd=[pK], wr=[X])
            uh = self.neumann(NN, X, levels)
            pY = mp[0]
            for h in range(4):
                pr, r0 = h // 2, (h % 2) * 64
                o_ = pY[r0:r0 + 64, pr * 64:(pr + 1) * 64]
                for si, s_ in enumerate(seqs):
                    a_, t_ = lk("r", pr, r0, si)
                    S.op("pe", lambda e: e.matmul(o_, lhsT=Hv[r0:r0 + 64, pr, s_, :], rhs=a_, start=(si == 0), stop=False), rd=t_ + [Ht], wr=[pY])
                S.op("pe", lambda e: e.matmul(o_, lhsT=uh(h)[0], rhs=ATrb(h), start=False, stop=False), rd=[uh(h)[1], AT2], wr=[pY])
                S.op("pe", lambda e: e.matmul(o_, lhsT=Vtok[0:64, h * 64:(h + 1) * 64], rhs=ATrk(h), start=False, stop=True), rd=[Vtok, AT3], wr=[pY])
            for pr in range(2):
                S.op("act", lambda e: e.activation(out=yall[pr][:, cols], in_=pY[:, pr * 64:(pr + 1) * 64], func=AF.Copy), rd=[pY], wr=[yall[pr]])
            for pr in range(2):
                pH = [mp[2], mp[3]]
                for hh in range(2):
                    h = pr * 2 + hh
                    r0 = hh * 64
                    if ns > 1:
                        Up = (tmp[0], tmp[1])
                        Vp = (tmp[2], tmp[3])
                        for half in range(2):
                            bs_ = self.cs("bsel", 0, 64)[:, half * 8:(half + 1) * 8].unsqueeze(2).to_broadcast([64, 8, 64])
                            for (dst_, src_ap, src_t) in ((Up, uh(h)[0], uh(h)[1]), (Vp, Vtok[0:64, h * 64:(h + 1) * 64], Vtok)):
                                S.op("pool", lambda e: e.tensor_tensor(
                                    out=dst_[half][0:64, 0:512].rearrange("p (s v) -> p s v", s=8),
                                    in0=src_ap.unsqueeze(1).to_broadcast([64, 8, 64]), in1=bs_, op=ALU.mult),
                                    rd=[src_t, cst], wr=[dst_[half]])
                            S.op("pe", lambda e: e.matmul(pH[half][r0:r0 + 64, 0:512], lhsT=Btok[0:64, h * 64:(h + 1) * 64], rhs=Up[half][0:64, 0:512],
                                                          start=True, stop=False), rd=[Btok, Up[half]], wr=[pH[half]])
                            S.op("pe", lambda e: e.matmul(pH[half][r0:r0 + 64, 0:512], lhsT=Ktok[0:64, h * 64:(h + 1) * 64], rhs=Vp[half][0:64, 0:512],
                                                          start=False, stop=True), rd=[Ktok, Vp[half]], wr=[pH[half]])
                    else:
                        S.op("pe", lambda e: e.matmul(pH[0][r0:r0 + 64, 0:64], lhsT=Btok[0:64, h * 64:(h + 1) * 64], rhs=uh(h)[0],
                                                      start=True, stop=False), rd=[Btok, uh(h)[1]], wr=[pH[0]])
                        S.op("pe", lambda e: e.matmul(pH[0][r0:r0 + 64, 0:64], lhsT=Ktok[0:64, h * 64:(h + 1) * 64], rhs=Vtok[0:64, h * 64:(h + 1) * 64],
                                                      start=False, stop=True), rd=[Ktok, Vtok], wr=[pH[0]])
                    wC = Wi[pr][r0:r0 + 64, cols].rearrange("p (s i) -> p s i", i=C)[:, :, C - 1]
                    if ns > 1:
                        for half in range(2):
                            hs2 = Hv[r0:r0 + 64, pr, half * 8:(half + 1) * 8, :]
                            S.op("dve", lambda e: e.tensor_tensor(out=hs2, in0=hs2, in1=pH[half][r0:r0 + 64, 0:512].rearrange("p (s v) -> p s v", s=8),
                                                                  op=ALU.add), rd=[Ht, pH[half]], wr=[Ht])
                        for si in range(ns):
                            hs1 = Hv[r0:r0 + 64, pr, si, :]
                            S.op("dve", lambda e: e.tensor_scalar(hs1, hs1, wC[:, si:si + 1], None, ALU.mult), rd=[Ht, Wi[pr]], wr=[Ht])
                    else:
                        hsl = Hv[r0:r0 + 64, pr, seqs[0], :]
                        S.op("dve", lambda e: e.tensor_tensor(out=hsl, in0=hsl, in1=pH[0][r0:r0 + 64, 0:64], op=ALU.add), rd=[Ht, pH[0]], wr=[Ht])
                        S.op("dve", lambda e: e.tensor_scalar(hsl, hsl, Wi[pr][r0:r0 + 64, c0 + 63:c0 + 64], None, ALU.mult), rd=[Ht, Wi[pr]], wr=[Ht])
        if last:
            og = self.o_rw_s if samp else self.o_rw_p
            S.dma("act", og[l], Hv, rd=[Ht])
        if "rw" not in self.mixers:
            return
        for pr in range(2):
            y = yall[pr]
            psM, psV = mp[0], mp[1]
            sq = sc[19]
            S.op("pe", lambda e: e.matmul(psM[:, 0:NT], lhsT=self.cs("blk64"), rhs=y[:, 0:NT], start=True, stop=True), rd=[cst, y], wr=[psM])
            S.op("act", lambda e: e.activation(out=sq[:, 0:NT], in_=y[:, 0:NT], func=AF.Square), rd=[y], wr=[sq])
            S.op("pe", lambda e: e.matmul(psV[:, 0:NT], lhsT=self.cs("blk64"), rhs=sq[:, 0:NT], start=True, stop=True), rd=[cst, sq], wr=[psV])
            mean, var = sc[0], sc[1]
            S.op("act", lambda e: e.activation(out=mean[:, 0:NT], in_=psM[:, 0:NT], func=AF.Copy, scale=1.0 / 64), rd=[psM], wr=[mean])
            S.op("dve", lambda e: e.tensor_tensor(out=var[:, 0:NT], in0=mean[:, 0:NT], in1=mean[:, 0:NT], op=ALU.mult), rd=[mean], wr=[var])
            S.op("dve", lambda e: e.scalar_tensor_tensor(out=var[:, 0:NT], in0=psV[:, 0:NT], scalar=1.0 / 64, in1=var[:, 0:NT],
                                                         op0=ALU.mult, op1=ALU.subtract), rd=[psV, var], wr=[var])
            S.op("act", lambda e: e.activation(out=var[:, 0:NT], in_=var[:, 0:NT], func=AF.Sqrt, bias=64e-5), rd=[var], wr=[var])
            S.op("dve", lambda e: e.reciprocal(var[:, 0:NT], var[:, 0:NT]), rd=[var], wr=[var])
            S.op("dve", lambda e: e.tensor_tensor(out=y[:, 0:NT], in0=y[:, 0:NT], in1=mean[:, 0:NT], op=ALU.subtract), rd=[y, mean], wr=[y])
            S.op("dve", lambda e: e.tensor_tensor(out=y[:, 0:NT], in0=y[:, 0:NT], in1=var[:, 0:NT], op=ALU.mult), rd=[y, var], wr=[y])
            S.op("dve", lambda e: e.tensor_scalar(y[:, 0:NT], y[:, 0:NT], ppc("rw_ln_w", pr), ppc("rw_ln_b", pr), ALU.mult, ALU.add), rd=[y, pp], wr=[y])
            S.op("dve", lambda e: e.tensor_tensor(out=y[:, 0:NT], in0=y[:, 0:NT], in1=bonus[pr][:, 0:NT], op=ALU.add), rd=[y, bonus[pr]], wr=[y])
            S.op("dve", lambda e: e.tensor_tensor(out=self.mix[pr][:, 0:NT], in0=y[:, 0:NT], in1=gate[pr][:, 0:NT], op=ALU.mult),
                 rd=[y, gate[pr]], wr=[self.mix[pr]])

    def gdn(self, g, st, l, NT, nseq, T, last):
        S = self.S
        samp = (g == "s")
        mp, sc, cst, tmp, pp = self.mp, self.sc, self.cst, self.tmp, self.pp[l]
        sfx = "s" if samp else "p"
        import os
        STOP = float(os.environ.get("KGDN_STOP", "99"))
        v3 = lambda ap: ap.rearrange("p (s t) -> p s t", t=T)
        if samp:
            cb = sc[17]
            S.dma("sp", cb[:, 0:6 * nseq * 3].rearrange("p (c s t) -> p c s t", c=6, s=nseq), self.conv_s[l], wr=[cb])
            stH = self.stage[self.wi]
            self.wi ^= 1
            S.dma("sp", stH[:, 0:2 * nseq * 64].rearrange("p (a s v) -> p a s v", a=2, s=nseq), self.gdn_s[l], wr=[stH])
            Ht = stH
        else:
            cb = self.gcb[l]
            Ht = self.gH[l]
        cbv = cb[:, 0:6 * nseq * 3].rearrange("p (c s t) -> p c s t", c=6, s=nseq)
        Hv = Ht[:, 0:2 * nseq * 64].rearrange("p (a s v) -> p a s v", a=2, s=nseq)
        for c in range(6):
            x = self.p[10 + c]
            xv = v3(x[:, 0:NT])
            acc = tmp[c]
            av = v3(acc[:, 0:NT])
            w = lambda i: pp[:, PPO["gdn_conv_w"] + i * 6 + c:PPO["gdn_conv_w"] + i * 6 + c + 1]
            S.op("dve", lambda e: e.tensor_scalar(acc[:, 0:NT], x[:, 0:NT], w(3), None, ALU.mult), rd=[x, pp], wr=[acc])
            for i in (1, 2, 3):
                S.op("dve", lambda e: e.scalar_tensor_tensor(out=av[:, :, i:T], in0=xv[:, :, 0:T - i], scalar=w(3 - i), in1=av[:, :, i:T],
                                                             op0=ALU.mult, op1=ALU.add), rd=[x, pp, acc], wr=[acc])
                S.op("dve", lambda e: e.scalar_tensor_tensor(out=av[:, :, 0:i], in0=cbv[:, c, :, 3 - i:3], scalar=w(3 - i), in1=av[:, :, 0:i],
                                                             op0=ALU.mult, op1=ALU.add), rd=[cb, pp, acc], wr=[acc])
            if last:
                oc_ = self.o_conv_s if samp else self.o_conv_p
                for t_ in range(3):
                    S.dma("act", oc_[l][:, t_, c * 128:(c + 1) * 128].rearrange("s p -> p s"), xv[:, :, T - 3 + t_], rd=[x])
            if not samp:
                S.op("dve", lambda e: e.tensor_copy(cbv[:, c, :, :], xv[:, :, T - 3:T]), rd=[x], wr=[cb])
            S.op("act", lambda e: e.activation(out=acc[:, 0:NT], in_=acc[:, 0:NT], func=AF.Silu), rd=[acc], wr=[acc])
        if STOP <= 1:
            return
        rs = sc[16]
        for c in range(4):
            self.head_rms(tmp[c], NT, None, None, rs)
            if c < 2:
                S.op("dve", lambda e: e.scalar_tensor_tensor(out=tmp[c][:, 0:NT], in0=tmp[c][:, 0:NT], scalar=0.125, in1=rs[:, 0:NT],
                                                             op0=ALU.mult, op1=ALU.mult), rd=[tmp[c], rs], wr=[tmp[c]])
            else:
                S.op("dve", lambda e: e.tensor_tensor(out=tmp[c][:, 0:NT], in0=tmp[c][:, 0:NT], in1=rs[:, 0:NT], op=ALU.mult),
                     rd=[tmp[c], rs], wr=[tmp[c]])
        qT, kT, vT = tmp[0:2], tmp[2:4], tmp[4:6]
        if STOP <= 2:
            return
        bg = sc[15]
        bg2 = sc[14]
        S.op("act", lambda e: e.activation(out=bg[0:8, 0:NT], in_=self.p[16][0:8, 0:NT], func=AF.Sigmoid), rd=[self.p[16]], wr=[bg])
        S.op("act", lambda e: e.activation(out=bg2[0:8, 0:NT], in_=self.p[16][0:8, 0:NT], func=AF.Exp,
                                           bias=pp[0:8, PPO["gdn_dtb"]:PPO["gdn_dtb"] + 1]), rd=[self.p[16], pp], wr=[bg2])
        S.op("act", lambda e: e.activation(out=bg2[0:8, 0:NT], in_=bg2[0:8, 0:NT], func=AF.Ln, bias=1.0), rd=[bg2], wr=[bg2])
        S.op("dve", lambda e: e.tensor_scalar(bg2[0:8, 0:NT], bg2[0:8, 0:NT], self.gnea[l][0:8, 0:1], None, ALU.mult),
             rd=[bg2, self.gnea[l]], wr=[bg2])
        S.op("dve", lambda e: e.tensor_tensor_scan(bg2[0:8, 0:NT], self.cs("rmask_" + sfx)[0:8, 0:NT], bg2[0:8, 0:NT], 0.0, ALU.mult, ALU.add),
             rd=[bg2, cst], wr=[bg2])
        oall = [sc[12], sc[13]]
        if STOP <= 3:
            return
        selrow = self.cs("selrow").rearrange("p (h n) -> p h n", h=4)
        selpair = self.cs("selpair").rearrange("p (a n) -> p a n", a=2)
        for (c0, seqs, C, ms, levels) in self.instances(g, nseq, T):
            cols = slice(c0, c0 + 64)
            ns = len(seqs)
            pt = mp[0]
            S.op("pe", lambda e: e.transpose(pt[0:64, 0:8], bg[0:8, cols], self.ident[0:8, 0:8]), rd=[bg, self.ident], wr=[pt])
            S.op("pe", lambda e: e.transpose(pt[0:64, 8:16], bg2[0:8, cols], self.ident[0:8, 0:8]), rd=[bg2, self.ident], wr=[pt])
            cT = sc[0]
            S.op("act", lambda e: e.activation(out=cT[0:64, 0:16], in_=pt[0:64, 0:16], func=AF.Copy), rd=[pt], wr=[cT])
            beta_c, gc_c = cT[0:64, 0:4], cT[0:64, 12:16]
            if STOP <= 4:
                continue
            pG = mp[1]
            for h in range(4):
                S.op("pe", lambda e: e.matmul(pG[:, h * 64:(h + 1) * 64], lhsT=selrow[0:8, h, :], rhs=bg2[0:8, cols], start=True, stop=True),
                     rd=[cst, bg2], wr=[pG])
            pGv = pG[0:64, 0:256].rearrange("p (h i) -> p h i", h=4)
            exG = sc[1]
            S.op("act", lambda e: e.activation(out=exG[:, 0:256], in_=pG[:, 0:256], func=AF.Exp), rd=[pG], wr=[exG])
            exGv = exG[:, 0:256].rearrange("p (h i) -> p h i", h=4)
            if STOP <= 4.2:
                continue
            E1, Da, Db = sc[2], sc[3], sc[4]
            gcB = gc_c.unsqueeze(2).to_broadcast([64, 4, 64])
            f3 = lambda t_: t_[0:64, 0:256].rearrange("p (h i) -> p h i", h=4)
            bm = lambda name: self.cs(name + "_" + ms, 0, 64).unsqueeze(1).to_broadcast([64, 4, 64])
            self.colop(E1, f3(E1), pGv, gc_c, ALU.subtract, [pG, cT])
            if STOP <= 4.4:
                continue
            S.op("dve", lambda e: e.tensor_scalar(Da[0:64, 0:256], E1[0:64, 0:256], 0.0, None, ALU.min), rd=[E1], wr=[Da])
            S.op("dve", lambda e: e.tensor_scalar(Db[0:64, 0:256], E1[0:64, 0:256], -1.0, 0.0, ALU.mult, ALU.min), rd=[E1], wr=[Db])
            if STOP <= 4.5:
                continue
            S.op("act", lambda e: e.activation(out=Da[0:64, 0:256], in_=Da[0:64, 0:256], func=AF.Exp), rd=[Da], wr=[Da])
            S.op("act", lambda e: e.activation(out=Db[0:64, 0:256], in_=Db[0:64, 0:256], func=AF.Exp), rd=[Db], wr=[Db])
            if STOP <= 4.6:
                continue
            S.op("dve", lambda e: e.tensor_tensor(out=f3(E1), in0=pGv, in1=self.cs("last_" + ms, 0, 64).unsqueeze(1).to_broadcast([64, 4, 64]),
                                                  op=ALU.mult), rd=[pG, cst], wr=[E1])
            S.op("dve", lambda e: e.tensor_reduce(out=cT[0:64, 16:20], in_=f3(E1), axis=AX.X, op=ALU.add), rd=[E1], wr=[cT])
            if STOP <= 4.8:
                continue
            S.op("dve", lambda e: e.tensor_tensor(out=cT[0:64, 20:24], in0=cT[0:64, 16:20], in1=gc_c, op=ALU.subtract), rd=[cT], wr=[cT])
            S.op("act", lambda e: e.activation(out=cT[0:64, 20:24], in_=cT[0:64, 20:24], func=AF.Exp), rd=[cT], wr=[cT])
            S.op("act", lambda e: e.activation(out=cT[0:64, 24:28], in_=gc_c, func=AF.Exp), rd=[cT], wr=[cT])
            S.op("dve", lambda e: e.scalar_tensor_tensor(out=cT[0:64, 28:32], in0=cT[0:64, 24:28], scalar=-1.0, in1=beta_c,
                                                         op0=ALU.mult, op1=ALU.mult), rd=[cT], wr=[cT])
            dec_c, nbg_c = cT[0:64, 20:24], cT[0:64, 28:32]
            if STOP <= 5:
                continue
            bkT = [sc[5], sc[6]]
            for pr in range(2):
                pb = mp[0]
                S.op("pe", lambda e: e.matmul(pb[:, 64:128], lhsT=selpair[0:8, pr, :], rhs=bg[0:8, cols], start=True, stop=True),
                     rd=[cst, bg], wr=[pb])
                S.op("dve", lambda e: e.tensor_tensor(out=bkT[pr][:, 0:64], in0=pb[:, 64:128], in1=kT[pr][:, cols], op=ALU.mult),
                     rd=[pb, kT[pr]], wr=[bkT[pr]])
            pA, pB = mp[2], mp[3]
            for h in range(4):
                pr, r0 = h // 2, (h % 2) * 64
                kh, bkh, qh = kT[pr][r0:r0 + 64, cols], bkT[pr][r0:r0 + 64, 0:64], qT[pr][r0:r0 + 64, cols]
                S.op("pe", lambda e: e.matmul(pA[0:64, h * 128:h * 128 + 64], lhsT=kh, rhs=bkh, start=True, stop=True),
                     rd=[kT[pr], bkT[pr]], wr=[pA])
                S.op("pe", lambda e: e.matmul(pA[0:64, h * 128 + 64:h * 128 + 128], lhsT=bkh, rhs=kh, start=True, stop=True),
                     rd=[kT[pr], bkT[pr]], wr=[pA])
                S.op("pe", lambda e: e.matmul(pB[0:64, h * 64:h * 64 + 64], lhsT=kh, rhs=qh, start=True, stop=True),
                     rd=[kT[pr], qT[pr]], wr=[pB])
            NN, AQ = self.nn_t[0], sc[8]
            pAv = pA[0:64, 0:512].rearrange("p (h n) -> p h n", h=4)
            nnv = NN[0:64, 0:512].rearrange("p (h n) -> p h n", h=4)
            S.op("dve", lambda e: e.tensor_tensor(out=f3(E1), in0=f3(Da), in1=bm("triU"), op=ALU.mult), rd=[Da, cst], wr=[E1])
            S.op("dve", lambda e: e.scalar_tensor_tensor(out=R_(nnv[:, :, 0:64]), in0=pAv[:, :, 0:64], scalar=-1.0, in1=f3(E1),
                                                         op0=ALU.mult, op1=ALU.mult), rd=[pA, E1], wr=[NN])
            S.op("dve", lambda e: e.tensor_tensor(out=f3(Db), in0=f3(Db), in1=bm("triL"), op=ALU.mult), rd=[Db, cst], wr=[Db])
            S.op("dve", lambda e: e.scalar_tensor_tensor(out=R_(nnv[:, :, 64:128]), in0=pAv[:, :, 64:128], scalar=-1.0, in1=f3(Db),
                                                         op0=ALU.mult, op1=ALU.mult), rd=[pA, Db], wr=[NN])
            S.op("dve", lambda e: e.tensor_tensor(out=f3(Da), in0=f3(Da), in1=bm("incU"), op=ALU.mult), rd=[Da, cst], wr=[Da])
            S.op("dve", lambda e: e.tensor_tensor(out=AQ[0:64, 0:256], in0=pB[0:64, 0:256], in1=Da[0:64, 0:256], op=ALU.mult),
                 rd=[pB, Da], wr=[AQ])
            if STOP <= 6:
                continue
            pT_ = mp[0]
            for pr in range(2):
                S.op("pe", lambda e: e.transpose(pT_[0:64, pr * 128:(pr + 1) * 128], vT[pr][:, cols], self.ident[:]),
                     rd=[vT[pr], self.ident], wr=[pT_])
                S.op("pe", lambda e: e.transpose(pT_[0:64, 256 + pr * 128:256 + (pr + 1) * 128], kT[pr][:, cols], self.ident[:]),
                     rd=[kT[pr], self.ident], wr=[pT_])
            Vb, Kd = sc[9], sc[10]
            self.colop(Vb, f3(Vb), pT_[0:64, 0:256].rearrange("p (h v) -> p h v", h=4), beta_c, ALU.mult, [pT_, cT])
            self.colop(Kd, f3(Kd), pT_[0:64, 256:512].rearrange("p (h v) -> p h v", h=4), dec_c, ALU.mult, [pT_, cT])
            if STOP <= 7:
                continue
            if ns > 1:
                kpad = [(sc[20], sc[21]), (sc[22], sc[23])]
                qpad = [(sc[24], sc[25]), (tmp[6], tmp[7])]
                padv = {}
                for pr in range(2):
                    for nm, src_t, tl in (("k", kT[pr], kpad[pr]), ("q", qT[pr], qpad[pr])):
                        for half in range(2):
                            eye = self.cs("eye16").rearrange("p (a b) -> p a b", a=16)[:, half * 8:(half + 1) * 8, :].unsqueeze(3).to_broadcast([128, 8, 16, 4])
                            in0 = src_t[:, cols].rearrange("p (b i) -> p b i", i=4).unsqueeze(1).to_broadcast([128, 8, 16, 4])
                            S.op("pool", lambda e: e.tensor_tensor(out=tl[half][:, 0:512].rearrange("p (a b i) -> p a b i", a=8, b=16),
                                                                   in0=in0, in1=eye, op=ALU.mult), rd=[src_t, cst], wr=[tl[half]])
                        padv[(nm, pr)] = tl
                lk = lambda nm, pr, r0, si: (padv[(nm, pr)][si // 8][r0:r0 + 64, (si % 8) * 64:(si % 8) * 64 + 64], [padv[(nm, pr)][si // 8]])
            else:
                lk = lambda nm, pr, r0, si: ((kT if nm == "k" else qT)[pr][r0:r0 + 64, cols], [(kT if nm == "k" else qT)[pr]])
            pK = mp[1]
            for h in range(4):
                pr, r0 = h // 2, (h % 2) * 64
                for si, s_ in enumerate(seqs):
                    a_, t_ = lk("k", pr, r0, si)
                    S.op("pe", lambda e: e.matmul(pK[0:64, h * 64:(h + 1) * 64], lhsT=a_, rhs=Hv[r0:r0 + 64, pr, s_, :],
                                                  start=(si == 0), stop=(si == ns - 1)), rd=t_ + [Ht], wr=[pK])
            X = self.nx_t[0]
            self.colop(E1, f3(E1), pK[0:64, 0:256].rearrange("p (h v) -> p h v", h=4), nbg_c, ALU.mult, [pK, cT])
            S.op("dve", lambda e: e.tensor_tensor(out=R_(X[0:64, 0:256]), in0=E1[0:64, 0:256], in1=Vb[0:64, 0:256], op=ALU.add),
                 rd=[E1, Vb], wr=[X])
            if STOP <= 8:
                continue
            uh = self.neumann(NN, X, levels)
            if STOP <= 9:
                continue
            pY1, pY2 = mp[0], mp[1]
            for h in range(4):
                pr, r0 = h // 2, (h % 2) * 64
                for si, s_ in enumerate(seqs):
                    a_, t_ = lk("q", pr, r0, si)
                    S.op("pe", lambda e: e.matmul(pY1[r0:r0 + 64, pr * 64:(pr + 1) * 64], lhsT=Hv[r0:r0 + 64, pr, s_, :], rhs=a_,
                                                  start=(si == 0), stop=(si == ns - 1)), rd=t_ + [Ht], wr=[pY1])
                S.op("pe", lambda e: e.matmul(pY2[r0:r0 + 64, pr * 64:(pr + 1) * 64], lhsT=uh(h)[0],
                                              rhs=AQ[0:64, h * 64:(h + 1) * 64], start=True, stop=True), rd=[uh(h)[1], AQ], wr=[pY2])
            for h in range(4):
                pr, r0 = h // 2, (h % 2) * 64
                S.op("dve", lambda e: e.tensor_tensor(out=oall[pr][r0:r0 + 64, cols], in0=pY1[r0:r0 + 64, pr * 64:(pr + 1) * 64],
                                                      in1=exGv[r0:r0 + 64, h, :], op=ALU.mult), rd=[pY1, exG], wr=[oall[pr]])
            for pr in range(2):
                S.op("dve", lambda e: e.tensor_tensor(out=oall[pr][:, cols], in0=pY2[:, pr * 64:(pr + 1) * 64], in1=oall[pr][:, cols], op=ALU.add),
                     rd=[pY2, oall[pr]], wr=[oall[pr]])
            if STOP <= 10:
                continue
            for pr in range(2):
                pH = [mp[2], mp[3]]
                for hh in range(2):
                    h = pr * 2 + hh
                    r0 = hh * 64
                    if ns > 1:
                        Up = (sc[5], sc[6]) if hh == 0 else (sc[9], sc[2])
                        for half in range(2):
                            S.op("pool", lambda e: e.tensor_tensor(
                                out=Up[half][0:64, 0:512].rearrange("p (s v) -> p s v", s=8),
                                in0=uh(h)[0].unsqueeze(1).to_broadcast([64, 8, 64]),
                                in1=self.cs("bsel", 0, 64)[:, half * 8:(half + 1) * 8].unsqueeze(2).to_broadcast([64, 8, 64]), op=ALU.mult),
                                rd=[uh(h)[1], cst], wr=[Up[half]])
                            S.op("pe", lambda e: e.matmul(pH[half][r0:r0 + 64, 0:512], lhsT=Kd[0:64, h * 64:(h + 1) * 64], rhs=Up[half][0:64, 0:512],
                                                          start=True, stop=True), rd=[Kd, Up[half]], wr=[pH[half]])
                    else:
                        S.op("pe", lambda e: e.matmul(pH[0][r0:r0 + 64, 0:64], lhsT=Kd[0:64, h * 64:(h + 1) * 64], rhs=uh(h)[0],
                                                      start=True, stop=True), rd=[Kd, uh(h)[1]], wr=[pH[0]])
                    gC = exGv[r0:r0 + 64, h, :].rearrange("p (s i) -> p s i", i=C)[:, :, C - 1]
                    if ns > 1:
                        for si in range(ns):
                            hs1 = Hv[r0:r0 + 64, pr, si, :]
                            S.op("dve", lambda e: e.tensor_scalar(hs1, hs1, gC[:, si:si + 1], None, ALU.mult), rd=[Ht, exG], wr=[Ht])
                        for half in range(2):
                            hs2 = Hv[r0:r0 + 64, pr, half * 8:(half + 1) * 8, :]
                            S.op("dve", lambda e: e.tensor_tensor(out=hs2, in0=hs2, in1=pH[half][r0:r0 + 64, 0:512].rearrange("p (s v) -> p s v", s=8),
                                                                  op=ALU.add), rd=[Ht, pH[half]], wr=[Ht])
                    else:
                        hsl = Hv[r0:r0 + 64, pr, seqs[0], :]
                        S.op("dve", lambda e: e.scalar_tensor_tensor(out=hsl, in0=hsl, scalar=exGv[r0:r0 + 64, h, 63:64], in1=pH[0][r0:r0 + 64, 0:64],
                                                                     op0=ALU.mult, op1=ALU.add), rd=[Ht, exG, pH[0]], wr=[Ht])
        if last:
            og = self.o_gdn_s if samp else self.o_gdn_p
            S.dma("act", og[l], Hv, rd=[Ht])
        if "gdn" not in self.mixers or STOP < 99:
            return
        for pr in range(2):
            self.head_rms(oall[pr], NT, 1.0 / 64, EPS, rs)
            S.op("dve", lambda e: e.scalar_tensor_tensor(out=oall[pr][:, 0:NT], in0=oall[pr][:, 0:NT], scalar=pp[:, PPO["gdn_nw"]:PPO["gdn_nw"] + 1],
                                                         in1=rs[:, 0:NT], op0=ALU.mult, op1=ALU.mult), rd=[oall[pr], pp, rs], wr=[oall[pr]])
            S.op("act", lambda e: e.activation(out=sc[0][:, 0:NT], in_=self.p[17 + pr][:, 0:NT], func=AF.Silu), rd=[self.p[17 + pr]], wr=[sc[0]])
            S.op("dve", lambda e: e.tensor_tensor(out=self.mix[4 + pr][:, 0:NT], in0=oall[pr][:, 0:NT], in1=sc[0][:, 0:NT], op=ALU.mult),
                 rd=[oall[pr], sc[0]], wr=[self.mix[4 + pr]])

    def s5_setup(self, l):
        S = self.S
        k = self.s5k[l]
        pp = self.pp[l]
        col = lambda i: k[:, i * 8:(i + 1) * 8]
        ppv = lambda n: pp[:, PPO[n]:PPO[n] + 8]
        D_, T1, MAG, TH, KF, SIN, COS, ABR, ABI, DEN, NR, CFR, CFI, NCFR, T2, T3 = [col(i) for i in range(16)]
        self.S5 = dict(MAG=2, TH=3, CFR=11, CFI=12, NCFR=13)
        ki = self.s5i[:, 0:8]
        r, w = [k, pp], [k]
        A = lambda eng, fn, rd=r, wr=w: S.op(eng, fn, rd=rd, wr=wr)
        A("act", lambda e: e.activation(out=D_, in_=ppv("s5_log_dt"), func=AF.Exp))
        A("dve", lambda e: e.tensor_tensor(out=T1, in0=D_, in1=ppv("s5_a_re"), op=ALU.mult))
        A("act", lambda e: e.activation(out=MAG, in_=T1, func=AF.Exp))
        A("dve", lambda e: e.tensor_tensor(out=TH, in0=D_, in1=ppv("s5_a_im"), op=ALU.mult))
        A("dve", lambda e: e.tensor_scalar(KF, TH, 1.0 / (2 * math.pi), None, ALU.mult))
        A("dve", lambda e: e.tensor_copy(ki, KF), rd=[k], wr=[self.s5i])
        A("dve", lambda e: e.tensor_copy(KF, ki), rd=[self.s5i], wr=[k])
        A("dve", lambda e: e.scalar_tensor_tensor(out=TH, in0=KF, scalar=-2 * math.pi, in1=TH, op0=ALU.mult, op1=ALU.add))
        A("dve", lambda e: e.tensor_scalar(TH, TH, -3.1415925, 3.1415925, ALU.max, ALU.min))
        A("act", lambda e: e.activation(out=SIN, in_=TH, func=AF.Sin))
        A("act", lambda e: e.activation(out=T2, in_=TH, func=AF.Abs))
        A("dve", lambda e: e.tensor_scalar(T2, T2, -1.0, math.pi / 2, ALU.mult, ALU.add))
        A("act", lambda e: e.activation(out=COS, in_=T2, func=AF.Sin))
        A("dve", lambda e: e.tensor_tensor(out=ABR, in0=MAG, in1=COS, op=ALU.mult))
        A("dve", lambda e: e.tensor_tensor(out=ABI, in0=MAG, in1=SIN, op=ALU.mult))
        A("dve", lambda e: e.tensor_tensor(out=DEN, in0=ppv("s5_a_re"), in1=ppv("s5_a_re"), op=ALU.mult))
        A("dve", lambda e: e.tensor_tensor(out=T2, in0=ppv("s5_a_im"), in1=ppv("s5_a_im"), op=ALU.mult))
        A("dve", lambda e: e.tensor_tensor(out=DEN, in0=DEN, in1=T2, op=ALU.add))
        A("dve", lambda e: e.reciprocal(DEN, DEN))
        A("dve", lambda e: e.tensor_scalar(NR, ABR, -1.0, None, ALU.add))
        A("dve", lambda e: e.tensor_tensor(out=T2, in0=NR, in1=ppv("s5_a_re"), op=ALU.mult))
        A("dve", lambda e: e.tensor_tensor(out=T3, in0=ABI, in1=ppv("s5_a_im"), op=ALU.mult))
        A("dve", lambda e: e.tensor_tensor(out=T2, in0=T2, in1=T3, op=ALU.add))
        A("dve", lambda e: e.tensor_tensor(out=CFR, in0=T2, in1=DEN, op=ALU.mult))
        A("dve", lambda e: e.tensor_tensor(out=T2, in0=ABI, in1=ppv("s5_a_re"), op=ALU.mult))
        A("dve", lambda e: e.tensor_tensor(out=T3, in0=NR, in1=ppv("s5_a_im"), op=ALU.mult))
        A("dve", lambda e: e.tensor_tensor(out=T2, in0=T2, in1=T3, op=ALU.subtract))
        A("dve", lambda e: e.tensor_tensor(out=CFI, in0=T2, in1=DEN, op=ALU.mult))
        A("dve", lambda e: e.tensor_scalar(NCFR, CFR, -1.0, None, ALU.mult))

    def s5(self, g, st, l, NT, nseq, T, last):
        S = self.S
        samp = (g == "s")
        mp, sc, cst = self.mp, self.sc, self.cst
        k = self.s5k[l]
        kc = lambda name, j: k[:, self.S5[name] * 8 + j:self.S5[name] * 8 + j + 1]
        v3 = lambda ap: ap.rearrange("p (s t) -> p s t", t=T)
        stgB, stgC = self.stage[0], self.stage[1]
        S.dma("sp", stgB[:, 0:2048], self.s5B[l], wr=[stgB])
        S.dma("sp", stgC[:, 0:2048], self.s5C[l], wr=[stgC])
        Bm = stgB[:, 0:2048].rearrange("p (a j n) -> p a j n", a=2, j=8)
        Cm = stgC[:, 0:2048].rearrange("p (a j n) -> p a j n", a=2, j=8)
        S.op("dve", lambda e: e.tensor_scalar(stgC[:, 1024:2048], stgC[:, 1024:2048], -1.0, None, ALU.mult), rd=[stgC], wr=[stgC])
        wg = sc[19]
        S.dma("sp", wg[:, 0:512].rearrange("p (k n) -> p k n", k=2), self.s5glu[l].rearrange("(k p) n -> p k n", p=128), wr=[wg])
        if samp:
            hre, him = sc[17], sc[18]
            S.dma("sp", hre[:, 0:8 * nseq].rearrange("p (j s) -> p j s", j=8), self.s5s_re[l], wr=[hre])
            S.dma("sp", him[:, 0:8 * nseq].rearrange("p (j s) -> p j s", j=8), self.s5s_im[l], wr=[him])
        else:
            hre, him = self.s5h[l]
        hv = lambda t: t[:, 0:8 * nseq].rearrange("p (j s) -> p j s", j=8)
        setA = [sc[0], sc[1], sc[2], sc[3], sc[4], sc[5], sc[6], sc[7], sc[8]]
        setB = [sc[12], sc[13], sc[14], sc[15], sc[16], sc[20], sc[21], sc[22], sc[23]]
        ei = self.s5i[:, 0:T]
        bc = lambda ap: ap.unsqueeze(1).to_broadcast([128, nseq, T])
        u = [self.p[8], self.p[9]]
        Y = [mp[2], mp[3]]
        zt = [sc[9], sc[10]]
        for j in range(8):
            ET, F, Z1, Z2, ZR, ZI, DK, HR, HI = setA if j % 2 == 0 else setB
            ER, EI = ET[:, 0:T], ET[:, 256:256 + T]
            S.op("act", lambda e: e.activation(out=Z1[:, 0:T], in_=self.cs("iota")[:, 0:T], func=AF.Copy, scale=kc("TH", j)),
                 rd=[cst, k], wr=[Z1])
            S.op("dve", lambda e: e.tensor_scalar(Z2[:, 0:T], Z1[:, 0:T], 1.0 / (2 * math.pi), None, ALU.mult), rd=[Z1], wr=[Z2])
            S.op("dve", lambda e: e.tensor_copy(ei, Z2[:, 0:T]), rd=[Z2], wr=[self.s5i])
            S.op("dve", lambda e: e.tensor_copy(Z2[:, 0:T], ei), rd=[self.s5i], wr=[Z2])
            S.op("dve", lambda e: e.scalar_tensor_tensor(out=Z1[:, 0:T], in0=Z2[:, 0:T], scalar=-2 * math.pi, in1=Z1[:, 0:T],
                                                          op0=ALU.mult, op1=ALU.add), rd=[Z1, Z2], wr=[Z1])
            S.op("dve", lambda e: e.tensor_scalar(Z1[:, 0:T], Z1[:, 0:T], -3.1415925, 3.1415925, ALU.max, ALU.min), rd=[Z1], wr=[Z1])
            S.op("act", lambda e: e.activation(out=EI, in_=Z1[:, 0:T], func=AF.Sin), rd=[Z1], wr=[ET])
            S.op("act", lambda e: e.activation(out=Z2[:, 0:T], in_=Z1[:, 0:T], func=AF.Abs), rd=[Z1], wr=[Z2])
            S.op("dve", lambda e: e.tensor_scalar(Z2[:, 0:T], Z2[:, 0:T], -1.0, math.pi / 2, ALU.mult, ALU.add), rd=[Z2], wr=[Z2])
            S.op("act", lambda e: e.activation(out=ER, in_=Z2[:, 0:T], func=AF.Sin), rd=[Z2], wr=[ET])
            FR, FI = F[:, 0:T], F[:, 256:256 + T]
            S.op("act", lambda e: e.activation(out=FR, in_=ER, func=AF.Copy, scale=kc("CFR", j)), rd=[ET, k], wr=[F])
            S.op("dve", lambda e: e.scalar_tensor_tensor(out=FR, in0=EI, scalar=kc("CFI", j), in1=FR, op0=ALU.mult, op1=ALU.add),
                 rd=[ET, k, F], wr=[F])
            S.op("act", lambda e: e.activation(out=FI, in_=ER, func=AF.Copy, scale=kc("CFI", j)), rd=[ET, k], wr=[F])
            S.op("dve", lambda e: e.scalar_tensor_tensor(out=FI, in0=EI, scalar=kc("NCFR", j), in1=FI, op0=ALU.mult, op1=ALU.add),
                 rd=[ET, k, F], wr=[F])
            pA, pB = (mp[0], mp[1]) if j % 2 == 0 else (mp[4], self.ps_lin[0])
            S.op("pe", lambda e: e.matmul(pA[:, 0:NT], lhsT=Bm[:, 0, j, :], rhs=u[j // 4][:, 0:NT], start=True, stop=True),
                 rd=[stgB, u[j // 4]], wr=[pA])
            S.op("pe", lambda e: e.matmul(pB[:, 0:NT], lhsT=Bm[:, 1, j, :], rhs=u[j // 4][:, 0:NT], start=True, stop=True),
                 rd=[stgB, u[j // 4]], wr=[pB])
            S.op("dve", lambda e: e.tensor_tensor(out=v3(Z1[:, 0:NT]), in0=v3(pA[:, 0:NT]), in1=bc(FR), op=ALU.mult), rd=[pA, F], wr=[Z1])
            S.op("dve", lambda e: e.tensor_tensor(out=v3(Z2[:, 0:NT]), in0=v3(pB[:, 0:NT]), in1=bc(FI), op=ALU.mult), rd=[pB, F], wr=[Z2])
            S.op("dve", lambda e: e.tensor_tensor(out=ZR[:, 0:NT], in0=Z1[:, 0:NT], in1=Z2[:, 0:NT], op=ALU.subtract), rd=[Z1, Z2], wr=[ZR])
            S.op("dve", lambda e: e.tensor_tensor(out=v3(Z1[:, 0:NT]), in0=v3(pB[:, 0:NT]), in1=bc(FR), op=ALU.mult), rd=[pB, F], wr=[Z1])
            S.op("dve", lambda e: e.tensor_tensor(out=v3(Z2[:, 0:NT]), in0=v3(pA[:, 0:NT]), in1=bc(FI), op=ALU.mult), rd=[pA, F], wr=[Z2])
            S.op("dve", lambda e: e.tensor_tensor(out=ZI[:, 0:NT], in0=Z1[:, 0:NT], in1=Z2[:, 0:NT], op=ALU.add), rd=[Z1, Z2], wr=[ZI])
            S.op("dve", lambda e: e.scalar_tensor_tensor(out=v3(ZR[:, 0:NT])[:, :, 0], in0=hv(hre)[:, j, :], scalar=kc("MAG", j),
                                                          in1=v3(ZR[:, 0:NT])[:, :, 0], op0=ALU.mult, op1=ALU.add),
                 rd=[hre, k, ZR], wr=[ZR])
            S.op("dve", lambda e: e.scalar_tensor_tensor(out=v3(ZI[:, 0:NT])[:, :, 0], in0=hv(him)[:, j, :], scalar=kc("MAG", j),
                                                          in1=v3(ZI[:, 0:NT])[:, :, 0], op0=ALU.mult, op1=ALU.add),
                 rd=[him, k, ZI], wr=[ZI])
            S.op("act", lambda e: e.activation(out=DK[:, 0:NT], in_=self.cs("rmask_p")[:, 0:NT], func=AF.Identity, scale=0.0, bias=kc("MAG", j)),
                 rd=[cst, k], wr=[DK])
            S.op("dve", lambda e: e.tensor_scalar(v3(DK[:, 0:NT])[:, :, 0], v3(DK[:, 0:NT])[:, :, 0], 0.0, None, ALU.mult), rd=[DK], wr=[DK])
            S.op("dve", lambda e: e.tensor_tensor_scan(Z1[:, 0:NT], DK[:, 0:NT], ZR[:, 0:NT], 0.0, ALU.mult, ALU.add), rd=[DK, ZR], wr=[Z1])
            S.op("dve", lambda e: e.tensor_tensor_scan(Z2[:, 0:NT], DK[:, 0:NT], ZI[:, 0:NT], 0.0, ALU.mult, ALU.add), rd=[DK, ZI], wr=[Z2])
            S.op("dve", lambda e: e.tensor_tensor(out=v3(ZR[:, 0:NT]), in0=v3(Z1[:, 0:NT]), in1=bc(ER), op=ALU.mult), rd=[Z1, ET], wr=[ZR])
            S.op("dve", lambda e: e.tensor_tensor(out=v3(ZI[:, 0:NT]), in0=v3(Z2[:, 0:NT]), in1=bc(EI), op=ALU.mult), rd=[Z2, ET], wr=[ZI])
            S.op("dve", lambda e: e.tensor_tensor(out=HR[:, 0:NT], in0=ZR[:, 0:NT], in1=ZI[:, 0:NT], op=ALU.subtract), rd=[ZR, ZI], wr=[HR])
            S.op("dve", lambda e: e.tensor_tensor(out=v3(ZR[:, 0:NT]), in0=v3(Z2[:, 0:NT]), in1=bc(ER), op=ALU.mult), rd=[Z2, ET], wr=[ZR])
            S.op("dve", lambda e: e.tensor_tensor(out=v3(ZI[:, 0:NT]), in0=v3(Z1[:, 0:NT]), in1=bc(EI), op=ALU.mult), rd=[Z1, ET], wr=[ZI])
            S.op("dve", lambda e: e.tensor_tensor(out=HI[:, 0:NT], in0=ZR[:, 0:NT], in1=ZI[:, 0:NT], op=ALU.add), rd=[ZR, ZI], wr=[HI])
            S.op("dve", lambda e: e.tensor_copy(hv(hre)[:, j, :], v3(HR[:, 0:NT])[:, :, T - 1]), rd=[HR], wr=[hre])
            S.op("dve", lambda e: e.tensor_copy(hv(him)[:, j, :], v3(HI[:, 0:NT])[:, :, T - 1]), rd=[HI], wr=[him])
            yy = Y[j // 4]
            S.op("pe", lambda e: e.matmul(yy[:, 0:NT], lhsT=Cm[:, 0, j, :], rhs=HR[:, 0:NT], start=(j % 4 == 0), stop=False),
                 rd=[stgC, HR], wr=[yy])
            S.op("pe", lambda e: e.matmul(yy[:, 0:NT], lhsT=Cm[:, 1, j, :], rhs=HI[:, 0:NT], start=False, stop=(j % 4 == 3)),
                 rd=[stgC, HI], wr=[yy])
        if last:
            ore, oim = (self.o_s5re_s, self.o_s5im_s) if samp else (self.o_s5re_p, self.o_s5im_p)
            S.dma("act", ore[l], hv(hre), rd=[hre])
            S.dma("act", oim[l], hv(him), rd=[him])
        if "s5" not in self.mixers:
            return
        for oc in range(2):
            S.op("dve", lambda e: e.scalar_tensor_tensor(out=sc[11][:, 0:NT], in0=u[oc][:, 0:NT], scalar=self.ppc(l, "s5_d", oc),
                                                         in1=Y[oc][:, 0:NT], op0=ALU.mult, op1=ALU.add),
                 rd=[u[oc], self.pp[l], Y[oc]], wr=[sc[11]])
            S.op("act", lambda e: e.activation(out=zt[oc][:, 0:NT], in_=sc[11][:, 0:NT], func=AF.Gelu_apprx_tanh), rd=[sc[11]], wr=[zt[oc]])
        wgv = wg[:, 0:512].rearrange("p (k n) -> p k n", k=2)
        for oc in range(2):
            pg = mp[oc]
            for kk_ in range(2):
                S.op("pe", lambda e: e.matmul(pg[:, 0:NT], lhsT=wgv[:, kk_, oc * 128:(oc + 1) * 128], rhs=zt[kk_][:, 0:NT],
                                              start=(kk_ == 0), stop=(kk_ == 1)), rd=[wg, zt[kk_]], wr=[pg])
            S.op("act", lambda e: e.activation(out=sc[11][:, 0:NT], in_=pg[:, 0:NT], func=AF.Sigmoid, bias=self.ppc(l, "s5_b_glu", oc)),
                 rd=[pg, self.pp[l]], wr=[sc[11]])
            S.op("dve", lambda e: e.tensor_tensor(out=self.mix[2 + oc][:, 0:NT], in0=zt[oc][:, 0:NT], in1=sc[11][:, 0:NT], op=ALU.mult),
                 rd=[zt[oc], sc[11]], wr=[self.mix[2 + oc]])

    def swa(self, g, st, l, NT, nseq, T, last):
        S = self.S
        samp = (g == "s")
        first = (not samp) and st == 0
        mp, sc, cst = self.mp, self.sc, self.cst
        do_mix = "swa" in self.mixers
        rt = sc[0]
        rsrc = self.ropeS if samp else self.ropeP[:, :, st * T:(st + 1) * T]
        S.dma("sp", rt[0:64, 0:2 * T].rearrange("p (a t) -> p a t", a=2), rsrc, wr=[rt])
        rtv = rt[0:64, 0:2 * T].rearrange("p (a t) -> p a t", a=2)
        cosb = rtv[:, 0, :].unsqueeze(1).to_broadcast([64, nseq, T])
        sinb = rtv[:, 1, :].unsqueeze(1).to_broadcast([64, nseq, T])
        selm = self.cs("selm").rearrange("p (a m) -> p a m", a=4)
        v3 = lambda ap: ap.rearrange("p (s t) -> p s t", t=T)
        ones64 = self.cs("ones")[:, 0:64]

        def rot(src, gsel, dst_ap, dst_t):
            pa, pb = mp[0], mp[1]
            S.op("pe", lambda e: e.matmul(pa[0:64, 0:NT], lhsT=selm[:, gsel, :], rhs=src[:, 0:NT], start=True, stop=True),
                 rd=[src, cst], wr=[pa])
            S.op("pe", lambda e: e.matmul(pb[0:64, 0:NT], lhsT=selm[:, 2 + gsel, :], rhs=src[:, 0:NT], start=True, stop=True),
                 rd=[src, cst], wr=[pb])
            S.op("dve", lambda e: e.tensor_tensor(out=v3(sc[1][0:64, 0:NT]), in0=v3(pa[0:64, 0:NT]), in1=cosb, op=ALU.mult),
                 rd=[pa, rt], wr=[sc[1]])
            S.op("dve", lambda e: e.tensor_tensor(out=v3(sc[2][0:64, 0:NT]), in0=v3(pb[0:64, 0:NT]), in1=sinb, op=ALU.mult),
                 rd=[pb, rt], wr=[sc[2]])
            S.op("dve", lambda e: e.tensor_tensor(out=dst_ap, in0=sc[1][0:64, 0:NT], in1=sc[2][0:64, 0:NT], op=ALU.add),
                 rd=[sc[1], sc[2]], wr=dst_t)

        vtok = [sc[3], sc[4]]
        if samp:
            stg = self.stage[0]
            vhis = stg[:, 0:2048].rearrange("p (s f) -> p s f", f=128)
            S.dma("sp", vhis, self.vc_s[l].rearrange("s j f -> j s f"), wr=[stg])
            S.op("pe", lambda e: e.transpose(mp[0][0:NT, 0:128], self.p[22][:, 0:NT], self.ident[:]),
                 rd=[self.p[22], self.ident], wr=[mp[0]])
            S.op("act", lambda e: e.activation(out=sc[3][0:NT, 0:128], in_=mp[0][0:NT, 0:128], func=AF.Copy),
                 rd=[mp[0]], wr=[sc[3]])
            if last:
                S.dma("act", self.o_swav_s[l, :, 0:124, :], self.vc_s[l, :, 4:128, :])
                for s_ in range(nseq):
                    S.dma("act", self.o_swav_s[l, s_, 124:128, :], sc[3][s_ * 4:s_ * 4 + 4, 0:128], rd=[sc[3]])
        else:
            def vblk(s_, b_):
                i = s_ * 3 + b_
                return vtok[i // 4][:, (i % 4) * 128:(i % 4 + 1) * 128], vtok[i // 4]
            for s_ in range(nseq):
                if not first:
                    a, t = vblk(s_, 0)
                    S.op("dve", lambda e: e.tensor_copy(a, self.vhist[l][:, s_ * 128:(s_ + 1) * 128]),
                         rd=[self.vhist[l]], wr=[t])
                for b_ in range(T // 128):
                    a, t = vblk(s_, 1 + b_)
                    c0 = s_ * T + b_ * 128
                    S.op("pe", lambda e: e.transpose(mp[0][:, 0:128], self.p[22][:, c0:c0 + 128], self.ident[:]),
                         rd=[self.p[22], self.ident], wr=[mp[0]])
                    S.op("act", lambda e: e.activation(out=a, in_=mp[0][:, 0:128], func=AF.Copy), rd=[mp[0]], wr=[t])
                a, t = vblk(s_, T // 128)
                S.op("dve", lambda e: e.tensor_copy(self.vhist[l][:, s_ * 128:(s_ + 1) * 128], a),
                     rd=[t], wr=[self.vhist[l]])
                if last:
                    S.dma("act", self.o_swav_p[l, s_, :, :], a, rd=[t])

        for kvh in range(2):
            qrot = [sc[5], sc[6]]
            knew = sc[7]
            for gg in range(2):
                rot(self.p[19 + kvh], gg, qrot[gg][0:64, 0:NT], [qrot[gg]])
            rot(self.p[21], kvh, knew[0:64, 0:NT], [knew])
            num_t = sc[10]
            if samp:
                stg_k = self.stage[1]
                khis = stg_k[0:64, 0:2048].rearrange("p (s j) -> p s j", j=128)
                S.dma("sp", khis, self.kc_s[l, :, kvh, :, :].rearrange("s d j -> d s j"), wr=[stg_k])
                if last:
                    S.dma("act", self.o_swak_s[l, :, kvh, :, 0:124], self.kc_s[l, :, kvh, :, 4:128])
                    S.dma("act", self.o_swak_s[l, :, kvh, :, 124:128].rearrange("s d t -> d s t"),
                          v3(knew[0:64, 0:NT]), rd=[knew])
                if not do_mix:
                    continue
                psS = mp[2]
                for s_ in range(nseq):
                    for gg in range(2):
                        S.op("pe", lambda e: e.matmul(psS[:, s_ * 8 + gg * 4:s_ * 8 + gg * 4 + 4], lhsT=khis[:, s_, :],
                                                      rhs=qrot[gg][0:64, s_ * 4:s_ * 4 + 4], start=True, stop=True),
                             rd=[stg_k, qrot[gg]], wr=[psS])
                for gg in range(2):
                    S.op("pe", lambda e: e.matmul(psS[0:64, 128 + gg * 64:128 + gg * 64 + 64], lhsT=knew[0:64, 0:NT],
                                                  rhs=qrot[gg][0:64, 0:NT], start=True, stop=True),
                         rd=[knew, qrot[gg]], wr=[psS])
                pT = sc[8]
                S.op("act", lambda e: e.activation(out=pT[:, 0:128], in_=psS[:, 0:128], func=AF.Exp, scale=0.125),
                     rd=[psS], wr=[pT])
                S.op("act", lambda e: e.activation(out=pT[0:64, 128:256], in_=psS[0:64, 128:256], func=AF.Exp, scale=0.125),
                     rd=[psS], wr=[pT])
                S.op("dve", lambda e: e.tensor_tensor(out=pT[:, 0:128], in0=pT[:, 0:128], in1=self.cs("mask_sh"), op=ALU.mult),
                     rd=[pT, cst], wr=[pT])
                S.op("dve", lambda e: e.tensor_tensor(out=pT[0:64, 128:256], in0=pT[0:64, 128:256], in1=self.cs("mask_sn", 0, 64),
                                                      op=ALU.mult), rd=[pT, cst], wr=[pT])
                psA = mp[3]
                for s_ in range(nseq):
                    for gg in range(2):
                        r_ = pT[:, s_ * 8 + gg * 4:s_ * 8 + gg * 4 + 4]
                        S.op("pe", lambda e: e.matmul(psA[gg * 64:gg * 64 + 64, s_ * 4:s_ * 4 + 4],
                                                      lhsT=vhis[:, s_, kvh * 64:kvh * 64 + 64], rhs=r_, start=True, stop=True),
                             rd=[stg, pT], wr=[psA])
                        S.op("pe", lambda e: e.matmul(psA[gg * 64:gg * 64 + 64, 64 + s_ * 4:64 + s_ * 4 + 4],
                                                      lhsT=ones64, rhs=r_, start=True, stop=True),
                             rd=[cst, pT], wr=[psA])
                for gg in range(2):
                    r_ = pT[0:64, 128 + gg * 64:128 + gg * 64 + 64]
                    S.op("pe", lambda e: e.matmul(psA[gg * 64:gg * 64 + 64, 128:192], lhsT=sc[3][0:64, kvh * 64:kvh * 64 + 64],
                                                  rhs=r_, start=True, stop=True), rd=[sc[3], pT], wr=[psA])
                    S.op("pe", lambda e: e.matmul(psA[gg * 64:gg * 64 + 64, 192:256], lhsT=ones64[0:64, :], rhs=r_,
                                                  start=True, stop=True), rd=[cst, pT], wr=[psA])
                S.op("act", lambda e: e.activation(out=sc[9][:, 0:128], in_=psA[:, 128:256], func=AF.Copy), rd=[psA], wr=[sc[9]])
                S.op("dve", lambda e: e.tensor_tensor(out=num_t[:, 0:128], in0=psA[:, 0:128], in1=sc[9][:, 0:128], op=ALU.add),
                     rd=[psA, sc[9]], wr=[num_t])
                num_ap, den_ap, nd_t = num_t[:, 0:64], num_t[:, 64:128], [num_t]
            else:
                kall = [sc[11], sc[12]]
                for s_ in range(nseq):
                    if not first:
                        S.op("dve", lambda e: e.tensor_copy(kall[s_][0:64, 0:128], self.khist[l][kvh][:, s_ * 128:(s_ + 1) * 128]),
                             rd=[self.khist[l][kvh]], wr=[kall[s_]])
                    S.op("dve", lambda e: e.tensor_copy(kall[s_][0:64, 128:128 + T], knew[0:64, s_ * T:(s_ + 1) * T]),
                         rd=[knew], wr=[kall[s_]])
                    S.op("dve", lambda e: e.tensor_copy(self.khist[l][kvh][:, s_ * 128:(s_ + 1) * 128], kall[s_][0:64, T:T + 128]),
                         rd=[kall[s_]], wr=[self.khist[l][kvh]])
                    if last:
                        S.dma("act", self.o_swak_p[l, s_, kvh, :, :], kall[s_][0:64, T:T + 128], rd=[kall[s_]])
                if not do_mix:
                    continue
                psN, psD = mp[3], mp[4]
                mask_p = self.cs("mask_p")
                for s_ in range(nseq):
                    for b_ in range(T // 128):
                        tok0 = s_ * T + b_ * 128
                        kts = []
                        if not (first and b_ == 0):
                            kts.append((0, b_ * 128, vblk(s_, b_)))
                        kts.append((1, (b_ + 1) * 128, vblk(s_, b_ + 1)))
                        psS = mp[2]
                        for (mi, k0, _) in kts:
                            for gg in range(2):
                                S.op("pe", lambda e: e.matmul(psS[:, mi * 256 + gg * 128:mi * 256 + gg * 128 + 128],
                                                              lhsT=kall[s_][0:64, k0:k0 + 128], rhs=qrot[gg][0:64, tok0:tok0 + 128],
                                                              start=True, stop=True), rd=[kall[s_], qrot[gg]], wr=[psS])
                        c0 = kts[0][0] * 256
                        pT = sc[8 + (b_ % 2)]
                        S.op("act", lambda e: e.activation(out=pT[:, c0:512], in_=psS[:, c0:512], func=AF.Exp, scale=0.125),
                             rd=[psS], wr=[pT])
                        nm_ = (512 - c0) // 256
                        pTv = pT[:, c0:512].rearrange("p (m g i) -> p m g i", g=2, i=128)
                        mkv = mask_p[:, c0 // 2:256].rearrange("p (m i) -> p m i", i=128).unsqueeze(2).to_broadcast([128, nm_, 2, 128])
                        S.op("dve", lambda e: e.tensor_tensor(out=pTv, in0=pTv, in1=mkv, op=ALU.mult), rd=[pT, cst], wr=[pT])
                        for gg in range(2):
                            for ki, (mi, k0, (va, vt)) in enumerate(kts):
                                r_ = pT[:, mi * 256 + gg * 128:mi * 256 + gg * 128 + 128]
                                S.op("pe", lambda e: e.matmul(psN[gg * 64:gg * 64 + 64, tok0:tok0 + 128], lhsT=va[:, kvh * 64:kvh * 64 + 64],
                                                              rhs=r_, start=(ki == 0), stop=(ki == len(kts) - 1)), rd=[vt, pT], wr=[psN])
                            for ki, (mi, k0, (va, vt)) in enumerate(kts):
                                r_ = pT[:, mi * 256 + gg * 128:mi * 256 + gg * 128 + 128]
                                S.op("pe", lambda e: e.matmul(psD[gg * 64:gg * 64 + 64, tok0:tok0 + 128], lhsT=ones64, rhs=r_,
                                                              start=(ki == 0), stop=(ki == len(kts) - 1)), rd=[cst, pT], wr=[psD])
                num_ap, den_ap, nd_t = psN[:, 0:NT], psD[:, 0:NT], [psN, psD]
            dt_ = sc[13]
            S.op("dve", lambda e: e.tensor_scalar(dt_[:, 0:NT], den_ap, self.esink[l][:, kvh:kvh + 1], None, ALU.add),
                 rd=nd_t + [self.esink[l]], wr=[dt_])
            S.op("dve", lambda e: e.reciprocal(dt_[:, 0:NT], dt_[:, 0:NT]), rd=[dt_], wr=[dt_])
            S.op("dve", lambda e: e.tensor_tensor(out=self.mix[6 + kvh][:, 0:NT], in0=num_ap, in1=dt_[:, 0:NT], op=ALU.mult),
                 rd=nd_t + [dt_], wr=[self.mix[6 + kvh]])

    def run_group(self, g, st):
        S = self.S
        if g == "s":
            NT = self.NSS * 4
            src = self.xT_s
            cols = [(0, NT, 0)]
            dst = self.yT_s
        else:
            NT = self.NSP * TSTEP
            src = self.xT_p
            dst = self.yT_p
            cols = [(q * self.SEQ + st * TSTEP, TSTEP, q * TSTEP) for q in range(self.NSP)]
        for c in range(8):
            for (d0, n, s0) in cols:
                S.dma("sp", self.x[c][:, s0:s0 + n], src[c * 128:(c + 1) * 128, d0:d0 + n], wr=[self.x[c]])
        for l in range(DEPTH):
            self.layer(g, st, l, NT)
        for c in range(8):
            for (d0, n, s0) in cols:
                S.dma("act", dst[c * 128:(c + 1) * 128, d0:d0 + n], self.x[c][:, s0:s0 + n], rd=[self.x[c]])

    def layer(self, g, st, l, NT):
        S = self.S
        nseq = self.NSS if g == "s" else self.NSP
        T = 4 if g == "s" else TSTEP
        last = (g == "s") or (st == self.nsteps - 1)
        self.prenorm(l, "g_mix_pre", NT)

        def cons_p(tag, pst, m):
            S.op("act", lambda e: e.activation(out=self.p[tag][0:m, 0:NT], in_=pst[0:m, 0:NT], func=AF.Copy),
                 rd=[pst], wr=[self.p[tag]])

        self.linear(self.w_in[l], 8, WIN_TILES, lambda k: (self.xn[k][:, 0:NT], [self.xn[k]]), NT, cons_p, wkey=(l, "in"))
        if last:
            o = self.o_shift_s if g == "s" else self.o_shift_p
            for c in range(8):
                src = self.p[c][:, 0:NT].rearrange("p (s t) -> p s t", t=T)[:, :, T - 1]
                S.dma("act", o[l, c * 128:(c + 1) * 128, :], src, rd=[self.p[c]])
        for nm, cs_ in (("rw", (0, 1)), ("s5", (2, 3)), ("gdn", (4, 5)), ("swa", (6, 7))):
            if nm not in self.mixers:
                for c in cs_:
                    S.op("pool", lambda e: e.memset(self.mix[c][:, 0:NT], 0.0), wr=[self.mix[c]])
        import os
        skip = os.environ.get("KSKIP", "").split(",")
        if "rw" not in skip:
            self.rwkv(g, st, l, NT, nseq, T, last)
        if "gdn" not in skip:
            self.gdn(g, st, l, NT, nseq, T, last)
        if "s5" not in skip:
            self.s5(g, st, l, NT, nseq, T, last)
        if "swa" not in skip:
            self.swa(g, st, l, NT, nseq, T, last)
        def cons_t(tag, pst, m):
            S.op("act", lambda e: e.activation(out=self.tmp[tag][:, 0:NT], in_=pst[:, 0:NT], func=AF.Copy),
                 rd=[pst], wr=[self.tmp[tag]])

        t_out = [(0, 512, [(i * 128, (i + 1) * 128, i) for i in range(4)]),
                 (512, 1024, [(i * 128, (i + 1) * 128, i) for i in range(4, 8)])]
        self.linear(self.w_out[l], 8, t_out, lambda k: (self.mix[k][:, 0:NT], [self.mix[k]]), NT, cons_t, wkey=(l, "out"))
        self.postnorm_residual(l, "g_mix_post", NT)
        self.prenorm(l, "g_mlp_pre", NT)
        hb = lambda i: self.p[i // 2][:, :].bitcast(BF16)[:, (i % 2) * 512:(i % 2) * 512 + NT]

        def cons_h(tag, pst, m):
            S.op("act", lambda e: e.activation(out=self.sq[0][:, 0:NT], in_=pst[:, 0:NT], func=AF.Relu),
                 rd=[pst], wr=[self.sq[0]])
            S.op("dve", lambda e: e.tensor_tensor(out=hb(tag), in0=self.sq[0][:, 0:NT], in1=self.sq[0][:, 0:NT],
                                                  op=ALU.mult), rd=[self.sq[0]], wr=[self.p[tag // 2]])

        t_up = [(j * 512, (j + 1) * 512, [(j * 512 + i * 128, j * 512 + (i + 1) * 128, j * 4 + i) for i in range(4)])
                for j in range(8)]
        self.linear(self.w_up[l], 8, t_up, lambda k: (self.xn[k][:, 0:NT], [self.xn[k]]), NT, cons_h, wkey=(l, "up"))
        t_dn = [(i * 128, (i + 1) * 128, [(i * 128, (i + 1) * 128, i)]) for i in range(8)]
        self.linear(self.w_down[l], 32, t_dn, lambda k: (hb(k), [self.p[k // 2]]), NT, cons_t, wkey=(l, "down"))
        self.postnorm_residual(l, "g_mlp_post", NT)


PPO = {}
NPP = 0


def _ppdef(name, n):
    global NPP
    PPO[name] = NPP
    NPP += n


for _n in ("g_mix_pre", "g_mix_post", "g_mlp_pre", "g_mlp_post", "rw_mu"):
    _ppdef(_n, 8)
_ppdef("sink", 2)
for _n in ("s5_a_re", "s5_a_im", "s5_log_dt"):
    _ppdef(_n, 8)
for _n in ("rw_w0", "rw_a0", "rw_kk", "rw_ka", "rw_rk", "rw_ln_w", "rw_ln_b"):
    _ppdef(_n, 2)
_ppdef("gdn_conv_w", 24)
_ppdef("gdn_nw", 1)
_ppdef("gdn_dtb", 1)
_ppdef("gdn_alog", 1)
_ppdef("s5_d", 2)
_ppdef("s5_b_glu", 2)


def colmajor(v):
    v = np.asarray(v, np.float32).reshape(-1, 128)
    return np.ascontiguousarray(v.T)


def pack_pp(inp):
    pp = np.zeros((DEPTH, 128, NPP), np.float32)
    for l in range(DEPTH):
        for n in ("g_mix_pre", "g_mix_post", "g_mlp_pre", "g_mlp_post", "rw_mu"):
            pp[l, :, PPO[n]:PPO[n] + 8] = colmajor(inp[n][l])
        g2 = lambda a: np.asarray(a, np.float32).reshape(8, 2, 64).transpose(1, 2, 0).reshape(128, 8)
        pp[l, :, PPO["s5_a_re"]:PPO["s5_a_re"] + 8] = g2(inp["s5_a_re"][l])
        pp[l, :, PPO["s5_a_im"]:PPO["s5_a_im"] + 8] = g2(inp["s5_a_im"][l])
        pp[l, :, PPO["s5_log_dt"]:PPO["s5_log_dt"] + 8] = g2(np.repeat(np.asarray(inp["s5_log_dt"][l])[:, None], 64, 1))
        pp[l, :, PPO["s5_d"]:PPO["s5_d"] + 2] = colmajor(inp["s5_d"][l])
        pp[l, :, PPO["s5_b_glu"]:PPO["s5_b_glu"] + 2] = colmajor(inp["s5_b_glu"][l])
        for n in ("rw_w0", "rw_a0", "rw_kk", "rw_ka", "rw_rk", "rw_ln_w", "rw_ln_b"):
            pp[l, :, PPO[n]:PPO[n] + 2] = colmajor(np.asarray(inp[n][l], np.float32).reshape(-1))
        cw = np.asarray(inp["gdn_conv_w"][l], np.float32)
        for i_ in range(4):
            pp[l, :, PPO["gdn_conv_w"] + i_ * 6:PPO["gdn_conv_w"] + i_ * 6 + 6] = colmajor(cw[i_])
        pp[l, :, PPO["gdn_nw"]] = np.tile(np.asarray(inp["gdn_norm_w"][l], np.float32), 2)
        pp[l, 4:8, PPO["gdn_dtb"]] = np.asarray(inp["gdn_dt_bias"][l], np.float32)
        pp[l, 4:8, PPO["gdn_alog"]] = np.asarray(inp["gdn_a_log"][l], np.float32)
        sk = np.asarray(inp["swa_sinks"][l], np.float32)
        pp[l, :, PPO["sink"]:PPO["sink"] + 2] = np.repeat(sk.reshape(2, 2, 1), 64, axis=2).reshape(2, 128).T
    return pp


CSO = {}
NCST = 0


def _cdef(name, n):
    global NCST
    CSO[name] = (NCST, n)
    NCST += n


for _n, _k in (("selm", 256), ("blk64", 128), ("mask_p", 256), ("mask_sh", 128), ("mask_sn", 128), ("ones", 128), ("iota", 256),
               ("triL_p", 64), ("triU_p", 64), ("incU_p", 64), ("triL_s", 64), ("triU_s", 64), ("incU_s", 64),
               ("last_p", 64), ("last_s", 64), ("eye16", 256), ("bsel", 16), ("rmask_p", 512), ("rmask_s", 64),
               ("selrow", 512), ("selpair", 256)):
    _cdef(_n, _k)


def build_cst():
    c = np.zeros((128, NCST), np.float32)

    def put(name, arr):
        o, n = CSO[name]
        arr = np.asarray(arr, np.float32).reshape(arr.shape[0], -1)
        assert arr.shape[1] == n, (name, arr.shape, n)
        c[:arr.shape[0], o:o + n] = arr

    selm = np.zeros((128, 4, 64), np.float32)
    for g in range(2):
        for m in range(64):
            selm[g * 64 + m, g, m] = 1.0
            if m < 8:
                selm[g * 64 + m + 8, 2 + g, m] = -1.0
            elif m < 16:
                selm[g * 64 + m - 8, 2 + g, m] = 1.0
    put("selm", selm)
    blk = np.zeros((128, 128), np.float32)
    blk[:64, :64] = 1
    blk[64:, 64:] = 1
    put("blk64", blk)
    j = np.arange(128)[:, None]
    i = np.arange(128)[None, :]
    mp = np.zeros((128, 2, 128), np.float32)
    mp[:, 0, :] = (j > i)
    mp[:, 1, :] = (j <= i)
    put("mask_p", mp)
    msh = np.zeros((128, 16, 2, 4), np.float32)
    msh[:] = (np.arange(128)[:, None, None, None] > np.arange(4)[None, None, None, :])
    put("mask_sh", msh)
    msn = np.zeros((64, 2, 16, 4), np.float32)
    for sp in range(16):
        for jp in range(4):
            for ii in range(4):
                if jp <= ii:
                    msn[sp * 4 + jp, :, sp, ii] = 1.0
    put("mask_sn", msn)
    put("ones", np.ones((128, 128), np.float32))
    put("iota", np.tile(np.arange(1, 257, dtype=np.float32)[None, :], (128, 1)))
    a64 = np.arange(64)
    for sfx, C in (("p", 64), ("s", 4)):
        same = (a64[:, None] // C) == (a64[None, :] // C)
        put("triL_" + sfx, (same & (a64[None, :] < a64[:, None])).astype(np.float32))
        put("triU_" + sfx, (same & (a64[:, None] < a64[None, :])).astype(np.float32))
        put("incU_" + sfx, (same & (a64[:, None] <= a64[None, :])).astype(np.float32))
        lastm = np.zeros((128, 64), np.float32)
        lastm[:, :] = ((a64[None, :] % C) == C - 1)
        lastm[:64] *= same.astype(np.float32)
        lastm[64:] *= same.astype(np.float32)
        put("last_" + sfx, lastm)
    put("eye16", np.tile(np.eye(16, dtype=np.float32).reshape(1, 256), (128, 1)))
    bs = np.zeros((128, 16), np.float32)
    for pp_ in range(64):
        bs[pp_, pp_ // 4] = 1.0
    put("bsel", bs)
    rm = np.ones((128, 512), np.float32)
    rm[:, ::64] = 0.0
    put("rmask_p", rm)
    rm = np.ones((128, 64), np.float32)
    rm[:, ::4] = 0.0
    put("rmask_s", rm)
    sr = np.zeros((128, 4, 128), np.float32)
    for h in range(4):
        sr[4 + h, h, :] = 1.0
    put("selrow", sr)
    spr = np.zeros((128, 2, 128), np.float32)
    for pr in range(2):
        spr[2 * pr, pr, 0:64] = 1.0
        spr[2 * pr + 1, pr, 64:128] = 1.0
    put("selpair", spr)
    return c


def rope_tables(pos):
    inv = (np.float32(500000.0) ** (-np.arange(0, 16, 2, dtype=np.float32) / np.float32(16))).astype(np.float32)
    ang = pos.astype(np.float32)[None, :] * inv[:, None]
    t = np.zeros((64, 2, len(pos)), np.float32)
    t[:, 0, :] = 1.0
    t[0:8, 0, :] = np.cos(ang)
    t[8:16, 0, :] = np.cos(ang)
    t[0:8, 1, :] = np.sin(ang)
    t[8:16, 1, :] = np.sin(ang)
    return t


_CACHE = {}


def run(inp, n_cores, nsp, seq, nss, mixers=("rw", "s5", "gdn", "swa")):
    key = (n_cores, nsp, seq, nss, tuple(mixers))
    if key not in _CACHE:
        k = Kern(nsp, seq, nss, mixers)
        k.build()
        _CACHE[key] = k
    k = _CACHE[key]
    f = lambda a: np.ascontiguousarray(np.asarray(a, np.float32))
    pp = pack_pp(inp)
    cst = build_cst()
    ropeP = rope_tables(np.arange(seq))
    ropeS = rope_tables(PAST + np.arange(4))
    s5B = np.zeros((DEPTH, 128, 2, 8, 128), np.float32)
    s5C = np.zeros((DEPTH, 128, 2, 8, 128), np.float32)
    for ri, (bn, cn) in enumerate((("s5_b_re", "s5_c_re"), ("s5_b_im", "s5_c_im"))):
        bb = f(inp[bn])
        cc = f(inp[cn])
        for gi in range(16):
            j, r0, c0 = gi // 2, (gi % 8) * 16, (gi % 2) * 64
            s5B[:, r0:r0 + 16, ri, j, c0:c0 + 64] = bb[:, gi].transpose(0, 2, 1)
            s5C[:, c0:c0 + 64, ri, j, r0:r0 + 16] = cc[:, gi].transpose(0, 2, 1)
    s5B = s5B.reshape(DEPTH, 128, -1)
    s5C = s5C.reshape(DEPTH, 128, -1)
    s5lay = lambda a: np.ascontiguousarray(a.reshape(DEPTH, -1, 8, 2, 64).transpose(0, 3, 4, 2, 1).reshape(DEPTH, 128, 8, -1))
    s5inv = lambda a: a.reshape(DEPTH, 2, 64, 8, -1).transpose(0, 4, 3, 1, 2).reshape(DEPTH, -1, 16, 64)
    hlay = lambda a: np.ascontiguousarray(a.reshape(DEPTH, -1, 2, 2, 64, 64).transpose(0, 3, 5, 2, 1, 4).reshape(DEPTH, 128, 2, -1, 64))
    hinv = lambda a: a.reshape(DEPTH, 2, 64, 2, -1, 64).transpose(0, 4, 3, 1, 5, 2).reshape(DEPTH, -1, 4, 64, 64)
    in_maps = []
    for c in range(n_cores):
        xp = f(inp["x_prompt"][c * nsp:(c + 1) * nsp]).reshape(nsp * seq, D)
        xs = f(inp["x_sample"][c * nss:(c + 1) * nss]).reshape(nss * 4, D)
        in_maps.append({
            "xT_p": np.ascontiguousarray(xp.T), "xT_s": np.ascontiguousarray(xs.T),
            "w_in": f(inp["w_in"]), "w_out": f(inp["w_out"]), "w_up": f(inp["w_up"]), "w_down": f(inp["w_down"]),
            "rw_s": hlay(f(inp["state_rwkv"][:, c * nss:(c + 1) * nss])),
            "rwsh_s": np.ascontiguousarray(f(inp["state_rwkv_shift"][:, c * nss:(c + 1) * nss]).reshape(DEPTH, nss, 8, 128).transpose(0, 3, 2, 1)),
            "rw_w2": f(inp["rw_w2"]), "rw_a2": f(inp["rw_a2"]), "rw_g2": f(inp["rw_g2"]),
            "gdn_s": hlay(f(inp["state_gdn"][:, c * nss:(c + 1) * nss])),
            "conv_s": np.ascontiguousarray(f(inp["state_gdn_conv"][:, c * nss:(c + 1) * nss]).reshape(DEPTH, nss, 3, 6, 128).transpose(0, 4, 3, 1, 2)),
            "s5B": s5B, "s5C": s5C, "s5glu": f(inp["s5_w_glu"]),
            "s5s_re": s5lay(f(inp["state_s5_re"][:, c * nss:(c + 1) * nss])),
            "s5s_im": s5lay(f(inp["state_s5_im"][:, c * nss:(c + 1) * nss])),
            "pp": pp, "cst": cst, "ropeP": ropeP, "ropeS": ropeS,
            "kc_s": np.ascontiguousarray(f(inp["cache_swa_k"][:, c * nss:(c + 1) * nss]).transpose(0, 1, 3, 4, 2)),
            "vc_s": f(inp["cache_swa_v"][:, c * nss:(c + 1) * nss]).reshape(DEPTH, nss, 128, 128),
        })
    import os
    if os.environ.get("KTRACE"):
        res = run_bass_kernel_spmd(k.nc, in_maps, core_ids=list(range(n_cores)), trace=True)
        print("EXEC_TIME_NS", res.exec_time_ns)
    else:
        res = run_bass_kernel_spmd(k.nc, in_maps, core_ids=list(range(n_cores)))
    R = res.results
    cat = lambda fn: np.concatenate([fn(r) for r in R], axis=0)
    y_p = cat(lambda r: r["yT_p"].T.reshape(nsp, seq, D))
    y_s = cat(lambda r: r["yT_s"].T.reshape(nss, 4, D))
    catb = lambda fn: np.concatenate([fn(r) for r in R], axis=1)
    shift_p = catb(lambda r: r["o_shift_p"].transpose(0, 2, 1)[:, :, None, :])
    shift_s = catb(lambda r: r["o_shift_s"].transpose(0, 2, 1)[:, :, None, :])
    swak_p = catb(lambda r: r["o_swak_p"].transpose(0, 1, 4, 2, 3))
    swak_s = catb(lambda r: r["o_swak_s"].transpose(0, 1, 4, 2, 3))
    swav_p = catb(lambda r: r["o_swav_p"].reshape(DEPTH, nsp, 128, 2, 64))
    swav_s = catb(lambda r: r["o_swav_s"].reshape(DEPTH, nss, 128, 2, 64))
    s5o = {n: catb(lambda r: s5inv(r["o_" + n])) for n in ("s5re_p", "s5im_p", "s5re_s", "s5im_s")}
    rwo = dict(rw_p=catb(lambda r: hinv(r["o_rw_p"])), rw_s=catb(lambda r: hinv(r["o_rw_s"])))
    gdo = dict(**rwo, gdn_p=catb(lambda r: hinv(r["o_gdn_p"])), gdn_s=catb(lambda r: hinv(r["o_gdn_s"])),
               conv_p=catb(lambda r: r["o_conv_p"]), conv_s=catb(lambda r: r["o_conv_s"]))
    return dict(y_p=y_p, y_s=y_s, shift_p=shift_p, shift_s=shift_s, swak_p=swak_p, **s5o, **gdo, swak_s=swak_s,
                swav_p=swav_p, swav_s=swav_s)


OUT_NAMES = ["y_p", "y_s", "rw_p", "rw_s", "shift_p", "shift_s", "s5re_p", "s5re_s", "s5im_p", "s5im_s",
             "gdn_p", "gdn_s", "conv_p", "conv_s", "swak_p", "swak_s", "swav_p", "swav_s"]


def out_shapes(B, BS):
    L = DEPTH
    return [(B, 2048, D), (BS, 4, D), (L, B, 4, 64, 64), (L, BS, 4, 64, 64), (L, B, 1, 1024), (L, BS, 1, 1024),
            (L, B, 16, 64), (L, BS, 16, 64), (L, B, 16, 64), (L, BS, 16, 64), (L, B, 4, 64, 64), (L, BS, 4, 64, 64),
            (L, B, 3, 768), (L, BS, 3, 768), (L, B, 128, 2, 64), (L, BS, 128, 2, 64), (L, B, 128, 2, 64),
            (L, BS, 128, 2, 64)]


def kernel(**inp):
    o = run(inp, 8, 2, 2048, 16)
    outs = []
    for n, shp in zip(OUT_NAMES, out_shapes(16, 128)):
        if n in o:
            outs.append(np.ascontiguousarray(o[n], dtype=np.float32).reshape(shp))
        else:
            outs.append(np.zeros(shp, np.float32))
    return tuple(outs)
```
